# Optimizing a Trainium2 kernel written in Bass

```python
import math
import jax, jax.numpy as jnp
from jax import lax
import numpy as np

D_MODEL = 1024
BATCH = 8
SEQ = 2048
DEPTH = 1
DEC_BATCH = 128
DEC_SEQ = 4
PAST_LEN = 16384
PAGE_SIZE = 128

D_MIX = D_MODEL
HEAD_DIM = 64
D_ATTN = D_MIX // 2
N_HEADS = D_ATTN // HEAD_DIM
N_KV_HEADS = 2
N_REP = N_HEADS // N_KV_HEADS
D_KV = N_KV_HEADS * HEAD_DIM
D_SSM = D_MIX - D_ATTN
SSM_GROUP = 16
N_SSM_GROUPS = D_SSM // SSM_GROUP
SSM_STATE = 64
WINDOW = 128
BLOCK = WINDOW
NUM_BUCKETS = 32
MAX_DISTANCE = 128
D_FF = 2816
D_IN = D_ATTN + 2 * D_KV + D_SSM
RMS_EPS = 1e-6
NEG_INF = -1e30
DT_MIN = 0.001
DT_MAX = 0.1

kernel_name = 'hymba_swa_sink_s5_macaron_step'


def rmsnorm(x, g):
    xf = x.astype(jnp.float32)
    r = lax.rsqrt(jnp.mean(xf * xf, axis=-1, keepdims=True) + RMS_EPS)
    return (xf * r).astype(x.dtype) * g


def macaron_half_ffn(x, g, wg, wu, wd):
    h = rmsnorm(x, g)
    return x + 0.5 * ((jax.nn.silu(h @ wg) * (h @ wu)) @ wd)


def t5_bucket(d):
    d = jnp.maximum(d, 0)
    max_exact = NUM_BUCKETS // 2
    df = jnp.maximum(d, 1).astype(jnp.float32)
    large = max_exact + (jnp.log(df / max_exact) / math.log(MAX_DISTANCE / max_exact)
                         * (NUM_BUCKETS - max_exact)).astype(jnp.int32)
    large = jnp.minimum(large, NUM_BUCKETS - 1)
    return jnp.where(d < max_exact, d, large)


def rel_bias_for(d, rel_bias):
    b = rel_bias[t5_bucket(d)].astype(jnp.float32)
    return jnp.transpose(b, (2, 0, 1)).reshape(N_KV_HEADS, N_REP, d.shape[0], d.shape[1])


def sink_probs(logits, sinks):
    s = sinks.astype(jnp.float32).reshape(N_KV_HEADS, N_REP)[:, :, None, None]
    m = jnp.maximum(jnp.max(logits, axis=-1, keepdims=True), s)
    e = jnp.exp(logits - m)
    return e / (jnp.sum(e, axis=-1, keepdims=True) + jnp.exp(s - m))


def split_projection(h, w_in):
    b, l = h.shape[:2]
    p = h @ w_in
    q = p[..., :D_ATTN].reshape(b, l, N_HEADS, HEAD_DIM)
    o = D_ATTN
    k = p[..., o:o + D_KV].reshape(b, l, N_KV_HEADS, HEAD_DIM)
    o += D_KV
    v = p[..., o:o + D_KV].reshape(b, l, N_KV_HEADS, HEAD_DIM)
    o += D_KV
    u = p[..., o:]
    return q, k, v, u


def swa_prompt(q, k, v, rel_bias, sinks):
    b, l = q.shape[:2]
    nb = l // BLOCK
    qb = q.reshape(b, nb, BLOCK, N_KV_HEADS, N_REP, HEAD_DIM)

    def band(t):
        cur = t.reshape(b, nb, BLOCK, N_KV_HEADS, HEAD_DIM)
        prev = jnp.concatenate([jnp.zeros_like(cur[:, :1]), cur[:, :-1]], axis=1)
        return jnp.concatenate([prev, cur], axis=2)

    kb, vb = band(k), band(v)
    logits = jnp.einsum('bnqgrd,bnkgd->bngrqk', qb, kb).astype(jnp.float32) * (HEAD_DIM ** -0.5)
    qi = jnp.arange(BLOCK)[:, None]
    kj = jnp.arange(2 * BLOCK)[None, :]
    d = qi - kj + BLOCK
    blk = jnp.arange(nb)[:, None, None]
    valid = (d >= 0) & (d < WINDOW) & ((blk > 0) | (kj >= BLOCK))
    logits = logits + rel_bias_for(d, rel_bias)
    logits = jnp.where(valid[:, None, None], logits, NEG_INF)
    p = sink_probs(logits, sinks)
    out = jnp.einsum('bngrqk,bnkgd->bnqgrd', p.astype(vb.dtype), vb)
    return out.reshape(b, l, D_ATTN)


def swa_sample(q, k_new, v_new, cache_k, cache_v, rel_bias, sinks):
    db, t = q.shape[:2]
    w = cache_k.shape[1]
    k_all = jnp.concatenate([cache_k.astype(k_new.dtype), k_new], axis=1)
    v_all = jnp.concatenate([cache_v.astype(v_new.dtype), v_new], axis=1)
    d = jnp.arange(t)[:, None] - jnp.arange(w + t)[None, :] + w
    valid = (d >= 0) & (d < WINDOW)
    qg = q.reshape(db, t, N_KV_HEADS, N_REP, HEAD_DIM)
    logits = jnp.einsum('bqgrd,bkgd->bgrqk', qg, k_all).astype(jnp.float32) * (HEAD_DIM ** -0.5)
    logits = logits + rel_bias_for(d, rel_bias)
    logits = jnp.where(valid, logits, NEG_INF)
    p = sink_probs(logits, sinks)
    out = jnp.einsum('bgrqk,bkgd->bqgrd', p.astype(v_all.dtype), v_all).reshape(db, t, D_ATTN)
    return out, k_all[:, t:], v_all[:, t:]


def s5_block(u, x0, log_dt, a_re, a_im, b_re, b_im, c_re, c_im, d_skip, w_glu, b_glu):
    f32 = jnp.float32
    bsz, l = u.shape[:2]
    uf = u.astype(f32)
    ug = uf.reshape(bsz, l, N_SSM_GROUPS, SSM_GROUP)
    lam = lax.complex(a_re.astype(f32), a_im.astype(f32))
    dt = jnp.exp(log_dt.astype(f32))[:, None]
    lam_bar = jnp.exp(lam * dt)
    b_mat = lax.complex(b_re.astype(f32), b_im.astype(f32))
    b_bar = ((lam_bar - 1.0) / lam)[..., None] * b_mat
    bu = jnp.einsum('blgc,gpc->blgp', ug.astype(jnp.complex64), b_bar)
    bu = bu.at[:, 0].add(lam_bar * x0)
    a = jnp.broadcast_to(lam_bar, bu.shape)

    def combine(e1, e2):
        a1, b1 = e1
        a2, b2 = e2
        return a1 * a2, a2 * b1 + b2

    _, xs = lax.associative_scan(combine, (a, bu), axis=1)
    c_mat = lax.complex(c_re.astype(f32), c_im.astype(f32))
    y = jnp.real(jnp.einsum('gcp,blgp->blgc', c_mat, xs)).reshape(bsz, l, D_SSM)
    y = y + d_skip.astype(f32) * uf
    y = jax.nn.gelu(y)
    y = y * jax.nn.sigmoid(y @ w_glu.astype(f32) + b_glu.astype(f32))
    x_last = xs[:, -1]
    return y.astype(u.dtype), jnp.real(x_last), jnp.imag(x_last)


def setup_inputs(seed: int = 0) -> dict:
    key = jax.random.key(seed)
    ks = iter(jax.random.split(key, 40))
    f32 = jnp.float32

    def nrm(shape, scale):
        return jax.random.normal(next(ks), shape, f32) * scale

    w_buf = min(WINDOW, PAST_LEN)
    L, G, P, C = DEPTH, N_SSM_GROUPS, SSM_STATE, SSM_GROUP
    inp = {}
    inp['x_prompt'] = nrm((BATCH, SEQ, D_MODEL), 1.0)
    inp['x_sample'] = nrm((DEC_BATCH, DEC_SEQ, D_MODEL), 1.0)
    inp['cache_k'] = nrm((L, DEC_BATCH, w_buf, N_KV_HEADS, HEAD_DIM), 1.0)
    inp['cache_v'] = nrm((L, DEC_BATCH, w_buf, N_KV_HEADS, HEAD_DIM), 1.0)
    inp['state_ssm_re'] = nrm((L, DEC_BATCH, G, P), 0.3)
    inp['state_ssm_im'] = nrm((L, DEC_BATCH, G, P), 0.3)
    inp['rel_bias'] = nrm((NUM_BUCKETS, N_HEADS), 0.5)
    inp['ffn1_norm'] = 1.0 + nrm((L, D_MODEL), 0.01)
    inp['ffn1_w_gate'] = nrm((L, D_MODEL, D_FF), D_MODEL ** -0.5)
    inp['ffn1_w_up'] = nrm((L, D_MODEL, D_FF), D_MODEL ** -0.5)
    inp['ffn1_w_down'] = nrm((L, D_FF, D_MODEL), D_FF ** -0.5)
    inp['mix_norm'] = 1.0 + nrm((L, D_MODEL), 0.01)
    inp['w_in'] = nrm((L, D_MODEL, D_IN), D_MODEL ** -0.5)
    inp['sinks'] = nrm((L, N_HEADS), 0.5)
    inp['log_dt'] = jax.random.uniform(next(ks), (L, G), f32, math.log(DT_MIN), math.log(DT_MAX))
    inp['a_re'] = -0.5 + nrm((L, G, P), 0.01)
    inp['a_im'] = math.pi * jnp.arange(P, dtype=f32) + nrm((L, G, P), 0.01)
    inp['b_re'] = nrm((L, G, P, C), (2.0 * C) ** -0.5)
    inp['b_im'] = nrm((L, G, P, C), (2.0 * C) ** -0.5)
    inp['c_re'] = nrm((L, G, C, P), (2.0 * P) ** -0.5)
    inp['c_im'] = nrm((L, G, C, P), (2.0 * P) ** -0.5)
    inp['d_skip'] = nrm((L, D_SSM), 0.5)
    inp['w_glu'] = nrm((L, D_SSM, D_SSM), D_SSM ** -0.5)
    inp['b_glu'] = nrm((L, D_SSM), 0.01)
    inp['w_out'] = nrm((L, D_MIX, D_MODEL), D_MIX ** -0.5)
    inp['ffn2_norm'] = 1.0 + nrm((L, D_MODEL), 0.01)
    inp['ffn2_w_gate'] = nrm((L, D_MODEL, D_FF), D_MODEL ** -0.5)
    inp['ffn2_w_up'] = nrm((L, D_MODEL, D_FF), D_MODEL ** -0.5)
    inp['ffn2_w_down'] = nrm((L, D_FF, D_MODEL), D_FF ** -0.5)
    inp['final_norm'] = 1.0 + nrm((D_MODEL,), 0.01)
    return inp


def reference(x_prompt, x_sample, cache_k, cache_v, state_ssm_re, state_ssm_im, rel_bias,
              ffn1_norm, ffn1_w_gate, ffn1_w_up, ffn1_w_down, mix_norm, w_in, sinks,
              log_dt, a_re, a_im, b_re, b_im, c_re, c_im, d_skip, w_glu, b_glu, w_out,
              ffn2_norm, ffn2_w_gate, ffn2_w_up, ffn2_w_down, final_norm):
    y_p, y_s = x_prompt, x_sample
    k_p_l, v_p_l, re_p_l, im_p_l = [], [], [], []
    k_s_l, v_s_l, re_s_l, im_s_l = [], [], [], []
    for i in range(DEPTH):
        ffn1 = (ffn1_norm[i], ffn1_w_gate[i], ffn1_w_up[i], ffn1_w_down[i])
        ffn2 = (ffn2_norm[i], ffn2_w_gate[i], ffn2_w_up[i], ffn2_w_down[i])
        ssm_w = (log_dt[i], a_re[i], a_im[i], b_re[i], b_im[i], c_re[i], c_im[i],
                 d_skip[i], w_glu[i], b_glu[i])

        y_p = macaron_half_ffn(y_p, *ffn1)
        q, k, v, u = split_projection(rmsnorm(y_p, mix_norm[i]), w_in[i])
        attn = swa_prompt(q, k, v, rel_bias, sinks[i])
        x0 = jnp.zeros((y_p.shape[0], N_SSM_GROUPS, SSM_STATE), jnp.complex64)
        ssm, s_re, s_im = s5_block(u, x0, *ssm_w)
        y_p = y_p + jnp.concatenate([attn, ssm], axis=-1) @ w_out[i]
        y_p = macaron_half_ffn(y_p, *ffn2)
        w_p = min(WINDOW, k.shape[1])
        k_p_l.append(k[:, k.shape[1] - w_p:])
        v_p_l.append(v[:, v.shape[1] - w_p:])
        re_p_l.append(s_re)
        im_p_l.append(s_im)

        y_s = macaron_half_ffn(y_s, *ffn1)
        q, k, v, u = split_projection(rmsnorm(y_s, mix_norm[i]), w_in[i])
        attn, k_buf, v_buf = swa_sample(q, k, v, cache_k[i], cache_v[i], rel_bias, sinks[i])
        x0 = lax.complex(state_ssm_re[i].astype(jnp.float32), state_ssm_im[i].astype(jnp.float32))
        ssm, s_re, s_im = s5_block(u, x0, *ssm_w)
        y_s = y_s + jnp.concatenate([attn, ssm], axis=-1) @ w_out[i]
        y_s = macaron_half_ffn(y_s, *ffn2)
        k_s_l.append(k_buf)
        v_s_l.append(v_buf)
        re_s_l.append(s_re)
        im_s_l.append(s_im)

    y_prompt = rmsnorm(y_p, final_norm)
    y_sample = rmsnorm(y_s, final_norm)
    return (y_prompt, y_sample,
            jnp.stack(k_p_l), jnp.stack(v_p_l), jnp.stack(re_p_l), jnp.stack(im_p_l),
            jnp.stack(k_s_l), jnp.stack(v_s_l), jnp.stack(re_s_l), jnp.stack(im_s_l))
```

```python
import contextlib
import numpy as np
import concourse.bass as bass
import concourse.mybir as mybir
from concourse.bass_utils import run_bass_kernel_spmd

F32 = mybir.dt.float32
BF16 = mybir.dt.bfloat16
I32 = mybir.dt.int32
AF = mybir.ActivationFunctionType
ALU = mybir.AluOpType

NCORES = 8
D = 1024
KC = 8
DFF = 2816
NJ = 22
NJH = 11
SEQ = 2048
NS = 64
NT = SEQ + NS
NWARM = 10
TT = [(0, 512), (512, 512), (1024, 512), (1536, 512), (2048, 64)]
EPS = 1e-6
ENGS = ('pe', 'act', 'dve', 'pool', 'sp')


class Prog:
    def __init__(self, nc):
        self.nc = nc
        self.q = {e: [] for e in ENGS}
        self.sem = {}
        self.cnt = {}
        self.waited = {e: {} for e in ENGS}
        self._stack = []
        self.last = {e: None for e in ENGS}

    def new_sem(self, name):
        cm = self.nc.semaphore(name)
        h = cm.__enter__()
        self._stack.append(cm)
        return h

    def _waits(self, eng, waits):
        out = []
        for w in waits:
            if w is None:
                continue
            sem, val = w
            key = id(sem)
            if self.waited[eng].get(key, 0) >= val:
                continue
            self.waited[eng][key] = val
            out.append((sem, val))
        return out

    def op(self, eng, fn, waits=(), signal=True):
        ws = self._waits(eng, waits)
        tok = None
        if signal:
            if eng not in self.sem:
                self.sem[eng] = self.new_sem('s_' + eng)
                self.cnt[eng] = 0
            self.cnt[eng] += 1
            tok = (self.sem[eng], self.cnt[eng])
            self.last[eng] = tok
        self.q[eng].append((fn, ws, tok, 1))
        return tok

    def dma(self, eng, fn, slot, waits=()):
        ws = self._waits(eng, waits)
        slot.count += 16
        tok = (slot.sem, slot.count)
        self.q[eng].append((fn, ws, tok, 16))
        return tok

    def wait_only(self, eng, waits):
        ws = self._waits(eng, waits)
        if ws:
            self.q[eng].append((None, ws, None, 0))

    def replay(self, eng, engine):
        for fn, ws, tok, inc in self.q[eng]:
            for sem, val in ws:
                engine.wait_ge(sem, val)
            if fn is None:
                continue
            ins = fn(engine)
            if tok is not None:
                ins.then_inc(tok[0], inc)

    def close(self):
        for cm in reversed(self._stack):
            cm.__exit__(None, None, None)


class DmaSlot:
    def __init__(self, prog, name):
        self.sem = prog.new_sem(name)
        self.count = 0


class Builder:
    def __init__(self, debug=None):
        self.debug = debug or {}
        self.nc = bass.Bass("TRN2", target_bir_lowering=False)
        self.P = Prog(self.nc)
        self.es = contextlib.ExitStack()
        self.dram = {}

    def din(self, name, shape, dt=F32):
        t = self.nc.dram_tensor(name, list(shape), dt, kind="ExternalInput").ap()
        self.dram[name] = t
        return t

    def dout(self, name, shape, dt=F32):
        t = self.nc.dram_tensor(name, list(shape), dt, kind="ExternalOutput").ap()
        self.dram[name] = t
        return t

    def sb(self, name, shape, dt):
        return self.es.enter_context(self.nc.sbuf_tensor(name, list(shape), dt))

    def slot(self, name):
        return DmaSlot(self.P, name)

    def ws_load(self, src_ap, shape_fn):
        P = self.P
        i = self.ws_next % self.WS_N
        self.ws_next += 1
        dst = shape_fn(self.ws[i])
        tok = P.dma('pool', lambda e, d=dst, s=src_ap: e.dma_start(out=d, in_=s), self.ws_slot[i],
                    waits=[self.ws_free[i]])
        return dst, tok, i

    def ws_release(self, i, tok):
        self.ws_free[i] = tok

    def build(self):
        nc, P = self.nc, self.P
        dbg = self.debug
        xT_d = self.din("xT", [128, KC, NT])
        norms_d = self.din("norms", [128, 4, KC])
        wgu_d = [self.din("wgu%d" % f, [NJ, 128, 2, KC, 128]) for f in (1, 2)]
        wd_d = [self.din("wd%d" % f, [2, KC, 128, NJH, 128]) for f in (1, 2)]
        yT_d = self.dout("yT", [128, KC, NT])
        self.mix_io()

        self.xT = self.sb("xT_s", [128, KC, NT], F32)
        self.hT = self.sb("hT_s", [128, KC, NT], BF16)
        self.aT = self.sb("aT_s", [128, NJH, NT], BF16)
        self.WS_N = 3
        self.ws = [self.sb("ws%d" % i, [128, 2048], BF16) for i in range(self.WS_N)]
        self.ws_slot = [self.slot("wsd%d" % i) for i in range(self.WS_N)]
        self.ws_free = [None] * self.WS_N
        self.ws_next = 0
        self.norms = self.sb("norms_s", [128, 4, KC], F32)
        self.ones = self.sb("ones_s", [128, 128], BF16)
        self.epst = self.sb("eps_s", [128, 1], F32)
        self.sq = [self.sb("sq%d" % i, [128, 512], BF16) for i in range(2)]
        self.sg = [self.sb("sg%d" % i, [128, 512], F32) for i in range(2)]
        self.rt = [self.sb("rt%d" % i, [128, 512], F32) for i in range(2)]
        self.ps = [self.es.enter_context(nc.psum_tensor("ps%d" % i, [128, 512], F32)) for i in range(8)]
        self.ps_rd = [None] * 8

        t_ones = P.op('dve', lambda e: e.memset(self.ones[:], 1.0 / D))
        t_eps = P.op('dve', lambda e: e.memset(self.epst[:], EPS))
        self.t_const = t_eps
        s_n = self.slot("d_norm")
        self.t_norms = P.dma('sp', lambda e: e.dma_start(out=self.norms[:], in_=norms_d), s_n)

        self.tok_x = {}
        for ti, (t0, n) in enumerate(TT):
            s = self.slot("d_x%d" % ti)
            tk = P.dma('sp', lambda e, t0=t0, n=n: e.dma_start(out=self.xT[:, :, t0:t0 + n], in_=xT_d[:, :, t0:t0 + n]), s,
                       waits=([self.tok_x[(0, 0)]] if ti == 1 else []))
            for kc in range(KC):
                self.tok_x[(kc, ti)] = tk
        self.tok_h = {}
        self.sq_rd = [None, None]
        self.sg_rd = [None, None]
        self.rt_rd = [None, None]
        self.sq_i = 0
        self.hT_free = None

        self.out_toks = []
        self.ffn(0, wgu_d[0], wd_d[0])
        if not dbg.get("skip_mixer"):
            self.mixer()
        self.yT_d = yT_d
        self.ffn(2, wgu_d[1], wd_d[1])
        self.final_norm(yT_d)

        with nc.Block() as block:
            @block.sync
            def _(e):
                P.replay('sp', e)

            @block.gpsimd
            def _(e):
                P.replay('pool', e)

            @block.tensor
            def _(e):
                P.replay('pe', e)

            @block.scalar
            def _(e):
                P.replay('act', e)

            @block.vector
            def _(e):
                P.replay('dve', e)
        P.close()
        self.es.close()
        return nc

    def norm(self, gi, final=False):
        for ti in range(len(TT)):
            self.norm_tile(gi, ti, final)

    def norm_tile(self, gi, ti, final=False):
        P = self.P
        xT, hT = self.xT, self.hT
        MS0 = 6
        toks = {}
        for ti, (t0, n) in [(ti, TT[ti])]:
            msb = MS0 + (ti % 2)
            ms = self.ps[msb]
            t_mm = None
            for kc in range(KC):
                b = self.sq_i % 2
                self.sq_i += 1
                t_sq = P.op('act', lambda e, b=b, kc=kc, t0=t0, n=n: e.activation(
                    out=self.sq[b][:, :n], in_=xT[:, kc, t0:t0 + n], func=AF.Square),
                    waits=[self.tok_x[(kc, ti)], self.sq_rd[b]])
                last = kc == KC - 1
                t_mm = P.op('pe', lambda e, b=b, kc=kc, n=n, ms=ms: e.matmul(
                    ms[:, :n], self.ones[:], self.sq[b][:, :n], start=(kc == 0), stop=(kc == KC - 1)),
                    waits=[t_sq, self.t_const] + ([self.ps_rd[msb]] if kc == 0 else []), signal=True)
                self.sq_rd[b] = t_mm
            rb = ti % 2
            rt = self.rt[rb]
            t_s = P.op('act', lambda e, n=n, ms=ms, rt=rt: e.activation(
                out=rt[:, :n], in_=ms[:, :n], func=AF.Ln, bias=self.epst[:, 0:1], scale=1.0),
                waits=[t_mm, self.rt_rd[rb], self.t_const])
            self.ps_rd[msb] = t_s
            t_r = P.op('act', lambda e, n=n, rt=rt: e.activation(
                out=rt[:, :n], in_=rt[:, :n], func=AF.Exp, scale=-0.5), waits=[t_s])
            t_h = None
            for kc in range(KC):
                if final:
                    t_h = P.op('dve', lambda e, kc=kc, t0=t0, n=n, rt=rt: e.scalar_tensor_tensor(
                        out=xT[:, kc, t0:t0 + n], in0=xT[:, kc, t0:t0 + n], scalar=self.norms[:, gi, kc:kc + 1],
                        in1=rt[:, :n], op0=ALU.mult, op1=ALU.mult),
                        waits=[t_r, self.t_norms, self.tok_x[(kc, ti)]])
                    self.tok_x[(kc, ti)] = t_h
                else:
                    t_h = P.op('dve', lambda e, kc=kc, t0=t0, n=n, rt=rt: e.scalar_tensor_tensor(
                        out=hT[:, kc, t0:t0 + n], in0=xT[:, kc, t0:t0 + n], scalar=self.norms[:, gi, kc:kc + 1],
                        in1=rt[:, :n], op0=ALU.mult, op1=ALU.mult),
                        waits=[t_r, self.t_norms, self.tok_x[(kc, ti)], self.hT_free])
                    self.tok_h[(kc, ti)] = t_h
            self.rt_rd[rb] = t_h
        return toks

    def ffn(self, gi, wgu_d, wd_d):
        P = self.P
        xT, hT, aT = self.xT, self.hT, self.aT
        LOOK = 2 if gi == 0 else 3
        normed = getattr(self, "normed_upto", {}).get(gi, 0)
        for ti in range(normed, LOOK):
            self.norm_tile(gi, ti)
        normed = max(normed, LOOK)
        GB, UB, YB = (0, 1), (2, 3), (4, 5, 0, 1)
        if not hasattr(self, "aT_rd"):
            self.aT_rd = None
        cnt = 0
        ycnt = 0
        for h in range(2):
            tok_a = {}
            for jj in range(NJH):
                j = h * NJH + jj
                wv, t_w, wi = self.ws_load(
                    wgu_d[j].rearrange("p g k n -> p (g k n)"),
                    lambda t: t[:, 0:2048])
                wv4 = wv.rearrange("p (g k n) -> p g k n", g=2, k=KC)
                t_last = None
                for ti, (t0, n) in enumerate(TT):
                    if h == 0 and jj == 0 and normed < len(TT):
                        self.norm_tile(gi, normed)
                        normed += 1
                    gb = GB[cnt % 2]
                    ub = UB[cnt % 2]
                    sgi = cnt % 2
                    cnt += 1
                    gps, ups = self.ps[gb], self.ps[ub]
                    for kc in range(KC):
                        t_g = P.op('pe', lambda e, kc=kc, t0=t0, n=n, gps=gps, wv4=wv4: e.matmul(
                            gps[:, :n], wv4[:, 0, kc, :], hT[:, kc, t0:t0 + n], start=(kc == 0), stop=(kc == KC - 1)),
                            waits=([t_w, self.ps_rd[gb]] if kc == 0 else []) + [self.tok_h[(kc, ti)]],
                            signal=(kc == KC - 1))
                    for kc in range(KC):
                        t_u = P.op('pe', lambda e, kc=kc, t0=t0, n=n, ups=ups, wv4=wv4: e.matmul(
                            ups[:, :n], wv4[:, 1, kc, :], hT[:, kc, t0:t0 + n], start=(kc == 0), stop=(kc == KC - 1)),
                            waits=([self.ps_rd[ub]] if kc == 0 else []),
                            signal=(kc == KC - 1))
                    t_last = t_u
                    sg = self.sg[sgi]
                    t_s = P.op('act', lambda e, n=n, gps=gps, sg=sg: e.activation(
                        out=sg[:, :n], in_=gps[:, :n], func=AF.Silu), waits=[t_g, self.sg_rd[sgi]])
                    self.ps_rd[gb] = t_s
                    t_a = P.op('dve', lambda e, jj=jj, t0=t0, n=n, ups=ups, sg=sg: e.tensor_tensor(
                        out=aT[:, jj, t0:t0 + n], in0=ups[:, :n], in1=sg[:, :n], op=ALU.mult),
                        waits=[t_s, t_u, self.aT_rd, getattr(self, 'dbg_tok', None)])
                    self.sg_rd[sgi] = t_a
                    self.ps_rd[ub] = t_a
                    tok_a[(jj, ti)] = t_a
                    self.bg_pump(2)
                self.ws_release(wi, t_last)
                if h == 0 and jj == 0 and gi == 0 and not self.debug.get("skip_mixer"):
                    self.mix_setup()
                    def chain():
                        yield from self._tables_gen()
                        if not self.debug.get("skip_ssm"):
                            yield from self.ssm_setup_gen(1)
                    self.bg = chain()
            if h == 1:
                self.hT_free = t_last
                if gi == 0 and not self.debug.get("skip_mixer") and not self.debug.get("skip_ssm"):
                    self.bg_pump(10 ** 9)
                    self.bg = self.ssm_setup_gen(2)
            if gi == 2 and h == 0 and getattr(self, "mix_end_tok", None) is not None and not self.debug.get("no_tail_opt"):
                stb = self.stage[:].rearrange("p a h q -> p (a h q)").bitcast(BF16)
                fl = lambda t: t[:].rearrange("p a b -> p (a b)")
                bufs = [stb[:, 0:1408], stb[:, 2048:3456], fl(self.cKT)[:, 0:1408], fl(self.cV)[:, 0:1408],
                        self.BT[:].rearrange("p a b c -> p (a b c)")[:, 0:1408]]
                self.res_wd = []
                for c_, bf in enumerate(bufs):
                    tk = P.dma('pool', lambda e, bf=bf, c_=c_: e.dma_start(out=bf, in_=wd_d[1, c_].rearrange("p j n -> p (j n)")),
                               self.slot("d_rwd%d" % c_), waits=[self.mix_end_tok])
                    self.res_wd.append((bf.rearrange("p (j n) -> p j n", j=NJH), tk))
            if gi == 2 and h == 1 and getattr(self, "res_wd", None):
                chunks = list(self.res_wd)
                ring = []
                for c in range(5, KC):
                    wv, t_w, wi = self.ws_load(wd_d[h, c].rearrange("p j n -> p (j n)"), lambda t: t[:, 0:NJH * 128])
                    chunks.append((wv.rearrange("p (j n) -> p j n", j=NJH), t_w))
                    ring.append(wi)
                t_y = None
                for ti, (t0, n) in enumerate(TT):
                    for c in range(KC):
                        wv3, t_w = chunks[c]
                        yb = YB[ycnt % 4]
                        ycnt += 1
                        yps = self.ps[yb]
                        for jj in range(NJH):
                            t_y = P.op('pe', lambda e, jj=jj, t0=t0, n=n, yps=yps, wv3=wv3: e.matmul(
                                yps[:, :n], wv3[:, jj, :], aT[:, jj, t0:t0 + n], start=(jj == 0), stop=(jj == NJH - 1)),
                                waits=([t_w, self.ps_rd[yb]] if jj == 0 else []) + [tok_a[(jj, ti)]],
                                signal=(jj == NJH - 1))
                        t_x = P.op('dve', lambda e, c=c, t0=t0, n=n, yps=yps: e.scalar_tensor_tensor(
                            out=xT[:, c, t0:t0 + n], in0=yps[:, :n], scalar=0.5, in1=xT[:, c, t0:t0 + n],
                            op0=ALU.mult, op1=ALU.add),
                            waits=[t_y, self.tok_x[(c, ti)]])
                        self.tok_x[(c, ti)] = t_x
                        self.ps_rd[yb] = t_x
                    self.final_tile(ti)
                for wi in ring:
                    self.ws_release(wi, t_y)
                self.aT_rd = t_y
                continue
            for c in range(KC):
                wv, t_w, wi = self.ws_load(
                    wd_d[h, c].rearrange("p j n -> p (j n)"),
                    lambda t: t[:, 0:NJH * 128])
                wv3 = wv.rearrange("p (j n) -> p j n", j=NJH)
                t_y = None
                for ti, (t0, n) in enumerate(TT):
                    yb = YB[ycnt % 4]
                    ycnt += 1
                    yps = self.ps[yb]
                    for jj in range(NJH):
                        t_y = P.op('pe', lambda e, jj=jj, t0=t0, n=n, yps=yps, wv3=wv3: e.matmul(
                            yps[:, :n], wv3[:, jj, :], aT[:, jj, t0:t0 + n], start=(jj == 0), stop=(jj == NJH - 1)),
                            waits=([t_w, self.ps_rd[yb]] if jj == 0 else []) + [tok_a[(jj, ti)]],
                            signal=(jj == NJH - 1))
                    t_x = P.op('dve', lambda e, c=c, t0=t0, n=n, yps=yps: e.scalar_tensor_tensor(
                        out=xT[:, c, t0:t0 + n], in0=yps[:, :n], scalar=0.5, in1=xT[:, c, t0:t0 + n],
                        op0=ALU.mult, op1=ALU.add),
                        waits=[t_y, self.tok_x[(c, ti)]])
                    self.tok_x[(c, ti)] = t_x
                    self.ps_rd[yb] = t_x
                    self.bg_pump(3)
                    if gi == 2 and h == 1 and c == KC - 1 and getattr(self, "yT_d", None) is not None:
                        self.final_tile(ti)
                self.ws_release(wi, t_y)
                self.aT_rd = t_y

    def final_tile(self, ti):
        P = self.P
        if not hasattr(self, "fin_slot"):
            self.fin_slot = self.slot("d_out")
            self.fin_done = set()
        self.norm_tile(3, ti, final=True)
        t0, n = TT[ti]
        P.dma('sp', lambda e, t0=t0, n=n: e.dma_start(out=self.yT_d[:, :, t0:t0 + n], in_=self.xT[:, :, t0:t0 + n]),
              self.fin_slot, waits=[self.tok_x[(kc, ti)] for kc in range(KC)])
        self.fin_done.add(ti)

    def final_norm(self, yT_d):
        P = self.P
        for ti in range(len(TT)):
            if ti not in getattr(self, "fin_done", set()):
                self.final_tile(ti)
        P.wait_only('sp', [(self.fin_slot.sem, self.fin_slot.count)] + list(self.out_toks))

    def mix_io(self):
        d = self.dram
        self.din("win", [128, KC, 1280])
        self.din("wout", [KC, 128, 8, 128])
        self.din("cst_ident", [128, 128])
        self.din("cst_oh", [32, 256])
        self.din("cst_blk", [64, 64])
        self.din("rel_bias", [32, 8])
        self.din("sinks", [1, 8])
        self.din("cKT", [128, 16, 128])
        self.din("cV", [128, 16, 128])
        self.din("cK_nat", [16, 128, 128])
        self.din("cV_nat", [16, 128, 128])
        self.dout("kp", [128, 128])
        self.dout("vp", [128, 128])
        self.dout("ks", [16, 128, 128])
        self.dout("vs", [16, 128, 128])
        self.scr = self.nc.dram_tensor("scr_f", [8, 2, 256], F32).ap()
        self.ssm_io()
        if self.debug.get("dump_attn"):
            self.dout("dbg_attn", [128, 4, NT], BF16)

    def mix_setup(self):
        nc, P, d = self.nc, self.P, self.dram
        aT = self.aT
        NEG = -30000.0
        self.qT2 = aT[:, 0:4, :]
        self.uT = aT[:, 4:8, :]
        self.kT = aT[:, 8, :]
        flat = aT[:, 9:11, :].rearrange("p a t -> p (a t)")
        self.vtok = flat[:, 0:17 * 128].rearrange("p (b n) -> p b n", n=128)
        self.ident = self.sb("ident_s", [128, 128], BF16)
        self.ones1 = self.sb("ones1_s", [128, 64], BF16)
        self.BT = self.sb("BT_s", [128, 2, 8, 128], BF16)
        self.BTsc = self.sb("BTsc_s", [128, 2, 16, 4, 4], BF16)
        self.BTsn = self.sb("BTsn_s", [128, 4, 64], BF16)
        self.ES = self.sb("ES_s", [128, 4], F32)
        self.cKT = self.sb("cKT_s", [128, 16, 128], BF16)
        self.cV = self.sb("cV_s", [128, 16, 128], BF16)
        self.stage = self.sb("stage_s", [128, 2, 8, 128], F32)
        self.kvo = [self.sq[i][:].bitcast(F32) for i in range(2)]
        self.pT = [self.sg[i // 2][:, 256 * (i % 2):256 * (i % 2) + 256].bitcast(BF16) for i in range(4)]
        rb_s = self.sb("rb_s", [32, 8], F32)
        oh_s = self.sb("oh_s", [32, 256], F32)
        fm = self.sb("fm_s", [8, 2, 256], F32)
        blk_s = self.sb("blk_s", [128, 64], F32)
        sk_s = self.sb("sk_s", [128, 4], F32)

        t1 = P.dma('sp', lambda e: e.dma_start(out=rb_s[:], in_=d["rel_bias"]), self.slot("d_rb"))
        t2 = P.dma('sp', lambda e: e.dma_start(out=oh_s[:], in_=d["cst_oh"]), self.slot("d_oh"))
        s_blk = self.slot("d_blk")
        for g in range(2):
            t3 = P.dma('sp', lambda e, g=g: e.dma_start(out=blk_s[64 * g:64 * g + 64, :], in_=d["cst_blk"]), s_blk)
        s_sk = self.slot("d_sk")
        for g in range(2):
            t4 = P.dma('sp', lambda e, g=g: e.dma_start(
                out=sk_s[64 * g:64 * g + 64, :], in_=d["sinks"][0:1, 4 * g:4 * g + 4].to_broadcast([64, 4])), s_sk)
        if self.debug.get('ckpt', 99) < 1:
            self.out_toks = []
            return
        t_es = P.op('act', lambda e: e.activation(out=self.ES[:], in_=sk_s[:], func=AF.Exp), waits=[t4])
        self.t_es = t_es
        t_o1 = P.op('dve', lambda e: e.memset(self.ones1[:], 1.0))
        self.t_ones1 = t_o1
        if self.debug.get('ckpt', 99) < 2:
            self.out_toks = []
            return
        def tables_gen():
            fps = self.ps[7]
            t_f = P.op('pe', lambda e: e.matmul(fps[0:8, 0:256], rb_s[0:32, 0:8], oh_s[0:32, :], start=True, stop=True),
                       waits=[t1, t2, self.ps_rd[7]])
            for _ in range(6):
                yield
            t_m = P.op('dve', lambda e: e.memset(fm[:], NEG))
            t_c1 = P.op('dve', lambda e: e.tensor_copy(out=fm[:, 0, 127:255], in_=fps[0:8, 0:128]), waits=[t_f, t_m])
            t_c2 = P.op('dve', lambda e: e.tensor_copy(out=fm[:, 1, 0:127], in_=fps[0:8, 1:128]), waits=[t_f, t_m])
            self.ps_rd[7] = t_c2
            t_sc = P.dma('sp', lambda e: e.dma_start(out=self.scr, in_=fm[:]), self.slot("d_scr"), waits=[t_c1, t_c2])
            from concourse.ap import AP as _AP
            scr_t = self.scr.tensor
            s_tp = self.slot("d_tp")
            tl = None
            for ty in range(2):
                for h in range(8):
                    src = _AP(scr_t, (h * 2 + ty) * 256, [[1, 128], [1, 128]])
                    tl = P.dma('sp', lambda e, ty=ty, h=h, src=src: e.dma_start(out=self.stage[:, ty, h, :], in_=src),
                               s_tp, waits=[t_sc])
            P.wait_only('sp', [tl])
            for _ in range(30):
                yield
            t_bt = P.op('dve', lambda e: e.tensor_copy(out=self.BT[:], in_=self.stage[:]), waits=[(s_tp.sem, s_tp.count)])
            stage2 = self.sb("stage2_s", [128, 8, 4], F32)
            stage3 = self.sb("stage3_s", [128, 4, 64], F32)
            s_tp2 = self.slot("d_tp2")
            for h in range(8):
                src = _AP(scr_t, (h * 2 + 1) * 256, [[1, 128], [1, 4]])
                P.dma('sp', lambda e, h=h, src=src: e.dma_start(out=stage2[:, h, :], in_=src), s_tp2, waits=[t_sc])
                src = _AP(scr_t, (h * 2 + 0) * 256 + 64, [[1, 64], [1, 64]])
                P.dma('sp', lambda e, h=h, src=src: e.dma_start(
                    out=stage3[64 * (h // 4):64 * (h // 4) + 64, h % 4, :], in_=src), s_tp2, waits=[t_sc])
            tk2 = (s_tp2.sem, s_tp2.count)
            tb = []
            for g in range(2):
                tb.append(P.op('dve', lambda e, g=g: e.tensor_copy(
                    out=self.BTsc[:, g, :, :, :], in_=stage2[:, 4 * g:4 * g + 4, :].unsqueeze(1).to_broadcast([128, 16, 4, 4])),
                    waits=[tk2]))
            tb.append(P.op('dve', lambda e: e.tensor_tensor(
                out=self.BTsn[:], in0=stage3[:], in1=blk_s[:].unsqueeze(1).to_broadcast([128, 4, 64]), op=ALU.add),
                waits=[tk2, t3]))
            self.t_tables_all = tb
            self.t_tables = [t_bt] + tb
            yield
        self._tables_gen = tables_gen
        if self.debug.get('ckpt', 99) < 6:
            self.out_toks = []
            return
        self.t_ident = P.dma('pool', lambda e: e.dma_start(out=self.ident[:], in_=d["cst_ident"]), self.slot("d_id"))
        self.ssm_alloc()
        s_cp = self.slot("d_cp")
        P.dma('sp', lambda e: e.dma_start(out=d["ks"][:, 0:124, :], in_=d["cK_nat"][:, 4:128, :]), s_cp)
        self.out_toks = [(s_cp.sem, s_cp.count)]
        if self.debug.get('cp', 2) > 1:
            s_cp2 = self.slot("d_cp2")
            self.out_toks.append(P.dma('sp', lambda e: e.dma_start(out=d["vs"][:, 0:124, :], in_=d["cV_nat"][:, 4:128, :]), s_cp2))

    def mixer(self):
        nc, P, d = self.nc, self.P, self.dram
        xT, hT = self.xT, self.hT
        dbg = self.debug
        if dbg.get("setup_only"):
            return
        self.bg_pump(10 ** 9)
        for ti_ in range(3):
            self.norm_tile(1, ti_)
        mix_normed = [3]
        win = d["win"]
        pcnt = 0
        tok_q = {}
        tok_k = {}
        tok_u = {}
        t_last = None
        def u_group(kt, ti, wv, t_w, b):
            t0, n = TT[ti]
            pp = self.ps[b]
            t_p = None
            for kc in range(KC):
                t_p = P.op('pe', lambda e, kc=kc, t0=t0, n=n, pp=pp, wv=wv: e.matmul(
                    pp[:, :n], wv[:, kc, :], hT[:, kc, t0:t0 + n], start=(kc == 0), stop=(kc == KC - 1)),
                    waits=([t_w, self.ps_rd[b], self.aT_rd] if kc == 0 else []) + [self.tok_h[(kc, ti)]],
                    signal=(kc == KC - 1))
            if ti % 2 == 0:
                t_e = P.op('dve', lambda e, kt=kt, t0=t0, n=n, pp=pp: e.tensor_copy(
                    out=self.uT[:, kt, t0:t0 + n], in_=pp[:, :n]), waits=[t_p])
            else:
                t_e = P.op('act', lambda e, kt=kt, t0=t0, n=n, pp=pp: e.activation(
                    out=self.uT[:, kt, t0:t0 + n], in_=pp[:, :n], func=AF.Copy), waits=[t_p])
            tok_u[(kt, ti)] = t_e
            self.ps_rd[b] = t_e
            return t_p

        for ch in [0, 1, 2, 3, 4]:
            wv, t_w, wi = self.ws_load(win[:, :, ch * 128:(ch + 1) * 128], lambda t: t[:, 0:1024].rearrange("p (k n) -> p k n", k=KC))
            for ti, (t0, n) in enumerate(TT):
                if mix_normed[0] < len(TT):
                    self.norm_tile(1, mix_normed[0])
                    mix_normed[0] += 1
                b = pcnt % 4
                pcnt += 1
                pp = self.ps[b]
                for kc in range(KC):
                    t_p = P.op('pe', lambda e, kc=kc, t0=t0, n=n, pp=pp, wv=wv: e.matmul(
                        pp[:, :n], wv[:, kc, :], hT[:, kc, t0:t0 + n], start=(kc == 0), stop=(kc == KC - 1)),
                        waits=([t_w, self.ps_rd[b], self.aT_rd] if kc == 0 else []) + [self.tok_h[(kc, ti)]],
                        signal=(kc == KC - 1))
                t_last = t_p
                if ch < 4:
                    t_e = P.op('act', lambda e, ch=ch, t0=t0, n=n, pp=pp: e.activation(
                        out=self.qT2[:, ch, t0:t0 + n], in_=pp[:, :n], func=AF.Copy, scale=0.125), waits=[t_p])
                    tok_q[(ch, ti)] = t_e
                elif ch == 4:
                    t_e = P.op('dve', lambda e, t0=t0, n=n, pp=pp: e.tensor_copy(
                        out=self.kT[:, t0:t0 + n], in_=pp[:, :n]), waits=[t_p])
                    tok_k[ti] = t_e
                else:
                    kt = ch - 6
                    eng = 'dve' if (ti % 2 == 0) else 'act'
                    if eng == 'dve':
                        t_e = P.op('dve', lambda e, kt=kt, t0=t0, n=n, pp=pp: e.tensor_copy(
                            out=self.uT[:, kt, t0:t0 + n], in_=pp[:, :n]), waits=[t_p])
                    else:
                        t_e = P.op('act', lambda e, kt=kt, t0=t0, n=n, pp=pp: e.activation(
                            out=self.uT[:, kt, t0:t0 + n], in_=pp[:, :n], func=AF.Copy), waits=[t_p])
                    tok_u[(kt, ti)] = t_e
                self.ps_rd[b] = t_e
            self.ws_release(wi, t_last)
        if dbg.get('mix_stop', 99) <= 1:
            self.hT_free = P.last['pe']
            return
        wv, t_w, wi = self.ws_load(win[:, :, 512:768], lambda t: t[:, 0:2048].rearrange("p (k n) -> p k n", k=KC))
        tok_v = {}
        kv_tok = {}
        for blk in range(17):
            t0 = blk * 128
            n = 128 if blk < 16 else 64
            full = blk >= 15
            b = 4 + (blk % 2)
            pp = self.ps[b]
            c0 = 0 if full else 128
            for kc in range(KC):
                t_p = P.op('pe', lambda e, kc=kc, t0=t0, n=n, pp=pp, wv=wv, c0=c0: e.matmul(
                    pp[0:n, c0:256], hT[:, kc, t0:t0 + n], wv[:, kc, c0:256], start=(kc == 0), stop=(kc == KC - 1)),
                    waits=([t_w, self.ps_rd[b]] if kc == 0 else []) + [self.tok_h[(kc, min(blk // 4, 4))]],
                    signal=(kc == KC - 1))
            t_last = t_p
            t_e = P.op('dve', lambda e, blk=blk, n=n, pp=pp: e.tensor_copy(
                out=self.vtok[0:n, blk, :], in_=pp[0:n, 128:256]), waits=[t_p])
            tok_v[blk] = t_e
            if full:
                ko = self.kvo[blk - 15]
                t_e = P.op('act', lambda e, n=n, pp=pp, ko=ko: e.activation(
                    out=ko[0:n, :], in_=pp[0:n, 0:256], func=AF.Copy), waits=[t_p, t_e, self.sq_rd[blk - 15]])
                kv_tok[blk] = t_e
            self.ps_rd[b] = t_e
        self.ws_release(wi, t_last)
        self.hT_free = t_last
        if dbg.get('mix_stop', 99) <= 2:
            self.hT_free = P.last['pe']
            return
        s_kv = self.slot("d_kvo")
        P.dma('sp', lambda e: e.dma_start(out=d["kp"], in_=self.kvo[0][:, 0:128]), s_kv, waits=[kv_tok[15]])
        P.dma('sp', lambda e: e.dma_start(out=d["vp"], in_=self.kvo[0][:, 128:256]), s_kv, waits=[kv_tok[15]])
        for b_ in range(16):
            P.dma('sp', lambda e, b_=b_: e.dma_start(
                out=d["ks"][b_, 124:128, :], in_=self.kvo[1][4 * b_:4 * b_ + 4, 0:128]), s_kv, waits=[kv_tok[16]])
            P.dma('sp', lambda e, b_=b_: e.dma_start(
                out=d["vs"][b_, 124:128, :], in_=self.kvo[1][4 * b_:4 * b_ + 4, 128:256]), s_kv, waits=[kv_tok[16]])
        self.out_toks.append((s_kv.sem, s_kv.count))
        self.sq_rd = [(s_kv.sem, s_kv.count), (s_kv.sem, s_kv.count)]

        if dbg.get('mix_stop', 99) <= 3:
            self.hT_free = P.last['pe']
            return
        qT2, kT, vtok = self.qT2, self.kT, self.vtok
        attn_tok = {}
        pcnt = [0]
        pT_rd = [self.sg_rd[0], self.sg_rd[0], self.sg_rd[1], self.sg_rd[1]]
        tq_all = lambda ti: [tok_q[(r, ti)] for r in range(4)]
        self.tok_u = tok_u
        self.ssm_y_tok = {}

        def attn_block(n_):
            ti = n_ // 4
            q0 = n_ * 128
            kbs = ([(n_ - 1, 1)] if n_ > 0 else []) + [(n_, 0)]
            pts = {}
            for g in range(2):
                gs = slice(64 * g, 64 * g + 64)
                for (kb, ty) in kbs:
                    sl = pcnt[0] % 4
                    bk = pcnt[0] % 2
                    pcnt[0] += 1
                    sb_ = self.ps[bk]
                    P.op('pe', lambda e, sb_=sb_, ty=ty, g=g: e.matmul(
                        sb_[:, :], self.ident[:], self.BT[:, ty, 4 * g:4 * g + 4, :], start=True, stop=False),
                        waits=[self.ps_rd[bk], self.t_ident] + self.t_tables, signal=False)
                    t_s = P.op('pe', lambda e, sb_=sb_, gs=gs, kb=kb, q0=q0: e.matmul(
                        sb_[:, :], kT[gs, kb * 128:(kb + 1) * 128], qT2[gs, :, q0:q0 + 128], start=False, stop=True),
                        waits=tq_all(ti) + [tok_k[ti], tok_k[kb // 4]])
                    t_e = P.op('act', lambda e, sb_=sb_, sl=sl: e.activation(
                        out=self.pT[sl], in_=sb_[:, :], func=AF.Exp), waits=[t_s, pT_rd[sl]])
                    self.ps_rd[bk] = t_e
                    pts[(g, kb)] = (sl, t_e)
            ob, db = 2, 3
            ops_, dps_ = self.ps[ob], self.ps[db]
            for g in range(2):
                gs = slice(64 * g, 64 * g + 64)
                for i, (kb, ty) in enumerate(kbs):
                    sl, t_e = pts[(g, kb)]
                    P.op('pe', lambda e, ops_=ops_, gs=gs, kb=kb, sl=sl, i=i, nkb=len(kbs): e.matmul(
                        ops_[gs, :], vtok[:, kb, gs], self.pT[sl], start=(i == 0), stop=(i == nkb - 1)),
                        waits=[t_e, tok_v[kb], self.ps_rd[ob]], signal=False)
                for i, (kb, ty) in enumerate(kbs):
                    sl, t_e = pts[(g, kb)]
                    t_d = P.op('pe', lambda e, dps_=dps_, gs=gs, sl=sl, i=i, nkb=len(kbs): e.matmul(
                        dps_[gs, :], self.ones1[:, :], self.pT[sl], start=(i == 0), stop=(i == nkb - 1)),
                        waits=[self.ps_rd[db], self.t_ones1], signal=(i == len(kbs) - 1))
                for (kb, ty) in kbs:
                    pT_rd[pts[(g, kb)][0]] = t_d
            rb = n_ % 2
            rt = self.rt[rb]
            for r_ in range(4):
                t_1 = P.op('act', lambda e, dps_=dps_, rt=rt, r_=r_: e.activation(
                    out=rt[:, 128 * r_:128 * r_ + 128], in_=dps_[:, 128 * r_:128 * r_ + 128], func=AF.Ln,
                    bias=self.ES[:, r_:r_ + 1], scale=1.0), waits=[t_d, self.t_es, self.rt_rd[rb]])
            t_2 = P.op('act', lambda e, rt=rt: e.activation(out=rt[:], in_=rt[:], func=AF.Exp, scale=-1.0), waits=[t_1])
            self.ps_rd[db] = t_1

            def back():
                t_3 = P.op('dve', lambda e, ops_=ops_, rt=rt, q0=q0: e.tensor_tensor(
                    out=qT2[:, :, q0:q0 + 128], in0=ops_[:].rearrange("p (r q) -> p r q", r=4),
                    in1=rt[:].rearrange("p (r q) -> p r q", r=4), op=ALU.mult), waits=[t_2, t_d])
                self.rt_rd[rb] = t_3
                self.ps_rd[ob] = t_3
                attn_tok[n_] = t_3
            return back

        do_ssm = not dbg.get("skip_ssm")
        bi = 0
        ucnt = 0
        t_lastu = None
        for ch in (6, 7, 8, 9):
            wv, t_w, wi = self.ws_load(win[:, :, ch * 128:(ch + 1) * 128], lambda t: t[:, 0:1024].rearrange("p (k n) -> p k n", k=KC))
            for ti in range(len(TT)):
                t_lastu = u_group(ch - 6, ti, wv, t_w, 4 + ucnt % 4)
                ucnt += 1
                if bi < 16:
                    attn_block(bi)()
                    bi += 1
            self.ws_release(wi, t_lastu)
        while bi < 16:
            attn_block(bi)()
            bi += 1
        self.hT_free = t_lastu
        if do_ssm:
            self.ssm_begin()
        if do_ssm:
            self.ssm_pads_bz(0)
            self.ssm_pads_cp(0)
            self.ssm_tables(0)
            self.ssm_tables(1)
            self.ssm_z(0)
        for i_ in range(16):
            if do_ssm and i_ + 2 < 16:
                self.ssm_tables(i_ + 2)
            if do_ssm:
                kt_, pl_ = i_ // 4, i_ % 4
                self.ssm_main(i_)
                if pl_ == 3:
                    self.ssm_sample(kt_)
                if i_ < 15:
                    self.ssm_z(i_ + 1)
                if pl_ == 2 and kt_ < 3:
                    self.ssm_pads_bz(kt_ + 1)
                if pl_ == 3:
                    for f_ in range(NWARM):
                        P.op('pe', lambda e: e.matmul(self.ps[6][:, :], self.ident[:], self.BT[:, 0, 0:4, :], start=True, stop=True),
                             waits=[self.ps_rd[6]] if f_ == 0 else [], signal=False)
                    self.ssm_y(kt_, (3, 2), False)
                    if kt_ == 3:
                        self.ssm_y(kt_, (1, 0), True)
                if pl_ == 0 and kt_ > 0:
                    self.ssm_y(kt_ - 1, (1, 0), True)
                if pl_ == 1 and kt_ > 0:
                    self.ssm_pads_cp(kt_)
        if do_ssm:
            t_g = P.op('dve', lambda e: e.memset(self.sgc2[:], 0.0), waits=[self.S['dmp'], self.S['dmc']])
            self.rt_rd = [t_g, t_g]
        if do_ssm:
            self.ssm_glu()
        wck = [self.S['pad_bz_rd'], self.S['pad_cp_rd']] if do_ssm else []
        self.t_ckt = P.dma('pool', lambda e: e.dma_start(out=self.cKT[:], in_=d["cKT"]), self.slot("d_ckt"), waits=wck)
        self.t_cv = P.dma('pool', lambda e: e.dma_start(out=self.cV[:], in_=d["cV"]), self.slot("d_cv"), waits=wck)
        if dbg.get('mix_stop', 99) <= 4:
            self.hT_free = P.last['pe']
            return
        S0 = SEQ
        pTc = [self.pT[0], self.pT[1]]
        pTn = [self.pT[2], self.pT[3]]
        te = {}
        for g in range(2):
            gs = slice(64 * g, 64 * g + 64)
            sc, sn = self.ps[2 * g], self.ps[2 * g + 1]
            P.op('pe', lambda e, sc=sc, g=g: e.matmul(
                sc[:, 0:256], self.ident[:], self.BTsc[:, g, :, :].rearrange("p b r t -> p (b r t)"), start=True, stop=False),
                waits=[self.ps_rd[2 * g], self.t_ident] + self.t_tables, signal=False)
            for b in range(16):
                t_s = P.op('pe', lambda e, sc=sc, gs=gs, b=b: e.matmul(
                    sc[:, 16 * b:16 * b + 16], self.cKT[gs, b, :], qT2[gs, :, S0 + 4 * b:S0 + 4 * b + 4],
                    start=False, stop=(b == 15)),
                    waits=tq_all(4) + [self.t_ckt], signal=(b == 15))
            t_e = P.op('act', lambda e, sc=sc, g=g: e.activation(
                out=pTc[g][:, 0:256], in_=sc[:, 0:256], func=AF.Exp), waits=[t_s, pT_rd[g]])
            self.ps_rd[2 * g] = t_e
            te[(g, 'c')] = t_e
            if dbg.get('sa', 9) < 2:
                continue
            P.op('pe', lambda e, sn=sn, g=g, gs=gs: e.matmul(
                sn[0:64, 0:256], self.ident[gs, 64 - 64 * g:128 - 64 * g], self.BTsn[gs, :, :],
                start=True, stop=(dbg.get('sa2', 9) < 2)),
                waits=[self.ps_rd[2 * g + 1]], signal=False)
            if dbg.get('sa2', 9) < 2:
                continue
            t_s = P.op('pe', lambda e, sn=sn, gs=gs: e.matmul(
                sn[0:64, 0:256], kT[gs, S0:S0 + 64], qT2[gs, :, S0:S0 + 64], start=False, stop=True),
                waits=[tok_k[4]])
            if dbg.get('sa2', 9) < 3:
                continue
            t_e = P.op('act', lambda e, sn=sn, g=g: e.activation(
                out=pTn[g][0:64, 0:256].rearrange("p (b r t) -> p r b t", b=16, r=4),
                in_=sn[0:64, 0:256].rearrange("p (r b t) -> p r b t", r=4, b=16), func=AF.Exp),
                waits=[t_s, pT_rd[2 + g]])
            self.ps_rd[2 * g + 1] = t_e
            te[(g, 'n')] = t_e
        ob, db = 0, 1
        ops_, dps_ = self.ps[ob], self.ps[db]
        if dbg.get('sa', 9) < 3:
            self.hT_free = P.last['pe']
            return
        for g in range(2):
            gs = slice(64 * g, 64 * g + 64)
            P.op('pe', lambda e, gs=gs, g=g: e.matmul(
                ops_[gs, 0:256], vtok[0:64, 16, gs], pTn[g][0:64, 0:256], start=True, stop=False),
                waits=[te[(g, 'n')], te[(g, 'c')], tok_v[16], self.ps_rd[ob], self.t_cv], signal=False)
            for b in range(16):
                P.op('pe', lambda e, gs=gs, g=g, b=b: e.matmul(
                    ops_[gs, 16 * b:16 * b + 16], self.cV[:, b, gs], pTc[g][:, 16 * b:16 * b + 16],
                    start=False, stop=(b == 15)), signal=False)
            P.op('pe', lambda e, gs=gs, g=g: e.matmul(
                dps_[gs, 0:256], self.ones1[0:64, :], pTn[g][0:64, 0:256], start=True, stop=False),
                waits=[self.ps_rd[db]], signal=False)
            t_d = P.op('pe', lambda e, gs=gs, g=g: e.matmul(
                dps_[gs, 0:256], self.ones1[:, :], pTc[g][:, 0:256], start=False, stop=True))
        if dbg.get('sa', 9) < 4:
            self.hT_free = P.last['pe']
            return
        rt = self.rt[0]
        v4 = lambda ap: ap.rearrange("p (b r t) -> p b r t", b=16, r=4)
        t_1 = P.op('dve', lambda e: e.tensor_tensor(
            out=v4(rt[:, 0:256]), in0=v4(dps_[:, 0:256]),
            in1=self.ES[:].unsqueeze(1).unsqueeze(3).to_broadcast([128, 16, 4, 4]), op=ALU.add),
            waits=[t_d, self.t_es, self.rt_rd[0]])
        t_2 = P.op('act', lambda e: e.activation(out=rt[:, 0:256], in_=rt[:, 0:256], func=AF.Ln), waits=[t_1])
        t_2 = P.op('act', lambda e: e.activation(out=rt[:, 0:256], in_=rt[:, 0:256], func=AF.Exp, scale=-1.0), waits=[t_2])
        t_3 = P.op('dve', lambda e: e.tensor_tensor(
            out=qT2[:, :, S0:S0 + 64].rearrange("p r (b t) -> p b r t", t=4), in0=v4(ops_[:, 0:256]),
            in1=v4(rt[:, 0:256]), op=ALU.mult), waits=[t_2, t_d])
        self.rt_rd[0] = t_3
        self.ps_rd[ob] = t_3
        self.ps_rd[db] = t_1
        attn_tok[16] = t_3
        self.attn_tok = attn_tok
        self.tok_u = tok_u

        if dbg.get('mix_stop', 99) <= 5:
            self.hT_free = P.last['pe']
            return
        ssm_tok = self.ssm_tok if do_ssm else None

        if dbg.get("dump_attn"):
            s_dbg = self.slot("d_dbg")
            tk = P.dma('sp', lambda e: e.dma_start(out=d["dbg_attn"], in_=qT2), s_dbg,
                       waits=[attn_tok[i] for i in range(17)])
            self.out_toks.append(tk)
            self.dbg_tok = tk

        nk = 4 if ssm_tok is None else 8
        ycnt = 0
        for c in range(KC):
            wv, t_w, wi = self.ws_load(d["wout"][c], lambda t: t[:, 0:1024].rearrange("p (k n) -> p k n", k=8))
            for ti, (t0, n) in enumerate(TT):
                yb = 4 + (ycnt % 2)
                ycnt += 1
                yps = self.ps[yb]
                blks = range(ti * 4, ti * 4 + 4) if ti < 4 else [16]
                for i in range(nk):
                    rhs = qT2[:, i, t0:t0 + n] if i < 4 else self.ssmT[:, i - 4, t0:t0 + n]
                    w_ = ([t_w, self.ps_rd[yb]] + [attn_tok[b_] for b_ in blks]) if i == 0 else []
                    if i == 4:
                        w_ = w_ + [ssm_tok[(k_, ti)] for k_ in range(4)]
                    t_y = P.op('pe', lambda e, i=i, n=n, yps=yps, wv=wv, rhs=rhs: e.matmul(
                        yps[:, :n], wv[:, i, :], rhs, start=(i == 0), stop=(i == nk - 1)),
                        waits=w_, signal=(i == nk - 1))
                t_x = P.op('dve', lambda e, c=c, t0=t0, n=n, yps=yps: e.tensor_tensor(
                    out=xT[:, c, t0:t0 + n], in0=yps[:, :n], in1=xT[:, c, t0:t0 + n], op=ALU.add),
                    waits=[t_y, self.tok_x[(c, ti)]])
                self.tok_x[(c, ti)] = t_x
                self.ps_rd[yb] = t_x
                if c == KC - 1:
                    self.norm_tile(2, ti)
            self.ws_release(wi, t_y)
        self.normed_upto = {2: len(TT)}
        self.aT_rd = t_y
        self.sg_rd = [t_y, t_y]
        self.mix_end_tok = t_y


    def ssm_io(self):
        self.din("ssm_pm", [128, 3, 16])
        self.din("ssm_cm", [128, 3, 4, 64])
        self.din("c_pm", [128, 2, 16, 16])
        self.din("b_pm", [128, 2, 16, 16])
        self.din("b_cm", [128, 2, 4, 64])
        self.din("dskip", [128, 4])
        self.din("bglu", [128, 4])
        self.din("wglu", [128, 4, 512])
        self.din("cst_m2", [128, 4, 8])
        self.din("cst_m3", [128, 4, 2])
        self.din("cst_eye", [128, 128])
        self.din("x0", [128, 16, 2, 16])
        self.dout("st_p", [128, 16, 2])
        self.dout("st_s", [128, 16, 2, 16])

    def ssm_alloc(self):
        sb = self.sb
        self.pm = {}
        for nm in ["L1r", "L1i", "L2r", "L2i", "L3r", "L3i", "L4r", "L4i", "cr", "ci", "rho4", "t0", "t1", "t2", "t3",
                   "t4", "t5", "turns"]:
            self.pm[nm] = sb("pm_" + nm, [128, 16], F32)
        self.pm_in = sb("pm_in", [128, 3, 16], F32)
        self.pm_i = sb("pm_i", [128, 16], I32)
        self.phi2 = sb("pm_phi2", [128, 16], I32)
        self.q30 = sb("q30", [128, 1], I32)
        self.CpC = sb("CpC", [128, 16, 5, 2, 16], BF16)
        self.jota = sb("jota", [128, 512], I32)
        self.m2 = sb("m2", [128, 4, 8], F32)
        self.m3 = sb("m3", [128, 4, 2], F32)
        self.eye = sb("eye", [128, 128], F32)
        self.dsk = sb("dsk", [128, 4], F32)
        self.bgl = sb("bgl", [128, 4], F32)
        self.x0 = sb("x0_s", [128, 16, 2, 16], F32)
        self.stp = sb("stp_s", [128, 16, 2], F32)
        self.sgc = sb("sgc", [128, 1], F32)
        self.sgc2 = sb("sgc2", [128, 1], F32)
        self.XbS = sb("XbS", [128, 2, 4, 2, 16], BF16)
        self.BzC = self.cKT[:].rearrange("p b n -> p (b n)").rearrange("p (k s r q) -> p k s r q", k=4, s=4, r=2)
        self.Kin = self.cV[:].rearrange("p b n -> p (b n)").rearrange("p (k t n) -> p k t n", k=4, t=4)

    def ssm_setup_gen(self, which):
        P, d = self.P, self.dram
        pm = self.pm
        prev = [None]
        self.kin_rd = getattr(self, 'kin_rd', [None] * 4)
        TWO_PI = 2.0 * np.pi
        hf = self.hT[:].rearrange("p k t -> p (k t)").bitcast(F32)

        def V(fn, extra=()):
            prev[0] = P.op('dve', fn, waits=[prev[0]] + list(extra))

        def A(fn, extra=()):
            prev[0] = P.op('act', fn, waits=[prev[0]] + list(extra))

        def lam(aR, aI, ldt, T, is_pm):
            yield A(lambda e: e.activation(out=ldt, in_=ldt, func=AF.Exp))
            yield V(lambda e: e.tensor_tensor(out=T['xr'], in0=aR, in1=ldt, op=ALU.mult))
            yield V(lambda e: e.tensor_tensor(out=T['xi'], in0=aI, in1=ldt, op=ALU.mult))
            yield A(lambda e: e.activation(out=T['mag'], in_=T['xr'], func=AF.Exp))
            yield V(lambda e: e.tensor_scalar(out=T['xi'], in0=T['xi'], scalar1=1.0 / TWO_PI, scalar2=None, op0=ALU.mult))
            if is_pm:
                yield V(lambda e: e.tensor_copy(out=pm['turns'][:], in_=T['xi']))
                yield A(lambda e: e.activation(out=pm['rho4'][:], in_=T['xr'], func=AF.Exp, scale=4.0))
            yield V(lambda e: e.tensor_copy(out=T['ni'], in_=T['xi']))
            yield V(lambda e: e.tensor_copy(out=T['nf'], in_=T['ni']))
            yield V(lambda e: e.tensor_tensor(out=T['xi'], in0=T['xi'], in1=T['nf'], op=ALU.subtract))
            yield A(lambda e: e.activation(out=T['s1'], in_=T['xi'], func=AF.Sin, scale=TWO_PI))
            yield A(lambda e: e.activation(out=T['sh'], in_=T['xi'], func=AF.Sin, scale=float(np.pi)))
            yield V(lambda e: e.tensor_tensor(out=T['sh'], in0=T['sh'], in1=T['sh'], op=ALU.mult))
            yield V(lambda e: e.tensor_scalar(out=T['sh'], in0=T['sh'], scalar1=-2.0, scalar2=1.0, op0=ALU.mult, op1=ALU.add))
            yield V(lambda e: e.tensor_tensor(out=T['L1r'], in0=T['mag'], in1=T['sh'], op=ALU.mult))
            yield V(lambda e: e.tensor_tensor(out=T['L1i'], in0=T['mag'], in1=T['s1'], op=ALU.mult))
            yield V(lambda e: e.tensor_scalar(out=T['mag'], in0=T['L1r'], scalar1=-1.0, scalar2=None, op0=ALU.add))
            yield V(lambda e: e.tensor_tensor(out=T['xr'], in0=aR, in1=aR, op=ALU.mult))
            yield V(lambda e: e.tensor_tensor(out=T['xi'], in0=aI, in1=aI, op=ALU.mult))
            yield V(lambda e: e.tensor_tensor(out=T['xr'], in0=T['xr'], in1=T['xi'], op=ALU.add))
            yield V(lambda e: e.reciprocal(out=T['xr'], in_=T['xr']))
            yield V(lambda e: e.tensor_tensor(out=T['xi'], in0=T['mag'], in1=aR, op=ALU.mult))
            yield V(lambda e: e.tensor_tensor(out=T['s1'], in0=T['L1i'], in1=aI, op=ALU.mult))
            yield V(lambda e: e.tensor_tensor(out=T['xi'], in0=T['xi'], in1=T['s1'], op=ALU.add))
            yield V(lambda e: e.tensor_tensor(out=T['cr'], in0=T['xi'], in1=T['xr'], op=ALU.mult))
            yield V(lambda e: e.tensor_tensor(out=T['xi'], in0=T['L1i'], in1=aR, op=ALU.mult))
            yield V(lambda e: e.tensor_tensor(out=T['s1'], in0=T['mag'], in1=aI, op=ALU.mult))
            yield V(lambda e: e.tensor_tensor(out=T['xi'], in0=T['xi'], in1=T['s1'], op=ALU.subtract))
            yield V(lambda e: e.tensor_tensor(out=T['ci'], in0=T['xi'], in1=T['xr'], op=ALU.mult))

        def cmul(o_r, o_i, a_r, a_i, b_r, b_i, t1, t2):
            yield V(lambda e: e.tensor_tensor(out=t1, in0=a_r, in1=b_r, op=ALU.mult))
            yield V(lambda e: e.tensor_tensor(out=t2, in0=a_i, in1=b_i, op=ALU.mult))
            yield V(lambda e: e.tensor_tensor(out=o_r, in0=t1, in1=t2, op=ALU.subtract))
            yield V(lambda e: e.tensor_tensor(out=t1, in0=a_r, in1=b_i, op=ALU.mult))
            yield V(lambda e: e.tensor_tensor(out=t2, in0=a_i, in1=b_r, op=ALU.mult))
            yield V(lambda e: e.tensor_tensor(out=o_i, in0=t1, in1=t2, op=ALU.add))

        if which == 1:
            prev[0] = None
            st = self.stage[:].rearrange("p a h q -> p (a h q)")
            s = self.slot("d_ssm_in")
            for (dst, src_) in [(self.pm_in[:], d["ssm_pm"]), (self.m2[:], d["cst_m2"]), (self.m3[:], d["cst_m3"]),
                                (self.eye[:], d["cst_eye"]), (self.dsk[:], d["dskip"]), (self.bgl[:], d["bglu"]),
                                (self.x0[:], d["x0"])]:
                t_in = P.dma('sp', lambda e, dst=dst, src_=src_: e.dma_start(out=dst, in_=src_), s)
            self.t_x0 = t_in
            cpm = st[:, 0:512].rearrange("p (r a c) -> p r a c", r=2, a=16)
            t_in2 = P.dma('sp', lambda e: e.dma_start(out=cpm, in_=d["c_pm"]), self.slot("d_ssm_in2"), waits=self.t_tables)
            for _ in range(8):
                yield
            yield V(lambda e: e.memset(self.sgc[:], TWO_PI / 2.0 ** 32), extra=[t_in, t_in2] + self.t_tables)
            yield V(lambda e: e.memset(self.sgc2[:], TWO_PI / 2.0 ** 33))
            aR, aI, ldt = self.pm_in[:, 0, :], self.pm_in[:, 1, :], self.pm_in[:, 2, :]
            T = {'xr': pm['t0'][:], 'xi': pm['t1'][:], 'mag': pm['t2'][:], 'ni': self.pm_i[:], 'nf': pm['t3'][:],
                 's1': pm['t4'][:], 'sh': pm['t5'][:], 'L1r': pm['L1r'][:], 'L1i': pm['L1i'][:], 'cr': pm['cr'][:], 'ci': pm['ci'][:]}
            yield from lam(aR, aI, ldt, T, True)
            t1_, t2_ = pm['t0'][:], pm['t1'][:]
            yield from cmul(pm['L2r'][:], pm['L2i'][:], pm['L1r'][:], pm['L1i'][:], pm['L1r'][:], pm['L1i'][:], t1_, t2_)
            yield from cmul(pm['L3r'][:], pm['L3i'][:], pm['L2r'][:], pm['L2i'][:], pm['L1r'][:], pm['L1i'][:], t1_, t2_)
            yield from cmul(pm['L4r'][:], pm['L4i'][:], pm['L2r'][:], pm['L2i'][:], pm['L2r'][:], pm['L2i'][:], t1_, t2_)
            tu = pm['turns'][:]
            yield V(lambda e: e.tensor_scalar(out=tu, in0=tu, scalar1=4.0, scalar2=None, op0=ALU.mult))
            yield V(lambda e: e.tensor_copy(out=self.pm_i[:], in_=tu))
            yield V(lambda e: e.tensor_copy(out=pm['t3'][:], in_=self.pm_i[:]))
            yield V(lambda e: e.tensor_tensor(out=tu, in0=tu, in1=pm['t3'][:], op=ALU.subtract))
            yield V(lambda e: e.tensor_scalar(out=tu, in0=tu, scalar1=4294967040.0, scalar2=None, op0=ALU.mult))
            yield V(lambda e: e.tensor_copy(out=self.phi2[:], in_=tu))
            cR, cI = cpm[:, 0, :, :], cpm[:, 1, :, :]
            w1 = st[:, 512:768].rearrange("p (a c) -> p a c", a=16)
            w2 = st[:, 768:1024].rearrange("p (a c) -> p a c", a=16)
            bc = lambda ap: ap.unsqueeze(2).to_broadcast([128, 16, 16])
            yield V(lambda e: e.tensor_copy(out=self.CpC[:, :, 0, 0, :], in_=cR))
            yield V(lambda e: e.tensor_scalar(out=self.CpC[:, :, 0, 1, :], in0=cI, scalar1=-1.0, scalar2=None, op0=ALU.mult))
            for k in range(1, 5):
                Lr, Li = pm['L%dr' % k][:], pm['L%di' % k][:]
                yield V(lambda e, Lr=Lr: e.tensor_tensor(out=w1, in0=cR, in1=bc(Lr), op=ALU.mult))
                yield V(lambda e, Li=Li: e.tensor_tensor(out=w2, in0=cI, in1=bc(Li), op=ALU.mult))
                yield V(lambda e, k=k: e.tensor_tensor(out=self.CpC[:, :, k, 0, :], in0=w1, in1=w2, op=ALU.subtract))
                yield V(lambda e, Li=Li: e.tensor_tensor(out=w1, in0=cR, in1=bc(Li), op=ALU.mult))
                yield V(lambda e, Lr=Lr: e.tensor_tensor(out=w2, in0=cI, in1=bc(Lr), op=ALU.mult))
                yield V(lambda e, k=k: e.scalar_tensor_tensor(out=self.CpC[:, :, k, 1, :], in0=w1, scalar=-1.0, in1=w2,
                                                              op0=ALU.mult, op1=ALU.subtract))
            self.t_pm_done = prev[0]
            for hk in range(2):
                cm_in = st[:, 0:384].rearrange("p (q k n) -> p q k n", q=3, k=2)
                bcm = st[:, 384:640].rearrange("p (r k n) -> p r k n", r=2, k=2)
                ct = [st[:, 640 + 128 * i:768 + 128 * i].rearrange("p (k n) -> p k n", k=2) for i in range(10)]
                cti = st[:, 1920:2048].bitcast(I32).rearrange("p (k n) -> p k n", k=2)
                sl_ = self.slot("d_ssm_cm%d" % hk)
                P.dma('sp', lambda e, hk=hk, cm_in=cm_in: e.dma_start(out=cm_in, in_=d["ssm_cm"][:, :, 2 * hk:2 * hk + 2, :]), sl_,
                      waits=[prev[0]])
                t_l = P.dma('sp', lambda e, hk=hk, bcm=bcm: e.dma_start(out=bcm, in_=d["b_cm"][:, :, 2 * hk:2 * hk + 2, :]), sl_,
                            waits=[prev[0]])
                prev[0] = t_l
                for _ in range(8):
                    yield
                aRc, aIc, ldc = cm_in[:, 0, :, :], cm_in[:, 1, :, :], cm_in[:, 2, :, :]
                Tc = {'xr': ct[0], 'xi': ct[1], 'mag': ct[2], 'ni': cti, 'nf': ct[3], 's1': ct[4], 'sh': ct[5],
                      'L1r': ct[6], 'L1i': ct[7], 'cr': ct[8], 'ci': ct[9]}
                yield from lam(aRc, aIc, ldc, Tc, False)
                bRc, bIc = bcm[:, 0, :, :], bcm[:, 1, :, :]
                c_r, c_i, u1, u2, u3, u4 = ct[0], ct[1], ct[2], ct[3], ct[4], ct[5]
                yield from cmul(c_r, c_i, ct[8], ct[9], bRc, bIc, u1, u2)
                for s_ in (3, 2, 1, 0):
                    yield V(lambda e, s_=s_, hk=hk, c_r=c_r: e.tensor_copy(out=self.BzC[:, 2 * hk:2 * hk + 2, s_, 0, :], in_=c_r))
                    yield V(lambda e, s_=s_, hk=hk, c_i=c_i: e.tensor_copy(out=self.BzC[:, 2 * hk:2 * hk + 2, s_, 1, :], in_=c_i))
                    if s_ > 0:
                        yield from cmul(u3, u4, ct[6], ct[7], c_r, c_i, u1, u2)
                        yield V(lambda e, c_r=c_r, u3=u3: e.tensor_copy(out=c_r, in_=u3))
                        yield V(lambda e, c_i=c_i, u4=u4: e.tensor_copy(out=c_i, in_=u4))
            self.t_cm_done = prev[0]
            yield
            return
        prev[0] = self.hT_free
        bpm = hf[:, 512:1024].rearrange("p (r a c) -> p r a c", r=2, a=16)
        t_in2 = P.dma('sp', lambda e: e.dma_start(out=bpm, in_=d["b_pm"]), self.slot("d_ssm_in3"), waits=[self.hT_free])
        for _ in range(9):
            yield
        yield V(lambda e: e.memset(self.q30[:], 1 << 30), extra=[t_in2, self.t_pm_done])
        w = [hf[:, 1024 + 256 * i:1280 + 256 * i].rearrange("p (a c) -> p a c", a=16) for i in range(4)]
        w1, w2, w3, w4 = w
        bc = lambda ap: ap.unsqueeze(2).to_broadcast([128, 16, 16])
        bR, bI = bpm[:, 0, :, :], bpm[:, 1, :, :]
        cur_r = hf[:, 2048:2304].rearrange("p (a c) -> p a c", a=16)
        cur_i = hf[:, 2304:2560].rearrange("p (a c) -> p a c", a=16)
        yield from cmul(cur_r, cur_i, bc(pm['cr'][:]), bc(pm['ci'][:]), bR, bI, w1, w2)
        BLc = hf[:, 2560:3584].bitcast(BF16).rearrange("p (a t r c) -> p a t r c", a=16, t=4, r=2)
        for tau in range(4):
            yield V(lambda e, tau=tau: e.tensor_copy(out=BLc[:, :, tau, 0, :], in_=cur_r))
            yield V(lambda e, tau=tau: e.tensor_copy(out=BLc[:, :, tau, 1, :], in_=cur_i))
            if tau < 3:
                yield from cmul(w3, w4, bc(pm['L1r'][:]), bc(pm['L1i'][:]), cur_r, cur_i, w1, w2)
                yield V(lambda e: e.tensor_copy(out=cur_r, in_=w3))
                yield V(lambda e: e.tensor_copy(out=cur_i, in_=w4))
        BLpads = [hf[:, 3584 + 512 * i:4096 + 512 * i].bitcast(BF16).rearrange("p (l r g c) -> p l r g c", l=4, r=2, g=8)
                  for i in range(2)]
        C0pads = [hf[:, 4608 + 512 * i:5120 + 512 * i].bitcast(BF16).rearrange("p (l r g c) -> p l r g c", l=4, r=2, g=8)
                  for i in range(2)]
        m2b = self.m2[:].unsqueeze(3).to_broadcast([128, 4, 8, 16])
        kb_ = 7
        pad_rd = [None, None]
        c0_rd = [None, None]
        cnt = 0
        pending = None
        for kt in range(4):
            C0pad = C0pads[kt % 2]
            for r in range(2):
                yield V(lambda e, kt=kt, r=r, C0pad=C0pad: e.tensor_tensor(
                    out=C0pad[:, :, r, :, :], in0=self.CpC[:, 4 * kt:4 * kt + 4, 0, r, :].unsqueeze(2).to_broadcast([128, 4, 8, 16]),
                    in1=m2b, op=ALU.mult), extra=[c0_rd[kt % 2]])
            for tau in range(4):
                BLpad = BLpads[cnt % 2]
                for r in range(2):
                    yield V(lambda e, kt=kt, r=r, tau=tau, BLpad=BLpad: e.tensor_tensor(
                        out=BLpad[:, :, r, :, :], in0=BLc[:, 4 * kt:4 * kt + 4, tau, r, :].unsqueeze(2).to_broadcast([128, 4, 8, 16]),
                        in1=m2b, op=ALU.mult), extra=[pad_rd[cnt % 2]])
                kb_ = 6 + cnt % 2
                kps = self.ps[kb_]
                col = 0
                t_mm = None
                for i, (pl, r) in enumerate([(pl, r) for pl in range(4) for r in range(2)]):
                    t_mm = P.op('pe', lambda e, pl=pl, r=r, i=i, kps=kps, col=col, BLpad=BLpad, C0pad=C0pad: e.matmul(
                        kps[:, col:col + 128], BLpad[:, pl, r, :, :].rearrange("p g c -> p (g c)"),
                        C0pad[:, pl, r, :, :].rearrange("p g c -> p (g c)"), start=(i == 0), stop=(i == 7)),
                        waits=[prev[0], self.ps_rd[kb_]] if i == 0 else [], signal=(i == 7))
                pad_rd[cnt % 2] = t_mm
                c0_rd[kt % 2] = t_mm
                if pending is not None:
                    yield self._kin_evac(pending, prev)
                pending = (kt, tau, col, t_mm, kb_)
                cnt += 1
                yield
        yield self._kin_evac(pending, prev)
        self.t_ssm_setup = prev[0]
        yield

    def _kin_evac(self, pending, prev):
        P = self.P
        kt, tau, col, t_mm, slot = pending
        kps = self.ps[slot]
        if tau == 0:
            t = P.op('dve', lambda e: e.scalar_tensor_tensor(
                out=self.Kin[:, kt, 0, :], in0=self.eye[:], scalar=self.dsk[:, kt:kt + 1], in1=kps[:, col:col + 128],
                op0=ALU.mult, op1=ALU.add), waits=[t_mm, prev[0]])
        else:
            t = P.op('dve', lambda e: e.tensor_copy(out=self.Kin[:, kt, tau, :], in_=kps[:, col:col + 128]), waits=[t_mm, prev[0]])
        prev[0] = t
        self.ps_rd[slot] = t
        return None

    def bg_pump(self, n):
        g = getattr(self, "bg", None)
        if g is None:
            return
        for _ in range(n):
            try:
                next(g)
            except StopIteration:
                self.bg = None
                return

    def ssm_begin(self):
        P = self.P
        N = self.WS_N
        i1, i2 = self.ws_next % N, (self.ws_next + 1) % N
        self.ws_next += 2
        self.cp_slots = (i1, i2)
        i3 = self.ws_next % N
        self.ws_next += 1
        self.tmp_slot = i3
        self.pA = self.ws[i3][:, 0:1024].bitcast(F32)
        self.pB = self.ws[i3][:, 1024:2048].bitcast(F32)
        self.t_tmp_free = self.ws_free[i3]
        self.CpPad = [self.ws[i][:, 0:2048].rearrange("p (l k r g c) -> p l k r g c", l=2, k=4, r=2, g=8) for i in (i1, i2)]
        stf = self.stage[:].rearrange("p a h q -> p (a h q)").bitcast(BF16)
        self.BzPad = stf.rearrange("p (l s r g q) -> p l s r g q", l=4, s=4, r=2, g=2)
        hb = self.hT[:].rearrange("p k t -> p (k t)")
        f = lambda a: hb[:, a:a + 1024].bitcast(F32)
        self.tabC = [f(0), f(2048), self.rt[0][:]]
        self.tabS = [f(1024), f(3072), self.rt[1][:]]
        self.ph = hb[:, 4096:5120].bitcast(I32)
        self.ph2 = hb[:, 5120:6144].bitcast(I32)
        self.ta, self.tb = f(6144), f(7168)
        self.Mb = [(f(8192), f(9216)), (f(10240), f(11264))]
        xmain = hb[:, 12288:12288 + 4112].rearrange("p (l r n) -> p l r n", l=4, r=2)
        spare = self.aT[:, 9:11, :].rearrange("p a t -> p (a t)")[:, 2176:4224]
        extra = [spare[:, 0:514], spare[:, 514:1028], spare[:, 1028:1542], hb[:, 5120:5634]]
        self.XbR = [[xmain[:, s, 0, :], xmain[:, s, 1, :]] for s in range(4)] + [[extra[0], extra[1]], [extra[2], extra[3]]]
        self.glu_tmp = hb[:, 0:4096].bitcast(F32).rearrange("p (o n) -> p o n", o=4)
        w0 = [self.hT_free, self.t_ssm_setup, self.t_cm_done]
        t_j = P.op('pool', lambda e: e.iota(self.jota[:], pattern=[[1, 512]], base=0, channel_multiplier=0))
        tz = []
        for k, i in enumerate((i1, i2)):
            tz.append(P.op('pool', lambda e, i=i: e.memset(self.ws[i][:, 0:2048], 0.0), waits=[self.ws_free[i]] + w0))
        for s_ in range(6):
            for r_ in range(2):
                tz.append(P.op('pool', lambda e, s_=s_, r_=r_: e.memset(self.XbR[s_][r_][:, 0:2], 0.0), waits=w0 + [self.aT_rd]))
        self.S = dict(t_j=t_j, tz=tz, ph_rd=None, dve=None, pool=tz[-1], dm_done=[[None], [None]], dm_tok={}, tab={}, dmc=None, dmp=None,
                      pad_bz_rd=None, pad_cp_rd=None, tz_tok={}, slot_rd=[None] * 6, xbs_rd=[None, None], xb_rd=None, zs_rd=None, gel=None, pend=[], xf_rd=None)
        self.ssm_tok = {}

    def ssm_tables(self, pr):
        P, S = self.P, self.S
        tb = pr % 3
        C, Sn = self.tabC[tb], self.tabS[tb]
        w0 = [self.hT_free, self.t_ssm_setup, self.t_cm_done]
        free_t = list(S['dm_tok'].get(pr - 3, [])) + ([self.rt_rd[0], self.rt_rd[1]] if tb == 2 else [])
        t_p1 = P.op('pool', lambda e, pr=pr: e.tensor_tensor(
            out=self.ph[:], in0=self.jota[:], in1=self.phi2[:, pr:pr + 1].to_broadcast([128, 512]), op=ALU.mult),
            waits=w0 + [S['t_j'], S['ph_rd'], S['pool']])
        S['pool'] = t_p1
        t_s = P.op('act', lambda e, Sn=Sn: e.activation(out=Sn, in_=self.ph[:], func=AF.Sin, scale=self.sgc[:, 0:1]),
                   waits=[t_p1] + free_t)
        t_c = P.op('act', lambda e, C=C: e.activation(out=C, in_=self.ph[:], func=AF.Sin, scale=self.sgc2[:, 0:1]),
                   waits=[t_p1] + free_t)
        t_c = P.op('act', lambda e, C=C: e.activation(out=C, in_=C, func=AF.Square), waits=[t_c])
        t_c = P.op('act', lambda e, C=C: e.activation(out=C, in_=C, func=AF.Copy, scale=-2.0, bias=1.0), waits=[t_c])
        S['ph_rd'] = t_c
        S['tab'][pr] = (t_s, t_c)

    def ssm_pads_bz(self, kt):
        P, S = self.P, self.S
        w0 = [self.hT_free, self.t_ssm_setup, self.t_cm_done]
        tb_ = None
        for pl_ in range(4):
            for gg in range(2):
                tb_ = P.op('act', lambda e, pl_=pl_, gg=gg, kt=kt: e.activation(
                    out=self.BzPad[:, pl_, :, :, gg, :], in_=self.BzC[:, kt, :, :, :], func=AF.Copy,
                    scale=self.m3[:, pl_, gg:gg + 1]), waits=w0 + [S['pad_bz_rd']] + self.t_tables)
        S['t_bz'] = tb_

    def ssm_pads_cp(self, kt):
        P, S = self.P, self.S
        w0 = [self.hT_free, self.t_ssm_setup, self.t_cm_done]
        tc_ = []
        for hh in range(2):
            for gg in range(2):
                for r in range(2):
                    for pq in range(2):
                        tc_.append(P.op('pool', lambda e, hh=hh, gg=gg, r=r, pq=pq, kt=kt: e.tensor_copy(
                            out=self.CpPad[hh][64 * gg:64 * gg + 64, pq, :, r, 2 * (2 * hh + pq) + gg, :],
                            in_=self.CpC[64 * gg:64 * gg + 64, 4 * kt + 2 * hh + pq, 1:5, r, :]),
                            waits=w0 + S['tz'] + [S['pad_cp_rd'], S['pool']]))
        S['t_cp'] = tc_
        S['pool'] = tc_[-1]

    def ssm_z(self, pr):
        P, S = self.P, self.S
        kt, pl = pr // 4, pr % 4
        uT = self.uT
        ZB = (4, 5)
        zps = [self.ps[ZB[0]], self.ps[ZB[1]]]
        tz_ = None
        for r in range(2):
            for s_ in range(4):
                tz_ = P.op('pe', lambda e, r=r, s_=s_, pl=pl, kt=kt: e.matmul(
                    zps[r][:, :], self.BzPad[:, pl, s_, r, :, :].rearrange("p g q -> p (g q)"), uT[:, kt, s_:SEQ:4],
                    start=(s_ == 0), stop=(s_ == 3)),
                    waits=([S['t_bz'], self.ps_rd[ZB[r]]] + [self.tok_u[(kt, ti)] for ti in range(5)]) if s_ == 0 else [],
                    signal=(s_ == 3))
        zs = self.ps[7]
        tzs = None
        for r in range(2):
            c0 = 32 * pl + 16 * r
            for s_ in range(4):
                tzs = P.op('pe', lambda e, r=r, s_=s_, pl=pl, kt=kt, c0=c0: e.matmul(
                    zs[:, c0:c0 + 16], self.BzPad[:, pl, s_, r, :, :].rearrange("p g q -> p (g q)"),
                    uT[:, kt, SEQ + s_:NT:4], start=(s_ == 0), stop=(s_ == 3)),
                    waits=[self.ps_rd[7], S['zs_rd']] if (s_ == 0 and r == 0) else [], signal=(s_ == 3))
        if pl == 3:
            S['pad_bz_rd'] = tzs
        S['tz_tok'][pr] = (tz_, tzs)

    def ssm_main(self, pr, mid=None):
        P = self.P
        S = self.S
        kt, pl = pr // 4, pr % 4
        w0 = [self.hT_free, self.t_ssm_setup, self.t_cm_done]
        ZB = (4, 5)
        mul, add, sub = ALU.mult, ALU.add, ALU.subtract
        zps = [self.ps[ZB[0]], self.ps[ZB[1]]]
        tz_, tzs = S['tz_tok'][pr]
        if pl == 0:
            S['t_x0c'] = P.op('act', lambda e, kt=kt: e.activation(
                out=self.XbS[:, kt % 2, :, :, :], in_=self.x0[:, 4 * kt:4 * kt + 4, :, :], func=AF.Copy),
                waits=w0 + [S['xbs_rd'][kt % 2], self.t_x0])
        ta = self.ta
        tb = pr % 2
        C, Sn = self.tabC[pr % 3], self.tabS[pr % 3]
        t_s, t_c = S['tab'][pr]
        Mre, Mim = self.Mb[tb]
        ta, tb2 = self.ta, self.tb
        rho = self.pm['rho4'][:, pr:pr + 1].to_broadcast([128, 512])
        zr, zi = zps[0], zps[1]
        free_m = S['dm_done'][tb]
        o1 = P.op('dve', lambda e: e.tensor_tensor(out=Mre, in0=zr[:, :], in1=C, op=mul), waits=w0 + [tz_, t_c] + free_m)
        o2 = P.op('dve', lambda e: e.tensor_tensor(out=tb2, in0=zi[:, :], in1=Sn, op=mul), waits=[t_s, S['dve']])
        o3 = P.op('dve', lambda e: e.tensor_tensor(out=Mim, in0=zi[:, :], in1=C, op=mul))
        o4 = P.op('dve', lambda e: e.tensor_tensor(out=ta, in0=zr[:, :], in1=Sn, op=mul), waits=[S['dve']])
        self.ps_rd[ZB[0]] = o4
        self.ps_rd[ZB[1]] = o4
        o5 = P.op('dve', lambda e: e.tensor_tensor(out=Mre, in0=Mre, in1=tb2, op=add), waits=[o1, o2])
        o6 = P.op('dve', lambda e: e.tensor_tensor(out=Mim, in0=Mim, in1=ta, op=sub), waits=[o3, o4])
        Wre, Wim, pa, pb = self.ps[0][:, :], self.ps[1][:, :], self.ps[2][:, :], self.ps[3][:, :]
        o7 = P.op('dve', lambda e: e.tensor_tensor_scan(out=Wre, data0=rho, data1=Mre, initial=0.0, op0=mul, op1=add),
                  waits=[o5, self.ps_rd[0], S['dve']])
        o8 = P.op('dve', lambda e: e.tensor_tensor_scan(out=Wim, data0=rho, data1=Mim, initial=0.0, op0=mul, op1=add),
                  waits=[o6, self.ps_rd[1]])
        S['dve'] = o8
        if pr + 1 < 16 and (pr + 1) not in S['tab']:
            self.ssm_tables(pr + 1)
        slot = pr % 6
        xre, xim = self.XbR[slot]
        d1 = P.op('dve', lambda e: e.tensor_tensor(out=pa, in0=Wre, in1=C, op=mul), waits=[o7, self.ps_rd[2]])
        d2 = P.op('dve', lambda e: e.tensor_tensor(out=tb2, in0=Wim, in1=Sn, op=mul), waits=[o8])
        q3 = P.op('dve', lambda e: e.tensor_tensor(out=xre[:, 2:513], in0=pa[:, 0:511], in1=tb2[:, 0:511], op=sub),
                  waits=[d1, d2, S['slot_rd'][slot]] + S['tz'])
        q4 = P.op('dve', lambda e, pr=pr: e.tensor_tensor(out=self.stp[:, pr, 0:1], in0=pa[:, 511:512], in1=tb2[:, 511:512], op=sub),
                  waits=[d1, d2])
        d5 = P.op('dve', lambda e: e.tensor_tensor(out=pb, in0=Wim, in1=C, op=mul), waits=[o8, self.ps_rd[3]])
        d6 = P.op('dve', lambda e: e.tensor_tensor(out=ta, in0=Wre, in1=Sn, op=mul), waits=[o7, q4])
        q7 = P.op('dve', lambda e: e.tensor_tensor(out=xim[:, 2:513], in0=pb[:, 0:511], in1=ta[:, 0:511], op=add),
                  waits=[d5, d6, S['slot_rd'][slot]] + S['tz'])
        q8 = P.op('dve', lambda e, pr=pr: e.tensor_tensor(out=self.stp[:, pr, 1:2], in0=pb[:, 511:512], in1=ta[:, 511:512], op=add),
                  waits=[d5, d6])
        for b_ in range(4):
            self.ps_rd[b_] = q8
        S['dmc'] = q8
        S['dm_done'][tb] = [q8]
        S['dm_tok'][pr] = [q8]
        S['xf_rd'] = [q4, q8]
        S['dve'] = q8
        S['pend'].append((q3, q7, S['t_x0c']))

    def ssm_sample(self, kt):
        P, S = self.P, self.S
        mul, add, sub = ALU.mult, ALU.add, ALU.subtract
        ta = self.ta
        zs = self.ps[7]
        tzs = S['tz_tok'][4 * kt + 3][1]
        zv = zs[:, 0:128].rearrange("p (l r b) -> p l r b", l=4, r=2)
        x0r, x0i = self.x0[:, 4 * kt:4 * kt + 4, 0, :], self.x0[:, 4 * kt:4 * kt + 4, 1, :]
        bcl = lambda ap: ap[:, 4 * kt:4 * kt + 4].unsqueeze(2).to_broadcast([128, 4, 16])
        L4r, L4i = bcl(self.pm['L4r'][:]), bcl(self.pm['L4i'][:])
        q = [ta[:, 64 * i:64 * i + 64].rearrange("p (l b) -> p l b", l=4) for i in range(4)]
        d = [S['dve']]

        def V(fn, extra=()):
            d[0] = P.op('dve', fn, waits=[d[0]] + list(extra))
            return d[0]
        V(lambda e: e.tensor_tensor(out=q[0], in0=x0r, in1=L4r, op=mul), [S['t_x0c']])
        V(lambda e: e.tensor_tensor(out=q[1], in0=x0i, in1=L4i, op=mul))
        V(lambda e: e.tensor_tensor(out=q[2], in0=x0i, in1=L4r, op=mul))
        V(lambda e: e.tensor_tensor(out=q[3], in0=x0r, in1=L4i, op=mul))
        V(lambda e: e.tensor_tensor(out=q[0], in0=q[0], in1=q[1], op=sub))
        V(lambda e: e.tensor_tensor(out=q[2], in0=q[2], in1=q[3], op=add))
        V(lambda e: e.tensor_tensor(out=x0r, in0=zv[:, :, 0, :], in1=q[0], op=add), [tzs])
        t_zs = V(lambda e: e.tensor_tensor(out=x0i, in0=zv[:, :, 1, :], in1=q[2], op=add))
        S['zs_rd'] = t_zs
        S['dve'] = t_zs

    def ssm_y(self, kt, ls, last):
        P, S = self.P, self.S
        uT = self.uT
        if ls[0] == 3:
            S['y_xb'] = [t for tpl in S['pend'] for t in tpl]
            S['pend'] = []
        xb_toks = S['y_xb']
        yb = 6
        ys = self.ps[7]
        t_ys = None
        for l in ls:
            yps = self.ps[yb]
            i = 0
            for pl in range(4):
                for r in range(2):
                    P.op('pe', lambda e, l=l, pl=pl, r=r, i=i, yps=yps, kt=kt: e.matmul(
                        yps[:, :], self.CpPad[pl // 2][:, pl % 2, l, r, :, :].rearrange("p g c -> p (g c)"),
                        self.XbR[(4 * kt + pl) % 6][r][:, 1:513], start=(i == 0), stop=False),
                        waits=(xb_toks + S['t_cp'] + [self.ps_rd[yb]]) if i == 0 else [], signal=False)
                    i += 1
            for s_ in range(l + 1):
                t_y = P.op('pe', lambda e, l=l, s_=s_, kt=kt, yps=yps: e.matmul(
                    yps[:, :], self.Kin[:, kt, l - s_, :], uT[:, kt, s_:SEQ:4], start=False, stop=(s_ == l)),
                    signal=(s_ == l))
            i = 0
            for pl in range(4):
                for r in range(2):
                    P.op('pe', lambda e, l=l, pl=pl, r=r, i=i, kt=kt: e.matmul(
                        ys[:, 128 + 16 * l:144 + 16 * l], self.CpPad[pl // 2][:, pl % 2, l, r, :, :].rearrange("p g c -> p (g c)"),
                        self.XbS[:, kt % 2, pl, r, :], start=(i == 0), stop=False),
                        waits=[self.ps_rd[7], S['zs_rd']] if i == 0 else [], signal=False)
                    i += 1
            for s_ in range(l + 1):
                t_ys = P.op('pe', lambda e, l=l, s_=s_, kt=kt: e.matmul(
                    ys[:, 128 + 16 * l:144 + 16 * l], self.Kin[:, kt, l - s_, :], uT[:, kt, SEQ + s_:NT:4],
                    start=False, stop=(s_ == l)), signal=(s_ == l))
            self._gelu(yps[:, :], uT[:, kt, l:SEQ:4], self.ta[:, 0:512], [t_y])
            self.ps_rd[yb] = S['gel']
        if not last:
            return
        S['pad_cp_rd'] = t_ys
        S['xb_rd'] = t_ys
        for pl in range(4):
            S['slot_rd'][(4 * kt + pl) % 6] = t_ys
        S['xbs_rd'][kt % 2] = t_ys
        v3 = lambda ap: ap.rearrange("p (t b) -> p t b", t=4)
        self._gelu(v3(ys[:, 128:192]), uT[:, kt, SEQ:NT].rearrange("p (b t) -> p t b", t=4), v3(self.ta[:, 0:64]), [t_ys])
        self.ps_rd[7] = S['gel']
        self.ssm_y_tok[kt] = S['gel']

    def _gelu(self, src, dst, a, waits):
        P, S = self.P, self.S
        t5 = P.op('act', lambda e: e.activation(out=dst, in_=src, func=AF.Gelu_apprx_tanh), waits=list(waits))
        S['gel'] = t5

    def ssm_glu(self):
        P, S, d = self.P, self.S, self.dram
        uT = self.uT
        for i in self.cp_slots:
            self.ws_release(i, S['pad_cp_rd'])
        self.ws_release(self.tmp_slot, S['dmp'])
        wv, t_w, wi = self.ws_load(d["wglu"], lambda t: t[:, 0:2048].rearrange("p (k n) -> p k n", k=4))
        gt = self.glu_tmp
        prod = None
        t_g = None
        for ti, (t0, n) in enumerate(TT):
            ta_ = []
            for oc in range(4):
                b = 4 + oc
                gps = self.ps[b]
                for kt in range(4):
                    t_g = P.op('pe', lambda e, kt=kt, oc=oc, t0=t0, n=n, gps=gps: e.matmul(
                        gps[:, :n], wv[:, kt, oc * 128:(oc + 1) * 128], uT[:, kt, t0:t0 + n], start=(kt == 0), stop=(kt == 3)),
                        waits=([t_w, self.ps_rd[b]] + [self.ssm_y_tok[k] for k in range(4)]) if kt == 0 else [],
                        signal=(kt == 3))
                t_a = P.op('act', lambda e, oc=oc, n=n, gps=gps: e.activation(
                    out=gt[:, oc, :n], in_=gps[:, :n], func=AF.Sigmoid, bias=self.bgl[:, oc:oc + 1], scale=1.0),
                    waits=[t_g, prod, S['dve']])
                self.ps_rd[b] = t_a
                ta_.append(t_a)
            for oc in range(4):
                prod = P.op('dve', lambda e, oc=oc, t0=t0, n=n: e.tensor_tensor(
                    out=uT[:, oc, t0:t0 + n], in0=uT[:, oc, t0:t0 + n], in1=gt[:, oc, :n], op=ALU.mult),
                    waits=ta_ + [t_g])
                self.ssm_tok[(oc, ti)] = prod
        self.ws_release(wi, t_g)
        self.ssmT = uT
        s_o = self.slot("d_st")
        self.out_toks.append(P.dma('sp', lambda e: e.dma_start(out=d["st_p"], in_=self.stp[:]), s_o, waits=list(S['xf_rd'])))
        s_o2 = self.slot("d_st2")
        self.out_toks.append(P.dma('sp', lambda e: e.dma_start(out=d["st_s"], in_=self.x0[:]), s_o2, waits=[S['zs_rd'], S['xb_rd']]))


def _layout(inputs):
    f32 = np.float32
    xp = np.asarray(inputs["x_prompt"], f32)
    xs = np.asarray(inputs["x_sample"], f32)
    shared = {}
    norms = np.stack([inputs["ffn1_norm"][0], inputs["mix_norm"][0], inputs["ffn2_norm"][0], inputs["final_norm"]], 0)
    shared["norms"] = np.ascontiguousarray(np.asarray(norms, f32).reshape(4, KC, 128).transpose(2, 0, 1))
    for f, pre in ((1, "ffn1"), (2, "ffn2")):
        wg = np.asarray(inputs[pre + "_w_gate"][0], f32).reshape(KC, 128, NJ, 128)
        wu = np.asarray(inputs[pre + "_w_up"][0], f32).reshape(KC, 128, NJ, 128)
        wgu = np.stack([wg, wu], 0)
        shared["wgu%d" % f] = np.ascontiguousarray(wgu.transpose(3, 2, 0, 1, 4))
        wd = np.asarray(inputs[pre + "_w_down"][0], f32).reshape(2, NJH, 128, KC, 128)
        shared["wd%d" % f] = np.ascontiguousarray(wd.transpose(0, 3, 2, 1, 4))
    w_in = np.asarray(inputs["w_in"][0], f32)
    qperm = w_in[:, :512].reshape(D, 2, 4, 64).transpose(0, 2, 1, 3).reshape(D, 512)
    winp = np.concatenate([qperm, w_in[:, 512:]], 1)
    shared["win"] = np.ascontiguousarray(winp.reshape(KC, 128, 1280).transpose(1, 0, 2))
    w_out = np.asarray(inputs["w_out"][0], f32)
    wa = w_out[:512].reshape(2, 4, 64, D).transpose(1, 0, 2, 3).reshape(4, 128, D)
    wperm = np.concatenate([wa, w_out[512:].reshape(4, 128, D)], 0)
    shared["wout"] = np.ascontiguousarray(wperm.reshape(8, 128, KC, 128).transpose(2, 1, 0, 3))
    shared["cst_ident"] = np.ascontiguousarray(np.eye(128, dtype=f32)[::-1])
    dd = np.arange(256)
    dfl = np.maximum(dd, 1).astype(f32)
    large = 16 + (np.log(dfl / f32(16)) / f32(np.log(128 / 16)) * f32(16)).astype(np.int32)
    bucket = np.where(dd < 16, dd, np.minimum(large, 31))
    oh = np.zeros((32, 256), f32)
    oh[bucket, dd] = 1.0
    shared["cst_oh"] = oh
    bi = np.arange(64) // 4
    shared["cst_blk"] = np.ascontiguousarray(np.where(bi[:, None] == bi[None, :], 0.0, -30000.0).astype(f32)[::-1])
    shared["rel_bias"] = np.asarray(inputs["rel_bias"], f32)
    shared["sinks"] = np.asarray(inputs["sinks"], f32).reshape(1, 8)
    a_re = np.asarray(inputs["a_re"][0], f32); a_im = np.asarray(inputs["a_im"][0], f32)
    ldt = np.broadcast_to(np.asarray(inputs["log_dt"][0], f32)[:, None], (32, 64))
    pm = lambda a: a.reshape(16, 2, 64).transpose(1, 2, 0).reshape(128, 16)
    shared["ssm_pm"] = np.ascontiguousarray(np.stack([pm(a_re), pm(a_im), pm(ldt)], 1))
    cm = lambda a: np.broadcast_to(a.reshape(4, 8, 1, 64).transpose(1, 2, 0, 3), (8, 16, 4, 64)).reshape(128, 4, 64)
    shared["ssm_cm"] = np.ascontiguousarray(np.stack([cm(a_re), cm(a_im), cm(ldt)], 1))
    cpm = lambda a: a.reshape(16, 2, 16, 64).transpose(1, 3, 0, 2).reshape(128, 16, 16)
    shared["c_pm"] = np.ascontiguousarray(np.stack([cpm(np.asarray(inputs["c_re"][0], f32)), cpm(np.asarray(inputs["c_im"][0], f32))], 1))
    bpm = lambda a: a.reshape(16, 2, 64, 16).transpose(1, 2, 0, 3).reshape(128, 16, 16)
    bcm = lambda a: a.reshape(4, 8, 64, 16).transpose(1, 3, 0, 2).reshape(128, 4, 64)
    b_re = np.asarray(inputs["b_re"][0], f32); b_im = np.asarray(inputs["b_im"][0], f32)
    shared["b_pm"] = np.ascontiguousarray(np.stack([bpm(b_re), bpm(b_im)], 1))
    shared["b_cm"] = np.ascontiguousarray(np.stack([bcm(b_re), bcm(b_im)], 1))
    shared["dskip"] = np.ascontiguousarray(np.asarray(inputs["d_skip"][0], f32).reshape(4, 128).T)
    shared["bglu"] = np.ascontiguousarray(np.asarray(inputs["b_glu"][0], f32).reshape(4, 128).T)
    shared["wglu"] = np.ascontiguousarray(np.asarray(inputs["w_glu"][0], f32).reshape(4, 128, 512).transpose(1, 0, 2))
    pidx = np.arange(128)
    m2 = np.zeros((128, 4, 8), f32); m3 = np.zeros((128, 4, 2), f32)
    for pl in range(4):
        for gg in range(2):
            m2[pidx // 64 == gg, pl, 2 * pl + gg] = 1.0
            m3[pidx // 16 == 2 * pl + gg, pl, gg] = 1.0
    shared["cst_m2"] = m2
    shared["cst_m3"] = m3
    shared["cst_eye"] = np.eye(128, dtype=f32)
    sre = np.asarray(inputs["state_ssm_re"][0], f32); sim = np.asarray(inputs["state_ssm_im"][0], f32)
    ck = np.asarray(inputs["cache_k"][0], f32)
    cv = np.asarray(inputs["cache_v"][0], f32)
    maps = []
    for c in range(NCORES):
        X = np.concatenate([xp[c], xs[16 * c:16 * c + 16].reshape(NS, D)], 0)
        m = dict(shared)
        m["xT"] = np.ascontiguousarray(X.T.reshape(KC, 128, NT).transpose(1, 0, 2))
        ckc, cvc = ck[16 * c:16 * c + 16], cv[16 * c:16 * c + 16]
        m["cKT"] = np.ascontiguousarray(ckc.transpose(2, 3, 0, 1).reshape(128, 16, 128))
        m["cV"] = np.ascontiguousarray(cvc.transpose(1, 0, 2, 3).reshape(128, 16, 128))
        m["cK_nat"] = np.ascontiguousarray(ckc.reshape(16, 128, 128))
        m["cV_nat"] = np.ascontiguousarray(cvc.reshape(16, 128, 128))
        x0 = np.stack([sre[16 * c:16 * c + 16], sim[16 * c:16 * c + 16]], 0)
        m["x0"] = np.ascontiguousarray(x0.reshape(2, 16, 16, 2, 64).transpose(3, 4, 2, 0, 1).reshape(128, 16, 2, 16))
        maps.append(m)
    return maps


_NC_CACHE = {}


def _get_nc(debug=None):
    key = repr(sorted((debug or {}).items()))
    if key not in _NC_CACHE:
        _NC_CACHE[key] = Builder(debug).build()
    return _NC_CACHE[key]


def kernel(**inputs):
    maps = _layout(inputs)
    nc = _get_nc()
    res = run_bass_kernel_spmd(nc, maps, core_ids=list(range(NCORES)))
    outs = res.results
    yp = np.zeros((8, SEQ, D), np.float32)
    ys = np.zeros((128, 4, D), np.float32)
    kp = np.zeros((1, 8, 128, 2, 64), np.float32); vp = np.zeros_like(kp)
    rp = np.zeros((1, 8, 32, 64), np.float32); ip = np.zeros_like(rp)
    ks = np.zeros((1, 128, 128, 2, 64), np.float32); vs = np.zeros_like(ks)
    rs = np.zeros((1, 128, 32, 64), np.float32); is_ = np.zeros_like(rs)
    for c in range(NCORES):
        o = outs[c]
        Y = np.asarray(o["yT"]).transpose(1, 0, 2).reshape(D, NT).T
        yp[c] = Y[:SEQ]
        ys[16 * c:16 * c + 16] = Y[SEQ:].reshape(16, 4, D)
        kp[0, c] = np.asarray(o["kp"]).reshape(128, 2, 64)
        vp[0, c] = np.asarray(o["vp"]).reshape(128, 2, 64)
        ks[0, 16 * c:16 * c + 16] = np.asarray(o["ks"]).reshape(16, 128, 2, 64)
        vs[0, 16 * c:16 * c + 16] = np.asarray(o["vs"]).reshape(16, 128, 2, 64)
        sp = np.asarray(o["st_p"]).reshape(2, 64, 16, 2).transpose(2, 0, 1, 3).reshape(32, 64, 2)
        rp[0, c], ip[0, c] = sp[..., 0], sp[..., 1]
        ss = np.asarray(o["st_s"]).reshape(2, 64, 16, 2, 16).transpose(4, 2, 0, 1, 3).reshape(16, 32, 64, 2)
        rs[0, 16 * c:16 * c + 16], is_[0, 16 * c:16 * c + 16] = ss[..., 0], ss[..., 1]
    return yp, ys, kp, vp, rp, ip, ks, vs, rs, is_
```

```python
import contextlib
import numpy as np
import concourse.bass as bass
import concourse.mybir as mybir
from concourse.bass_utils import run_bass_kernel_spmd

F32 = mybir.dt.float32
BF16 = mybir.dt.bfloat16
I32 = mybir.dt.int32
AF = mybir.ActivationFunctionType
ALU = mybir.AluOpType

NCORES = 8
D = 1024
KC = 8
DFF = 2816
NJ = 22
NJH = 11
SEQ = 2048
NS = 64
NT = SEQ + NS
TT = [(0, 512), (512, 512), (1024, 512), (1536, 512), (2048, 64)]
EPS = 1e-6
ENGS = ('pe', 'act', 'dve', 'pool', 'sp')


class Prog:
    def __init__(self, nc):
        self.nc = nc
        self.q = {e: [] for e in ENGS}
        self.sem = {}
        self.cnt = {}
        self.waited = {e: {} for e in ENGS}
        self._stack = []
        self.last = {e: None for e in ENGS}

    def new_sem(self, name):
        cm = self.nc.semaphore(name)
        h = cm.__enter__()
        self._stack.append(cm)
        return h

    def _waits(self, eng, waits):
        out = []
        for w in waits:
            if w is None:
                continue
            sem, val = w
            key = id(sem)
            if self.waited[eng].get(key, 0) >= val:
                continue
            self.waited[eng][key] = val
            out.append((sem, val))
        return out

    def op(self, eng, fn, waits=(), signal=True):
        ws = self._waits(eng, waits)
        tok = None
        if signal:
            if eng not in self.sem:
                self.sem[eng] = self.new_sem('s_' + eng)
                self.cnt[eng] = 0
            self.cnt[eng] += 1
            tok = (self.sem[eng], self.cnt[eng])
            self.last[eng] = tok
        self.q[eng].append((fn, ws, tok, 1))
        return tok

    def dma(self, eng, fn, slot, waits=()):
        ws = self._waits(eng, waits)
        slot.count += 16
        tok = (slot.sem, slot.count)
        self.q[eng].append((fn, ws, tok, 16))
        return tok

    def wait_only(self, eng, waits):
        ws = self._waits(eng, waits)
        if ws:
            self.q[eng].append((None, ws, None, 0))

    def replay(self, eng, engine):
        for fn, ws, tok, inc in self.q[eng]:
            for sem, val in ws:
                engine.wait_ge(sem, val)
            if fn is None:
                continue
            ins = fn(engine)
            if tok is not None:
                ins.then_inc(tok[0], inc)

    def close(self):
        for cm in reversed(self._stack):
            cm.__exit__(None, None, None)


class DmaSlot:
    def __init__(self, prog, name):
        self.sem = prog.new_sem(name)
        self.count = 0


class Builder:
    def __init__(self, debug=None):
        self.debug = debug or {}
        self.nc = bass.Bass("TRN2", target_bir_lowering=False)
        self.P = Prog(self.nc)
        self.es = contextlib.ExitStack()
        self.dram = {}

    def din(self, name, shape, dt=F32):
        t = self.nc.dram_tensor(name, list(shape), dt, kind="ExternalInput").ap()
        self.dram[name] = t
        return t

    def dout(self, name, shape, dt=F32):
        t = self.nc.dram_tensor(name, list(shape), dt, kind="ExternalOutput").ap()
        self.dram[name] = t
        return t

    def sb(self, name, shape, dt):
        return self.es.enter_context(self.nc.sbuf_tensor(name, list(shape), dt))

    def slot(self, name):
        return DmaSlot(self.P, name)

    def ws_load(self, src_ap, shape_fn):
        P = self.P
        i = self.ws_next % self.WS_N
        self.ws_next += 1
        dst = shape_fn(self.ws[i])
        tok = P.dma('pool', lambda e, d=dst, s=src_ap: e.dma_start(out=d, in_=s), self.ws_slot[i],
                    waits=[self.ws_free[i]])
        return dst, tok, i

    def ws_release(self, i, tok):
        self.ws_free[i] = tok

    def build(self):
        nc, P = self.nc, self.P
        dbg = self.debug
        xT_d = self.din("xT", [128, KC, NT])
        norms_d = self.din("norms", [128, 4, KC])
        wgu_d = [self.din("wgu%d" % f, [NJ, 128, 2, KC, 128]) for f in (1, 2)]
        wd_d = [self.din("wd%d" % f, [2, KC, 128, NJH, 128]) for f in (1, 2)]
        yT_d = self.dout("yT", [128, KC, NT])
        self.mix_io()

        self.xT = self.sb("xT_s", [128, KC, NT], F32)
        self.hT = self.sb("hT_s", [128, KC, NT], BF16)
        self.aT = self.sb("aT_s", [128, NJH, NT], BF16)
        self.WS_N = 3
        self.ws = [self.sb("ws%d" % i, [128, 2048], BF16) for i in range(self.WS_N)]
        self.ws_slot = [self.slot("wsd%d" % i) for i in range(self.WS_N)]
        self.ws_free = [None] * self.WS_N
        self.ws_next = 0
        self.norms = self.sb("norms_s", [128, 4, KC], F32)
        self.ones = self.sb("ones_s", [128, 128], BF16)
        self.epst = self.sb("eps_s", [128, 1], F32)
        self.sq = [self.sb("sq%d" % i, [128, 512], BF16) for i in range(2)]
        self.sg = [self.sb("sg%d" % i, [128, 512], F32) for i in range(2)]
        self.rt = [self.sb("rt%d" % i, [128, 512], F32) for i in range(2)]
        self.ps = [self.es.enter_context(nc.psum_tensor("ps%d" % i, [128, 512], F32)) for i in range(8)]
        self.ps_rd = [None] * 8

        t_ones = P.op('dve', lambda e: e.memset(self.ones[:], 1.0 / D))
        t_eps = P.op('dve', lambda e: e.memset(self.epst[:], EPS))
        self.t_const = t_eps
        s_n = self.slot("d_norm")
        self.t_norms = P.dma('sp', lambda e: e.dma_start(out=self.norms[:], in_=norms_d), s_n)

        self.tok_x = {}
        for ti, (t0, n) in enumerate(TT):
            s = self.slot("d_x%d" % ti)
            tk = P.dma('sp', lambda e, t0=t0, n=n: e.dma_start(out=self.xT[:, :, t0:t0 + n], in_=xT_d[:, :, t0:t0 + n]), s,
                       waits=([self.tok_x[(0, 0)]] if ti == 1 else []))
            for kc in range(KC):
                self.tok_x[(kc, ti)] = tk
        self.tok_h = {}
        self.sq_rd = [None, None]
        self.sg_rd = [None, None]
        self.rt_rd = [None, None]
        self.sq_i = 0
        self.hT_free = None

        self.out_toks = []
        self.ffn(0, wgu_d[0], wd_d[0])
        if not dbg.get("skip_mixer"):
            self.mixer()
        self.yT_d = yT_d
        self.ffn(2, wgu_d[1], wd_d[1])
        self.final_norm(yT_d)

        with nc.Block() as block:
            @block.sync
            def _(e):
                P.replay('sp', e)

            @block.gpsimd
            def _(e):
                P.replay('pool', e)

            @block.tensor
            def _(e):
                P.replay('pe', e)

            @block.scalar
            def _(e):
                P.replay('act', e)

            @block.vector
            def _(e):
                P.replay('dve', e)
        P.close()
        self.es.close()
        return nc

    def norm(self, gi, final=False):
        for ti in range(len(TT)):
            self.norm_tile(gi, ti, final)

    def norm_tile(self, gi, ti, final=False):
        P = self.P
        xT, hT = self.xT, self.hT
        MS0 = 6
        toks = {}
        for ti, (t0, n) in [(ti, TT[ti])]:
            msb = MS0 + (ti % 2)
            ms = self.ps[msb]
            t_mm = None
            for kc in range(KC):
                b = self.sq_i % 2
                self.sq_i += 1
                t_sq = P.op('act', lambda e, b=b, kc=kc, t0=t0, n=n: e.activation(
                    out=self.sq[b][:, :n], in_=xT[:, kc, t0:t0 + n], func=AF.Square),
                    waits=[self.tok_x[(kc, ti)], self.sq_rd[b]])
                last = kc == KC - 1
                t_mm = P.op('pe', lambda e, b=b, kc=kc, n=n, ms=ms: e.matmul(
                    ms[:, :n], self.ones[:], self.sq[b][:, :n], start=(kc == 0), stop=(kc == KC - 1)),
                    waits=[t_sq, self.t_const] + ([self.ps_rd[msb]] if kc == 0 else []), signal=True)
                self.sq_rd[b] = t_mm
            rb = ti % 2
            rt = self.rt[rb]
            t_s = P.op('act', lambda e, n=n, ms=ms, rt=rt: e.activation(
                out=rt[:, :n], in_=ms[:, :n], func=AF.Ln, bias=self.epst[:, 0:1], scale=1.0),
                waits=[t_mm, self.rt_rd[rb], self.t_const])
            self.ps_rd[msb] = t_s
            t_r = P.op('act', lambda e, n=n, rt=rt: e.activation(
                out=rt[:, :n], in_=rt[:, :n], func=AF.Exp, scale=-0.5), waits=[t_s])
            t_h = None
            for kc in range(KC):
                if final:
                    t_h = P.op('dve', lambda e, kc=kc, t0=t0, n=n, rt=rt: e.scalar_tensor_tensor(
                        out=xT[:, kc, t0:t0 + n], in0=xT[:, kc, t0:t0 + n], scalar=self.norms[:, gi, kc:kc + 1],
                        in1=rt[:, :n], op0=ALU.mult, op1=ALU.mult),
                        waits=[t_r, self.t_norms, self.tok_x[(kc, ti)]])
                    self.tok_x[(kc, ti)] = t_h
                else:
                    t_h = P.op('dve', lambda e, kc=kc, t0=t0, n=n, rt=rt: e.scalar_tensor_tensor(
                        out=hT[:, kc, t0:t0 + n], in0=xT[:, kc, t0:t0 + n], scalar=self.norms[:, gi, kc:kc + 1],
                        in1=rt[:, :n], op0=ALU.mult, op1=ALU.mult),
                        waits=[t_r, self.t_norms, self.tok_x[(kc, ti)], self.hT_free])
                    self.tok_h[(kc, ti)] = t_h
            self.rt_rd[rb] = t_h
        return toks

    def ffn(self, gi, wgu_d, wd_d):
        P = self.P
        xT, hT, aT = self.xT, self.hT, self.aT
        LOOK = 2 if gi == 0 else 3
        normed = getattr(self, "normed_upto", {}).get(gi, 0)
        for ti in range(normed, LOOK):
            self.norm_tile(gi, ti)
        normed = max(normed, LOOK)
        GB, UB, YB = (0, 1), (2, 3), (4, 5, 0, 1)
        if not hasattr(self, "aT_rd"):
            self.aT_rd = None
        cnt = 0
        ycnt = 0
        for h in range(2):
            tok_a = {}
            for jj in range(NJH):
                j = h * NJH + jj
                wv, t_w, wi = self.ws_load(
                    wgu_d[j].rearrange("p g k n -> p (g k n)"),
                    lambda t: t[:, 0:2048])
                wv4 = wv.rearrange("p (g k n) -> p g k n", g=2, k=KC)
                t_last = None
                for ti, (t0, n) in enumerate(TT):
                    if h == 0 and jj == 0 and normed < len(TT):
                        self.norm_tile(gi, normed)
                        normed += 1
                    gb = GB[cnt % 2]
                    ub = UB[cnt % 2]
                    sgi = cnt % 2
                    cnt += 1
                    gps, ups = self.ps[gb], self.ps[ub]
                    for kc in range(KC):
                        t_g = P.op('pe', lambda e, kc=kc, t0=t0, n=n, gps=gps, wv4=wv4: e.matmul(
                            gps[:, :n], wv4[:, 0, kc, :], hT[:, kc, t0:t0 + n], start=(kc == 0), stop=(kc == KC - 1)),
                            waits=([t_w, self.ps_rd[gb]] if kc == 0 else []) + [self.tok_h[(kc, ti)]],
                            signal=(kc == KC - 1))
                    for kc in range(KC):
                        t_u = P.op('pe', lambda e, kc=kc, t0=t0, n=n, ups=ups, wv4=wv4: e.matmul(
                            ups[:, :n], wv4[:, 1, kc, :], hT[:, kc, t0:t0 + n], start=(kc == 0), stop=(kc == KC - 1)),
                            waits=([self.ps_rd[ub]] if kc == 0 else []),
                            signal=(kc == KC - 1))
                    t_last = t_u
                    sg = self.sg[sgi]
                    t_s = P.op('act', lambda e, n=n, gps=gps, sg=sg: e.activation(
                        out=sg[:, :n], in_=gps[:, :n], func=AF.Silu), waits=[t_g, self.sg_rd[sgi]])
                    self.ps_rd[gb] = t_s
                    t_a = P.op('dve', lambda e, jj=jj, t0=t0, n=n, ups=ups, sg=sg: e.tensor_tensor(
                        out=aT[:, jj, t0:t0 + n], in0=ups[:, :n], in1=sg[:, :n], op=ALU.mult),
                        waits=[t_s, t_u, self.aT_rd, getattr(self, 'dbg_tok', None)])
                    self.sg_rd[sgi] = t_a
                    self.ps_rd[ub] = t_a
                    tok_a[(jj, ti)] = t_a
                    self.bg_pump(2)
                self.ws_release(wi, t_last)
                if h == 0 and jj == 0 and gi == 0 and not self.debug.get("skip_mixer"):
                    self.mix_setup()
                    def chain():
                        yield from self._tables_gen()
                        if not self.debug.get("skip_ssm"):
                            yield from self.ssm_setup_gen(1)
                    self.bg = chain()
            if h == 1:
                self.hT_free = t_last
                if gi == 0 and not self.debug.get("skip_mixer") and not self.debug.get("skip_ssm"):
                    self.bg_pump(10 ** 9)
                    self.bg = self.ssm_setup_gen(2)
            if gi == 2 and h == 0 and getattr(self, "mix_end_tok", None) is not None and not self.debug.get("no_tail_opt"):
                stb = self.stage[:].rearrange("p a h q -> p (a h q)").bitcast(BF16)
                fl = lambda t: t[:].rearrange("p a b -> p (a b)")
                bufs = [stb[:, 0:1408], stb[:, 2048:3456], fl(self.cKT)[:, 0:1408], fl(self.cV)[:, 0:1408],
                        self.BT[:].rearrange("p a b c -> p (a b c)")[:, 0:1408]]
                self.res_wd = []
                for c_, bf in enumerate(bufs):
                    tk = P.dma('pool', lambda e, bf=bf, c_=c_: e.dma_start(out=bf, in_=wd_d[1, c_].rearrange("p j n -> p (j n)")),
                               self.slot("d_rwd%d" % c_), waits=[self.mix_end_tok])
                    self.res_wd.append((bf.rearrange("p (j n) -> p j n", j=NJH), tk))
            if gi == 2 and h == 1 and getattr(self, "res_wd", None):
                chunks = list(self.res_wd)
                ring = []
                for c in range(5, KC):
                    wv, t_w, wi = self.ws_load(wd_d[h, c].rearrange("p j n -> p (j n)"), lambda t: t[:, 0:NJH * 128])
                    chunks.append((wv.rearrange("p (j n) -> p j n", j=NJH), t_w))
                    ring.append(wi)
                t_y = None
                for ti, (t0, n) in enumerate(TT):
                    for c in range(KC):
                        wv3, t_w = chunks[c]
                        yb = YB[ycnt % 4]
                        ycnt += 1
                        yps = self.ps[yb]
                        for jj in range(NJH):
                            t_y = P.op('pe', lambda e, jj=jj, t0=t0, n=n, yps=yps, wv3=wv3: e.matmul(
                                yps[:, :n], wv3[:, jj, :], aT[:, jj, t0:t0 + n], start=(jj == 0), stop=(jj == NJH - 1)),
                                waits=([t_w, self.ps_rd[yb]] if jj == 0 else []) + [tok_a[(jj, ti)]],
                                signal=(jj == NJH - 1))
                        t_x = P.op('dve', lambda e, c=c, t0=t0, n=n, yps=yps: e.scalar_tensor_tensor(
                            out=xT[:, c, t0:t0 + n], in0=yps[:, :n], scalar=0.5, in1=xT[:, c, t0:t0 + n],
                            op0=ALU.mult, op1=ALU.add),
                            waits=[t_y, self.tok_x[(c, ti)]])
                        self.tok_x[(c, ti)] = t_x
                        self.ps_rd[yb] = t_x
                    self.final_tile(ti)
                for wi in ring:
                    self.ws_release(wi, t_y)
                self.aT_rd = t_y
                continue
            for c in range(KC):
                wv, t_w, wi = self.ws_load(
                    wd_d[h, c].rearrange("p j n -> p (j n)"),
                    lambda t: t[:, 0:NJH * 128])
                wv3 = wv.rearrange("p (j n) -> p j n", j=NJH)
                t_y = None
                for ti, (t0, n) in enumerate(TT):
                    yb = YB[ycnt % 4]
                    ycnt += 1
                    yps = self.ps[yb]
                    for jj in range(NJH):
                        t_y = P.op('pe', lambda e, jj=jj, t0=t0, n=n, yps=yps, wv3=wv3: e.matmul(
                            yps[:, :n], wv3[:, jj, :], aT[:, jj, t0:t0 + n], start=(jj == 0), stop=(jj == NJH - 1)),
                            waits=([t_w, self.ps_rd[yb]] if jj == 0 else []) + [tok_a[(jj, ti)]],
                            signal=(jj == NJH - 1))
                    t_x = P.op('dve', lambda e, c=c, t0=t0, n=n, yps=yps: e.scalar_tensor_tensor(
                        out=xT[:, c, t0:t0 + n], in0=yps[:, :n], scalar=0.5, in1=xT[:, c, t0:t0 + n],
                        op0=ALU.mult, op1=ALU.add),
                        waits=[t_y, self.tok_x[(c, ti)]])
                    self.tok_x[(c, ti)] = t_x
                    self.ps_rd[yb] = t_x
                    self.bg_pump(3)
                    if gi == 2 and h == 1 and c == KC - 1 and getattr(self, "yT_d", None) is not None:
                        self.final_tile(ti)
                self.ws_release(wi, t_y)
                self.aT_rd = t_y

    def final_tile(self, ti):
        P = self.P
        if not hasattr(self, "fin_slot"):
            self.fin_slot = self.slot("d_out")
            self.fin_done = set()
        self.norm_tile(3, ti, final=True)
        t0, n = TT[ti]
        P.dma('sp', lambda e, t0=t0, n=n: e.dma_start(out=self.yT_d[:, :, t0:t0 + n], in_=self.xT[:, :, t0:t0 + n]),
              self.fin_slot, waits=[self.tok_x[(kc, ti)] for kc in range(KC)])
        self.fin_done.add(ti)

    def final_norm(self, yT_d):
        P = self.P
        for ti in range(len(TT)):
            if ti not in getattr(self, "fin_done", set()):
                self.final_tile(ti)
        P.wait_only('sp', [(self.fin_slot.sem, self.fin_slot.count)] + list(self.out_toks))

    def mix_io(self):
        d = self.dram
        self.din("win", [128, KC, 1280])
        self.din("wout", [KC, 128, 8, 128])
        self.din("cst_ident", [128, 128])
        self.din("cst_oh", [32, 256])
        self.din("cst_blk", [64, 64])
        self.din("rel_bias", [32, 8])
        self.din("sinks", [1, 8])
        self.din("cKT", [128, 16, 128])
        self.din("cV", [128, 16, 128])
        self.din("cK_nat", [16, 128, 128])
        self.din("cV_nat", [16, 128, 128])
        self.dout("kp", [128, 128])
        self.dout("vp", [128, 128])
        self.dout("ks", [16, 128, 128])
        self.dout("vs", [16, 128, 128])
        self.scr = self.nc.dram_tensor("scr_f", [8, 2, 256], F32).ap()
        self.ssm_io()
        if self.debug.get("dump_attn"):
            self.dout("dbg_attn", [128, 4, NT], BF16)

    def mix_setup(self):
        nc, P, d = self.nc, self.P, self.dram
        aT = self.aT
        NEG = -30000.0
        self.qT2 = aT[:, 0:4, :]
        self.uT = aT[:, 4:8, :]
        self.kT = aT[:, 8, :]
        flat = aT[:, 9:11, :].rearrange("p a t -> p (a t)")
        self.vtok = flat[:, 0:17 * 128].rearrange("p (b n) -> p b n", n=128)
        self.ident = self.sb("ident_s", [128, 128], BF16)
        self.ones1 = self.sb("ones1_s", [128, 64], BF16)
        self.BT = self.sb("BT_s", [128, 2, 8, 128], BF16)
        self.BTsc = self.sb("BTsc_s", [128, 2, 16, 4, 4], BF16)
        self.BTsn = self.sb("BTsn_s", [128, 4, 64], BF16)
        self.ES = self.sb("ES_s", [128, 4], F32)
        self.cKT = self.sb("cKT_s", [128, 16, 128], BF16)
        self.cV = self.sb("cV_s", [128, 16, 128], BF16)
        self.stage = self.sb("stage_s", [128, 2, 8, 128], F32)
        self.kvo = [self.sq[i][:].bitcast(F32) for i in range(2)]
        self.pT = [self.sg[i // 2][:, 256 * (i % 2):256 * (i % 2) + 256].bitcast(BF16) for i in range(4)]
        rb_s = self.sb("rb_s", [32, 8], F32)
        oh_s = self.sb("oh_s", [32, 256], F32)
        fm = self.sb("fm_s", [8, 2, 256], F32)
        blk_s = self.sb("blk_s", [128, 64], F32)
        sk_s = self.sb("sk_s", [128, 4], F32)

        t1 = P.dma('sp', lambda e: e.dma_start(out=rb_s[:], in_=d["rel_bias"]), self.slot("d_rb"))
        t2 = P.dma('sp', lambda e: e.dma_start(out=oh_s[:], in_=d["cst_oh"]), self.slot("d_oh"))
        s_blk = self.slot("d_blk")
        for g in range(2):
            t3 = P.dma('sp', lambda e, g=g: e.dma_start(out=blk_s[64 * g:64 * g + 64, :], in_=d["cst_blk"]), s_blk)
        s_sk = self.slot("d_sk")
        for g in range(2):
            t4 = P.dma('sp', lambda e, g=g: e.dma_start(
                out=sk_s[64 * g:64 * g + 64, :], in_=d["sinks"][0:1, 4 * g:4 * g + 4].to_broadcast([64, 4])), s_sk)
        if self.debug.get('ckpt', 99) < 1:
            self.out_toks = []
            return
        t_es = P.op('act', lambda e: e.activation(out=self.ES[:], in_=sk_s[:], func=AF.Exp), waits=[t4])
        self.t_es = t_es
        t_o1 = P.op('dve', lambda e: e.memset(self.ones1[:], 1.0))
        self.t_ones1 = t_o1
        if self.debug.get('ckpt', 99) < 2:
            self.out_toks = []
            return
        def tables_gen():
            fps = self.ps[7]
            t_f = P.op('pe', lambda e: e.matmul(fps[0:8, 0:256], rb_s[0:32, 0:8], oh_s[0:32, :], start=True, stop=True),
                       waits=[t1, t2, self.ps_rd[7]])
            for _ in range(6):
                yield
            t_m = P.op('dve', lambda e: e.memset(fm[:], NEG))
            t_c1 = P.op('dve', lambda e: e.tensor_copy(out=fm[:, 0, 127:255], in_=fps[0:8, 0:128]), waits=[t_f, t_m])
            t_c2 = P.op('dve', lambda e: e.tensor_copy(out=fm[:, 1, 0:127], in_=fps[0:8, 1:128]), waits=[t_f, t_m])
            self.ps_rd[7] = t_c2
            t_sc = P.dma('sp', lambda e: e.dma_start(out=self.scr, in_=fm[:]), self.slot("d_scr"), waits=[t_c1, t_c2])
            from concourse.ap import AP as _AP
            scr_t = self.scr.tensor
            s_tp = self.slot("d_tp")
            tl = None
            for ty in range(2):
                for h in range(8):
                    src = _AP(scr_t, (h * 2 + ty) * 256, [[1, 128], [1, 128]])
                    tl = P.dma('sp', lambda e, ty=ty, h=h, src=src: e.dma_start(out=self.stage[:, ty, h, :], in_=src),
                               s_tp, waits=[t_sc])
            P.wait_only('sp', [tl])
            for _ in range(30):
                yield
            t_bt = P.op('dve', lambda e: e.tensor_copy(out=self.BT[:], in_=self.stage[:]), waits=[(s_tp.sem, s_tp.count)])
            stage2 = self.sb("stage2_s", [128, 8, 4], F32)
            stage3 = self.sb("stage3_s", [128, 4, 64], F32)
            s_tp2 = self.slot("d_tp2")
            for h in range(8):
                src = _AP(scr_t, (h * 2 + 1) * 256, [[1, 128], [1, 4]])
                P.dma('sp', lambda e, h=h, src=src: e.dma_start(out=stage2[:, h, :], in_=src), s_tp2, waits=[t_sc])
                src = _AP(scr_t, (h * 2 + 0) * 256 + 64, [[1, 64], [1, 64]])
                P.dma('sp', lambda e, h=h, src=src: e.dma_start(
                    out=stage3[64 * (h // 4):64 * (h // 4) + 64, h % 4, :], in_=src), s_tp2, waits=[t_sc])
            tk2 = (s_tp2.sem, s_tp2.count)
            tb = []
            for g in range(2):
                tb.append(P.op('dve', lambda e, g=g: e.tensor_copy(
                    out=self.BTsc[:, g, :, :, :], in_=stage2[:, 4 * g:4 * g + 4, :].unsqueeze(1).to_broadcast([128, 16, 4, 4])),
                    waits=[tk2]))
            tb.append(P.op('dve', lambda e: e.tensor_tensor(
                out=self.BTsn[:], in0=stage3[:], in1=blk_s[:].unsqueeze(1).to_broadcast([128, 4, 64]), op=ALU.add),
                waits=[tk2, t3]))
            self.t_tables_all = tb
            self.t_tables = [t_bt] + tb
            yield
        self._tables_gen = tables_gen
        if self.debug.get('ckpt', 99) < 6:
            self.out_toks = []
            return
        self.t_ident = P.dma('pool', lambda e: e.dma_start(out=self.ident[:], in_=d["cst_ident"]), self.slot("d_id"))
        self.ssm_alloc()
        s_cp = self.slot("d_cp")
        P.dma('sp', lambda e: e.dma_start(out=d["ks"][:, 0:124, :], in_=d["cK_nat"][:, 4:128, :]), s_cp)
        self.out_toks = [(s_cp.sem, s_cp.count)]
        if self.debug.get('cp', 2) > 1:
            s_cp2 = self.slot("d_cp2")
            self.out_toks.append(P.dma('sp', lambda e: e.dma_start(out=d["vs"][:, 0:124, :], in_=d["cV_nat"][:, 4:128, :]), s_cp2))

    def mixer(self):
        nc, P, d = self.nc, self.P, self.dram
        xT, hT = self.xT, self.hT
        dbg = self.debug
        if dbg.get("setup_only"):
            return
        self.bg_pump(10 ** 9)
        for ti_ in range(3):
            self.norm_tile(1, ti_)
        mix_normed = [3]
        win = d["win"]
        pcnt = 0
        tok_q = {}
        tok_k = {}
        tok_u = {}
        t_last = None
        def u_group(kt, ti, wv, t_w, b):
            t0, n = TT[ti]
            pp = self.ps[b]
            t_p = None
            for kc in range(KC):
                t_p = P.op('pe', lambda e, kc=kc, t0=t0, n=n, pp=pp, wv=wv: e.matmul(
                    pp[:, :n], wv[:, kc, :], hT[:, kc, t0:t0 + n], start=(kc == 0), stop=(kc == KC - 1)),
                    waits=([t_w, self.ps_rd[b], self.aT_rd] if kc == 0 else []) + [self.tok_h[(kc, ti)]],
                    signal=(kc == KC - 1))
            if ti % 2 == 0:
                t_e = P.op('dve', lambda e, kt=kt, t0=t0, n=n, pp=pp: e.tensor_copy(
                    out=self.uT[:, kt, t0:t0 + n], in_=pp[:, :n]), waits=[t_p])
            else:
                t_e = P.op('act', lambda e, kt=kt, t0=t0, n=n, pp=pp: e.activation(
                    out=self.uT[:, kt, t0:t0 + n], in_=pp[:, :n], func=AF.Copy), waits=[t_p])
            tok_u[(kt, ti)] = t_e
            self.ps_rd[b] = t_e
            return t_p

        for ch in [0, 1, 2, 3, 4]:
            wv, t_w, wi = self.ws_load(win[:, :, ch * 128:(ch + 1) * 128], lambda t: t[:, 0:1024].rearrange("p (k n) -> p k n", k=KC))
            for ti, (t0, n) in enumerate(TT):
                if mix_normed[0] < len(TT):
                    self.norm_tile(1, mix_normed[0])
                    mix_normed[0] += 1
                b = pcnt % 4
                pcnt += 1
                pp = self.ps[b]
                for kc in range(KC):
                    t_p = P.op('pe', lambda e, kc=kc, t0=t0, n=n, pp=pp, wv=wv: e.matmul(
                        pp[:, :n], wv[:, kc, :], hT[:, kc, t0:t0 + n], start=(kc == 0), stop=(kc == KC - 1)),
                        waits=([t_w, self.ps_rd[b], self.aT_rd] if kc == 0 else []) + [self.tok_h[(kc, ti)]],
                        signal=(kc == KC - 1))
                t_last = t_p
                if ch < 4:
                    t_e = P.op('act', lambda e, ch=ch, t0=t0, n=n, pp=pp: e.activation(
                        out=self.qT2[:, ch, t0:t0 + n], in_=pp[:, :n], func=AF.Copy, scale=0.125), waits=[t_p])
                    tok_q[(ch, ti)] = t_e
                elif ch == 4:
                    t_e = P.op('dve', lambda e, t0=t0, n=n, pp=pp: e.tensor_copy(
                        out=self.kT[:, t0:t0 + n], in_=pp[:, :n]), waits=[t_p])
                    tok_k[ti] = t_e
                else:
                    kt = ch - 6
                    eng = 'dve' if (ti % 2 == 0) else 'act'
                    if eng == 'dve':
                        t_e = P.op('dve', lambda e, kt=kt, t0=t0, n=n, pp=pp: e.tensor_copy(
                            out=self.uT[:, kt, t0:t0 + n], in_=pp[:, :n]), waits=[t_p])
                    else:
                        t_e = P.op('act', lambda e, kt=kt, t0=t0, n=n, pp=pp: e.activation(
                            out=self.uT[:, kt, t0:t0 + n], in_=pp[:, :n], func=AF.Copy), waits=[t_p])
                    tok_u[(kt, ti)] = t_e
                self.ps_rd[b] = t_e
            self.ws_release(wi, t_last)
        if dbg.get('mix_stop', 99) <= 1:
            self.hT_free = P.last['pe']
            return
        wv, t_w, wi = self.ws_load(win[:, :, 512:768], lambda t: t[:, 0:2048].rearrange("p (k n) -> p k n", k=KC))
        tok_v = {}
        kv_tok = {}
        for blk in range(17):
            t0 = blk * 128
            n = 128 if blk < 16 else 64
            full = blk >= 15
            b = 4 + (blk % 2)
            pp = self.ps[b]
            c0 = 0 if full else 128
            for kc in range(KC):
                t_p = P.op('pe', lambda e, kc=kc, t0=t0, n=n, pp=pp, wv=wv, c0=c0: e.matmul(
                    pp[0:n, c0:256], hT[:, kc, t0:t0 + n], wv[:, kc, c0:256], start=(kc == 0), stop=(kc == KC - 1)),
                    waits=([t_w, self.ps_rd[b]] if kc == 0 else []) + [self.tok_h[(kc, min(blk // 4, 4))]],
                    signal=(kc == KC - 1))
            t_last = t_p
            t_e = P.op('dve', lambda e, blk=blk, n=n, pp=pp: e.tensor_copy(
                out=self.vtok[0:n, blk, :], in_=pp[0:n, 128:256]), waits=[t_p])
            tok_v[blk] = t_e
            if full:
                ko = self.kvo[blk - 15]
                t_e = P.op('act', lambda e, n=n, pp=pp, ko=ko: e.activation(
                    out=ko[0:n, :], in_=pp[0:n, 0:256], func=AF.Copy), waits=[t_p, t_e, self.sq_rd[blk - 15]])
                kv_tok[blk] = t_e
            self.ps_rd[b] = t_e
        self.ws_release(wi, t_last)
        self.hT_free = t_last
        if dbg.get('mix_stop', 99) <= 2:
            self.hT_free = P.last['pe']
            return
        s_kv = self.slot("d_kvo")
        P.dma('sp', lambda e: e.dma_start(out=d["kp"], in_=self.kvo[0][:, 0:128]), s_kv, waits=[kv_tok[15]])
        P.dma('sp', lambda e: e.dma_start(out=d["vp"], in_=self.kvo[0][:, 128:256]), s_kv, waits=[kv_tok[15]])
        for b_ in range(16):
            P.dma('sp', lambda e, b_=b_: e.dma_start(
                out=d["ks"][b_, 124:128, :], in_=self.kvo[1][4 * b_:4 * b_ + 4, 0:128]), s_kv, waits=[kv_tok[16]])
            P.dma('sp', lambda e, b_=b_: e.dma_start(
                out=d["vs"][b_, 124:128, :], in_=self.kvo[1][4 * b_:4 * b_ + 4, 128:256]), s_kv, waits=[kv_tok[16]])
        self.out_toks.append((s_kv.sem, s_kv.count))
        self.sq_rd = [(s_kv.sem, s_kv.count), (s_kv.sem, s_kv.count)]

        if dbg.get('mix_stop', 99) <= 3:
            self.hT_free = P.last['pe']
            return
        qT2, kT, vtok = self.qT2, self.kT, self.vtok
        attn_tok = {}
        pcnt = [0]
        pT_rd = [self.sg_rd[0], self.sg_rd[0], self.sg_rd[1], self.sg_rd[1]]
        tq_all = lambda ti: [tok_q[(r, ti)] for r in range(4)]
        self.tok_u = tok_u
        self.ssm_y_tok = {}

        def attn_block(n_):
            ti = n_ // 4
            q0 = n_ * 128
            kbs = ([(n_ - 1, 1)] if n_ > 0 else []) + [(n_, 0)]
            pts = {}
            for g in range(2):
                gs = slice(64 * g, 64 * g + 64)
                for (kb, ty) in kbs:
                    sl = pcnt[0] % 4
                    bk = pcnt[0] % 2
                    pcnt[0] += 1
                    sb_ = self.ps[bk]
                    P.op('pe', lambda e, sb_=sb_, ty=ty, g=g: e.matmul(
                        sb_[:, :], self.ident[:], self.BT[:, ty, 4 * g:4 * g + 4, :], start=True, stop=False),
                        waits=[self.ps_rd[bk], self.t_ident] + self.t_tables, signal=False)
                    t_s = P.op('pe', lambda e, sb_=sb_, gs=gs, kb=kb, q0=q0: e.matmul(
                        sb_[:, :], kT[gs, kb * 128:(kb + 1) * 128], qT2[gs, :, q0:q0 + 128], start=False, stop=True),
                        waits=tq_all(ti) + [tok_k[ti], tok_k[kb // 4]])
                    t_e = P.op('act', lambda e, sb_=sb_, sl=sl: e.activation(
                        out=self.pT[sl], in_=sb_[:, :], func=AF.Exp), waits=[t_s, pT_rd[sl]])
                    self.ps_rd[bk] = t_e
                    pts[(g, kb)] = (sl, t_e)
            ob, db = 2, 3
            ops_, dps_ = self.ps[ob], self.ps[db]
            for g in range(2):
                gs = slice(64 * g, 64 * g + 64)
                for i, (kb, ty) in enumerate(kbs):
                    sl, t_e = pts[(g, kb)]
                    P.op('pe', lambda e, ops_=ops_, gs=gs, kb=kb, sl=sl, i=i, nkb=len(kbs): e.matmul(
                        ops_[gs, :], vtok[:, kb, gs], self.pT[sl], start=(i == 0), stop=(i == nkb - 1)),
                        waits=[t_e, tok_v[kb], self.ps_rd[ob]], signal=False)
                for i, (kb, ty) in enumerate(kbs):
                    sl, t_e = pts[(g, kb)]
                    t_d = P.op('pe', lambda e, dps_=dps_, gs=gs, sl=sl, i=i, nkb=len(kbs): e.matmul(
                        dps_[gs, :], self.ones1[:, :], self.pT[sl], start=(i == 0), stop=(i == nkb - 1)),
                        waits=[self.ps_rd[db], self.t_ones1], signal=(i == len(kbs) - 1))
                for (kb, ty) in kbs:
                    pT_rd[pts[(g, kb)][0]] = t_d
            rb = n_ % 2
            rt = self.rt[rb]
            for r_ in range(4):
                t_1 = P.op('act', lambda e, dps_=dps_, rt=rt, r_=r_: e.activation(
                    out=rt[:, 128 * r_:128 * r_ + 128], in_=dps_[:, 128 * r_:128 * r_ + 128], func=AF.Ln,
                    bias=self.ES[:, r_:r_ + 1], scale=1.0), waits=[t_d, self.t_es, self.rt_rd[rb]])
            t_2 = P.op('act', lambda e, rt=rt: e.activation(out=rt[:], in_=rt[:], func=AF.Exp, scale=-1.0), waits=[t_1])
            self.ps_rd[db] = t_1

            def back():
                t_3 = P.op('dve', lambda e, ops_=ops_, rt=rt, q0=q0: e.tensor_tensor(
                    out=qT2[:, :, q0:q0 + 128], in0=ops_[:].rearrange("p (r q) -> p r q", r=4),
                    in1=rt[:].rearrange("p (r q) -> p r q", r=4), op=ALU.mult), waits=[t_2, t_d])
                self.rt_rd[rb] = t_3
                self.ps_rd[ob] = t_3
                attn_tok[n_] = t_3
            return back

        do_ssm = not dbg.get("skip_ssm")
        bi = 0
        ucnt = 0
        t_lastu = None
        for ch in (6, 7, 8, 9):
            wv, t_w, wi = self.ws_load(win[:, :, ch * 128:(ch + 1) * 128], lambda t: t[:, 0:1024].rearrange("p (k n) -> p k n", k=KC))
            for ti in range(len(TT)):
                t_lastu = u_group(ch - 6, ti, wv, t_w, 4 + ucnt % 4)
                ucnt += 1
                if bi < 16:
                    attn_block(bi)()
                    bi += 1
            self.ws_release(wi, t_lastu)
        while bi < 16:
            attn_block(bi)()
            bi += 1
        self.hT_free = t_lastu
        if do_ssm:
            self.ssm_begin()
        if do_ssm:
            self.ssm_pads_bz(0)
            self.ssm_pads_cp(0)
            self.ssm_tables(0)
            self.ssm_tables(1)
            self.ssm_z(0)
        for i_ in range(16):
            if do_ssm and i_ + 2 < 16:
                self.ssm_tables(i_ + 2)
            if do_ssm:
                kt_, pl_ = i_ // 4, i_ % 4
                self.ssm_main(i_)
                if pl_ == 3:
                    self.ssm_sample(kt_)
                if i_ < 15:
                    self.ssm_z(i_ + 1)
                if pl_ == 2 and kt_ < 3:
                    self.ssm_pads_bz(kt_ + 1)
                if pl_ == 3:
                    self.ssm_y(kt_, (3,), False)
                    if kt_ == 3:
                        self.ssm_y(kt_, (2,), False)
                        self.ssm_y(kt_, (1, 0), True)
                if pl_ == 0 and kt_ > 0:
                    self.ssm_y(kt_ - 1, (2,), False)
                if pl_ == 1 and kt_ > 0:
                    self.ssm_y(kt_ - 1, (1, 0), True)
                if pl_ == 2 and kt_ > 0:
                    self.ssm_pads_cp(kt_)
        if do_ssm:
            t_g = P.op('dve', lambda e: e.memset(self.sgc2[:], 0.0), waits=[self.S['dmp'], self.S['dmc']])
            self.rt_rd = [t_g, t_g]
        if do_ssm:
            self.ssm_glu()
        wck = [self.S['pad_bz_rd'], self.S['pad_cp_rd']] if do_ssm else []
        self.t_ckt = P.dma('pool', lambda e: e.dma_start(out=self.cKT[:], in_=d["cKT"]), self.slot("d_ckt"), waits=wck)
        self.t_cv = P.dma('pool', lambda e: e.dma_start(out=self.cV[:], in_=d["cV"]), self.slot("d_cv"), waits=wck)
        if dbg.get('mix_stop', 99) <= 4:
            self.hT_free = P.last['pe']
            return
        S0 = SEQ
        pTc = [self.pT[0], self.pT[1]]
        pTn = [self.pT[2], self.pT[3]]
        te = {}
        for g in range(2):
            gs = slice(64 * g, 64 * g + 64)
            sc, sn = self.ps[2 * g], self.ps[2 * g + 1]
            P.op('pe', lambda e, sc=sc, g=g: e.matmul(
                sc[:, 0:256], self.ident[:], self.BTsc[:, g, :, :].rearrange("p b r t -> p (b r t)"), start=True, stop=False),
                waits=[self.ps_rd[2 * g], self.t_ident] + self.t_tables, signal=False)
            for b in range(16):
                t_s = P.op('pe', lambda e, sc=sc, gs=gs, b=b: e.matmul(
                    sc[:, 16 * b:16 * b + 16], self.cKT[gs, b, :], qT2[gs, :, S0 + 4 * b:S0 + 4 * b + 4],
                    start=False, stop=(b == 15)),
                    waits=tq_all(4) + [self.t_ckt], signal=(b == 15))
            t_e = P.op('act', lambda e, sc=sc, g=g: e.activation(
                out=pTc[g][:, 0:256], in_=sc[:, 0:256], func=AF.Exp), waits=[t_s, pT_rd[g]])
            self.ps_rd[2 * g] = t_e
            te[(g, 'c')] = t_e
            if dbg.get('sa', 9) < 2:
                continue
            P.op('pe', lambda e, sn=sn, g=g, gs=gs: e.matmul(
                sn[0:64, 0:256], self.ident[gs, 64 - 64 * g:128 - 64 * g], self.BTsn[gs, :, :],
                start=True, stop=(dbg.get('sa2', 9) < 2)),
                waits=[self.ps_rd[2 * g + 1]], signal=False)
            if dbg.get('sa2', 9) < 2:
                continue
            t_s = P.op('pe', lambda e, sn=sn, gs=gs: e.matmul(
                sn[0:64, 0:256], kT[gs, S0:S0 + 64], qT2[gs, :, S0:S0 + 64], start=False, stop=True),
                waits=[tok_k[4]])
            if dbg.get('sa2', 9) < 3:
                continue
            t_e = P.op('act', lambda e, sn=sn, g=g: e.activation(
                out=pTn[g][0:64, 0:256].rearrange("p (b r t) -> p r b t", b=16, r=4),
                in_=sn[0:64, 0:256].rearrange("p (r b t) -> p r b t", r=4, b=16), func=AF.Exp),
                waits=[t_s, pT_rd[2 + g]])
            self.ps_rd[2 * g + 1] = t_e
            te[(g, 'n')] = t_e
        ob, db = 0, 1
        ops_, dps_ = self.ps[ob], self.ps[db]
        if dbg.get('sa', 9) < 3:
            self.hT_free = P.last['pe']
            return
        for g in range(2):
            gs = slice(64 * g, 64 * g + 64)
            P.op('pe', lambda e, gs=gs, g=g: e.matmul(
                ops_[gs, 0:256], vtok[0:64, 16, gs], pTn[g][0:64, 0:256], start=True, stop=False),
                waits=[te[(g, 'n')], te[(g, 'c')], tok_v[16], self.ps_rd[ob], self.t_cv], signal=False)
            for b in range(16):
                P.op('pe', lambda e, gs=gs, g=g, b=b: e.matmul(
                    ops_[gs, 16 * b:16 * b + 16], self.cV[:, b, gs], pTc[g][:, 16 * b:16 * b + 16],
                    start=False, stop=(b == 15)), signal=False)
            P.op('pe', lambda e, gs=gs, g=g: e.matmul(
                dps_[gs, 0:256], self.ones1[0:64, :], pTn[g][0:64, 0:256], start=True, stop=False),
                waits=[self.ps_rd[db]], signal=False)
            t_d = P.op('pe', lambda e, gs=gs, g=g: e.matmul(
                dps_[gs, 0:256], self.ones1[:, :], pTc[g][:, 0:256], start=False, stop=True))
        if dbg.get('sa', 9) < 4:
            self.hT_free = P.last['pe']
            return
        rt = self.rt[0]
        v4 = lambda ap: ap.rearrange("p (b r t) -> p b r t", b=16, r=4)
        t_1 = P.op('dve', lambda e: e.tensor_tensor(
            out=v4(rt[:, 0:256]), in0=v4(dps_[:, 0:256]),
            in1=self.ES[:].unsqueeze(1).unsqueeze(3).to_broadcast([128, 16, 4, 4]), op=ALU.add),
            waits=[t_d, self.t_es, self.rt_rd[0]])
        t_2 = P.op('act', lambda e: e.activation(out=rt[:, 0:256], in_=rt[:, 0:256], func=AF.Ln), waits=[t_1])
        t_2 = P.op('act', lambda e: e.activation(out=rt[:, 0:256], in_=rt[:, 0:256], func=AF.Exp, scale=-1.0), waits=[t_2])
        t_3 = P.op('dve', lambda e: e.tensor_tensor(
            out=qT2[:, :, S0:S0 + 64].rearrange("p r (b t) -> p b r t", t=4), in0=v4(ops_[:, 0:256]),
            in1=v4(rt[:, 0:256]), op=ALU.mult), waits=[t_2, t_d])
        self.rt_rd[0] = t_3
        self.ps_rd[ob] = t_3
        self.ps_rd[db] = t_1
        attn_tok[16] = t_3
        self.attn_tok = attn_tok
        self.tok_u = tok_u

        if dbg.get('mix_stop', 99) <= 5:
            self.hT_free = P.last['pe']
            return
        ssm_tok = self.ssm_tok if do_ssm else None

        if dbg.get("dump_attn"):
            s_dbg = self.slot("d_dbg")
            tk = P.dma('sp', lambda e: e.dma_start(out=d["dbg_attn"], in_=qT2), s_dbg,
                       waits=[attn_tok[i] for i in range(17)])
            self.out_toks.append(tk)
            self.dbg_tok = tk

        nk = 4 if ssm_tok is None else 8
        ycnt = 0
        for c in range(KC):
            wv, t_w, wi = self.ws_load(d["wout"][c], lambda t: t[:, 0:1024].rearrange("p (k n) -> p k n", k=8))
            for ti, (t0, n) in enumerate(TT):
                yb = 4 + (ycnt % 2)
                ycnt += 1
                yps = self.ps[yb]
                blks = range(ti * 4, ti * 4 + 4) if ti < 4 else [16]
                for i in range(nk):
                    rhs = qT2[:, i, t0:t0 + n] if i < 4 else self.ssmT[:, i - 4, t0:t0 + n]
                    w_ = ([t_w, self.ps_rd[yb]] + [attn_tok[b_] for b_ in blks]) if i == 0 else []
                    if i == 4:
                        w_ = w_ + [ssm_tok[(k_, ti)] for k_ in range(4)]
                    t_y = P.op('pe', lambda e, i=i, n=n, yps=yps, wv=wv, rhs=rhs: e.matmul(
                        yps[:, :n], wv[:, i, :], rhs, start=(i == 0), stop=(i == nk - 1)),
                        waits=w_, signal=(i == nk - 1))
                t_x = P.op('dve', lambda e, c=c, t0=t0, n=n, yps=yps: e.tensor_tensor(
                    out=xT[:, c, t0:t0 + n], in0=yps[:, :n], in1=xT[:, c, t0:t0 + n], op=ALU.add),
                    waits=[t_y, self.tok_x[(c, ti)]])
                self.tok_x[(c, ti)] = t_x
                self.ps_rd[yb] = t_x
                if c == KC - 1:
                    self.norm_tile(2, ti)
            self.ws_release(wi, t_y)
        self.normed_upto = {2: len(TT)}
        self.aT_rd = t_y
        self.sg_rd = [t_y, t_y]
        self.mix_end_tok = t_y


    def ssm_io(self):
        self.din("ssm_pm", [128, 3, 16])
        self.din("ssm_cm", [128, 3, 4, 64])
        self.din("c_pm", [128, 2, 16, 16])
        self.din("b_pm", [128, 2, 16, 16])
        self.din("b_cm", [128, 2, 4, 64])
        self.din("dskip", [128, 4])
        self.din("bglu", [128, 4])
        self.din("wglu", [128, 4, 512])
        self.din("cst_m2", [128, 4, 8])
        self.din("cst_m3", [128, 4, 2])
        self.din("cst_eye", [128, 128])
        self.din("x0", [128, 16, 2, 16])
        self.dout("st_p", [128, 16, 2])
        self.dout("st_s", [128, 16, 2, 16])

    def ssm_alloc(self):
        sb = self.sb
        self.pm = {}
        for nm in ["L1r", "L1i", "L2r", "L2i", "L3r", "L3i", "L4r", "L4i", "cr", "ci", "rho4", "t0", "t1", "t2", "t3",
                   "t4", "t5", "turns"]:
            self.pm[nm] = sb("pm_" + nm, [128, 16], F32)
        self.pm_in = sb("pm_in", [128, 3, 16], F32)
        self.pm_i = sb("pm_i", [128, 16], I32)
        self.phi2 = sb("pm_phi2", [128, 16], I32)
        self.q30 = sb("q30", [128, 1], I32)
        self.CpC = sb("CpC", [128, 16, 5, 2, 16], BF16)
        self.jota = sb("jota", [128, 512], I32)
        self.m2 = sb("m2", [128, 4, 8], F32)
        self.m3 = sb("m3", [128, 4, 2], F32)
        self.eye = sb("eye", [128, 128], F32)
        self.dsk = sb("dsk", [128, 4], F32)
        self.bgl = sb("bgl", [128, 4], F32)
        self.x0 = sb("x0_s", [128, 16, 2, 16], F32)
        self.stp = sb("stp_s", [128, 16, 2], F32)
        self.sgc = sb("sgc", [128, 1], F32)
        self.sgc2 = sb("sgc2", [128, 1], F32)
        self.XbS = sb("XbS", [128, 2, 4, 2, 16], BF16)
        self.BzC = self.cKT[:].rearrange("p b n -> p (b n)").rearrange("p (k s r q) -> p k s r q", k=4, s=4, r=2)
        self.Kin = self.cV[:].rearrange("p b n -> p (b n)").rearrange("p (k t n) -> p k t n", k=4, t=4)

    def ssm_setup_gen(self, which):
        P, d = self.P, self.dram
        pm = self.pm
        prev = [None]
        self.kin_rd = getattr(self, 'kin_rd', [None] * 4)
        TWO_PI = 2.0 * np.pi
        hf = self.hT[:].rearrange("p k t -> p (k t)").bitcast(F32)

        def V(fn, extra=()):
            prev[0] = P.op('dve', fn, waits=[prev[0]] + list(extra))

        def A(fn, extra=()):
            prev[0] = P.op('act', fn, waits=[prev[0]] + list(extra))

        def lam(aR, aI, ldt, T, is_pm):
            yield A(lambda e: e.activation(out=ldt, in_=ldt, func=AF.Exp))
            yield V(lambda e: e.tensor_tensor(out=T['xr'], in0=aR, in1=ldt, op=ALU.mult))
            yield V(lambda e: e.tensor_tensor(out=T['xi'], in0=aI, in1=ldt, op=ALU.mult))
            yield A(lambda e: e.activation(out=T['mag'], in_=T['xr'], func=AF.Exp))
            yield V(lambda e: e.tensor_scalar(out=T['xi'], in0=T['xi'], scalar1=1.0 / TWO_PI, scalar2=None, op0=ALU.mult))
            if is_pm:
                yield V(lambda e: e.tensor_copy(out=pm['turns'][:], in_=T['xi']))
                yield A(lambda e: e.activation(out=pm['rho4'][:], in_=T['xr'], func=AF.Exp, scale=4.0))
            yield V(lambda e: e.tensor_copy(out=T['ni'], in_=T['xi']))
            yield V(lambda e: e.tensor_copy(out=T['nf'], in_=T['ni']))
            yield V(lambda e: e.tensor_tensor(out=T['xi'], in0=T['xi'], in1=T['nf'], op=ALU.subtract))
            yield A(lambda e: e.activation(out=T['s1'], in_=T['xi'], func=AF.Sin, scale=TWO_PI))
            yield A(lambda e: e.activation(out=T['sh'], in_=T['xi'], func=AF.Sin, scale=float(np.pi)))
            yield V(lambda e: e.tensor_tensor(out=T['sh'], in0=T['sh'], in1=T['sh'], op=ALU.mult))
            yield V(lambda e: e.tensor_scalar(out=T['sh'], in0=T['sh'], scalar1=-2.0, scalar2=1.0, op0=ALU.mult, op1=ALU.add))
            yield V(lambda e: e.tensor_tensor(out=T['L1r'], in0=T['mag'], in1=T['sh'], op=ALU.mult))
            yield V(lambda e: e.tensor_tensor(out=T['L1i'], in0=T['mag'], in1=T['s1'], op=ALU.mult))
            yield V(lambda e: e.tensor_scalar(out=T['mag'], in0=T['L1r'], scalar1=-1.0, scalar2=None, op0=ALU.add))
            yield V(lambda e: e.tensor_tensor(out=T['xr'], in0=aR, in1=aR, op=ALU.mult))
            yield V(lambda e: e.tensor_tensor(out=T['xi'], in0=aI, in1=aI, op=ALU.mult))
            yield V(lambda e: e.tensor_tensor(out=T['xr'], in0=T['xr'], in1=T['xi'], op=ALU.add))
            yield V(lambda e: e.reciprocal(out=T['xr'], in_=T['xr']))
            yield V(lambda e: e.tensor_tensor(out=T['xi'], in0=T['mag'], in1=aR, op=ALU.mult))
            yield V(lambda e: e.tensor_tensor(out=T['s1'], in0=T['L1i'], in1=aI, op=ALU.mult))
            yield V(lambda e: e.tensor_tensor(out=T['xi'], in0=T['xi'], in1=T['s1'], op=ALU.add))
            yield V(lambda e: e.tensor_tensor(out=T['cr'], in0=T['xi'], in1=T['xr'], op=ALU.mult))
            yield V(lambda e: e.tensor_tensor(out=T['xi'], in0=T['L1i'], in1=aR, op=ALU.mult))
            yield V(lambda e: e.tensor_tensor(out=T['s1'], in0=T['mag'], in1=aI, op=ALU.mult))
            yield V(lambda e: e.tensor_tensor(out=T['xi'], in0=T['xi'], in1=T['s1'], op=ALU.subtract))
            yield V(lambda e: e.tensor_tensor(out=T['ci'], in0=T['xi'], in1=T['xr'], op=ALU.mult))

        def cmul(o_r, o_i, a_r, a_i, b_r, b_i, t1, t2):
            yield V(lambda e: e.tensor_tensor(out=t1, in0=a_r, in1=b_r, op=ALU.mult))
            yield V(lambda e: e.tensor_tensor(out=t2, in0=a_i, in1=b_i, op=ALU.mult))
            yield V(lambda e: e.tensor_tensor(out=o_r, in0=t1, in1=t2, op=ALU.subtract))
            yield V(lambda e: e.tensor_tensor(out=t1, in0=a_r, in1=b_i, op=ALU.mult))
            yield V(lambda e: e.tensor_tensor(out=t2, in0=a_i, in1=b_r, op=ALU.mult))
            yield V(lambda e: e.tensor_tensor(out=o_i, in0=t1, in1=t2, op=ALU.add))

        if which == 1:
            prev[0] = None
            st = self.stage[:].rearrange("p a h q -> p (a h q)")
            s = self.slot("d_ssm_in")
            for (dst, src_) in [(self.pm_in[:], d["ssm_pm"]), (self.m2[:], d["cst_m2"]), (self.m3[:], d["cst_m3"]),
                                (self.eye[:], d["cst_eye"]), (self.dsk[:], d["dskip"]), (self.bgl[:], d["bglu"]),
                                (self.x0[:], d["x0"])]:
                t_in = P.dma('sp', lambda e, dst=dst, src_=src_: e.dma_start(out=dst, in_=src_), s)
            self.t_x0 = t_in
            cpm = st[:, 0:512].rearrange("p (r a c) -> p r a c", r=2, a=16)
            t_in2 = P.dma('sp', lambda e: e.dma_start(out=cpm, in_=d["c_pm"]), self.slot("d_ssm_in2"), waits=self.t_tables)
            for _ in range(8):
                yield
            yield V(lambda e: e.memset(self.sgc[:], TWO_PI / 2.0 ** 32), extra=[t_in, t_in2] + self.t_tables)
            yield V(lambda e: e.memset(self.sgc2[:], TWO_PI / 2.0 ** 33))
            aR, aI, ldt = self.pm_in[:, 0, :], self.pm_in[:, 1, :], self.pm_in[:, 2, :]
            T = {'xr': pm['t0'][:], 'xi': pm['t1'][:], 'mag': pm['t2'][:], 'ni': self.pm_i[:], 'nf': pm['t3'][:],
                 's1': pm['t4'][:], 'sh': pm['t5'][:], 'L1r': pm['L1r'][:], 'L1i': pm['L1i'][:], 'cr': pm['cr'][:], 'ci': pm['ci'][:]}
            yield from lam(aR, aI, ldt, T, True)
            t1_, t2_ = pm['t0'][:], pm['t1'][:]
            yield from cmul(pm['L2r'][:], pm['L2i'][:], pm['L1r'][:], pm['L1i'][:], pm['L1r'][:], pm['L1i'][:], t1_, t2_)
            yield from cmul(pm['L3r'][:], pm['L3i'][:], pm['L2r'][:], pm['L2i'][:], pm['L1r'][:], pm['L1i'][:], t1_, t2_)
            yield from cmul(pm['L4r'][:], pm['L4i'][:], pm['L2r'][:], pm['L2i'][:], pm['L2r'][:], pm['L2i'][:], t1_, t2_)
            tu = pm['turns'][:]
            yield V(lambda e: e.tensor_scalar(out=tu, in0=tu, scalar1=4.0, scalar2=None, op0=ALU.mult))
            yield V(lambda e: e.tensor_copy(out=self.pm_i[:], in_=tu))
            yield V(lambda e: e.tensor_copy(out=pm['t3'][:], in_=self.pm_i[:]))
            yield V(lambda e: e.tensor_tensor(out=tu, in0=tu, in1=pm['t3'][:], op=ALU.subtract))
            yield V(lambda e: e.tensor_scalar(out=tu, in0=tu, scalar1=4294967040.0, scalar2=None, op0=ALU.mult))
            yield V(lambda e: e.tensor_copy(out=self.phi2[:], in_=tu))
            cR, cI = cpm[:, 0, :, :], cpm[:, 1, :, :]
            w1 = st[:, 512:768].rearrange("p (a c) -> p a c", a=16)
            w2 = st[:, 768:1024].rearrange("p (a c) -> p a c", a=16)
            bc = lambda ap: ap.unsqueeze(2).to_broadcast([128, 16, 16])
            yield V(lambda e: e.tensor_copy(out=self.CpC[:, :, 0, 0, :], in_=cR))
            yield V(lambda e: e.tensor_scalar(out=self.CpC[:, :, 0, 1, :], in0=cI, scalar1=-1.0, scalar2=None, op0=ALU.mult))
            for k in range(1, 5):
                Lr, Li = pm['L%dr' % k][:], pm['L%di' % k][:]
                yield V(lambda e, Lr=Lr: e.tensor_tensor(out=w1, in0=cR, in1=bc(Lr), op=ALU.mult))
                yield V(lambda e, Li=Li: e.tensor_tensor(out=w2, in0=cI, in1=bc(Li), op=ALU.mult))
                yield V(lambda e, k=k: e.tensor_tensor(out=self.CpC[:, :, k, 0, :], in0=w1, in1=w2, op=ALU.subtract))
                yield V(lambda e, Li=Li: e.tensor_tensor(out=w1, in0=cR, in1=bc(Li), op=ALU.mult))
                yield V(lambda e, Lr=Lr: e.tensor_tensor(out=w2, in0=cI, in1=bc(Lr), op=ALU.mult))
                yield V(lambda e, k=k: e.scalar_tensor_tensor(out=self.CpC[:, :, k, 1, :], in0=w1, scalar=-1.0, in1=w2,
                                                              op0=ALU.mult, op1=ALU.subtract))
            self.t_pm_done = prev[0]
            for hk in range(2):
                cm_in = st[:, 0:384].rearrange("p (q k n) -> p q k n", q=3, k=2)
                bcm = st[:, 384:640].rearrange("p (r k n) -> p r k n", r=2, k=2)
                ct = [st[:, 640 + 128 * i:768 + 128 * i].rearrange("p (k n) -> p k n", k=2) for i in range(10)]
                cti = st[:, 1920:2048].bitcast(I32).rearrange("p (k n) -> p k n", k=2)
                sl_ = self.slot("d_ssm_cm%d" % hk)
                P.dma('sp', lambda e, hk=hk, cm_in=cm_in: e.dma_start(out=cm_in, in_=d["ssm_cm"][:, :, 2 * hk:2 * hk + 2, :]), sl_,
                      waits=[prev[0]])
                t_l = P.dma('sp', lambda e, hk=hk, bcm=bcm: e.dma_start(out=bcm, in_=d["b_cm"][:, :, 2 * hk:2 * hk + 2, :]), sl_,
                            waits=[prev[0]])
                prev[0] = t_l
                for _ in range(8):
                    yield
                aRc, aIc, ldc = cm_in[:, 0, :, :], cm_in[:, 1, :, :], cm_in[:, 2, :, :]
                Tc = {'xr': ct[0], 'xi': ct[1], 'mag': ct[2], 'ni': cti, 'nf': ct[3], 's1': ct[4], 'sh': ct[5],
                      'L1r': ct[6], 'L1i': ct[7], 'cr': ct[8], 'ci': ct[9]}
                yield from lam(aRc, aIc, ldc, Tc, False)
                bRc, bIc = bcm[:, 0, :, :], bcm[:, 1, :, :]
                c_r, c_i, u1, u2, u3, u4 = ct[0], ct[1], ct[2], ct[3], ct[4], ct[5]
                yield from cmul(c_r, c_i, ct[8], ct[9], bRc, bIc, u1, u2)
                for s_ in (3, 2, 1, 0):
                    yield V(lambda e, s_=s_, hk=hk, c_r=c_r: e.tensor_copy(out=self.BzC[:, 2 * hk:2 * hk + 2, s_, 0, :], in_=c_r))
                    yield V(lambda e, s_=s_, hk=hk, c_i=c_i: e.tensor_copy(out=self.BzC[:, 2 * hk:2 * hk + 2, s_, 1, :], in_=c_i))
                    if s_ > 0:
                        yield from cmul(u3, u4, ct[6], ct[7], c_r, c_i, u1, u2)
                        yield V(lambda e, c_r=c_r, u3=u3: e.tensor_copy(out=c_r, in_=u3))
                        yield V(lambda e, c_i=c_i, u4=u4: e.tensor_copy(out=c_i, in_=u4))
            self.t_cm_done = prev[0]
            yield
            return
        prev[0] = self.hT_free
        bpm = hf[:, 512:1024].rearrange("p (r a c) -> p r a c", r=2, a=16)
        t_in2 = P.dma('sp', lambda e: e.dma_start(out=bpm, in_=d["b_pm"]), self.slot("d_ssm_in3"), waits=[self.hT_free])
        for _ in range(9):
            yield
        yield V(lambda e: e.memset(self.q30[:], 1 << 30), extra=[t_in2, self.t_pm_done])
        w = [hf[:, 1024 + 256 * i:1280 + 256 * i].rearrange("p (a c) -> p a c", a=16) for i in range(4)]
        w1, w2, w3, w4 = w
        bc = lambda ap: ap.unsqueeze(2).to_broadcast([128, 16, 16])
        bR, bI = bpm[:, 0, :, :], bpm[:, 1, :, :]
        cur_r = hf[:, 2048:2304].rearrange("p (a c) -> p a c", a=16)
        cur_i = hf[:, 2304:2560].rearrange("p (a c) -> p a c", a=16)
        yield from cmul(cur_r, cur_i, bc(pm['cr'][:]), bc(pm['ci'][:]), bR, bI, w1, w2)
        BLc = hf[:, 2560:3584].bitcast(BF16).rearrange("p (a t r c) -> p a t r c", a=16, t=4, r=2)
        for tau in range(4):
            yield V(lambda e, tau=tau: e.tensor_copy(out=BLc[:, :, tau, 0, :], in_=cur_r))
            yield V(lambda e, tau=tau: e.tensor_copy(out=BLc[:, :, tau, 1, :], in_=cur_i))
            if tau < 3:
                yield from cmul(w3, w4, bc(pm['L1r'][:]), bc(pm['L1i'][:]), cur_r, cur_i, w1, w2)
                yield V(lambda e: e.tensor_copy(out=cur_r, in_=w3))
                yield V(lambda e: e.tensor_copy(out=cur_i, in_=w4))
        BLpads = [hf[:, 3584 + 512 * i:4096 + 512 * i].bitcast(BF16).rearrange("p (l r g c) -> p l r g c", l=4, r=2, g=8)
                  for i in range(2)]
        C0pads = [hf[:, 4608 + 512 * i:5120 + 512 * i].bitcast(BF16).rearrange("p (l r g c) -> p l r g c", l=4, r=2, g=8)
                  for i in range(2)]
        m2b = self.m2[:].unsqueeze(3).to_broadcast([128, 4, 8, 16])
        kb_ = 7
        pad_rd = [None, None]
        c0_rd = [None, None]
        cnt = 0
        pending = None
        for kt in range(4):
            C0pad = C0pads[kt % 2]
            for r in range(2):
                yield V(lambda e, kt=kt, r=r, C0pad=C0pad: e.tensor_tensor(
                    out=C0pad[:, :, r, :, :], in0=self.CpC[:, 4 * kt:4 * kt + 4, 0, r, :].unsqueeze(2).to_broadcast([128, 4, 8, 16]),
                    in1=m2b, op=ALU.mult), extra=[c0_rd[kt % 2]])
            for tau in range(4):
                BLpad = BLpads[cnt % 2]
                for r in range(2):
                    yield V(lambda e, kt=kt, r=r, tau=tau, BLpad=BLpad: e.tensor_tensor(
                        out=BLpad[:, :, r, :, :], in0=BLc[:, 4 * kt:4 * kt + 4, tau, r, :].unsqueeze(2).to_broadcast([128, 4, 8, 16]),
                        in1=m2b, op=ALU.mult), extra=[pad_rd[cnt % 2]])
                kb_ = 6 + cnt % 2
                kps = self.ps[kb_]
                col = 0
                t_mm = None
                for i, (pl, r) in enumerate([(pl, r) for pl in range(4) for r in range(2)]):
                    t_mm = P.op('pe', lambda e, pl=pl, r=r, i=i, kps=kps, col=col, BLpad=BLpad, C0pad=C0pad: e.matmul(
                        kps[:, col:col + 128], BLpad[:, pl, r, :, :].rearrange("p g c -> p (g c)"),
                        C0pad[:, pl, r, :, :].rearrange("p g c -> p (g c)"), start=(i == 0), stop=(i == 7)),
                        waits=[prev[0], self.ps_rd[kb_]] if i == 0 else [], signal=(i == 7))
                pad_rd[cnt % 2] = t_mm
                c0_rd[kt % 2] = t_mm
                if pending is not None:
                    yield self._kin_evac(pending, prev)
                pending = (kt, tau, col, t_mm, kb_)
                cnt += 1
                yield
        yield self._kin_evac(pending, prev)
        self.t_ssm_setup = prev[0]
        yield

    def _kin_evac(self, pending, prev):
        P = self.P
        kt, tau, col, t_mm, slot = pending
        kps = self.ps[slot]
        if tau == 0:
            t = P.op('dve', lambda e: e.scalar_tensor_tensor(
                out=self.Kin[:, kt, 0, :], in0=self.eye[:], scalar=self.dsk[:, kt:kt + 1], in1=kps[:, col:col + 128],
                op0=ALU.mult, op1=ALU.add), waits=[t_mm, prev[0]])
        else:
            t = P.op('dve', lambda e: e.tensor_copy(out=self.Kin[:, kt, tau, :], in_=kps[:, col:col + 128]), waits=[t_mm, prev[0]])
        prev[0] = t
        self.ps_rd[slot] = t
        return None

    def bg_pump(self, n):
        g = getattr(self, "bg", None)
        if g is None:
            return
        for _ in range(n):
            try:
                next(g)
            except StopIteration:
                self.bg = None
                return

    def ssm_begin(self):
        P = self.P
        N = self.WS_N
        i1, i2 = self.ws_next % N, (self.ws_next + 1) % N
        self.ws_next += 2
        self.cp_slots = (i1, i2)
        i3 = self.ws_next % N
        self.ws_next += 1
        self.tmp_slot = i3
        self.pA = self.ws[i3][:, 0:1024].bitcast(F32)
        self.pB = self.ws[i3][:, 1024:2048].bitcast(F32)
        self.t_tmp_free = self.ws_free[i3]
        self.CpPad = [self.ws[i][:, 0:2048].rearrange("p (l k r g c) -> p l k r g c", l=2, k=4, r=2, g=8) for i in (i1, i2)]
        stf = self.stage[:].rearrange("p a h q -> p (a h q)").bitcast(BF16)
        self.BzPad = stf.rearrange("p (l s r g q) -> p l s r g q", l=4, s=4, r=2, g=2)
        hb = self.hT[:].rearrange("p k t -> p (k t)")
        f = lambda a: hb[:, a:a + 1024].bitcast(F32)
        self.tabC = [f(0), f(2048), self.rt[0][:]]
        self.tabS = [f(1024), f(3072), self.rt[1][:]]
        self.ph = hb[:, 4096:5120].bitcast(I32)
        self.ph2 = hb[:, 5120:6144].bitcast(I32)
        self.ta, self.tb = f(6144), f(7168)
        self.Mb = [(f(8192), f(9216)), (f(10240), f(11264))]
        xmain = hb[:, 12288:12288 + 4112].rearrange("p (l r n) -> p l r n", l=4, r=2)
        spare = self.aT[:, 9:11, :].rearrange("p a t -> p (a t)")[:, 2176:4224]
        extra = [spare[:, 0:514], spare[:, 514:1028], spare[:, 1028:1542], hb[:, 5120:5634]]
        self.XbR = [[xmain[:, s, 0, :], xmain[:, s, 1, :]] for s in range(4)] + [[extra[0], extra[1]], [extra[2], extra[3]]]
        self.glu_tmp = hb[:, 0:4096].bitcast(F32).rearrange("p (o n) -> p o n", o=4)
        w0 = [self.hT_free, self.t_ssm_setup, self.t_cm_done]
        t_j = P.op('pool', lambda e: e.iota(self.jota[:], pattern=[[1, 512]], base=0, channel_multiplier=0))
        tz = []
        for k, i in enumerate((i1, i2)):
            tz.append(P.op('pool', lambda e, i=i: e.memset(self.ws[i][:, 0:2048], 0.0), waits=[self.ws_free[i]] + w0))
        for s_ in range(6):
            for r_ in range(2):
                tz.append(P.op('pool', lambda e, s_=s_, r_=r_: e.memset(self.XbR[s_][r_][:, 0:2], 0.0), waits=w0 + [self.aT_rd]))
        self.S = dict(t_j=t_j, tz=tz, ph_rd=None, dve=None, pool=tz[-1], dm_done=[[None], [None]], dm_tok={}, tab={}, dmc=None, dmp=None,
                      pad_bz_rd=None, pad_cp_rd=None, tz_tok={}, slot_rd=[None] * 6, xbs_rd=[None, None], xb_rd=None, zs_rd=None, gel=None, pend=[], xf_rd=None)
        self.ssm_tok = {}

    def ssm_tables(self, pr):
        P, S = self.P, self.S
        tb = pr % 3
        C, Sn = self.tabC[tb], self.tabS[tb]
        w0 = [self.hT_free, self.t_ssm_setup, self.t_cm_done]
        free_t = list(S['dm_tok'].get(pr - 3, [])) + ([self.rt_rd[0], self.rt_rd[1]] if tb == 2 else [])
        t_p1 = P.op('pool', lambda e, pr=pr: e.tensor_tensor(
            out=self.ph[:], in0=self.jota[:], in1=self.phi2[:, pr:pr + 1].to_broadcast([128, 512]), op=ALU.mult),
            waits=w0 + [S['t_j'], S['ph_rd'], S['pool']])
        S['pool'] = t_p1
        t_s = P.op('act', lambda e, Sn=Sn: e.activation(out=Sn, in_=self.ph[:], func=AF.Sin, scale=self.sgc[:, 0:1]),
                   waits=[t_p1] + free_t)
        t_c = P.op('act', lambda e, C=C: e.activation(out=C, in_=self.ph[:], func=AF.Sin, scale=self.sgc2[:, 0:1]),
                   waits=[t_p1] + free_t)
        t_c = P.op('act', lambda e, C=C: e.activation(out=C, in_=C, func=AF.Square), waits=[t_c])
        t_c = P.op('act', lambda e, C=C: e.activation(out=C, in_=C, func=AF.Copy, scale=-2.0, bias=1.0), waits=[t_c])
        S['ph_rd'] = t_c
        S['tab'][pr] = (t_s, t_c)

    def ssm_pads_bz(self, kt):
        P, S = self.P, self.S
        w0 = [self.hT_free, self.t_ssm_setup, self.t_cm_done]
        tb_ = None
        for pl_ in range(4):
            for gg in range(2):
                tb_ = P.op('act', lambda e, pl_=pl_, gg=gg, kt=kt: e.activation(
                    out=self.BzPad[:, pl_, :, :, gg, :], in_=self.BzC[:, kt, :, :, :], func=AF.Copy,
                    scale=self.m3[:, pl_, gg:gg + 1]), waits=w0 + [S['pad_bz_rd']] + self.t_tables)
        S['t_bz'] = tb_

    def ssm_pads_cp(self, kt):
        P, S = self.P, self.S
        w0 = [self.hT_free, self.t_ssm_setup, self.t_cm_done]
        tc_ = []
        for hh in range(2):
            for gg in range(2):
                for r in range(2):
                    for pq in range(2):
                        tc_.append(P.op('pool', lambda e, hh=hh, gg=gg, r=r, pq=pq, kt=kt: e.tensor_copy(
                            out=self.CpPad[hh][64 * gg:64 * gg + 64, pq, :, r, 2 * (2 * hh + pq) + gg, :],
                            in_=self.CpC[64 * gg:64 * gg + 64, 4 * kt + 2 * hh + pq, 1:5, r, :]),
                            waits=w0 + S['tz'] + [S['pad_cp_rd'], S['pool']]))
        S['t_cp'] = tc_
        S['pool'] = tc_[-1]

    def ssm_z(self, pr):
        P, S = self.P, self.S
        kt, pl = pr // 4, pr % 4
        uT = self.uT
        ZB = (4, 5)
        zps = [self.ps[ZB[0]], self.ps[ZB[1]]]
        tz_ = None
        for r in range(2):
            for s_ in range(4):
                tz_ = P.op('pe', lambda e, r=r, s_=s_, pl=pl, kt=kt: e.matmul(
                    zps[r][:, :], self.BzPad[:, pl, s_, r, :, :].rearrange("p g q -> p (g q)"), uT[:, kt, s_:SEQ:4],
                    start=(s_ == 0), stop=(s_ == 3)),
                    waits=([S['t_bz'], self.ps_rd[ZB[r]]] + [self.tok_u[(kt, ti)] for ti in range(5)]) if s_ == 0 else [],
                    signal=(s_ == 3))
        zs = self.ps[7]
        tzs = None
        for r in range(2):
            c0 = 32 * pl + 16 * r
            for s_ in range(4):
                tzs = P.op('pe', lambda e, r=r, s_=s_, pl=pl, kt=kt, c0=c0: e.matmul(
                    zs[:, c0:c0 + 16], self.BzPad[:, pl, s_, r, :, :].rearrange("p g q -> p (g q)"),
                    uT[:, kt, SEQ + s_:NT:4], start=(s_ == 0), stop=(s_ == 3)),
                    waits=[self.ps_rd[7], S['zs_rd']] if (s_ == 0 and r == 0) else [], signal=(s_ == 3))
        if pl == 3:
            S['pad_bz_rd'] = tzs
        S['tz_tok'][pr] = (tz_, tzs)

    def ssm_main(self, pr, mid=None):
        P = self.P
        S = self.S
        kt, pl = pr // 4, pr % 4
        w0 = [self.hT_free, self.t_ssm_setup, self.t_cm_done]
        ZB = (4, 5)
        mul, add, sub = ALU.mult, ALU.add, ALU.subtract
        zps = [self.ps[ZB[0]], self.ps[ZB[1]]]
        tz_, tzs = S['tz_tok'][pr]
        if pl == 0:
            S['t_x0c'] = P.op('act', lambda e, kt=kt: e.activation(
                out=self.XbS[:, kt % 2, :, :, :], in_=self.x0[:, 4 * kt:4 * kt + 4, :, :], func=AF.Copy),
                waits=w0 + [S['xbs_rd'][kt % 2], self.t_x0])
        ta = self.ta
        tb = pr % 2
        C, Sn = self.tabC[pr % 3], self.tabS[pr % 3]
        t_s, t_c = S['tab'][pr]
        Mre, Mim = self.Mb[tb]
        ta, tb2 = self.ta, self.tb
        rho = self.pm['rho4'][:, pr:pr + 1].to_broadcast([128, 512])
        zr, zi = zps[0], zps[1]
        free_m = S['dm_done'][tb]
        o1 = P.op('dve', lambda e: e.tensor_tensor(out=Mre, in0=zr[:, :], in1=C, op=mul), waits=w0 + [tz_, t_c] + free_m)
        o2 = P.op('dve', lambda e: e.tensor_tensor(out=tb2, in0=zi[:, :], in1=Sn, op=mul), waits=[t_s, S['dve']])
        o3 = P.op('dve', lambda e: e.tensor_tensor(out=Mim, in0=zi[:, :], in1=C, op=mul))
        o4 = P.op('dve', lambda e: e.tensor_tensor(out=ta, in0=zr[:, :], in1=Sn, op=mul), waits=[S['dve']])
        self.ps_rd[ZB[0]] = o4
        self.ps_rd[ZB[1]] = o4
        o5 = P.op('dve', lambda e: e.tensor_tensor(out=Mre, in0=Mre, in1=tb2, op=add), waits=[o1, o2])
        o6 = P.op('dve', lambda e: e.tensor_tensor(out=Mim, in0=Mim, in1=ta, op=sub), waits=[o3, o4])
        Wre, Wim, pa, pb = self.ps[0][:, :], self.ps[1][:, :], self.ps[2][:, :], self.ps[3][:, :]
        o7 = P.op('dve', lambda e: e.tensor_tensor_scan(out=Wre, data0=rho, data1=Mre, initial=0.0, op0=mul, op1=add),
                  waits=[o5, self.ps_rd[0], S['dve']])
        o8 = P.op('dve', lambda e: e.tensor_tensor_scan(out=Wim, data0=rho, data1=Mim, initial=0.0, op0=mul, op1=add),
                  waits=[o6, self.ps_rd[1]])
        S['dve'] = o8
        if pr + 1 < 16 and (pr + 1) not in S['tab']:
            self.ssm_tables(pr + 1)
        slot = pr % 6
        xre, xim = self.XbR[slot]
        d1 = P.op('dve', lambda e: e.tensor_tensor(out=pa, in0=Wre, in1=C, op=mul), waits=[o7, self.ps_rd[2]])
        d2 = P.op('dve', lambda e: e.tensor_tensor(out=tb2, in0=Wim, in1=Sn, op=mul), waits=[o8])
        q3 = P.op('dve', lambda e: e.tensor_tensor(out=xre[:, 2:513], in0=pa[:, 0:511], in1=tb2[:, 0:511], op=sub),
                  waits=[d1, d2, S['slot_rd'][slot]] + S['tz'])
        q4 = P.op('dve', lambda e, pr=pr: e.tensor_tensor(out=self.stp[:, pr, 0:1], in0=pa[:, 511:512], in1=tb2[:, 511:512], op=sub),
                  waits=[d1, d2])
        d5 = P.op('dve', lambda e: e.tensor_tensor(out=pb, in0=Wim, in1=C, op=mul), waits=[o8, self.ps_rd[3]])
        d6 = P.op('dve', lambda e: e.tensor_tensor(out=ta, in0=Wre, in1=Sn, op=mul), waits=[o7, q4])
        q7 = P.op('dve', lambda e: e.tensor_tensor(out=xim[:, 2:513], in0=pb[:, 0:511], in1=ta[:, 0:511], op=add),
                  waits=[d5, d6, S['slot_rd'][slot]] + S['tz'])
        q8 = P.op('dve', lambda e, pr=pr: e.tensor_tensor(out=self.stp[:, pr, 1:2], in0=pb[:, 511:512], in1=ta[:, 511:512], op=add),
                  waits=[d5, d6])
        for b_ in range(4):
            self.ps_rd[b_] = q8
        S['dmc'] = q8
        S['dm_done'][tb] = [q8]
        S['dm_tok'][pr] = [q8]
        S['xf_rd'] = [q4, q8]
        S['dve'] = q8
        S['pend'].append((q3, q7, S['t_x0c']))

    def ssm_sample(self, kt):
        P, S = self.P, self.S
        mul, add, sub = ALU.mult, ALU.add, ALU.subtract
        ta = self.ta
        zs = self.ps[7]
        tzs = S['tz_tok'][4 * kt + 3][1]
        zv = zs[:, 0:128].rearrange("p (l r b) -> p l r b", l=4, r=2)
        x0r, x0i = self.x0[:, 4 * kt:4 * kt + 4, 0, :], self.x0[:, 4 * kt:4 * kt + 4, 1, :]
        bcl = lambda ap: ap[:, 4 * kt:4 * kt + 4].unsqueeze(2).to_broadcast([128, 4, 16])
        L4r, L4i = bcl(self.pm['L4r'][:]), bcl(self.pm['L4i'][:])
        q = [ta[:, 64 * i:64 * i + 64].rearrange("p (l b) -> p l b", l=4) for i in range(4)]
        d = [S['dve']]

        def V(fn, extra=()):
            d[0] = P.op('dve', fn, waits=[d[0]] + list(extra))
            return d[0]
        V(lambda e: e.tensor_tensor(out=q[0], in0=x0r, in1=L4r, op=mul), [S['t_x0c']])
        V(lambda e: e.tensor_tensor(out=q[1], in0=x0i, in1=L4i, op=mul))
        V(lambda e: e.tensor_tensor(out=q[2], in0=x0i, in1=L4r, op=mul))
        V(lambda e: e.tensor_tensor(out=q[3], in0=x0r, in1=L4i, op=mul))
        V(lambda e: e.tensor_tensor(out=q[0], in0=q[0], in1=q[1], op=sub))
        V(lambda e: e.tensor_tensor(out=q[2], in0=q[2], in1=q[3], op=add))
        V(lambda e: e.tensor_tensor(out=x0r, in0=zv[:, :, 0, :], in1=q[0], op=add), [tzs])
        t_zs = V(lambda e: e.tensor_tensor(out=x0i, in0=zv[:, :, 1, :], in1=q[2], op=add))
        S['zs_rd'] = t_zs
        S['dve'] = t_zs

    def ssm_y(self, kt, ls, last):
        P, S = self.P, self.S
        uT = self.uT
        if ls[0] == 3:
            S['y_xb'] = [t for tpl in S['pend'] for t in tpl]
            S['pend'] = []
        xb_toks = S['y_xb']
        yb = 6
        ys = self.ps[7]
        t_ys = None
        for l in ls:
            yps = self.ps[yb]
            i = 0
            for pl in range(4):
                for r in range(2):
                    P.op('pe', lambda e, l=l, pl=pl, r=r, i=i, yps=yps, kt=kt: e.matmul(
                        yps[:, :], self.CpPad[pl // 2][:, pl % 2, l, r, :, :].rearrange("p g c -> p (g c)"),
                        self.XbR[(4 * kt + pl) % 6][r][:, 1:513], start=(i == 0), stop=False),
                        waits=(xb_toks + S['t_cp'] + [self.ps_rd[yb]]) if i == 0 else [], signal=False)
                    i += 1
            for s_ in range(l + 1):
                t_y = P.op('pe', lambda e, l=l, s_=s_, kt=kt, yps=yps: e.matmul(
                    yps[:, :], self.Kin[:, kt, l - s_, :], uT[:, kt, s_:SEQ:4], start=False, stop=(s_ == l)),
                    signal=(s_ == l))
            i = 0
            for pl in range(4):
                for r in range(2):
                    P.op('pe', lambda e, l=l, pl=pl, r=r, i=i, kt=kt: e.matmul(
                        ys[:, 128 + 16 * l:144 + 16 * l], self.CpPad[pl // 2][:, pl % 2, l, r, :, :].rearrange("p g c -> p (g c)"),
                        self.XbS[:, kt % 2, pl, r, :], start=(i == 0), stop=False),
                        waits=[self.ps_rd[7], S['zs_rd']] if i == 0 else [], signal=False)
                    i += 1
            for s_ in range(l + 1):
                t_ys = P.op('pe', lambda e, l=l, s_=s_, kt=kt: e.matmul(
                    ys[:, 128 + 16 * l:144 + 16 * l], self.Kin[:, kt, l - s_, :], uT[:, kt, SEQ + s_:NT:4],
                    start=False, stop=(s_ == l)), signal=(s_ == l))
            self._gelu(yps[:, :], uT[:, kt, l:SEQ:4], self.ta[:, 0:512], [t_y])
            self.ps_rd[yb] = S['gel']
        if not last:
            return
        S['pad_cp_rd'] = t_ys
        S['xb_rd'] = t_ys
        for pl in range(4):
            S['slot_rd'][(4 * kt + pl) % 6] = t_ys
        S['xbs_rd'][kt % 2] = t_ys
        v3 = lambda ap: ap.rearrange("p (t b) -> p t b", t=4)
        self._gelu(v3(ys[:, 128:192]), uT[:, kt, SEQ:NT].rearrange("p (b t) -> p t b", t=4), v3(self.ta[:, 0:64]), [t_ys])
        self.ps_rd[7] = S['gel']
        self.ssm_y_tok[kt] = S['gel']

    def _gelu(self, src, dst, a, waits):
        P, S = self.P, self.S
        t5 = P.op('act', lambda e: e.activation(out=dst, in_=src, func=AF.Gelu_apprx_tanh), waits=list(waits))
        S['gel'] = t5

    def ssm_glu(self):
        P, S, d = self.P, self.S, self.dram
        uT = self.uT
        for i in self.cp_slots:
            self.ws_release(i, S['pad_cp_rd'])
        self.ws_release(self.tmp_slot, S['dmp'])
        wv, t_w, wi = self.ws_load(d["wglu"], lambda t: t[:, 0:2048].rearrange("p (k n) -> p k n", k=4))
        gt = self.glu_tmp
        prod = None
        t_g = None
        for ti, (t0, n) in enumerate(TT):
            ta_ = []
            for oc in range(4):
                b = 4 + oc
                gps = self.ps[b]
                for kt in range(4):
                    t_g = P.op('pe', lambda e, kt=kt, oc=oc, t0=t0, n=n, gps=gps: e.matmul(
                        gps[:, :n], wv[:, kt, oc * 128:(oc + 1) * 128], uT[:, kt, t0:t0 + n], start=(kt == 0), stop=(kt == 3)),
                        waits=([t_w, self.ps_rd[b]] + [self.ssm_y_tok[k] for k in range(4)]) if kt == 0 else [],
                        signal=(kt == 3))
                t_a = P.op('act', lambda e, oc=oc, n=n, gps=gps: e.activation(
                    out=gt[:, oc, :n], in_=gps[:, :n], func=AF.Sigmoid, bias=self.bgl[:, oc:oc + 1], scale=1.0),
                    waits=[t_g, prod, S['dve']])
                self.ps_rd[b] = t_a
                ta_.append(t_a)
            for oc in range(4):
                prod = P.op('dve', lambda e, oc=oc, t0=t0, n=n: e.tensor_tensor(
                    out=uT[:, oc, t0:t0 + n], in0=uT[:, oc, t0:t0 + n], in1=gt[:, oc, :n], op=ALU.mult),
                    waits=ta_ + [t_g])
                self.ssm_tok[(oc, ti)] = prod
        self.ws_release(wi, t_g)
        self.ssmT = uT
        s_o = self.slot("d_st")
        self.out_toks.append(P.dma('sp', lambda e: e.dma_start(out=d["st_p"], in_=self.stp[:]), s_o, waits=list(S['xf_rd'])))
        s_o2 = self.slot("d_st2")
        self.out_toks.append(P.dma('sp', lambda e: e.dma_start(out=d["st_s"], in_=self.x0[:]), s_o2, waits=[S['zs_rd'], S['xb_rd']]))


def _layout(inputs):
    f32 = np.float32
    xp = np.asarray(inputs["x_prompt"], f32)
    xs = np.asarray(inputs["x_sample"], f32)
    shared = {}
    norms = np.stack([inputs["ffn1_norm"][0], inputs["mix_norm"][0], inputs["ffn2_norm"][0], inputs["final_norm"]], 0)
    shared["norms"] = np.ascontiguousarray(np.asarray(norms, f32).reshape(4, KC, 128).transpose(2, 0, 1))
    for f, pre in ((1, "ffn1"), (2, "ffn2")):
        wg = np.asarray(inputs[pre + "_w_gate"][0], f32).reshape(KC, 128, NJ, 128)
        wu = np.asarray(inputs[pre + "_w_up"][0], f32).reshape(KC, 128, NJ, 128)
        wgu = np.stack([wg, wu], 0)
        shared["wgu%d" % f] = np.ascontiguousarray(wgu.transpose(3, 2, 0, 1, 4))
        wd = np.asarray(inputs[pre + "_w_down"][0], f32).reshape(2, NJH, 128, KC, 128)
        shared["wd%d" % f] = np.ascontiguousarray(wd.transpose(0, 3, 2, 1, 4))
    w_in = np.asarray(inputs["w_in"][0], f32)
    qperm = w_in[:, :512].reshape(D, 2, 4, 64).transpose(0, 2, 1, 3).reshape(D, 512)
    winp = np.concatenate([qperm, w_in[:, 512:]], 1)
    shared["win"] = np.ascontiguousarray(winp.reshape(KC, 128, 1280).transpose(1, 0, 2))
    w_out = np.asarray(inputs["w_out"][0], f32)
    wa = w_out[:512].reshape(2, 4, 64, D).transpose(1, 0, 2, 3).reshape(4, 128, D)
    wperm = np.concatenate([wa, w_out[512:].reshape(4, 128, D)], 0)
    shared["wout"] = np.ascontiguousarray(wperm.reshape(8, 128, KC, 128).transpose(2, 1, 0, 3))
    shared["cst_ident"] = np.ascontiguousarray(np.eye(128, dtype=f32)[::-1])
    dd = np.arange(256)
    dfl = np.maximum(dd, 1).astype(f32)
    large = 16 + (np.log(dfl / f32(16)) / f32(np.log(128 / 16)) * f32(16)).astype(np.int32)
    bucket = np.where(dd < 16, dd, np.minimum(large, 31))
    oh = np.zeros((32, 256), f32)
    oh[bucket, dd] = 1.0
    shared["cst_oh"] = oh
    bi = np.arange(64) // 4
    shared["cst_blk"] = np.ascontiguousarray(np.where(bi[:, None] == bi[None, :], 0.0, -30000.0).astype(f32)[::-1])
    shared["rel_bias"] = np.asarray(inputs["rel_bias"], f32)
    shared["sinks"] = np.asarray(inputs["sinks"], f32).reshape(1, 8)
    a_re = np.asarray(inputs["a_re"][0], f32); a_im = np.asarray(inputs["a_im"][0], f32)
    ldt = np.broadcast_to(np.asarray(inputs["log_dt"][0], f32)[:, None], (32, 64))
    pm = lambda a: a.reshape(16, 2, 64).transpose(1, 2, 0).reshape(128, 16)
    shared["ssm_pm"] = np.ascontiguousarray(np.stack([pm(a_re), pm(a_im), pm(ldt)], 1))
    cm = lambda a: np.broadcast_to(a.reshape(4, 8, 1, 64).transpose(1, 2, 0, 3), (8, 16, 4, 64)).reshape(128, 4, 64)
    shared["ssm_cm"] = np.ascontiguousarray(np.stack([cm(a_re), cm(a_im), cm(ldt)], 1))
    cpm = lambda a: a.reshape(16, 2, 16, 64).transpose(1, 3, 0, 2).reshape(128, 16, 16)
    shared["c_pm"] = np.ascontiguousarray(np.stack([cpm(np.asarray(inputs["c_re"][0], f32)), cpm(np.asarray(inputs["c_im"][0], f32))], 1))
    bpm = lambda a: a.reshape(16, 2, 64, 16).transpose(1, 2, 0, 3).reshape(128, 16, 16)
    bcm = lambda a: a.reshape(4, 8, 64, 16).transpose(1, 3, 0, 2).reshape(128, 4, 64)
    b_re = np.asarray(inputs["b_re"][0], f32); b_im = np.asarray(inputs["b_im"][0], f32)
    shared["b_pm"] = np.ascontiguousarray(np.stack([bpm(b_re), bpm(b_im)], 1))
    shared["b_cm"] = np.ascontiguousarray(np.stack([bcm(b_re), bcm(b_im)], 1))
    shared["dskip"] = np.ascontiguousarray(np.asarray(inputs["d_skip"][0], f32).reshape(4, 128).T)
    shared["bglu"] = np.ascontiguousarray(np.asarray(inputs["b_glu"][0], f32).reshape(4, 128).T)
    shared["wglu"] = np.ascontiguousarray(np.asarray(inputs["w_glu"][0], f32).reshape(4, 128, 512).transpose(1, 0, 2))
    pidx = np.arange(128)
    m2 = np.zeros((128, 4, 8), f32); m3 = np.zeros((128, 4, 2), f32)
    for pl in range(4):
        for gg in range(2):
            m2[pidx // 64 == gg, pl, 2 * pl + gg] = 1.0
            m3[pidx // 16 == 2 * pl + gg, pl, gg] = 1.0
    shared["cst_m2"] = m2
    shared["cst_m3"] = m3
    shared["cst_eye"] = np.eye(128, dtype=f32)
    sre = np.asarray(inputs["state_ssm_re"][0], f32); sim = np.asarray(inputs["state_ssm_im"][0], f32)
    ck = np.asarray(inputs["cache_k"][0], f32)
    cv = np.asarray(inputs["cache_v"][0], f32)
    maps = []
    for c in range(NCORES):
        X = np.concatenate([xp[c], xs[16 * c:16 * c + 16].reshape(NS, D)], 0)
        m = dict(shared)
        m["xT"] = np.ascontiguousarray(X.T.reshape(KC, 128, NT).transpose(1, 0, 2))
        ckc, cvc = ck[16 * c:16 * c + 16], cv[16 * c:16 * c + 16]
        m["cKT"] = np.ascontiguousarray(ckc.transpose(2, 3, 0, 1).reshape(128, 16, 128))
        m["cV"] = np.ascontiguousarray(cvc.transpose(1, 0, 2, 3).reshape(128, 16, 128))
        m["cK_nat"] = np.ascontiguousarray(ckc.reshape(16, 128, 128))
        m["cV_nat"] = np.ascontiguousarray(cvc.reshape(16, 128, 128))
        x0 = np.stack([sre[16 * c:16 * c + 16], sim[16 * c:16 * c + 16]], 0)
        m["x0"] = np.ascontiguousarray(x0.reshape(2, 16, 16, 2, 64).transpose(3, 4, 2, 0, 1).reshape(128, 16, 2, 16))
        maps.append(m)
    return maps


_NC_CACHE = {}


def _get_nc(debug=None):
    key = repr(sorted((debug or {}).items()))
    if key not in _NC_CACHE:
        _NC_CACHE[key] = Builder(debug).build()
    return _NC_CACHE[key]


def kernel(**inputs):
    maps = _layout(inputs)
    nc = _get_nc()
    res = run_bass_kernel_spmd(nc, maps, core_ids=list(range(NCORES)))
    outs = res.results
    yp = np.zeros((8, SEQ, D), np.float32)
    ys = np.zeros((128, 4, D), np.float32)
    kp = np.zeros((1, 8, 128, 2, 64), np.float32); vp = np.zeros_like(kp)
    rp = np.zeros((1, 8, 32, 64), np.float32); ip = np.zeros_like(rp)
    ks = np.zeros((1, 128, 128, 2, 64), np.float32); vs = np.zeros_like(ks)
    rs = np.zeros((1, 128, 32, 64), np.float32); is_ = np.zeros_like(rs)
    for c in range(NCORES):
        o = outs[c]
        Y = np.asarray(o["yT"]).transpose(1, 0, 2).reshape(D, NT).T
        yp[c] = Y[:SEQ]
        ys[16 * c:16 * c + 16] = Y[SEQ:].reshape(16, 4, D)
        kp[0, c] = np.asarray(o["kp"]).reshape(128, 2, 64)
        vp[0, c] = np.asarray(o["vp"]).reshape(128, 2, 64)
        ks[0, 16 * c:16 * c + 16] = np.asarray(o["ks"]).reshape(16, 128, 2, 64)
        vs[0, 16 * c:16 * c + 16] = np.asarray(o["vs"]).reshape(16, 128, 2, 64)
        sp = np.asarray(o["st_p"]).reshape(2, 64, 16, 2).transpose(2, 0, 1, 3).reshape(32, 64, 2)
        rp[0, c], ip[0, c] = sp[..., 0], sp[..., 1]
        ss = np.asarray(o["st_s"]).reshape(2, 64, 16, 2, 16).transpose(4, 2, 0, 1, 3).reshape(16, 32, 64, 2)
        rs[0, 16 * c:16 * c + 16], is_[0, 16 * c:16 * c + 16] = ss[..., 0], ss[..., 1]
    return yp, ys, kp, vp, rp, ip, ks, vs, rs, is_
```

```python
import contextlib
import numpy as np
import concourse.bass as bass
import concourse.mybir as mybir
from concourse.bass_utils import run_bass_kernel_spmd

F32 = mybir.dt.float32
BF16 = mybir.dt.bfloat16
I32 = mybir.dt.int32
AF = mybir.ActivationFunctionType
ALU = mybir.AluOpType

NCORES = 8
D = 1024
KC = 8
DFF = 2816
NJ = 22
NJH = 11
SEQ = 2048
NS = 64
NT = SEQ + NS
TT = [(0, 512), (512, 512), (1024, 512), (1536, 512), (2048, 64)]
EPS = 1e-6
ENGS = ('pe', 'act', 'dve', 'pool', 'sp')


class Prog:
    def __init__(self, nc):
        self.nc = nc
        self.q = {e: [] for e in ENGS}
        self.sem = {}
        self.cnt = {}
        self.waited = {e: {} for e in ENGS}
        self._stack = []
        self.last = {e: None for e in ENGS}

    def new_sem(self, name):
        cm = self.nc.semaphore(name)
        h = cm.__enter__()
        self._stack.append(cm)
        return h

    def _waits(self, eng, waits):
        out = []
        for w in waits:
            if w is None:
                continue
            sem, val = w
            key = id(sem)
            if self.waited[eng].get(key, 0) >= val:
                continue
            self.waited[eng][key] = val
            out.append((sem, val))
        return out

    def op(self, eng, fn, waits=(), signal=True):
        ws = self._waits(eng, waits)
        tok = None
        if signal:
            if eng not in self.sem:
                self.sem[eng] = self.new_sem('s_' + eng)
                self.cnt[eng] = 0
            self.cnt[eng] += 1
            tok = (self.sem[eng], self.cnt[eng])
            self.last[eng] = tok
        self.q[eng].append((fn, ws, tok, 1))
        return tok

    def dma(self, eng, fn, slot, waits=()):
        ws = self._waits(eng, waits)
        slot.count += 16
        tok = (slot.sem, slot.count)
        self.q[eng].append((fn, ws, tok, 16))
        return tok

    def wait_only(self, eng, waits):
        ws = self._waits(eng, waits)
        if ws:
            self.q[eng].append((None, ws, None, 0))

    def replay(self, eng, engine):
        for fn, ws, tok, inc in self.q[eng]:
            for sem, val in ws:
                engine.wait_ge(sem, val)
            if fn is None:
                continue
            ins = fn(engine)
            if tok is not None:
                ins.then_inc(tok[0], inc)

    def close(self):
        for cm in reversed(self._stack):
            cm.__exit__(None, None, None)


class DmaSlot:
    def __init__(self, prog, name):
        self.sem = prog.new_sem(name)
        self.count = 0


class Builder:
    def __init__(self, debug=None):
        self.debug = debug or {}
        self.nc = bass.Bass("TRN2", target_bir_lowering=False)
        self.P = Prog(self.nc)
        self.es = contextlib.ExitStack()
        self.dram = {}

    def din(self, name, shape, dt=F32):
        t = self.nc.dram_tensor(name, list(shape), dt, kind="ExternalInput").ap()
        self.dram[name] = t
        return t

    def dout(self, name, shape, dt=F32):
        t = self.nc.dram_tensor(name, list(shape), dt, kind="ExternalOutput").ap()
        self.dram[name] = t
        return t

    def sb(self, name, shape, dt):
        return self.es.enter_context(self.nc.sbuf_tensor(name, list(shape), dt))

    def slot(self, name):
        return DmaSlot(self.P, name)

    def ws_load(self, src_ap, shape_fn):
        P = self.P
        i = self.ws_next % self.WS_N
        self.ws_next += 1
        dst = shape_fn(self.ws[i])
        tok = P.dma('pool', lambda e, d=dst, s=src_ap: e.dma_start(out=d, in_=s), self.ws_slot[i],
                    waits=[self.ws_free[i]])
        return dst, tok, i

    def ws_release(self, i, tok):
        self.ws_free[i] = tok

    def build(self):
        nc, P = self.nc, self.P
        dbg = self.debug
        xT_d = self.din("xT", [128, KC, NT])
        norms_d = self.din("norms", [128, 4, KC])
        wgu_d = [self.din("wgu%d" % f, [NJ, 128, 2, KC, 128]) for f in (1, 2)]
        wd_d = [self.din("wd%d" % f, [2, KC, 128, NJH, 128]) for f in (1, 2)]
        yT_d = self.dout("yT", [128, KC, NT])
        self.mix_io()

        self.xT = self.sb("xT_s", [128, KC, NT], F32)
        self.hT = self.sb("hT_s", [128, KC, NT], BF16)
        self.aT = self.sb("aT_s", [128, NJH, NT], BF16)
        self.WS_N = 3
        self.ws = [self.sb("ws%d" % i, [128, 2048], BF16) for i in range(self.WS_N)]
        self.ws_slot = [self.slot("wsd%d" % i) for i in range(self.WS_N)]
        self.ws_free = [None] * self.WS_N
        self.ws_next = 0
        self.norms = self.sb("norms_s", [128, 4, KC], F32)
        self.ones = self.sb("ones_s", [128, 128], BF16)
        self.epst = self.sb("eps_s", [128, 1], F32)
        self.sq = [self.sb("sq%d" % i, [128, 512], BF16) for i in range(2)]
        self.sg = [self.sb("sg%d" % i, [128, 512], F32) for i in range(2)]
        self.rt = [self.sb("rt%d" % i, [128, 512], F32) for i in range(2)]
        self.ps = [self.es.enter_context(nc.psum_tensor("ps%d" % i, [128, 512], F32)) for i in range(8)]
        self.ps_rd = [None] * 8

        t_ones = P.op('dve', lambda e: e.memset(self.ones[:], 1.0 / D))
        t_eps = P.op('dve', lambda e: e.memset(self.epst[:], EPS))
        self.t_const = t_eps
        s_n = self.slot("d_norm")
        self.t_norms = P.dma('sp', lambda e: e.dma_start(out=self.norms[:], in_=norms_d), s_n)

        self.tok_x = {}
        for ti, (t0, n) in enumerate(TT):
            s = self.slot("d_x%d" % ti)
            tk = P.dma('sp', lambda e, t0=t0, n=n: e.dma_start(out=self.xT[:, :, t0:t0 + n], in_=xT_d[:, :, t0:t0 + n]), s,
                       waits=([self.tok_x[(0, 0)]] if ti == 1 else []))
            for kc in range(KC):
                self.tok_x[(kc, ti)] = tk
        self.tok_h = {}
        self.sq_rd = [None, None]
        self.sg_rd = [None, None]
        self.rt_rd = [None, None]
        self.sq_i = 0
        self.hT_free = None

        self.out_toks = []
        self.ffn(0, wgu_d[0], wd_d[0])
        if not dbg.get("skip_mixer"):
            self.mixer()
        self.yT_d = yT_d
        self.ffn(2, wgu_d[1], wd_d[1])
        self.final_norm(yT_d)

        with nc.Block() as block:
            @block.sync
            def _(e):
                P.replay('sp', e)

            @block.gpsimd
            def _(e):
                P.replay('pool', e)

            @block.tensor
            def _(e):
                P.replay('pe', e)

            @block.scalar
            def _(e):
                P.replay('act', e)

            @block.vector
            def _(e):
                P.replay('dve', e)
        P.close()
        self.es.close()
        return nc

    def norm(self, gi, final=False):
        for ti in range(len(TT)):
            self.norm_tile(gi, ti, final)

    def norm_tile(self, gi, ti, final=False):
        P = self.P
        xT, hT = self.xT, self.hT
        MS0 = 6
        toks = {}
        for ti, (t0, n) in [(ti, TT[ti])]:
            msb = MS0 + (ti % 2)
            ms = self.ps[msb]
            t_mm = None
            for kc in range(KC):
                b = self.sq_i % 2
                self.sq_i += 1
                t_sq = P.op('act', lambda e, b=b, kc=kc, t0=t0, n=n: e.activation(
                    out=self.sq[b][:, :n], in_=xT[:, kc, t0:t0 + n], func=AF.Square),
                    waits=[self.tok_x[(kc, ti)], self.sq_rd[b]])
                last = kc == KC - 1
                t_mm = P.op('pe', lambda e, b=b, kc=kc, n=n, ms=ms: e.matmul(
                    ms[:, :n], self.ones[:], self.sq[b][:, :n], start=(kc == 0), stop=(kc == KC - 1)),
                    waits=[t_sq, self.t_const] + ([self.ps_rd[msb]] if kc == 0 else []), signal=True)
                self.sq_rd[b] = t_mm
            rb = ti % 2
            rt = self.rt[rb]
            t_s = P.op('act', lambda e, n=n, ms=ms, rt=rt: e.activation(
                out=rt[:, :n], in_=ms[:, :n], func=AF.Ln, bias=self.epst[:, 0:1], scale=1.0),
                waits=[t_mm, self.rt_rd[rb], self.t_const])
            self.ps_rd[msb] = t_s
            t_r = P.op('act', lambda e, n=n, rt=rt: e.activation(
                out=rt[:, :n], in_=rt[:, :n], func=AF.Exp, scale=-0.5), waits=[t_s])
            t_h = None
            for kc in range(KC):
                if final:
                    t_h = P.op('dve', lambda e, kc=kc, t0=t0, n=n, rt=rt: e.scalar_tensor_tensor(
                        out=xT[:, kc, t0:t0 + n], in0=xT[:, kc, t0:t0 + n], scalar=self.norms[:, gi, kc:kc + 1],
                        in1=rt[:, :n], op0=ALU.mult, op1=ALU.mult),
                        waits=[t_r, self.t_norms, self.tok_x[(kc, ti)]])
                    self.tok_x[(kc, ti)] = t_h
                else:
                    t_h = P.op('dve', lambda e, kc=kc, t0=t0, n=n, rt=rt: e.scalar_tensor_tensor(
                        out=hT[:, kc, t0:t0 + n], in0=xT[:, kc, t0:t0 + n], scalar=self.norms[:, gi, kc:kc + 1],
                        in1=rt[:, :n], op0=ALU.mult, op1=ALU.mult),
                        waits=[t_r, self.t_norms, self.tok_x[(kc, ti)], self.hT_free])
                    self.tok_h[(kc, ti)] = t_h
            self.rt_rd[rb] = t_h
        return toks

    def ffn(self, gi, wgu_d, wd_d):
        P = self.P
        xT, hT, aT = self.xT, self.hT, self.aT
        LOOK = 2 if gi == 0 else 3
        normed = getattr(self, "normed_upto", {}).get(gi, 0)
        for ti in range(normed, LOOK):
            self.norm_tile(gi, ti)
        normed = max(normed, LOOK)
        GB, UB, YB = (0, 1), (2, 3), (4, 5, 0, 1)
        if not hasattr(self, "aT_rd"):
            self.aT_rd = None
        cnt = 0
        ycnt = 0
        for h in range(2):
            tok_a = {}
            for jj in range(NJH):
                j = h * NJH + jj
                wv, t_w, wi = self.ws_load(
                    wgu_d[j].rearrange("p g k n -> p (g k n)"),
                    lambda t: t[:, 0:2048])
                wv4 = wv.rearrange("p (g k n) -> p g k n", g=2, k=KC)
                t_last = None
                for ti, (t0, n) in enumerate(TT):
                    if h == 0 and jj == 0 and normed < len(TT):
                        self.norm_tile(gi, normed)
                        normed += 1
                    gb = GB[cnt % 2]
                    ub = UB[cnt % 2]
                    sgi = cnt % 2
                    cnt += 1
                    gps, ups = self.ps[gb], self.ps[ub]
                    for kc in range(KC):
                        t_g = P.op('pe', lambda e, kc=kc, t0=t0, n=n, gps=gps, wv4=wv4: e.matmul(
                            gps[:, :n], wv4[:, 0, kc, :], hT[:, kc, t0:t0 + n], start=(kc == 0), stop=(kc == KC - 1)),
                            waits=([t_w, self.ps_rd[gb]] if kc == 0 else []) + [self.tok_h[(kc, ti)]],
                            signal=(kc == KC - 1))
                    for kc in range(KC):
                        t_u = P.op('pe', lambda e, kc=kc, t0=t0, n=n, ups=ups, wv4=wv4: e.matmul(
                            ups[:, :n], wv4[:, 1, kc, :], hT[:, kc, t0:t0 + n], start=(kc == 0), stop=(kc == KC - 1)),
                            waits=([self.ps_rd[ub]] if kc == 0 else []),
                            signal=(kc == KC - 1))
                    t_last = t_u
                    sg = self.sg[sgi]
                    t_s = P.op('act', lambda e, n=n, gps=gps, sg=sg: e.activation(
                        out=sg[:, :n], in_=gps[:, :n], func=AF.Silu), waits=[t_g, self.sg_rd[sgi]])
                    self.ps_rd[gb] = t_s
                    t_a = P.op('dve', lambda e, jj=jj, t0=t0, n=n, ups=ups, sg=sg: e.tensor_tensor(
                        out=aT[:, jj, t0:t0 + n], in0=ups[:, :n], in1=sg[:, :n], op=ALU.mult),
                        waits=[t_s, t_u, self.aT_rd, getattr(self, 'dbg_tok', None)])
                    self.sg_rd[sgi] = t_a
                    self.ps_rd[ub] = t_a
                    tok_a[(jj, ti)] = t_a
                    self.bg_pump(2)
                self.ws_release(wi, t_last)
                if h == 0 and jj == 0 and gi == 0 and not self.debug.get("skip_mixer"):
                    self.mix_setup()
                    def chain():
                        yield from self._tables_gen()
                        if not self.debug.get("skip_ssm"):
                            yield from self.ssm_setup_gen(1)
                    self.bg = chain()
            if h == 1:
                self.hT_free = t_last
                if gi == 0 and not self.debug.get("skip_mixer") and not self.debug.get("skip_ssm"):
                    self.bg_pump(10 ** 9)
                    self.bg = self.ssm_setup_gen(2)
            if gi == 2 and h == 0 and getattr(self, "mix_end_tok", None) is not None and not self.debug.get("no_tail_opt"):
                stb = self.stage[:].rearrange("p a h q -> p (a h q)").bitcast(BF16)
                fl = lambda t: t[:].rearrange("p a b -> p (a b)")
                bufs = [stb[:, 0:1408], stb[:, 2048:3456], fl(self.cKT)[:, 0:1408], fl(self.cV)[:, 0:1408],
                        self.BT[:].rearrange("p a b c -> p (a b c)")[:, 0:1408]]
                self.res_wd = []
                for c_, bf in enumerate(bufs):
                    tk = P.dma('pool', lambda e, bf=bf, c_=c_: e.dma_start(out=bf, in_=wd_d[1, c_].rearrange("p j n -> p (j n)")),
                               self.slot("d_rwd%d" % c_), waits=[self.mix_end_tok])
                    self.res_wd.append((bf.rearrange("p (j n) -> p j n", j=NJH), tk))
            if gi == 2 and h == 1 and getattr(self, "res_wd", None):
                chunks = list(self.res_wd)
                ring = []
                for c in range(5, KC):
                    wv, t_w, wi = self.ws_load(wd_d[h, c].rearrange("p j n -> p (j n)"), lambda t: t[:, 0:NJH * 128])
                    chunks.append((wv.rearrange("p (j n) -> p j n", j=NJH), t_w))
                    ring.append(wi)
                t_y = None
                for ti, (t0, n) in enumerate(TT):
                    for c in range(KC):
                        wv3, t_w = chunks[c]
                        yb = YB[ycnt % 4]
                        ycnt += 1
                        yps = self.ps[yb]
                        for jj in range(NJH):
                            t_y = P.op('pe', lambda e, jj=jj, t0=t0, n=n, yps=yps, wv3=wv3: e.matmul(
                                yps[:, :n], wv3[:, jj, :], aT[:, jj, t0:t0 + n], start=(jj == 0), stop=(jj == NJH - 1)),
                                waits=([t_w, self.ps_rd[yb]] if jj == 0 else []) + [tok_a[(jj, ti)]],
                                signal=(jj == NJH - 1))
                        t_x = P.op('dve', lambda e, c=c, t0=t0, n=n, yps=yps: e.scalar_tensor_tensor(
                            out=xT[:, c, t0:t0 + n], in0=yps[:, :n], scalar=0.5, in1=xT[:, c, t0:t0 + n],
                            op0=ALU.mult, op1=ALU.add),
                            waits=[t_y, self.tok_x[(c, ti)]])
                        self.tok_x[(c, ti)] = t_x
                        self.ps_rd[yb] = t_x
                    self.final_tile(ti)
                for wi in ring:
                    self.ws_release(wi, t_y)
                self.aT_rd = t_y
                continue
            for c in range(KC):
                wv, t_w, wi = self.ws_load(
                    wd_d[h, c].rearrange("p j n -> p (j n)"),
                    lambda t: t[:, 0:NJH * 128])
                wv3 = wv.rearrange("p (j n) -> p j n", j=NJH)
                t_y = None
                for ti, (t0, n) in enumerate(TT):
                    yb = YB[ycnt % 4]
                    ycnt += 1
                    yps = self.ps[yb]
                    for jj in range(NJH):
                        t_y = P.op('pe', lambda e, jj=jj, t0=t0, n=n, yps=yps, wv3=wv3: e.matmul(
                            yps[:, :n], wv3[:, jj, :], aT[:, jj, t0:t0 + n], start=(jj == 0), stop=(jj == NJH - 1)),
                            waits=([t_w, self.ps_rd[yb]] if jj == 0 else []) + [tok_a[(jj, ti)]],
                            signal=(jj == NJH - 1))
                    t_x = P.op('dve', lambda e, c=c, t0=t0, n=n, yps=yps: e.scalar_tensor_tensor(
                        out=xT[:, c, t0:t0 + n], in0=yps[:, :n], scalar=0.5, in1=xT[:, c, t0:t0 + n],
                        op0=ALU.mult, op1=ALU.add),
                        waits=[t_y, self.tok_x[(c, ti)]])
                    self.tok_x[(c, ti)] = t_x
                    self.ps_rd[yb] = t_x
                    self.bg_pump(3)
                    if gi == 2 and h == 1 and c == KC - 1 and getattr(self, "yT_d", None) is not None:
                        self.final_tile(ti)
                self.ws_release(wi, t_y)
                self.aT_rd = t_y

    def final_tile(self, ti):
        P = self.P
        if not hasattr(self, "fin_slot"):
            self.fin_slot = self.slot("d_out")
            self.fin_done = set()
        self.norm_tile(3, ti, final=True)
        t0, n = TT[ti]
        P.dma('sp', lambda e, t0=t0, n=n: e.dma_start(out=self.yT_d[:, :, t0:t0 + n], in_=self.xT[:, :, t0:t0 + n]),
              self.fin_slot, waits=[self.tok_x[(kc, ti)] for kc in range(KC)])
        self.fin_done.add(ti)

    def final_norm(self, yT_d):
        P = self.P
        for ti in range(len(TT)):
            if ti not in getattr(self, "fin_done", set()):
                self.final_tile(ti)
        P.wait_only('sp', [(self.fin_slot.sem, self.fin_slot.count)] + list(self.out_toks))

    def mix_io(self):
        d = self.dram
        self.din("win", [128, KC, 1280])
        self.din("wout", [KC, 128, 8, 128])
        self.din("cst_ident", [128, 128])
        self.din("cst_oh", [32, 256])
        self.din("cst_blk", [64, 64])
        self.din("rel_bias", [32, 8])
        self.din("sinks", [1, 8])
        self.din("cKT", [128, 16, 128])
        self.din("cV", [128, 16, 128])
        self.din("cK_nat", [16, 128, 128])
        self.din("cV_nat", [16, 128, 128])
        self.dout("kp", [128, 128])
        self.dout("vp", [128, 128])
        self.dout("ks", [16, 128, 128])
        self.dout("vs", [16, 128, 128])
        self.scr = self.nc.dram_tensor("scr_f", [8, 2, 256], F32).ap()
        self.ssm_io()
        if self.debug.get("dump_attn"):
            self.dout("dbg_attn", [128, 4, NT], BF16)

    def mix_setup(self):
        nc, P, d = self.nc, self.P, self.dram
        aT = self.aT
        NEG = -30000.0
        self.qT2 = aT[:, 0:4, :]
        self.uT = aT[:, 4:8, :]
        self.kT = aT[:, 8, :]
        flat = aT[:, 9:11, :].rearrange("p a t -> p (a t)")
        self.vtok = flat[:, 0:17 * 128].rearrange("p (b n) -> p b n", n=128)
        self.ident = self.sb("ident_s", [128, 128], BF16)
        self.ones1 = self.sb("ones1_s", [128, 64], BF16)
        self.BT = self.sb("BT_s", [128, 2, 8, 128], BF16)
        self.BTsc = self.sb("BTsc_s", [128, 2, 16, 4, 4], BF16)
        self.BTsn = self.sb("BTsn_s", [128, 4, 64], BF16)
        self.ES = self.sb("ES_s", [128, 4], F32)
        self.cKT = self.sb("cKT_s", [128, 16, 128], BF16)
        self.cV = self.sb("cV_s", [128, 16, 128], BF16)
        self.stage = self.sb("stage_s", [128, 2, 8, 128], F32)
        self.kvo = [self.sq[i][:].bitcast(F32) for i in range(2)]
        self.pT = [self.sg[i // 2][:, 256 * (i % 2):256 * (i % 2) + 256].bitcast(BF16) for i in range(4)]
        rb_s = self.sb("rb_s", [32, 8], F32)
        oh_s = self.sb("oh_s", [32, 256], F32)
        fm = self.sb("fm_s", [8, 2, 256], F32)
        blk_s = self.sb("blk_s", [128, 64], F32)
        sk_s = self.sb("sk_s", [128, 4], F32)

        t1 = P.dma('sp', lambda e: e.dma_start(out=rb_s[:], in_=d["rel_bias"]), self.slot("d_rb"))
        t2 = P.dma('sp', lambda e: e.dma_start(out=oh_s[:], in_=d["cst_oh"]), self.slot("d_oh"))
        s_blk = self.slot("d_blk")
        for g in range(2):
            t3 = P.dma('sp', lambda e, g=g: e.dma_start(out=blk_s[64 * g:64 * g + 64, :], in_=d["cst_blk"]), s_blk)
        s_sk = self.slot("d_sk")
        for g in range(2):
            t4 = P.dma('sp', lambda e, g=g: e.dma_start(
                out=sk_s[64 * g:64 * g + 64, :], in_=d["sinks"][0:1, 4 * g:4 * g + 4].to_broadcast([64, 4])), s_sk)
        if self.debug.get('ckpt', 99) < 1:
            self.out_toks = []
            return
        t_es = P.op('act', lambda e: e.activation(out=self.ES[:], in_=sk_s[:], func=AF.Exp), waits=[t4])
        self.t_es = t_es
        t_o1 = P.op('dve', lambda e: e.memset(self.ones1[:], 1.0))
        self.t_ones1 = t_o1
        if self.debug.get('ckpt', 99) < 2:
            self.out_toks = []
            return
        def tables_gen():
            fps = self.ps[7]
            t_f = P.op('pe', lambda e: e.matmul(fps[0:8, 0:256], rb_s[0:32, 0:8], oh_s[0:32, :], start=True, stop=True),
                       waits=[t1, t2, self.ps_rd[7]])
            for _ in range(6):
                yield
            t_m = P.op('dve', lambda e: e.memset(fm[:], NEG))
            t_c1 = P.op('dve', lambda e: e.tensor_copy(out=fm[:, 0, 127:255], in_=fps[0:8, 0:128]), waits=[t_f, t_m])
            t_c2 = P.op('dve', lambda e: e.tensor_copy(out=fm[:, 1, 0:127], in_=fps[0:8, 1:128]), waits=[t_f, t_m])
            self.ps_rd[7] = t_c2
            t_sc = P.dma('sp', lambda e: e.dma_start(out=self.scr, in_=fm[:]), self.slot("d_scr"), waits=[t_c1, t_c2])
            from concourse.ap import AP as _AP
            scr_t = self.scr.tensor
            s_tp = self.slot("d_tp")
            tl = None
            for ty in range(2):
                for h in range(8):
                    src = _AP(scr_t, (h * 2 + ty) * 256, [[1, 128], [1, 128]])
                    tl = P.dma('sp', lambda e, ty=ty, h=h, src=src: e.dma_start(out=self.stage[:, ty, h, :], in_=src),
                               s_tp, waits=[t_sc])
            P.wait_only('sp', [tl])
            for _ in range(30):
                yield
            t_bt = P.op('dve', lambda e: e.tensor_copy(out=self.BT[:], in_=self.stage[:]), waits=[(s_tp.sem, s_tp.count)])
            stage2 = self.sb("stage2_s", [128, 8, 4], F32)
            stage3 = self.sb("stage3_s", [128, 4, 64], F32)
            s_tp2 = self.slot("d_tp2")
            for h in range(8):
                src = _AP(scr_t, (h * 2 + 1) * 256, [[1, 128], [1, 4]])
                P.dma('sp', lambda e, h=h, src=src: e.dma_start(out=stage2[:, h, :], in_=src), s_tp2, waits=[t_sc])
                src = _AP(scr_t, (h * 2 + 0) * 256 + 64, [[1, 64], [1, 64]])
                P.dma('sp', lambda e, h=h, src=src: e.dma_start(
                    out=stage3[64 * (h // 4):64 * (h // 4) + 64, h % 4, :], in_=src), s_tp2, waits=[t_sc])
            tk2 = (s_tp2.sem, s_tp2.count)
            tb = []
            for g in range(2):
                tb.append(P.op('dve', lambda e, g=g: e.tensor_copy(
                    out=self.BTsc[:, g, :, :, :], in_=stage2[:, 4 * g:4 * g + 4, :].unsqueeze(1).to_broadcast([128, 16, 4, 4])),
                    waits=[tk2]))
            tb.append(P.op('dve', lambda e: e.tensor_tensor(
                out=self.BTsn[:], in0=stage3[:], in1=blk_s[:].unsqueeze(1).to_broadcast([128, 4, 64]), op=ALU.add),
                waits=[tk2, t3]))
            self.t_tables_all = tb
            self.t_tables = [t_bt] + tb
            yield
        self._tables_gen = tables_gen
        if self.debug.get('ckpt', 99) < 6:
            self.out_toks = []
            return
        self.t_ident = P.dma('pool', lambda e: e.dma_start(out=self.ident[:], in_=d["cst_ident"]), self.slot("d_id"))
        self.ssm_alloc()
        s_cp = self.slot("d_cp")
        P.dma('sp', lambda e: e.dma_start(out=d["ks"][:, 0:124, :], in_=d["cK_nat"][:, 4:128, :]), s_cp)
        self.out_toks = [(s_cp.sem, s_cp.count)]
        if self.debug.get('cp', 2) > 1:
            s_cp2 = self.slot("d_cp2")
            self.out_toks.append(P.dma('sp', lambda e: e.dma_start(out=d["vs"][:, 0:124, :], in_=d["cV_nat"][:, 4:128, :]), s_cp2))

    def mixer(self):
        nc, P, d = self.nc, self.P, self.dram
        xT, hT = self.xT, self.hT
        dbg = self.debug
        if dbg.get("setup_only"):
            return
        self.bg_pump(10 ** 9)
        for ti_ in range(3):
            self.norm_tile(1, ti_)
        mix_normed = [3]
        win = d["win"]
        pcnt = 0
        tok_q = {}
        tok_k = {}
        tok_u = {}
        t_last = None
        def u_group(kt, ti, wv, t_w, b):
            t0, n = TT[ti]
            pp = self.ps[b]
            t_p = None
            for kc in range(KC):
                t_p = P.op('pe', lambda e, kc=kc, t0=t0, n=n, pp=pp, wv=wv: e.matmul(
                    pp[:, :n], wv[:, kc, :], hT[:, kc, t0:t0 + n], start=(kc == 0), stop=(kc == KC - 1)),
                    waits=([t_w, self.ps_rd[b], self.aT_rd] if kc == 0 else []) + [self.tok_h[(kc, ti)]],
                    signal=(kc == KC - 1))
            if ti % 2 == 0:
                t_e = P.op('dve', lambda e, kt=kt, t0=t0, n=n, pp=pp: e.tensor_copy(
                    out=self.uT[:, kt, t0:t0 + n], in_=pp[:, :n]), waits=[t_p])
            else:
                t_e = P.op('act', lambda e, kt=kt, t0=t0, n=n, pp=pp: e.activation(
                    out=self.uT[:, kt, t0:t0 + n], in_=pp[:, :n], func=AF.Copy), waits=[t_p])
            tok_u[(kt, ti)] = t_e
            self.ps_rd[b] = t_e
            return t_p

        for ch in [0, 1, 2, 3, 4]:
            wv, t_w, wi = self.ws_load(win[:, :, ch * 128:(ch + 1) * 128], lambda t: t[:, 0:1024].rearrange("p (k n) -> p k n", k=KC))
            for ti, (t0, n) in enumerate(TT):
                if mix_normed[0] < len(TT):
                    self.norm_tile(1, mix_normed[0])
                    mix_normed[0] += 1
                b = pcnt % 4
                pcnt += 1
                pp = self.ps[b]
                for kc in range(KC):
                    t_p = P.op('pe', lambda e, kc=kc, t0=t0, n=n, pp=pp, wv=wv: e.matmul(
                        pp[:, :n], wv[:, kc, :], hT[:, kc, t0:t0 + n], start=(kc == 0), stop=(kc == KC - 1)),
                        waits=([t_w, self.ps_rd[b], self.aT_rd] if kc == 0 else []) + [self.tok_h[(kc, ti)]],
                        signal=(kc == KC - 1))
                t_last = t_p
                if ch < 4:
                    t_e = P.op('act', lambda e, ch=ch, t0=t0, n=n, pp=pp: e.activation(
                        out=self.qT2[:, ch, t0:t0 + n], in_=pp[:, :n], func=AF.Copy, scale=0.125), waits=[t_p])
                    tok_q[(ch, ti)] = t_e
                elif ch == 4:
                    t_e = P.op('dve', lambda e, t0=t0, n=n, pp=pp: e.tensor_copy(
                        out=self.kT[:, t0:t0 + n], in_=pp[:, :n]), waits=[t_p])
                    tok_k[ti] = t_e
                else:
                    kt = ch - 6
                    eng = 'dve' if (ti % 2 == 0) else 'act'
                    if eng == 'dve':
                        t_e = P.op('dve', lambda e, kt=kt, t0=t0, n=n, pp=pp: e.tensor_copy(
                            out=self.uT[:, kt, t0:t0 + n], in_=pp[:, :n]), waits=[t_p])
                    else:
                        t_e = P.op('act', lambda e, kt=kt, t0=t0, n=n, pp=pp: e.activation(
                            out=self.uT[:, kt, t0:t0 + n], in_=pp[:, :n], func=AF.Copy), waits=[t_p])
                    tok_u[(kt, ti)] = t_e
                self.ps_rd[b] = t_e
            self.ws_release(wi, t_last)
        if dbg.get('mix_stop', 99) <= 1:
            self.hT_free = P.last['pe']
            return
        wv, t_w, wi = self.ws_load(win[:, :, 512:768], lambda t: t[:, 0:2048].rearrange("p (k n) -> p k n", k=KC))
        tok_v = {}
        kv_tok = {}
        for blk in range(17):
            t0 = blk * 128
            n = 128 if blk < 16 else 64
            full = blk >= 15
            b = 4 + (blk % 2)
            pp = self.ps[b]
            c0 = 0 if full else 128
            for kc in range(KC):
                t_p = P.op('pe', lambda e, kc=kc, t0=t0, n=n, pp=pp, wv=wv, c0=c0: e.matmul(
                    pp[0:n, c0:256], hT[:, kc, t0:t0 + n], wv[:, kc, c0:256], start=(kc == 0), stop=(kc == KC - 1)),
                    waits=([t_w, self.ps_rd[b]] if kc == 0 else []) + [self.tok_h[(kc, min(blk // 4, 4))]],
                    signal=(kc == KC - 1))
            t_last = t_p
            t_e = P.op('dve', lambda e, blk=blk, n=n, pp=pp: e.tensor_copy(
                out=self.vtok[0:n, blk, :], in_=pp[0:n, 128:256]), waits=[t_p])
            tok_v[blk] = t_e
            if full:
                ko = self.kvo[blk - 15]
                t_e = P.op('act', lambda e, n=n, pp=pp, ko=ko: e.activation(
                    out=ko[0:n, :], in_=pp[0:n, 0:256], func=AF.Copy), waits=[t_p, t_e, self.sq_rd[blk - 15]])
                kv_tok[blk] = t_e
            self.ps_rd[b] = t_e
        self.ws_release(wi, t_last)
        self.hT_free = t_last
        if dbg.get('mix_stop', 99) <= 2:
            self.hT_free = P.last['pe']
            return
        s_kv = self.slot("d_kvo")
        P.dma('sp', lambda e: e.dma_start(out=d["kp"], in_=self.kvo[0][:, 0:128]), s_kv, waits=[kv_tok[15]])
        P.dma('sp', lambda e: e.dma_start(out=d["vp"], in_=self.kvo[0][:, 128:256]), s_kv, waits=[kv_tok[15]])
        for b_ in range(16):
            P.dma('sp', lambda e, b_=b_: e.dma_start(
                out=d["ks"][b_, 124:128, :], in_=self.kvo[1][4 * b_:4 * b_ + 4, 0:128]), s_kv, waits=[kv_tok[16]])
            P.dma('sp', lambda e, b_=b_: e.dma_start(
                out=d["vs"][b_, 124:128, :], in_=self.kvo[1][4 * b_:4 * b_ + 4, 128:256]), s_kv, waits=[kv_tok[16]])
        self.out_toks.append((s_kv.sem, s_kv.count))
        self.sq_rd = [(s_kv.sem, s_kv.count), (s_kv.sem, s_kv.count)]

        if dbg.get('mix_stop', 99) <= 3:
            self.hT_free = P.last['pe']
            return
        qT2, kT, vtok = self.qT2, self.kT, self.vtok
        attn_tok = {}
        pcnt = [0]
        pT_rd = [self.sg_rd[0], self.sg_rd[0], self.sg_rd[1], self.sg_rd[1]]
        tq_all = lambda ti: [tok_q[(r, ti)] for r in range(4)]
        self.tok_u = tok_u
        self.ssm_y_tok = {}

        def attn_block(n_):
            ti = n_ // 4
            q0 = n_ * 128
            kbs = ([(n_ - 1, 1)] if n_ > 0 else []) + [(n_, 0)]
            pts = {}
            for g in range(2):
                gs = slice(64 * g, 64 * g + 64)
                for (kb, ty) in kbs:
                    sl = pcnt[0] % 4
                    bk = pcnt[0] % 2
                    pcnt[0] += 1
                    sb_ = self.ps[bk]
                    P.op('pe', lambda e, sb_=sb_, ty=ty, g=g: e.matmul(
                        sb_[:, :], self.ident[:], self.BT[:, ty, 4 * g:4 * g + 4, :], start=True, stop=False),
                        waits=[self.ps_rd[bk], self.t_ident] + self.t_tables, signal=False)
                    t_s = P.op('pe', lambda e, sb_=sb_, gs=gs, kb=kb, q0=q0: e.matmul(
                        sb_[:, :], kT[gs, kb * 128:(kb + 1) * 128], qT2[gs, :, q0:q0 + 128], start=False, stop=True),
                        waits=tq_all(ti) + [tok_k[ti], tok_k[kb // 4]])
                    t_e = P.op('act', lambda e, sb_=sb_, sl=sl: e.activation(
                        out=self.pT[sl], in_=sb_[:, :], func=AF.Exp), waits=[t_s, pT_rd[sl]])
                    self.ps_rd[bk] = t_e
                    pts[(g, kb)] = (sl, t_e)
            ob, db = 2, 3
            ops_, dps_ = self.ps[ob], self.ps[db]
            for g in range(2):
                gs = slice(64 * g, 64 * g + 64)
                for i, (kb, ty) in enumerate(kbs):
                    sl, t_e = pts[(g, kb)]
                    P.op('pe', lambda e, ops_=ops_, gs=gs, kb=kb, sl=sl, i=i, nkb=len(kbs): e.matmul(
                        ops_[gs, :], vtok[:, kb, gs], self.pT[sl], start=(i == 0), stop=(i == nkb - 1)),
                        waits=[t_e, tok_v[kb], self.ps_rd[ob]], signal=False)
                for i, (kb, ty) in enumerate(kbs):
                    sl, t_e = pts[(g, kb)]
                    t_d = P.op('pe', lambda e, dps_=dps_, gs=gs, sl=sl, i=i, nkb=len(kbs): e.matmul(
                        dps_[gs, :], self.ones1[:, :], self.pT[sl], start=(i == 0), stop=(i == nkb - 1)),
                        waits=[self.ps_rd[db], self.t_ones1], signal=(i == len(kbs) - 1))
                for (kb, ty) in kbs:
                    pT_rd[pts[(g, kb)][0]] = t_d
            rb = n_ % 2
            rt = self.rt[rb]
            for r_ in range(4):
                t_1 = P.op('act', lambda e, dps_=dps_, rt=rt, r_=r_: e.activation(
                    out=rt[:, 128 * r_:128 * r_ + 128], in_=dps_[:, 128 * r_:128 * r_ + 128], func=AF.Ln,
                    bias=self.ES[:, r_:r_ + 1], scale=1.0), waits=[t_d, self.t_es, self.rt_rd[rb]])
            t_2 = P.op('act', lambda e, rt=rt: e.activation(out=rt[:], in_=rt[:], func=AF.Exp, scale=-1.0), waits=[t_1])
            self.ps_rd[db] = t_1

            def back():
                t_3 = P.op('dve', lambda e, ops_=ops_, rt=rt, q0=q0: e.tensor_tensor(
                    out=qT2[:, :, q0:q0 + 128], in0=ops_[:].rearrange("p (r q) -> p r q", r=4),
                    in1=rt[:].rearrange("p (r q) -> p r q", r=4), op=ALU.mult), waits=[t_2, t_d])
                self.rt_rd[rb] = t_3
                self.ps_rd[ob] = t_3
                attn_tok[n_] = t_3
            return back

        do_ssm = not dbg.get("skip_ssm")
        bi = 0
        ucnt = 0
        t_lastu = None
        for ch in (6, 7, 8, 9):
            wv, t_w, wi = self.ws_load(win[:, :, ch * 128:(ch + 1) * 128], lambda t: t[:, 0:1024].rearrange("p (k n) -> p k n", k=KC))
            for ti in range(len(TT)):
                t_lastu = u_group(ch - 6, ti, wv, t_w, 4 + ucnt % 4)
                ucnt += 1
                if bi < 16:
                    attn_block(bi)()
                    bi += 1
            self.ws_release(wi, t_lastu)
        while bi < 16:
            attn_block(bi)()
            bi += 1
        self.hT_free = t_lastu
        if do_ssm:
            self.ssm_begin()
        if do_ssm:
            self.ssm_pads_bz(0)
            self.ssm_pads_cp(0)
            self.ssm_tables(0)
            self.ssm_tables(1)
            self.ssm_z(0)
        for i_ in range(16):
            if do_ssm and i_ + 2 < 16:
                self.ssm_tables(i_ + 2)
            if do_ssm:
                kt_, pl_ = i_ // 4, i_ % 4
                if pl_ == 2 and kt_ > 0:
                    self.ssm_y(kt_ - 1, (0,), True)
                self.ssm_main(i_)
                if pl_ == 3:
                    self.ssm_sample(kt_)
                if i_ < 15:
                    self.ssm_z(i_ + 1)
                if pl_ == 2 and kt_ < 3:
                    self.ssm_pads_bz(kt_ + 1)
                if pl_ == 3:
                    self.ssm_y(kt_, (3,), False)
                    if kt_ == 3:
                        self.ssm_y(kt_, (2,), False)
                        self.ssm_y(kt_, (1, 0), True)
                if pl_ == 0 and kt_ > 0:
                    self.ssm_y(kt_ - 1, (2,), False)
                if pl_ == 1 and kt_ > 0:
                    self.ssm_y(kt_ - 1, (1,), False)
                if pl_ == 2 and kt_ > 0:
                    self.ssm_pads_cp(kt_)
        if do_ssm:
            t_g = P.op('dve', lambda e: e.memset(self.sgc2[:], 0.0), waits=[self.S['dmp'], self.S['dmc']])
            self.rt_rd = [t_g, t_g]
        if do_ssm:
            self.ssm_glu()
        wck = [self.S['pad_bz_rd'], self.S['pad_cp_rd']] if do_ssm else []
        self.t_ckt = P.dma('pool', lambda e: e.dma_start(out=self.cKT[:], in_=d["cKT"]), self.slot("d_ckt"), waits=wck)
        self.t_cv = P.dma('pool', lambda e: e.dma_start(out=self.cV[:], in_=d["cV"]), self.slot("d_cv"), waits=wck)
        if dbg.get('mix_stop', 99) <= 4:
            self.hT_free = P.last['pe']
            return
        S0 = SEQ
        pTc = [self.pT[0], self.pT[1]]
        pTn = [self.pT[2], self.pT[3]]
        te = {}
        for g in range(2):
            gs = slice(64 * g, 64 * g + 64)
            sc, sn = self.ps[2 * g], self.ps[2 * g + 1]
            P.op('pe', lambda e, sc=sc, g=g: e.matmul(
                sc[:, 0:256], self.ident[:], self.BTsc[:, g, :, :].rearrange("p b r t -> p (b r t)"), start=True, stop=False),
                waits=[self.ps_rd[2 * g], self.t_ident] + self.t_tables, signal=False)
            for b in range(16):
                t_s = P.op('pe', lambda e, sc=sc, gs=gs, b=b: e.matmul(
                    sc[:, 16 * b:16 * b + 16], self.cKT[gs, b, :], qT2[gs, :, S0 + 4 * b:S0 + 4 * b + 4],
                    start=False, stop=(b == 15)),
                    waits=tq_all(4) + [self.t_ckt], signal=(b == 15))
            t_e = P.op('act', lambda e, sc=sc, g=g: e.activation(
                out=pTc[g][:, 0:256], in_=sc[:, 0:256], func=AF.Exp), waits=[t_s, pT_rd[g]])
            self.ps_rd[2 * g] = t_e
            te[(g, 'c')] = t_e
            if dbg.get('sa', 9) < 2:
                continue
            P.op('pe', lambda e, sn=sn, g=g, gs=gs: e.matmul(
                sn[0:64, 0:256], self.ident[gs, 64 - 64 * g:128 - 64 * g], self.BTsn[gs, :, :],
                start=True, stop=(dbg.get('sa2', 9) < 2)),
                waits=[self.ps_rd[2 * g + 1]], signal=False)
            if dbg.get('sa2', 9) < 2:
                continue
            t_s = P.op('pe', lambda e, sn=sn, gs=gs: e.matmul(
                sn[0:64, 0:256], kT[gs, S0:S0 + 64], qT2[gs, :, S0:S0 + 64], start=False, stop=True),
                waits=[tok_k[4]])
            if dbg.get('sa2', 9) < 3:
                continue
            t_e = P.op('act', lambda e, sn=sn, g=g: e.activation(
                out=pTn[g][0:64, 0:256].rearrange("p (b r t) -> p r b t", b=16, r=4),
                in_=sn[0:64, 0:256].rearrange("p (r b t) -> p r b t", r=4, b=16), func=AF.Exp),
                waits=[t_s, pT_rd[2 + g]])
            self.ps_rd[2 * g + 1] = t_e
            te[(g, 'n')] = t_e
        ob, db = 0, 1
        ops_, dps_ = self.ps[ob], self.ps[db]
        if dbg.get('sa', 9) < 3:
            self.hT_free = P.last['pe']
            return
        for g in range(2):
            gs = slice(64 * g, 64 * g + 64)
            P.op('pe', lambda e, gs=gs, g=g: e.matmul(
                ops_[gs, 0:256], vtok[0:64, 16, gs], pTn[g][0:64, 0:256], start=True, stop=False),
                waits=[te[(g, 'n')], te[(g, 'c')], tok_v[16], self.ps_rd[ob], self.t_cv], signal=False)
            for b in range(16):
                P.op('pe', lambda e, gs=gs, g=g, b=b: e.matmul(
                    ops_[gs, 16 * b:16 * b + 16], self.cV[:, b, gs], pTc[g][:, 16 * b:16 * b + 16],
                    start=False, stop=(b == 15)), signal=False)
            P.op('pe', lambda e, gs=gs, g=g: e.matmul(
                dps_[gs, 0:256], self.ones1[0:64, :], pTn[g][0:64, 0:256], start=True, stop=False),
                waits=[self.ps_rd[db]], signal=False)
            t_d = P.op('pe', lambda e, gs=gs, g=g: e.matmul(
                dps_[gs, 0:256], self.ones1[:, :], pTc[g][:, 0:256], start=False, stop=True))
        if dbg.get('sa', 9) < 4:
            self.hT_free = P.last['pe']
            return
        rt = self.rt[0]
        v4 = lambda ap: ap.rearrange("p (b r t) -> p b r t", b=16, r=4)
        t_1 = P.op('dve', lambda e: e.tensor_tensor(
            out=v4(rt[:, 0:256]), in0=v4(dps_[:, 0:256]),
            in1=self.ES[:].unsqueeze(1).unsqueeze(3).to_broadcast([128, 16, 4, 4]), op=ALU.add),
            waits=[t_d, self.t_es, self.rt_rd[0]])
        t_2 = P.op('act', lambda e: e.activation(out=rt[:, 0:256], in_=rt[:, 0:256], func=AF.Ln), waits=[t_1])
        t_2 = P.op('act', lambda e: e.activation(out=rt[:, 0:256], in_=rt[:, 0:256], func=AF.Exp, scale=-1.0), waits=[t_2])
        t_3 = P.op('dve', lambda e: e.tensor_tensor(
            out=qT2[:, :, S0:S0 + 64].rearrange("p r (b t) -> p b r t", t=4), in0=v4(ops_[:, 0:256]),
            in1=v4(rt[:, 0:256]), op=ALU.mult), waits=[t_2, t_d])
        self.rt_rd[0] = t_3
        self.ps_rd[ob] = t_3
        self.ps_rd[db] = t_1
        attn_tok[16] = t_3
        self.attn_tok = attn_tok
        self.tok_u = tok_u

        if dbg.get('mix_stop', 99) <= 5:
            self.hT_free = P.last['pe']
            return
        ssm_tok = self.ssm_tok if do_ssm else None

        if dbg.get("dump_attn"):
            s_dbg = self.slot("d_dbg")
            tk = P.dma('sp', lambda e: e.dma_start(out=d["dbg_attn"], in_=qT2), s_dbg,
                       waits=[attn_tok[i] for i in range(17)])
            self.out_toks.append(tk)
            self.dbg_tok = tk

        nk = 4 if ssm_tok is None else 8
        ycnt = 0
        for c in range(KC):
            wv, t_w, wi = self.ws_load(d["wout"][c], lambda t: t[:, 0:1024].rearrange("p (k n) -> p k n", k=8))
            for ti, (t0, n) in enumerate(TT):
                yb = 4 + (ycnt % 2)
                ycnt += 1
                yps = self.ps[yb]
                blks = range(ti * 4, ti * 4 + 4) if ti < 4 else [16]
                for i in range(nk):
                    rhs = qT2[:, i, t0:t0 + n] if i < 4 else self.ssmT[:, i - 4, t0:t0 + n]
                    w_ = ([t_w, self.ps_rd[yb]] + [attn_tok[b_] for b_ in blks]) if i == 0 else []
                    if i == 4:
                        w_ = w_ + [ssm_tok[(k_, ti)] for k_ in range(4)]
                    t_y = P.op('pe', lambda e, i=i, n=n, yps=yps, wv=wv, rhs=rhs: e.matmul(
                        yps[:, :n], wv[:, i, :], rhs, start=(i == 0), stop=(i == nk - 1)),
                        waits=w_, signal=(i == nk - 1))
                t_x = P.op('dve', lambda e, c=c, t0=t0, n=n, yps=yps: e.tensor_tensor(
                    out=xT[:, c, t0:t0 + n], in0=yps[:, :n], in1=xT[:, c, t0:t0 + n], op=ALU.add),
                    waits=[t_y, self.tok_x[(c, ti)]])
                self.tok_x[(c, ti)] = t_x
                self.ps_rd[yb] = t_x
                if c == KC - 1:
                    self.norm_tile(2, ti)
            self.ws_release(wi, t_y)
        self.normed_upto = {2: len(TT)}
        self.aT_rd = t_y
        self.sg_rd = [t_y, t_y]
        self.mix_end_tok = t_y


    def ssm_io(self):
        self.din("ssm_pm", [128, 3, 16])
        self.din("ssm_cm", [128, 3, 4, 64])
        self.din("c_pm", [128, 2, 16, 16])
        self.din("b_pm", [128, 2, 16, 16])
        self.din("b_cm", [128, 2, 4, 64])
        self.din("dskip", [128, 4])
        self.din("bglu", [128, 4])
        self.din("wglu", [128, 4, 512])
        self.din("cst_m2", [128, 4, 8])
        self.din("cst_m3", [128, 4, 2])
        self.din("cst_eye", [128, 128])
        self.din("x0", [128, 16, 2, 16])
        self.dout("st_p", [128, 16, 2])
        self.dout("st_s", [128, 16, 2, 16])

    def ssm_alloc(self):
        sb = self.sb
        self.pm = {}
        for nm in ["L1r", "L1i", "L2r", "L2i", "L3r", "L3i", "L4r", "L4i", "cr", "ci", "rho4", "t0", "t1", "t2", "t3",
                   "t4", "t5", "turns"]:
            self.pm[nm] = sb("pm_" + nm, [128, 16], F32)
        self.pm_in = sb("pm_in", [128, 3, 16], F32)
        self.pm_i = sb("pm_i", [128, 16], I32)
        self.phi2 = sb("pm_phi2", [128, 16], I32)
        self.q30 = sb("q30", [128, 1], I32)
        self.CpC = sb("CpC", [128, 16, 5, 2, 16], BF16)
        self.jota = sb("jota", [128, 512], I32)
        self.m2 = sb("m2", [128, 4, 8], F32)
        self.m3 = sb("m3", [128, 4, 2], F32)
        self.eye = sb("eye", [128, 128], F32)
        self.dsk = sb("dsk", [128, 4], F32)
        self.bgl = sb("bgl", [128, 4], F32)
        self.x0 = sb("x0_s", [128, 16, 2, 16], F32)
        self.stp = sb("stp_s", [128, 16, 2], F32)
        self.sgc = sb("sgc", [128, 1], F32)
        self.sgc2 = sb("sgc2", [128, 1], F32)
        self.XbS = sb("XbS", [128, 2, 4, 2, 16], BF16)
        self.BzC = self.cKT[:].rearrange("p b n -> p (b n)").rearrange("p (k s r q) -> p k s r q", k=4, s=4, r=2)
        self.Kin = self.cV[:].rearrange("p b n -> p (b n)").rearrange("p (k t n) -> p k t n", k=4, t=4)

    def ssm_setup_gen(self, which):
        P, d = self.P, self.dram
        pm = self.pm
        prev = [None]
        self.kin_rd = getattr(self, 'kin_rd', [None] * 4)
        TWO_PI = 2.0 * np.pi
        hf = self.hT[:].rearrange("p k t -> p (k t)").bitcast(F32)

        def V(fn, extra=()):
            prev[0] = P.op('dve', fn, waits=[prev[0]] + list(extra))

        def A(fn, extra=()):
            prev[0] = P.op('act', fn, waits=[prev[0]] + list(extra))

        def lam(aR, aI, ldt, T, is_pm):
            yield A(lambda e: e.activation(out=ldt, in_=ldt, func=AF.Exp))
            yield V(lambda e: e.tensor_tensor(out=T['xr'], in0=aR, in1=ldt, op=ALU.mult))
            yield V(lambda e: e.tensor_tensor(out=T['xi'], in0=aI, in1=ldt, op=ALU.mult))
            yield A(lambda e: e.activation(out=T['mag'], in_=T['xr'], func=AF.Exp))
            yield V(lambda e: e.tensor_scalar(out=T['xi'], in0=T['xi'], scalar1=1.0 / TWO_PI, scalar2=None, op0=ALU.mult))
            if is_pm:
                yield V(lambda e: e.tensor_copy(out=pm['turns'][:], in_=T['xi']))
                yield A(lambda e: e.activation(out=pm['rho4'][:], in_=T['xr'], func=AF.Exp, scale=4.0))
            yield V(lambda e: e.tensor_copy(out=T['ni'], in_=T['xi']))
            yield V(lambda e: e.tensor_copy(out=T['nf'], in_=T['ni']))
            yield V(lambda e: e.tensor_tensor(out=T['xi'], in0=T['xi'], in1=T['nf'], op=ALU.subtract))
            yield A(lambda e: e.activation(out=T['s1'], in_=T['xi'], func=AF.Sin, scale=TWO_PI))
            yield A(lambda e: e.activation(out=T['sh'], in_=T['xi'], func=AF.Sin, scale=float(np.pi)))
            yield V(lambda e: e.tensor_tensor(out=T['sh'], in0=T['sh'], in1=T['sh'], op=ALU.mult))
            yield V(lambda e: e.tensor_scalar(out=T['sh'], in0=T['sh'], scalar1=-2.0, scalar2=1.0, op0=ALU.mult, op1=ALU.add))
            yield V(lambda e: e.tensor_tensor(out=T['L1r'], in0=T['mag'], in1=T['sh'], op=ALU.mult))
            yield V(lambda e: e.tensor_tensor(out=T['L1i'], in0=T['mag'], in1=T['s1'], op=ALU.mult))
            yield V(lambda e: e.tensor_scalar(out=T['mag'], in0=T['L1r'], scalar1=-1.0, scalar2=None, op0=ALU.add))
            yield V(lambda e: e.tensor_tensor(out=T['xr'], in0=aR, in1=aR, op=ALU.mult))
            yield V(lambda e: e.tensor_tensor(out=T['xi'], in0=aI, in1=aI, op=ALU.mult))
            yield V(lambda e: e.tensor_tensor(out=T['xr'], in0=T['xr'], in1=T['xi'], op=ALU.add))
            yield V(lambda e: e.reciprocal(out=T['xr'], in_=T['xr']))
            yield V(lambda e: e.tensor_tensor(out=T['xi'], in0=T['mag'], in1=aR, op=ALU.mult))
            yield V(lambda e: e.tensor_tensor(out=T['s1'], in0=T['L1i'], in1=aI, op=ALU.mult))
            yield V(lambda e: e.tensor_tensor(out=T['xi'], in0=T['xi'], in1=T['s1'], op=ALU.add))
            yield V(lambda e: e.tensor_tensor(out=T['cr'], in0=T['xi'], in1=T['xr'], op=ALU.mult))
            yield V(lambda e: e.tensor_tensor(out=T['xi'], in0=T['L1i'], in1=aR, op=ALU.mult))
            yield V(lambda e: e.tensor_tensor(out=T['s1'], in0=T['mag'], in1=aI, op=ALU.mult))
            yield V(lambda e: e.tensor_tensor(out=T['xi'], in0=T['xi'], in1=T['s1'], op=ALU.subtract))
            yield V(lambda e: e.tensor_tensor(out=T['ci'], in0=T['xi'], in1=T['xr'], op=ALU.mult))

        def cmul(o_r, o_i, a_r, a_i, b_r, b_i, t1, t2):
            yield V(lambda e: e.tensor_tensor(out=t1, in0=a_r, in1=b_r, op=ALU.mult))
            yield V(lambda e: e.tensor_tensor(out=t2, in0=a_i, in1=b_i, op=ALU.mult))
            yield V(lambda e: e.tensor_tensor(out=o_r, in0=t1, in1=t2, op=ALU.subtract))
            yield V(lambda e: e.tensor_tensor(out=t1, in0=a_r, in1=b_i, op=ALU.mult))
            yield V(lambda e: e.tensor_tensor(out=t2, in0=a_i, in1=b_r, op=ALU.mult))
            yield V(lambda e: e.tensor_tensor(out=o_i, in0=t1, in1=t2, op=ALU.add))

        if which == 1:
            prev[0] = None
            st = self.stage[:].rearrange("p a h q -> p (a h q)")
            s = self.slot("d_ssm_in")
            for (dst, src_) in [(self.pm_in[:], d["ssm_pm"]), (self.m2[:], d["cst_m2"]), (self.m3[:], d["cst_m3"]),
                                (self.eye[:], d["cst_eye"]), (self.dsk[:], d["dskip"]), (self.bgl[:], d["bglu"]),
                                (self.x0[:], d["x0"])]:
                t_in = P.dma('sp', lambda e, dst=dst, src_=src_: e.dma_start(out=dst, in_=src_), s)
            self.t_x0 = t_in
            cpm = st[:, 0:512].rearrange("p (r a c) -> p r a c", r=2, a=16)
            t_in2 = P.dma('sp', lambda e: e.dma_start(out=cpm, in_=d["c_pm"]), self.slot("d_ssm_in2"), waits=self.t_tables)
            for _ in range(8):
                yield
            yield V(lambda e: e.memset(self.sgc[:], TWO_PI / 2.0 ** 32), extra=[t_in, t_in2] + self.t_tables)
            yield V(lambda e: e.memset(self.sgc2[:], TWO_PI / 2.0 ** 33))
            aR, aI, ldt = self.pm_in[:, 0, :], self.pm_in[:, 1, :], self.pm_in[:, 2, :]
            T = {'xr': pm['t0'][:], 'xi': pm['t1'][:], 'mag': pm['t2'][:], 'ni': self.pm_i[:], 'nf': pm['t3'][:],
                 's1': pm['t4'][:], 'sh': pm['t5'][:], 'L1r': pm['L1r'][:], 'L1i': pm['L1i'][:], 'cr': pm['cr'][:], 'ci': pm['ci'][:]}
            yield from lam(aR, aI, ldt, T, True)
            t1_, t2_ = pm['t0'][:], pm['t1'][:]
            yield from cmul(pm['L2r'][:], pm['L2i'][:], pm['L1r'][:], pm['L1i'][:], pm['L1r'][:], pm['L1i'][:], t1_, t2_)
            yield from cmul(pm['L3r'][:], pm['L3i'][:], pm['L2r'][:], pm['L2i'][:], pm['L1r'][:], pm['L1i'][:], t1_, t2_)
            yield from cmul(pm['L4r'][:], pm['L4i'][:], pm['L2r'][:], pm['L2i'][:], pm['L2r'][:], pm['L2i'][:], t1_, t2_)
            tu = pm['turns'][:]
            yield V(lambda e: e.tensor_scalar(out=tu, in0=tu, scalar1=4.0, scalar2=None, op0=ALU.mult))
            yield V(lambda e: e.tensor_copy(out=self.pm_i[:], in_=tu))
            yield V(lambda e: e.tensor_copy(out=pm['t3'][:], in_=self.pm_i[:]))
            yield V(lambda e: e.tensor_tensor(out=tu, in0=tu, in1=pm['t3'][:], op=ALU.subtract))
            yield V(lambda e: e.tensor_scalar(out=tu, in0=tu, scalar1=4294967040.0, scalar2=None, op0=ALU.mult))
            yield V(lambda e: e.tensor_copy(out=self.phi2[:], in_=tu))
            cR, cI = cpm[:, 0, :, :], cpm[:, 1, :, :]
            w1 = st[:, 512:768].rearrange("p (a c) -> p a c", a=16)
            w2 = st[:, 768:1024].rearrange("p (a c) -> p a c", a=16)
            bc = lambda ap: ap.unsqueeze(2).to_broadcast([128, 16, 16])
            yield V(lambda e: e.tensor_copy(out=self.CpC[:, :, 0, 0, :], in_=cR))
            yield V(lambda e: e.tensor_scalar(out=self.CpC[:, :, 0, 1, :], in0=cI, scalar1=-1.0, scalar2=None, op0=ALU.mult))
            for k in range(1, 5):
                Lr, Li = pm['L%dr' % k][:], pm['L%di' % k][:]
                yield V(lambda e, Lr=Lr: e.tensor_tensor(out=w1, in0=cR, in1=bc(Lr), op=ALU.mult))
                yield V(lambda e, Li=Li: e.tensor_tensor(out=w2, in0=cI, in1=bc(Li), op=ALU.mult))
                yield V(lambda e, k=k: e.tensor_tensor(out=self.CpC[:, :, k, 0, :], in0=w1, in1=w2, op=ALU.subtract))
                yield V(lambda e, Li=Li: e.tensor_tensor(out=w1, in0=cR, in1=bc(Li), op=ALU.mult))
                yield V(lambda e, Lr=Lr: e.tensor_tensor(out=w2, in0=cI, in1=bc(Lr), op=ALU.mult))
                yield V(lambda e, k=k: e.scalar_tensor_tensor(out=self.CpC[:, :, k, 1, :], in0=w1, scalar=-1.0, in1=w2,
                                                              op0=ALU.mult, op1=ALU.subtract))
            self.t_pm_done = prev[0]
            for hk in range(2):
                cm_in = st[:, 0:384].rearrange("p (q k n) -> p q k n", q=3, k=2)
                bcm = st[:, 384:640].rearrange("p (r k n) -> p r k n", r=2, k=2)
                ct = [st[:, 640 + 128 * i:768 + 128 * i].rearrange("p (k n) -> p k n", k=2) for i in range(10)]
                cti = st[:, 1920:2048].bitcast(I32).rearrange("p (k n) -> p k n", k=2)
                sl_ = self.slot("d_ssm_cm%d" % hk)
                P.dma('sp', lambda e, hk=hk, cm_in=cm_in: e.dma_start(out=cm_in, in_=d["ssm_cm"][:, :, 2 * hk:2 * hk + 2, :]), sl_,
                      waits=[prev[0]])
                t_l = P.dma('sp', lambda e, hk=hk, bcm=bcm: e.dma_start(out=bcm, in_=d["b_cm"][:, :, 2 * hk:2 * hk + 2, :]), sl_,
                            waits=[prev[0]])
                prev[0] = t_l
                for _ in range(8):
                    yield
                aRc, aIc, ldc = cm_in[:, 0, :, :], cm_in[:, 1, :, :], cm_in[:, 2, :, :]
                Tc = {'xr': ct[0], 'xi': ct[1], 'mag': ct[2], 'ni': cti, 'nf': ct[3], 's1': ct[4], 'sh': ct[5],
                      'L1r': ct[6], 'L1i': ct[7], 'cr': ct[8], 'ci': ct[9]}
                yield from lam(aRc, aIc, ldc, Tc, False)
                bRc, bIc = bcm[:, 0, :, :], bcm[:, 1, :, :]
                c_r, c_i, u1, u2, u3, u4 = ct[0], ct[1], ct[2], ct[3], ct[4], ct[5]
                yield from cmul(c_r, c_i, ct[8], ct[9], bRc, bIc, u1, u2)
                for s_ in (3, 2, 1, 0):
                    yield V(lambda e, s_=s_, hk=hk, c_r=c_r: e.tensor_copy(out=self.BzC[:, 2 * hk:2 * hk + 2, s_, 0, :], in_=c_r))
                    yield V(lambda e, s_=s_, hk=hk, c_i=c_i: e.tensor_copy(out=self.BzC[:, 2 * hk:2 * hk + 2, s_, 1, :], in_=c_i))
                    if s_ > 0:
                        yield from cmul(u3, u4, ct[6], ct[7], c_r, c_i, u1, u2)
                        yield V(lambda e, c_r=c_r, u3=u3: e.tensor_copy(out=c_r, in_=u3))
                        yield V(lambda e, c_i=c_i, u4=u4: e.tensor_copy(out=c_i, in_=u4))
            self.t_cm_done = prev[0]
            yield
            return
        prev[0] = self.hT_free
        bpm = hf[:, 512:1024].rearrange("p (r a c) -> p r a c", r=2, a=16)
        t_in2 = P.dma('sp', lambda e: e.dma_start(out=bpm, in_=d["b_pm"]), self.slot("d_ssm_in3"), waits=[self.hT_free])
        for _ in range(9):
            yield
        yield V(lambda e: e.memset(self.q30[:], 1 << 30), extra=[t_in2, self.t_pm_done])
        w = [hf[:, 1024 + 256 * i:1280 + 256 * i].rearrange("p (a c) -> p a c", a=16) for i in range(4)]
        w1, w2, w3, w4 = w
        bc = lambda ap: ap.unsqueeze(2).to_broadcast([128, 16, 16])
        bR, bI = bpm[:, 0, :, :], bpm[:, 1, :, :]
        cur_r = hf[:, 2048:2304].rearrange("p (a c) -> p a c", a=16)
        cur_i = hf[:, 2304:2560].rearrange("p (a c) -> p a c", a=16)
        yield from cmul(cur_r, cur_i, bc(pm['cr'][:]), bc(pm['ci'][:]), bR, bI, w1, w2)
        BLc = hf[:, 2560:3584].bitcast(BF16).rearrange("p (a t r c) -> p a t r c", a=16, t=4, r=2)
        for tau in range(4):
            yield V(lambda e, tau=tau: e.tensor_copy(out=BLc[:, :, tau, 0, :], in_=cur_r))
            yield V(lambda e, tau=tau: e.tensor_copy(out=BLc[:, :, tau, 1, :], in_=cur_i))
            if tau < 3:
                yield from cmul(w3, w4, bc(pm['L1r'][:]), bc(pm['L1i'][:]), cur_r, cur_i, w1, w2)
                yield V(lambda e: e.tensor_copy(out=cur_r, in_=w3))
                yield V(lambda e: e.tensor_copy(out=cur_i, in_=w4))
        BLpads = [hf[:, 3584 + 512 * i:4096 + 512 * i].bitcast(BF16).rearrange("p (l r g c) -> p l r g c", l=4, r=2, g=8)
                  for i in range(2)]
        C0pads = [hf[:, 4608 + 512 * i:5120 + 512 * i].bitcast(BF16).rearrange("p (l r g c) -> p l r g c", l=4, r=2, g=8)
                  for i in range(2)]
        m2b = self.m2[:].unsqueeze(3).to_broadcast([128, 4, 8, 16])
        kb_ = 7
        pad_rd = [None, None]
        c0_rd = [None, None]
        cnt = 0
        pending = None
        for kt in range(4):
            C0pad = C0pads[kt % 2]
            for r in range(2):
                yield V(lambda e, kt=kt, r=r, C0pad=C0pad: e.tensor_tensor(
                    out=C0pad[:, :, r, :, :], in0=self.CpC[:, 4 * kt:4 * kt + 4, 0, r, :].unsqueeze(2).to_broadcast([128, 4, 8, 16]),
                    in1=m2b, op=ALU.mult), extra=[c0_rd[kt % 2]])
            for tau in range(4):
                BLpad = BLpads[cnt % 2]
                for r in range(2):
                    yield V(lambda e, kt=kt, r=r, tau=tau, BLpad=BLpad: e.tensor_tensor(
                        out=BLpad[:, :, r, :, :], in0=BLc[:, 4 * kt:4 * kt + 4, tau, r, :].unsqueeze(2).to_broadcast([128, 4, 8, 16]),
                        in1=m2b, op=ALU.mult), extra=[pad_rd[cnt % 2]])
                kb_ = 6 + cnt % 2
                kps = self.ps[kb_]
                col = 0
                t_mm = None
                for i, (pl, r) in enumerate([(pl, r) for pl in range(4) for r in range(2)]):
                    t_mm = P.op('pe', lambda e, pl=pl, r=r, i=i, kps=kps, col=col, BLpad=BLpad, C0pad=C0pad: e.matmul(
                        kps[:, col:col + 128], BLpad[:, pl, r, :, :].rearrange("p g c -> p (g c)"),
                        C0pad[:, pl, r, :, :].rearrange("p g c -> p (g c)"), start=(i == 0), stop=(i == 7)),
                        waits=[prev[0], self.ps_rd[kb_]] if i == 0 else [], signal=(i == 7))
                pad_rd[cnt % 2] = t_mm
                c0_rd[kt % 2] = t_mm
                if pending is not None:
                    yield self._kin_evac(pending, prev)
                pending = (kt, tau, col, t_mm, kb_)
                cnt += 1
                yield
        yield self._kin_evac(pending, prev)
        self.t_ssm_setup = prev[0]
        yield

    def _kin_evac(self, pending, prev):
        P = self.P
        kt, tau, col, t_mm, slot = pending
        kps = self.ps[slot]
        if tau == 0:
            t = P.op('dve', lambda e: e.scalar_tensor_tensor(
                out=self.Kin[:, kt, 0, :], in0=self.eye[:], scalar=self.dsk[:, kt:kt + 1], in1=kps[:, col:col + 128],
                op0=ALU.mult, op1=ALU.add), waits=[t_mm, prev[0]])
        else:
            t = P.op('dve', lambda e: e.tensor_copy(out=self.Kin[:, kt, tau, :], in_=kps[:, col:col + 128]), waits=[t_mm, prev[0]])
        prev[0] = t
        self.ps_rd[slot] = t
        return None

    def bg_pump(self, n):
        g = getattr(self, "bg", None)
        if g is None:
            return
        for _ in range(n):
            try:
                next(g)
            except StopIteration:
                self.bg = None
                return

    def ssm_begin(self):
        P = self.P
        N = self.WS_N
        i1, i2 = self.ws_next % N, (self.ws_next + 1) % N
        self.ws_next += 2
        self.cp_slots = (i1, i2)
        i3 = self.ws_next % N
        self.ws_next += 1
        self.tmp_slot = i3
        self.pA = self.ws[i3][:, 0:1024].bitcast(F32)
        self.pB = self.ws[i3][:, 1024:2048].bitcast(F32)
        self.t_tmp_free = self.ws_free[i3]
        self.CpPad = [self.ws[i][:, 0:2048].rearrange("p (l k r g c) -> p l k r g c", l=2, k=4, r=2, g=8) for i in (i1, i2)]
        stf = self.stage[:].rearrange("p a h q -> p (a h q)").bitcast(BF16)
        self.BzPad = stf.rearrange("p (l s r g q) -> p l s r g q", l=4, s=4, r=2, g=2)
        hb = self.hT[:].rearrange("p k t -> p (k t)")
        f = lambda a: hb[:, a:a + 1024].bitcast(F32)
        self.tabC = [f(0), f(2048), self.rt[0][:]]
        self.tabS = [f(1024), f(3072), self.rt[1][:]]
        self.ph = hb[:, 4096:5120].bitcast(I32)
        self.ph2 = hb[:, 5120:6144].bitcast(I32)
        self.ta, self.tb = f(6144), f(7168)
        self.Mb = [(f(8192), f(9216)), (f(10240), f(11264))]
        xmain = hb[:, 12288:12288 + 4112].rearrange("p (l r n) -> p l r n", l=4, r=2)
        spare = self.aT[:, 9:11, :].rearrange("p a t -> p (a t)")[:, 2176:4224]
        extra = [spare[:, 0:514], spare[:, 514:1028], spare[:, 1028:1542], hb[:, 5120:5634]]
        self.XbR = [[xmain[:, s, 0, :], xmain[:, s, 1, :]] for s in range(4)] + [[extra[0], extra[1]], [extra[2], extra[3]]]
        self.glu_tmp = hb[:, 0:4096].bitcast(F32).rearrange("p (o n) -> p o n", o=4)
        w0 = [self.hT_free, self.t_ssm_setup, self.t_cm_done]
        t_j = P.op('pool', lambda e: e.iota(self.jota[:], pattern=[[1, 512]], base=0, channel_multiplier=0))
        tz = []
        for k, i in enumerate((i1, i2)):
            tz.append(P.op('pool', lambda e, i=i: e.memset(self.ws[i][:, 0:2048], 0.0), waits=[self.ws_free[i]] + w0))
        for s_ in range(6):
            for r_ in range(2):
                tz.append(P.op('pool', lambda e, s_=s_, r_=r_: e.memset(self.XbR[s_][r_][:, 0:2], 0.0), waits=w0 + [self.aT_rd]))
        self.S = dict(t_j=t_j, tz=tz, ph_rd=None, dve=None, pool=tz[-1], dm_done=[[None], [None]], dm_tok={}, tab={}, dmc=None, dmp=None,
                      pad_bz_rd=None, pad_cp_rd=None, tz_tok={}, slot_rd=[None] * 6, xbs_rd=[None, None], xb_rd=None, zs_rd=None, gel=None, pend=[], xf_rd=None)
        self.ssm_tok = {}

    def ssm_tables(self, pr):
        P, S = self.P, self.S
        tb = pr % 3
        C, Sn = self.tabC[tb], self.tabS[tb]
        w0 = [self.hT_free, self.t_ssm_setup, self.t_cm_done]
        free_t = list(S['dm_tok'].get(pr - 3, [])) + ([self.rt_rd[0], self.rt_rd[1]] if tb == 2 else [])
        t_p1 = P.op('pool', lambda e, pr=pr: e.tensor_tensor(
            out=self.ph[:], in0=self.jota[:], in1=self.phi2[:, pr:pr + 1].to_broadcast([128, 512]), op=ALU.mult),
            waits=w0 + [S['t_j'], S['ph_rd'], S['pool']])
        S['pool'] = t_p1
        t_s = P.op('act', lambda e, Sn=Sn: e.activation(out=Sn, in_=self.ph[:], func=AF.Sin, scale=self.sgc[:, 0:1]),
                   waits=[t_p1] + free_t)
        t_c = P.op('act', lambda e, C=C: e.activation(out=C, in_=self.ph[:], func=AF.Sin, scale=self.sgc2[:, 0:1]),
                   waits=[t_p1] + free_t)
        t_c = P.op('act', lambda e, C=C: e.activation(out=C, in_=C, func=AF.Square), waits=[t_c])
        t_c = P.op('act', lambda e, C=C: e.activation(out=C, in_=C, func=AF.Copy, scale=-2.0, bias=1.0), waits=[t_c])
        S['ph_rd'] = t_c
        S['tab'][pr] = (t_s, t_c)

    def ssm_pads_bz(self, kt):
        P, S = self.P, self.S
        w0 = [self.hT_free, self.t_ssm_setup, self.t_cm_done]
        tb_ = None
        for pl_ in range(4):
            for gg in range(2):
                tb_ = P.op('act', lambda e, pl_=pl_, gg=gg, kt=kt: e.activation(
                    out=self.BzPad[:, pl_, :, :, gg, :], in_=self.BzC[:, kt, :, :, :], func=AF.Copy,
                    scale=self.m3[:, pl_, gg:gg + 1]), waits=w0 + [S['pad_bz_rd']] + self.t_tables)
        S['t_bz'] = tb_

    def ssm_pads_cp(self, kt):
        P, S = self.P, self.S
        w0 = [self.hT_free, self.t_ssm_setup, self.t_cm_done]
        tc_ = []
        for hh in range(2):
            for gg in range(2):
                for r in range(2):
                    for pq in range(2):
                        tc_.append(P.op('pool', lambda e, hh=hh, gg=gg, r=r, pq=pq, kt=kt: e.tensor_copy(
                            out=self.CpPad[hh][64 * gg:64 * gg + 64, pq, :, r, 2 * (2 * hh + pq) + gg, :],
                            in_=self.CpC[64 * gg:64 * gg + 64, 4 * kt + 2 * hh + pq, 1:5, r, :]),
                            waits=w0 + S['tz'] + [S['pad_cp_rd'], S['pool']]))
        S['t_cp'] = tc_
        S['pool'] = tc_[-1]

    def ssm_z(self, pr):
        P, S = self.P, self.S
        kt, pl = pr // 4, pr % 4
        uT = self.uT
        ZB = (4, 5)
        zps = [self.ps[ZB[0]], self.ps[ZB[1]]]
        tz_ = None
        for r in range(2):
            for s_ in range(4):
                tz_ = P.op('pe', lambda e, r=r, s_=s_, pl=pl, kt=kt: e.matmul(
                    zps[r][:, :], self.BzPad[:, pl, s_, r, :, :].rearrange("p g q -> p (g q)"), uT[:, kt, s_:SEQ:4],
                    start=(s_ == 0), stop=(s_ == 3)),
                    waits=([S['t_bz'], self.ps_rd[ZB[r]]] + [self.tok_u[(kt, ti)] for ti in range(5)]) if s_ == 0 else [],
                    signal=(s_ == 3))
        zs = self.ps[7]
        tzs = None
        for r in range(2):
            c0 = 32 * pl + 16 * r
            for s_ in range(4):
                tzs = P.op('pe', lambda e, r=r, s_=s_, pl=pl, kt=kt, c0=c0: e.matmul(
                    zs[:, c0:c0 + 16], self.BzPad[:, pl, s_, r, :, :].rearrange("p g q -> p (g q)"),
                    uT[:, kt, SEQ + s_:NT:4], start=(s_ == 0), stop=(s_ == 3)),
                    waits=[self.ps_rd[7], S['zs_rd']] if (s_ == 0 and r == 0) else [], signal=(s_ == 3))
        if pl == 3:
            S['pad_bz_rd'] = tzs
        S['tz_tok'][pr] = (tz_, tzs)

    def ssm_main(self, pr, mid=None):
        P = self.P
        S = self.S
        kt, pl = pr // 4, pr % 4
        w0 = [self.hT_free, self.t_ssm_setup, self.t_cm_done]
        ZB = (4, 5)
        mul, add, sub = ALU.mult, ALU.add, ALU.subtract
        zps = [self.ps[ZB[0]], self.ps[ZB[1]]]
        tz_, tzs = S['tz_tok'][pr]
        if pl == 0:
            S['t_x0c'] = P.op('act', lambda e, kt=kt: e.activation(
                out=self.XbS[:, kt % 2, :, :, :], in_=self.x0[:, 4 * kt:4 * kt + 4, :, :], func=AF.Copy),
                waits=w0 + [S['xbs_rd'][kt % 2], self.t_x0])
        ta = self.ta
        tb = pr % 2
        C, Sn = self.tabC[pr % 3], self.tabS[pr % 3]
        t_s, t_c = S['tab'][pr]
        Mre, Mim = self.Mb[tb]
        ta, tb2 = self.ta, self.tb
        rho = self.pm['rho4'][:, pr:pr + 1].to_broadcast([128, 512])
        zr, zi = zps[0], zps[1]
        free_m = S['dm_done'][tb]
        o1 = P.op('dve', lambda e: e.tensor_tensor(out=Mre, in0=zr[:, :], in1=C, op=mul), waits=w0 + [tz_, t_c] + free_m)
        o2 = P.op('dve', lambda e: e.tensor_tensor(out=tb2, in0=zi[:, :], in1=Sn, op=mul), waits=[t_s, S['dve']])
        o3 = P.op('dve', lambda e: e.tensor_tensor(out=Mim, in0=zi[:, :], in1=C, op=mul))
        o4 = P.op('dve', lambda e: e.tensor_tensor(out=ta, in0=zr[:, :], in1=Sn, op=mul), waits=[S['dve']])
        self.ps_rd[ZB[0]] = o4
        self.ps_rd[ZB[1]] = o4
        o5 = P.op('dve', lambda e: e.tensor_tensor(out=Mre, in0=Mre, in1=tb2, op=add), waits=[o1, o2])
        o6 = P.op('dve', lambda e: e.tensor_tensor(out=Mim, in0=Mim, in1=ta, op=sub), waits=[o3, o4])
        Wre, Wim, pa, pb = self.ps[0][:, :], self.ps[1][:, :], self.ps[2][:, :], self.ps[3][:, :]
        o7 = P.op('dve', lambda e: e.tensor_tensor_scan(out=Wre, data0=rho, data1=Mre, initial=0.0, op0=mul, op1=add),
                  waits=[o5, self.ps_rd[0], S['dve']])
        o8 = P.op('dve', lambda e: e.tensor_tensor_scan(out=Wim, data0=rho, data1=Mim, initial=0.0, op0=mul, op1=add),
                  waits=[o6, self.ps_rd[1]])
        S['dve'] = o8
        if pr + 1 < 16 and (pr + 1) not in S['tab']:
            self.ssm_tables(pr + 1)
        slot = pr % 6
        xre, xim = self.XbR[slot]
        d1 = P.op('dve', lambda e: e.tensor_tensor(out=pa, in0=Wre, in1=C, op=mul), waits=[o7, self.ps_rd[2]])
        d2 = P.op('dve', lambda e: e.tensor_tensor(out=tb2, in0=Wim, in1=Sn, op=mul), waits=[o8])
        q3 = P.op('dve', lambda e: e.tensor_tensor(out=xre[:, 2:513], in0=pa[:, 0:511], in1=tb2[:, 0:511], op=sub),
                  waits=[d1, d2, S['slot_rd'][slot]] + S['tz'])
        q4 = P.op('dve', lambda e, pr=pr: e.tensor_tensor(out=self.stp[:, pr, 0:1], in0=pa[:, 511:512], in1=tb2[:, 511:512], op=sub),
                  waits=[d1, d2])
        d5 = P.op('dve', lambda e: e.tensor_tensor(out=pb, in0=Wim, in1=C, op=mul), waits=[o8, self.ps_rd[3]])
        d6 = P.op('dve', lambda e: e.tensor_tensor(out=ta, in0=Wre, in1=Sn, op=mul), waits=[o7, q4])
        q7 = P.op('dve', lambda e: e.tensor_tensor(out=xim[:, 2:513], in0=pb[:, 0:511], in1=ta[:, 0:511], op=add),
                  waits=[d5, d6, S['slot_rd'][slot]] + S['tz'])
        q8 = P.op('dve', lambda e, pr=pr: e.tensor_tensor(out=self.stp[:, pr, 1:2], in0=pb[:, 511:512], in1=ta[:, 511:512], op=add),
                  waits=[d5, d6])
        for b_ in range(4):
            self.ps_rd[b_] = q8
        S['dmc'] = q8
        S['dm_done'][tb] = [q8]
        S['dm_tok'][pr] = [q8]
        S['xf_rd'] = [q4, q8]
        S['dve'] = q8
        S['pend'].append((q3, q7, S['t_x0c']))

    def ssm_sample(self, kt):
        P, S = self.P, self.S
        mul, add, sub = ALU.mult, ALU.add, ALU.subtract
        ta = self.ta
        zs = self.ps[7]
        tzs = S['tz_tok'][4 * kt + 3][1]
        zv = zs[:, 0:128].rearrange("p (l r b) -> p l r b", l=4, r=2)
        x0r, x0i = self.x0[:, 4 * kt:4 * kt + 4, 0, :], self.x0[:, 4 * kt:4 * kt + 4, 1, :]
        bcl = lambda ap: ap[:, 4 * kt:4 * kt + 4].unsqueeze(2).to_broadcast([128, 4, 16])
        L4r, L4i = bcl(self.pm['L4r'][:]), bcl(self.pm['L4i'][:])
        q = [ta[:, 64 * i:64 * i + 64].rearrange("p (l b) -> p l b", l=4) for i in range(4)]
        d = [S['dve']]

        def V(fn, extra=()):
            d[0] = P.op('dve', fn, waits=[d[0]] + list(extra))
            return d[0]
        V(lambda e: e.tensor_tensor(out=q[0], in0=x0r, in1=L4r, op=mul), [S['t_x0c']])
        V(lambda e: e.tensor_tensor(out=q[1], in0=x0i, in1=L4i, op=mul))
        V(lambda e: e.tensor_tensor(out=q[2], in0=x0i, in1=L4r, op=mul))
        V(lambda e: e.tensor_tensor(out=q[3], in0=x0r, in1=L4i, op=mul))
        V(lambda e: e.tensor_tensor(out=q[0], in0=q[0], in1=q[1], op=sub))
        V(lambda e: e.tensor_tensor(out=q[2], in0=q[2], in1=q[3], op=add))
        V(lambda e: e.tensor_tensor(out=x0r, in0=zv[:, :, 0, :], in1=q[0], op=add), [tzs])
        t_zs = V(lambda e: e.tensor_tensor(out=x0i, in0=zv[:, :, 1, :], in1=q[2], op=add))
        S['zs_rd'] = t_zs
        S['dve'] = t_zs

    def ssm_y(self, kt, ls, last):
        P, S = self.P, self.S
        uT = self.uT
        if ls[0] == 3:
            S['y_xb'] = [t for tpl in S['pend'] for t in tpl]
            S['pend'] = []
        xb_toks = S['y_xb']
        yb = 6
        ys = self.ps[7]
        t_ys = None
        for l in ls:
            yps = self.ps[yb]
            i = 0
            for pl in range(4):
                for r in range(2):
                    P.op('pe', lambda e, l=l, pl=pl, r=r, i=i, yps=yps, kt=kt: e.matmul(
                        yps[:, :], self.CpPad[pl // 2][:, pl % 2, l, r, :, :].rearrange("p g c -> p (g c)"),
                        self.XbR[(4 * kt + pl) % 6][r][:, 1:513], start=(i == 0), stop=False),
                        waits=(xb_toks + S['t_cp'] + [self.ps_rd[yb]]) if i == 0 else [], signal=False)
                    i += 1
            for s_ in range(l + 1):
                t_y = P.op('pe', lambda e, l=l, s_=s_, kt=kt, yps=yps: e.matmul(
                    yps[:, :], self.Kin[:, kt, l - s_, :], uT[:, kt, s_:SEQ:4], start=False, stop=(s_ == l)),
                    signal=(s_ == l))
            i = 0
            for pl in range(4):
                for r in range(2):
                    P.op('pe', lambda e, l=l, pl=pl, r=r, i=i, kt=kt: e.matmul(
                        ys[:, 128 + 16 * l:144 + 16 * l], self.CpPad[pl // 2][:, pl % 2, l, r, :, :].rearrange("p g c -> p (g c)"),
                        self.XbS[:, kt % 2, pl, r, :], start=(i == 0), stop=False),
                        waits=[self.ps_rd[7], S['zs_rd']] if i == 0 else [], signal=False)
                    i += 1
            for s_ in range(l + 1):
                t_ys = P.op('pe', lambda e, l=l, s_=s_, kt=kt: e.matmul(
                    ys[:, 128 + 16 * l:144 + 16 * l], self.Kin[:, kt, l - s_, :], uT[:, kt, SEQ + s_:NT:4],
                    start=False, stop=(s_ == l)), signal=(s_ == l))
            self._gelu(yps[:, :], uT[:, kt, l:SEQ:4], self.ta[:, 0:512], [t_y])
            self.ps_rd[yb] = S['gel']
        if not last:
            return
        S['pad_cp_rd'] = t_ys
        S['xb_rd'] = t_ys
        for pl in range(4):
            S['slot_rd'][(4 * kt + pl) % 6] = t_ys
        S['xbs_rd'][kt % 2] = t_ys
        v3 = lambda ap: ap.rearrange("p (t b) -> p t b", t=4)
        self._gelu(v3(ys[:, 128:192]), uT[:, kt, SEQ:NT].rearrange("p (b t) -> p t b", t=4), v3(self.ta[:, 0:64]), [t_ys])
        self.ps_rd[7] = S['gel']
        self.ssm_y_tok[kt] = S['gel']

    def _gelu(self, src, dst, a, waits):
        P, S = self.P, self.S
        t5 = P.op('act', lambda e: e.activation(out=dst, in_=src, func=AF.Gelu_apprx_tanh), waits=list(waits))
        S['gel'] = t5

    def ssm_glu(self):
        P, S, d = self.P, self.S, self.dram
        uT = self.uT
        for i in self.cp_slots:
            self.ws_release(i, S['pad_cp_rd'])
        self.ws_release(self.tmp_slot, S['dmp'])
        wv, t_w, wi = self.ws_load(d["wglu"], lambda t: t[:, 0:2048].rearrange("p (k n) -> p k n", k=4))
        gt = self.glu_tmp
        prod = None
        t_g = None
        for ti, (t0, n) in enumerate(TT):
            ta_ = []
            for oc in range(4):
                b = 4 + oc
                gps = self.ps[b]
                for kt in range(4):
                    t_g = P.op('pe', lambda e, kt=kt, oc=oc, t0=t0, n=n, gps=gps: e.matmul(
                        gps[:, :n], wv[:, kt, oc * 128:(oc + 1) * 128], uT[:, kt, t0:t0 + n], start=(kt == 0), stop=(kt == 3)),
                        waits=([t_w, self.ps_rd[b]] + [self.ssm_y_tok[k] for k in range(4)]) if kt == 0 else [],
                        signal=(kt == 3))
                t_a = P.op('act', lambda e, oc=oc, n=n, gps=gps: e.activation(
                    out=gt[:, oc, :n], in_=gps[:, :n], func=AF.Sigmoid, bias=self.bgl[:, oc:oc + 1], scale=1.0),
                    waits=[t_g, prod, S['dve']])
                self.ps_rd[b] = t_a
                ta_.append(t_a)
            for oc in range(4):
                prod = P.op('dve', lambda e, oc=oc, t0=t0, n=n: e.tensor_tensor(
                    out=uT[:, oc, t0:t0 + n], in0=uT[:, oc, t0:t0 + n], in1=gt[:, oc, :n], op=ALU.mult),
                    waits=ta_ + [t_g])
                self.ssm_tok[(oc, ti)] = prod
        self.ws_release(wi, t_g)
        self.ssmT = uT
        s_o = self.slot("d_st")
        self.out_toks.append(P.dma('sp', lambda e: e.dma_start(out=d["st_p"], in_=self.stp[:]), s_o, waits=list(S['xf_rd'])))
        s_o2 = self.slot("d_st2")
        self.out_toks.append(P.dma('sp', lambda e: e.dma_start(out=d["st_s"], in_=self.x0[:]), s_o2, waits=[S['zs_rd'], S['xb_rd']]))


def _layout(inputs):
    f32 = np.float32
    xp = np.asarray(inputs["x_prompt"], f32)
    xs = np.asarray(inputs["x_sample"], f32)
    shared = {}
    norms = np.stack([inputs["ffn1_norm"][0], inputs["mix_norm"][0], inputs["ffn2_norm"][0], inputs["final_norm"]], 0)
    shared["norms"] = np.ascontiguousarray(np.asarray(norms, f32).reshape(4, KC, 128).transpose(2, 0, 1))
    for f, pre in ((1, "ffn1"), (2, "ffn2")):
        wg = np.asarray(inputs[pre + "_w_gate"][0], f32).reshape(KC, 128, NJ, 128)
        wu = np.asarray(inputs[pre + "_w_up"][0], f32).reshape(KC, 128, NJ, 128)
        wgu = np.stack([wg, wu], 0)
        shared["wgu%d" % f] = np.ascontiguousarray(wgu.transpose(3, 2, 0, 1, 4))
        wd = np.asarray(inputs[pre + "_w_down"][0], f32).reshape(2, NJH, 128, KC, 128)
        shared["wd%d" % f] = np.ascontiguousarray(wd.transpose(0, 3, 2, 1, 4))
    w_in = np.asarray(inputs["w_in"][0], f32)
    qperm = w_in[:, :512].reshape(D, 2, 4, 64).transpose(0, 2, 1, 3).reshape(D, 512)
    winp = np.concatenate([qperm, w_in[:, 512:]], 1)
    shared["win"] = np.ascontiguousarray(winp.reshape(KC, 128, 1280).transpose(1, 0, 2))
    w_out = np.asarray(inputs["w_out"][0], f32)
    wa = w_out[:512].reshape(2, 4, 64, D).transpose(1, 0, 2, 3).reshape(4, 128, D)
    wperm = np.concatenate([wa, w_out[512:].reshape(4, 128, D)], 0)
    shared["wout"] = np.ascontiguousarray(wperm.reshape(8, 128, KC, 128).transpose(2, 1, 0, 3))
    shared["cst_ident"] = np.ascontiguousarray(np.eye(128, dtype=f32)[::-1])
    dd = np.arange(256)
    dfl = np.maximum(dd, 1).astype(f32)
    large = 16 + (np.log(dfl / f32(16)) / f32(np.log(128 / 16)) * f32(16)).astype(np.int32)
    bucket = np.where(dd < 16, dd, np.minimum(large, 31))
    oh = np.zeros((32, 256), f32)
    oh[bucket, dd] = 1.0
    shared["cst_oh"] = oh
    bi = np.arange(64) // 4
    shared["cst_blk"] = np.ascontiguousarray(np.where(bi[:, None] == bi[None, :], 0.0, -30000.0).astype(f32)[::-1])
    shared["rel_bias"] = np.asarray(inputs["rel_bias"], f32)
    shared["sinks"] = np.asarray(inputs["sinks"], f32).reshape(1, 8)
    a_re = np.asarray(inputs["a_re"][0], f32); a_im = np.asarray(inputs["a_im"][0], f32)
    ldt = np.broadcast_to(np.asarray(inputs["log_dt"][0], f32)[:, None], (32, 64))
    pm = lambda a: a.reshape(16, 2, 64).transpose(1, 2, 0).reshape(128, 16)
    shared["ssm_pm"] = np.ascontiguousarray(np.stack([pm(a_re), pm(a_im), pm(ldt)], 1))
    cm = lambda a: np.broadcast_to(a.reshape(4, 8, 1, 64).transpose(1, 2, 0, 3), (8, 16, 4, 64)).reshape(128, 4, 64)
    shared["ssm_cm"] = np.ascontiguousarray(np.stack([cm(a_re), cm(a_im), cm(ldt)], 1))
    cpm = lambda a: a.reshape(16, 2, 16, 64).transpose(1, 3, 0, 2).reshape(128, 16, 16)
    shared["c_pm"] = np.ascontiguousarray(np.stack([cpm(np.asarray(inputs["c_re"][0], f32)), cpm(np.asarray(inputs["c_im"][0], f32))], 1))
    bpm = lambda a: a.reshape(16, 2, 64, 16).transpose(1, 2, 0, 3).reshape(128, 16, 16)
    bcm = lambda a: a.reshape(4, 8, 64, 16).transpose(1, 3, 0, 2).reshape(128, 4, 64)
    b_re = np.asarray(inputs["b_re"][0], f32); b_im = np.asarray(inputs["b_im"][0], f32)
    shared["b_pm"] = np.ascontiguousarray(np.stack([bpm(b_re), bpm(b_im)], 1))
    shared["b_cm"] = np.ascontiguousarray(np.stack([bcm(b_re), bcm(b_im)], 1))
    shared["dskip"] = np.ascontiguousarray(np.asarray(inputs["d_skip"][0], f32).reshape(4, 128).T)
    shared["bglu"] = np.ascontiguousarray(np.asarray(inputs["b_glu"][0], f32).reshape(4, 128).T)
    shared["wglu"] = np.ascontiguousarray(np.asarray(inputs["w_glu"][0], f32).reshape(4, 128, 512).transpose(1, 0, 2))
    pidx = np.arange(128)
    m2 = np.zeros((128, 4, 8), f32); m3 = np.zeros((128, 4, 2), f32)
    for pl in range(4):
        for gg in range(2):
            m2[pidx // 64 == gg, pl, 2 * pl + gg] = 1.0
            m3[pidx // 16 == 2 * pl + gg, pl, gg] = 1.0
    shared["cst_m2"] = m2
    shared["cst_m3"] = m3
    shared["cst_eye"] = np.eye(128, dtype=f32)
    sre = np.asarray(inputs["state_ssm_re"][0], f32); sim = np.asarray(inputs["state_ssm_im"][0], f32)
    ck = np.asarray(inputs["cache_k"][0], f32)
    cv = np.asarray(inputs["cache_v"][0], f32)
    maps = []
    for c in range(NCORES):
        X = np.concatenate([xp[c], xs[16 * c:16 * c + 16].reshape(NS, D)], 0)
        m = dict(shared)
        m["xT"] = np.ascontiguousarray(X.T.reshape(KC, 128, NT).transpose(1, 0, 2))
        ckc, cvc = ck[16 * c:16 * c + 16], cv[16 * c:16 * c + 16]
        m["cKT"] = np.ascontiguousarray(ckc.transpose(2, 3, 0, 1).reshape(128, 16, 128))
        m["cV"] = np.ascontiguousarray(cvc.transpose(1, 0, 2, 3).reshape(128, 16, 128))
        m["cK_nat"] = np.ascontiguousarray(ckc.reshape(16, 128, 128))
        m["cV_nat"] = np.ascontiguousarray(cvc.reshape(16, 128, 128))
        x0 = np.stack([sre[16 * c:16 * c + 16], sim[16 * c:16 * c + 16]], 0)
        m["x0"] = np.ascontiguousarray(x0.reshape(2, 16, 16, 2, 64).transpose(3, 4, 2, 0, 1).reshape(128, 16, 2, 16))
        maps.append(m)
    return maps


_NC_CACHE = {}


def _get_nc(debug=None):
    key = repr(sorted((debug or {}).items()))
    if key not in _NC_CACHE:
        _NC_CACHE[key] = Builder(debug).build()
    return _NC_CACHE[key]


def kernel(**inputs):
    maps = _layout(inputs)
    nc = _get_nc()
    res = run_bass_kernel_spmd(nc, maps, core_ids=list(range(NCORES)))
    outs = res.results
    yp = np.zeros((8, SEQ, D), np.float32)
    ys = np.zeros((128, 4, D), np.float32)
    kp = np.zeros((1, 8, 128, 2, 64), np.float32); vp = np.zeros_like(kp)
    rp = np.zeros((1, 8, 32, 64), np.float32); ip = np.zeros_like(rp)
    ks = np.zeros((1, 128, 128, 2, 64), np.float32); vs = np.zeros_like(ks)
    rs = np.zeros((1, 128, 32, 64), np.float32); is_ = np.zeros_like(rs)
    for c in range(NCORES):
        o = outs[c]
        Y = np.asarray(o["yT"]).transpose(1, 0, 2).reshape(D, NT).T
        yp[c] = Y[:SEQ]
        ys[16 * c:16 * c + 16] = Y[SEQ:].reshape(16, 4, D)
        kp[0, c] = np.asarray(o["kp"]).reshape(128, 2, 64)
        vp[0, c] = np.asarray(o["vp"]).reshape(128, 2, 64)
        ks[0, 16 * c:16 * c + 16] = np.asarray(o["ks"]).reshape(16, 128, 2, 64)
        vs[0, 16 * c:16 * c + 16] = np.asarray(o["vs"]).reshape(16, 128, 2, 64)
        sp = np.asarray(o["st_p"]).reshape(2, 64, 16, 2).transpose(2, 0, 1, 3).reshape(32, 64, 2)
        rp[0, c], ip[0, c] = sp[..., 0], sp[..., 1]
        ss = np.asarray(o["st_s"]).reshape(2, 64, 16, 2, 16).transpose(4, 2, 0, 1, 3).reshape(16, 32, 64, 2)
        rs[0, 16 * c:16 * c + 16], is_[0, 16 * c:16 * c + 16] = ss[..., 0], ss[..., 1]
    return yp, ys, kp, vp, rp, ip, ks, vs, rs, is_
```

```python
import contextlib
import numpy as np
import concourse.bass as bass
import concourse.mybir as mybir
from concourse.bass_utils import run_bass_kernel_spmd

F32 = mybir.dt.float32
BF16 = mybir.dt.bfloat16
I32 = mybir.dt.int32
AF = mybir.ActivationFunctionType
ALU = mybir.AluOpType

NCORES = 8
D = 1024
KC = 8
DFF = 2816
NJ = 22
NJH = 11
SEQ = 2048
NS = 64
NT = SEQ + NS
TT = [(0, 512), (512, 512), (1024, 512), (1536, 512), (2048, 64)]
EPS = 1e-6
ENGS = ('pe', 'act', 'dve', 'pool', 'sp')


class Prog:
    def __init__(self, nc):
        self.nc = nc
        self.q = {e: [] for e in ENGS}
        self.sem = {}
        self.cnt = {}
        self.waited = {e: {} for e in ENGS}
        self._stack = []
        self.last = {e: None for e in ENGS}

    def new_sem(self, name):
        cm = self.nc.semaphore(name)
        h = cm.__enter__()
        self._stack.append(cm)
        return h

    def _waits(self, eng, waits):
        out = []
        for w in waits:
            if w is None:
                continue
            sem, val = w
            key = id(sem)
            if self.waited[eng].get(key, 0) >= val:
                continue
            self.waited[eng][key] = val
            out.append((sem, val))
        return out

    def op(self, eng, fn, waits=(), signal=True):
        ws = self._waits(eng, waits)
        tok = None
        if signal:
            if eng not in self.sem:
                self.sem[eng] = self.new_sem('s_' + eng)
                self.cnt[eng] = 0
            self.cnt[eng] += 1
            tok = (self.sem[eng], self.cnt[eng])
            self.last[eng] = tok
        self.q[eng].append((fn, ws, tok, 1))
        return tok

    def dma(self, eng, fn, slot, waits=()):
        ws = self._waits(eng, waits)
        slot.count += 16
        tok = (slot.sem, slot.count)
        self.q[eng].append((fn, ws, tok, 16))
        return tok

    def wait_only(self, eng, waits):
        ws = self._waits(eng, waits)
        if ws:
            self.q[eng].append((None, ws, None, 0))

    def replay(self, eng, engine):
        for fn, ws, tok, inc in self.q[eng]:
            for sem, val in ws:
                engine.wait_ge(sem, val)
            if fn is None:
                continue
            ins = fn(engine)
            if tok is not None:
                ins.then_inc(tok[0], inc)

    def close(self):
        for cm in reversed(self._stack):
            cm.__exit__(None, None, None)


class DmaSlot:
    def __init__(self, prog, name):
        self.sem = prog.new_sem(name)
        self.count = 0


class Builder:
    def __init__(self, debug=None):
        self.debug = debug or {}
        self.nc = bass.Bass("TRN2", target_bir_lowering=False)
        self.P = Prog(self.nc)
        self.es = contextlib.ExitStack()
        self.dram = {}

    def din(self, name, shape, dt=F32):
        t = self.nc.dram_tensor(name, list(shape), dt, kind="ExternalInput").ap()
        self.dram[name] = t
        return t

    def dout(self, name, shape, dt=F32):
        t = self.nc.dram_tensor(name, list(shape), dt, kind="ExternalOutput").ap()
        self.dram[name] = t
        return t

    def sb(self, name, shape, dt):
        return self.es.enter_context(self.nc.sbuf_tensor(name, list(shape), dt))

    def slot(self, name):
        return DmaSlot(self.P, name)

    def ws_load(self, src_ap, shape_fn):
        P = self.P
        i = self.ws_next % self.WS_N
        self.ws_next += 1
        dst = shape_fn(self.ws[i])
        tok = P.dma('pool', lambda e, d=dst, s=src_ap: e.dma_start(out=d, in_=s), self.ws_slot[i],
                    waits=[self.ws_free[i]])
        return dst, tok, i

    def ws_release(self, i, tok):
        self.ws_free[i] = tok

    def build(self):
        nc, P = self.nc, self.P
        dbg = self.debug
        xT_d = self.din("xT", [128, KC, NT])
        norms_d = self.din("norms", [128, 4, KC])
        wgu_d = [self.din("wgu%d" % f, [NJ, 128, 2, KC, 128]) for f in (1, 2)]
        wd_d = [self.din("wd%d" % f, [2, KC, 128, NJH, 128]) for f in (1, 2)]
        yT_d = self.dout("yT", [128, KC, NT])
        self.mix_io()

        self.xT = self.sb("xT_s", [128, KC, NT], F32)
        self.hT = self.sb("hT_s", [128, KC, NT], BF16)
        self.aT = self.sb("aT_s", [128, NJH, NT], BF16)
        self.WS_N = 3
        self.ws = [self.sb("ws%d" % i, [128, 2048], BF16) for i in range(self.WS_N)]
        self.ws_slot = [self.slot("wsd%d" % i) for i in range(self.WS_N)]
        self.ws_free = [None] * self.WS_N
        self.ws_next = 0
        self.norms = self.sb("norms_s", [128, 4, KC], F32)
        self.ones = self.sb("ones_s", [128, 128], BF16)
        self.epst = self.sb("eps_s", [128, 1], F32)
        self.sq = [self.sb("sq%d" % i, [128, 512], BF16) for i in range(2)]
        self.sg = [self.sb("sg%d" % i, [128, 512], F32) for i in range(2)]
        self.rt = [self.sb("rt%d" % i, [128, 512], F32) for i in range(2)]
        self.ps = [self.es.enter_context(nc.psum_tensor("ps%d" % i, [128, 512], F32)) for i in range(8)]
        self.ps_rd = [None] * 8

        t_ones = P.op('dve', lambda e: e.memset(self.ones[:], 1.0 / D))
        t_eps = P.op('dve', lambda e: e.memset(self.epst[:], EPS))
        self.t_const = t_eps
        s_n = self.slot("d_norm")
        self.t_norms = P.dma('sp', lambda e: e.dma_start(out=self.norms[:], in_=norms_d), s_n)

        self.tok_x = {}
        for ti, (t0, n) in enumerate(TT):
            s = self.slot("d_x%d" % ti)
            tk = P.dma('sp', lambda e, t0=t0, n=n: e.dma_start(out=self.xT[:, :, t0:t0 + n], in_=xT_d[:, :, t0:t0 + n]), s,
                       waits=([self.tok_x[(0, 0)]] if ti == 1 else []))
            for kc in range(KC):
                self.tok_x[(kc, ti)] = tk
        self.tok_h = {}
        self.sq_rd = [None, None]
        self.sg_rd = [None, None]
        self.rt_rd = [None, None]
        self.sq_i = 0
        self.hT_free = None

        self.out_toks = []
        self.ffn(0, wgu_d[0], wd_d[0])
        if not dbg.get("skip_mixer"):
            self.mixer()
        self.yT_d = yT_d
        self.ffn(2, wgu_d[1], wd_d[1])
        self.final_norm(yT_d)

        with nc.Block() as block:
            @block.sync
            def _(e):
                P.replay('sp', e)

            @block.gpsimd
            def _(e):
                P.replay('pool', e)

            @block.tensor
            def _(e):
                P.replay('pe', e)

            @block.scalar
            def _(e):
                P.replay('act', e)

            @block.vector
            def _(e):
                P.replay('dve', e)
        P.close()
        self.es.close()
        return nc

    def norm(self, gi, final=False):
        for ti in range(len(TT)):
            self.norm_tile(gi, ti, final)

    def norm_tile(self, gi, ti, final=False):
        P = self.P
        xT, hT = self.xT, self.hT
        MS0 = 6
        toks = {}
        for ti, (t0, n) in [(ti, TT[ti])]:
            msb = MS0 + (ti % 2)
            ms = self.ps[msb]
            t_mm = None
            for kc in range(KC):
                b = self.sq_i % 2
                self.sq_i += 1
                t_sq = P.op('act', lambda e, b=b, kc=kc, t0=t0, n=n: e.activation(
                    out=self.sq[b][:, :n], in_=xT[:, kc, t0:t0 + n], func=AF.Square),
                    waits=[self.tok_x[(kc, ti)], self.sq_rd[b]])
                last = kc == KC - 1
                t_mm = P.op('pe', lambda e, b=b, kc=kc, n=n, ms=ms: e.matmul(
                    ms[:, :n], self.ones[:], self.sq[b][:, :n], start=(kc == 0), stop=(kc == KC - 1)),
                    waits=[t_sq, self.t_const] + ([self.ps_rd[msb]] if kc == 0 else []), signal=True)
                self.sq_rd[b] = t_mm
            rb = ti % 2
            rt = self.rt[rb]
            t_s = P.op('act', lambda e, n=n, ms=ms, rt=rt: e.activation(
                out=rt[:, :n], in_=ms[:, :n], func=AF.Ln, bias=self.epst[:, 0:1], scale=1.0),
                waits=[t_mm, self.rt_rd[rb], self.t_const])
            self.ps_rd[msb] = t_s
            t_r = P.op('act', lambda e, n=n, rt=rt: e.activation(
                out=rt[:, :n], in_=rt[:, :n], func=AF.Exp, scale=-0.5), waits=[t_s])
            t_h = None
            for kc in range(KC):
                if final:
                    t_h = P.op('dve', lambda e, kc=kc, t0=t0, n=n, rt=rt: e.scalar_tensor_tensor(
                        out=xT[:, kc, t0:t0 + n], in0=xT[:, kc, t0:t0 + n], scalar=self.norms[:, gi, kc:kc + 1],
                        in1=rt[:, :n], op0=ALU.mult, op1=ALU.mult),
                        waits=[t_r, self.t_norms, self.tok_x[(kc, ti)]])
                    self.tok_x[(kc, ti)] = t_h
                else:
                    t_h = P.op('dve', lambda e, kc=kc, t0=t0, n=n, rt=rt: e.scalar_tensor_tensor(
                        out=hT[:, kc, t0:t0 + n], in0=xT[:, kc, t0:t0 + n], scalar=self.norms[:, gi, kc:kc + 1],
                        in1=rt[:, :n], op0=ALU.mult, op1=ALU.mult),
                        waits=[t_r, self.t_norms, self.tok_x[(kc, ti)], self.hT_free])
                    self.tok_h[(kc, ti)] = t_h
            self.rt_rd[rb] = t_h
        return toks

    def ffn(self, gi, wgu_d, wd_d):
        P = self.P
        xT, hT, aT = self.xT, self.hT, self.aT
        LOOK = 2 if gi == 0 else 3
        normed = getattr(self, "normed_upto", {}).get(gi, 0)
        for ti in range(normed, LOOK):
            self.norm_tile(gi, ti)
        normed = max(normed, LOOK)
        GB, UB, YB = (0, 1), (2, 3), (4, 5, 0, 1)
        if not hasattr(self, "aT_rd"):
            self.aT_rd = None
        cnt = 0
        ycnt = 0
        for h in range(2):
            tok_a = {}
            for jj in range(NJH):
                j = h * NJH + jj
                wv, t_w, wi = self.ws_load(
                    wgu_d[j].rearrange("p g k n -> p (g k n)"),
                    lambda t: t[:, 0:2048])
                wv4 = wv.rearrange("p (g k n) -> p g k n", g=2, k=KC)
                t_last = None
                for ti, (t0, n) in enumerate(TT):
                    if h == 0 and jj == 0 and normed < len(TT):
                        self.norm_tile(gi, normed)
                        normed += 1
                    gb = GB[cnt % 2]
                    ub = UB[cnt % 2]
                    sgi = cnt % 2
                    cnt += 1
                    gps, ups = self.ps[gb], self.ps[ub]
                    for kc in range(KC):
                        t_g = P.op('pe', lambda e, kc=kc, t0=t0, n=n, gps=gps, wv4=wv4: e.matmul(
                            gps[:, :n], wv4[:, 0, kc, :], hT[:, kc, t0:t0 + n], start=(kc == 0), stop=(kc == KC - 1)),
                            waits=([t_w, self.ps_rd[gb]] if kc == 0 else []) + [self.tok_h[(kc, ti)]],
                            signal=(kc == KC - 1))
                    for kc in range(KC):
                        t_u = P.op('pe', lambda e, kc=kc, t0=t0, n=n, ups=ups, wv4=wv4: e.matmul(
                            ups[:, :n], wv4[:, 1, kc, :], hT[:, kc, t0:t0 + n], start=(kc == 0), stop=(kc == KC - 1)),
                            waits=([self.ps_rd[ub]] if kc == 0 else []),
                            signal=(kc == KC - 1))
                    t_last = t_u
                    sg = self.sg[sgi]
                    t_s = P.op('act', lambda e, n=n, gps=gps, sg=sg: e.activation(
                        out=sg[:, :n], in_=gps[:, :n], func=AF.Silu), waits=[t_g, self.sg_rd[sgi]])
                    self.ps_rd[gb] = t_s
                    t_a = P.op('dve', lambda e, jj=jj, t0=t0, n=n, ups=ups, sg=sg: e.tensor_tensor(
                        out=aT[:, jj, t0:t0 + n], in0=ups[:, :n], in1=sg[:, :n], op=ALU.mult),
                        waits=[t_s, t_u, self.aT_rd, getattr(self, 'dbg_tok', None)])
                    self.sg_rd[sgi] = t_a
                    self.ps_rd[ub] = t_a
                    tok_a[(jj, ti)] = t_a
                    self.bg_pump(2)
                self.ws_release(wi, t_last)
                if h == 0 and jj == 0 and gi == 0 and not self.debug.get("skip_mixer"):
                    self.mix_setup()
                    def chain():
                        yield from self._tables_gen()
                        if not self.debug.get("skip_ssm"):
                            yield from self.ssm_setup_gen(1)
                    self.bg = chain()
            if h == 1:
                self.hT_free = t_last
                if gi == 0 and not self.debug.get("skip_mixer") and not self.debug.get("skip_ssm"):
                    self.bg_pump(10 ** 9)
                    self.bg = self.ssm_setup_gen(2)
            if gi == 2 and h == 0 and getattr(self, "mix_end_tok", None) is not None and not self.debug.get("no_tail_opt"):
                stb = self.stage[:].rearrange("p a h q -> p (a h q)").bitcast(BF16)
                fl = lambda t: t[:].rearrange("p a b -> p (a b)")
                bufs = [stb[:, 0:1408], stb[:, 2048:3456], fl(self.cKT)[:, 0:1408], fl(self.cV)[:, 0:1408],
                        self.BT[:].rearrange("p a b c -> p (a b c)")[:, 0:1408]]
                self.res_wd = []
                for c_, bf in enumerate(bufs):
                    tk = P.dma('pool', lambda e, bf=bf, c_=c_: e.dma_start(out=bf, in_=wd_d[1, c_].rearrange("p j n -> p (j n)")),
                               self.slot("d_rwd%d" % c_), waits=[self.mix_end_tok])
                    self.res_wd.append((bf.rearrange("p (j n) -> p j n", j=NJH), tk))
            if gi == 2 and h == 1 and getattr(self, "res_wd", None):
                chunks = list(self.res_wd)
                ring = []
                for c in range(5, KC):
                    wv, t_w, wi = self.ws_load(wd_d[h, c].rearrange("p j n -> p (j n)"), lambda t: t[:, 0:NJH * 128])
                    chunks.append((wv.rearrange("p (j n) -> p j n", j=NJH), t_w))
                    ring.append(wi)
                t_y = None
                for ti, (t0, n) in enumerate(TT):
                    for c in range(KC):
                        wv3, t_w = chunks[c]
                        yb = YB[ycnt % 4]
                        ycnt += 1
                        yps = self.ps[yb]
                        for jj in range(NJH):
                            t_y = P.op('pe', lambda e, jj=jj, t0=t0, n=n, yps=yps, wv3=wv3: e.matmul(
                                yps[:, :n], wv3[:, jj, :], aT[:, jj, t0:t0 + n], start=(jj == 0), stop=(jj == NJH - 1)),
                                waits=([t_w, self.ps_rd[yb]] if jj == 0 else []) + [tok_a[(jj, ti)]],
                                signal=(jj == NJH - 1))
                        t_x = P.op('dve', lambda e, c=c, t0=t0, n=n, yps=yps: e.scalar_tensor_tensor(
                            out=xT[:, c, t0:t0 + n], in0=yps[:, :n], scalar=0.5, in1=xT[:, c, t0:t0 + n],
                            op0=ALU.mult, op1=ALU.add),
                            waits=[t_y, self.tok_x[(c, ti)]])
                        self.tok_x[(c, ti)] = t_x
                        self.ps_rd[yb] = t_x
                    self.final_tile(ti)
                for wi in ring:
                    self.ws_release(wi, t_y)
                self.aT_rd = t_y
                continue
            for c in range(KC):
                wv, t_w, wi = self.ws_load(
                    wd_d[h, c].rearrange("p j n -> p (j n)"),
                    lambda t: t[:, 0:NJH * 128])
                wv3 = wv.rearrange("p (j n) -> p j n", j=NJH)
                t_y = None
                for ti, (t0, n) in enumerate(TT):
                    yb = YB[ycnt % 4]
                    ycnt += 1
                    yps = self.ps[yb]
                    for jj in range(NJH):
                        t_y = P.op('pe', lambda e, jj=jj, t0=t0, n=n, yps=yps, wv3=wv3: e.matmul(
                            yps[:, :n], wv3[:, jj, :], aT[:, jj, t0:t0 + n], start=(jj == 0), stop=(jj == NJH - 1)),
                            waits=([t_w, self.ps_rd[yb]] if jj == 0 else []) + [tok_a[(jj, ti)]],
                            signal=(jj == NJH - 1))
                    t_x = P.op('dve', lambda e, c=c, t0=t0, n=n, yps=yps: e.scalar_tensor_tensor(
                        out=xT[:, c, t0:t0 + n], in0=yps[:, :n], scalar=0.5, in1=xT[:, c, t0:t0 + n],
                        op0=ALU.mult, op1=ALU.add),
                        waits=[t_y, self.tok_x[(c, ti)]])
                    self.tok_x[(c, ti)] = t_x
                    self.ps_rd[yb] = t_x
                    self.bg_pump(3)
                    if gi == 2 and h == 1 and c == KC - 1 and getattr(self, "yT_d", None) is not None:
                        self.final_tile(ti)
                self.ws_release(wi, t_y)
                self.aT_rd = t_y

    def final_tile(self, ti):
        P = self.P
        if not hasattr(self, "fin_slot"):
            self.fin_slot = self.slot("d_out")
            self.fin_done = set()
        self.norm_tile(3, ti, final=True)
        t0, n = TT[ti]
        P.dma('sp', lambda e, t0=t0, n=n: e.dma_start(out=self.yT_d[:, :, t0:t0 + n], in_=self.xT[:, :, t0:t0 + n]),
              self.fin_slot, waits=[self.tok_x[(kc, ti)] for kc in range(KC)])
        self.fin_done.add(ti)

    def final_norm(self, yT_d):
        P = self.P
        for ti in range(len(TT)):
            if ti not in getattr(self, "fin_done", set()):
                self.final_tile(ti)
        P.wait_only('sp', [(self.fin_slot.sem, self.fin_slot.count)] + list(self.out_toks))

    def mix_io(self):
        d = self.dram
        self.din("win", [128, KC, 1280])
        self.din("wout", [KC, 128, 8, 128])
        self.din("cst_ident", [128, 128])
        self.din("cst_oh", [32, 256])
        self.din("cst_blk", [64, 64])
        self.din("rel_bias", [32, 8])
        self.din("sinks", [1, 8])
        self.din("cKT", [128, 16, 128])
        self.din("cV", [128, 16, 128])
        self.din("cK_nat", [16, 128, 128])
        self.din("cV_nat", [16, 128, 128])
        self.dout("kp", [128, 128])
        self.dout("vp", [128, 128])
        self.dout("ks", [16, 128, 128])
        self.dout("vs", [16, 128, 128])
        self.scr = self.nc.dram_tensor("scr_f", [8, 2, 256], F32).ap()
        self.ssm_io()
        if self.debug.get("dump_attn"):
            self.dout("dbg_attn", [128, 4, NT], BF16)

    def mix_setup(self):
        nc, P, d = self.nc, self.P, self.dram
        aT = self.aT
        NEG = -30000.0
        self.qT2 = aT[:, 0:4, :]
        self.uT = aT[:, 4:8, :]
        self.kT = aT[:, 8, :]
        flat = aT[:, 9:11, :].rearrange("p a t -> p (a t)")
        self.vtok = flat[:, 0:17 * 128].rearrange("p (b n) -> p b n", n=128)
        self.ident = self.sb("ident_s", [128, 128], BF16)
        self.ones1 = self.sb("ones1_s", [128, 64], BF16)
        self.BT = self.sb("BT_s", [128, 2, 8, 128], BF16)
        self.BTsc = self.sb("BTsc_s", [128, 2, 16, 4, 4], BF16)
        self.BTsn = self.sb("BTsn_s", [128, 4, 64], BF16)
        self.ES = self.sb("ES_s", [128, 4], F32)
        self.cKT = self.sb("cKT_s", [128, 16, 128], BF16)
        self.cV = self.sb("cV_s", [128, 16, 128], BF16)
        self.stage = self.sb("stage_s", [128, 2, 8, 128], F32)
        self.kvo = [self.sq[i][:].bitcast(F32) for i in range(2)]
        self.pT = [self.sg[i // 2][:, 256 * (i % 2):256 * (i % 2) + 256].bitcast(BF16) for i in range(4)]
        rb_s = self.sb("rb_s", [32, 8], F32)
        oh_s = self.sb("oh_s", [32, 256], F32)
        fm = self.sb("fm_s", [8, 2, 256], F32)
        blk_s = self.sb("blk_s", [128, 64], F32)
        sk_s = self.sb("sk_s", [128, 4], F32)

        t1 = P.dma('sp', lambda e: e.dma_start(out=rb_s[:], in_=d["rel_bias"]), self.slot("d_rb"))
        t2 = P.dma('sp', lambda e: e.dma_start(out=oh_s[:], in_=d["cst_oh"]), self.slot("d_oh"))
        s_blk = self.slot("d_blk")
        for g in range(2):
            t3 = P.dma('sp', lambda e, g=g: e.dma_start(out=blk_s[64 * g:64 * g + 64, :], in_=d["cst_blk"]), s_blk)
        s_sk = self.slot("d_sk")
        for g in range(2):
            t4 = P.dma('sp', lambda e, g=g: e.dma_start(
                out=sk_s[64 * g:64 * g + 64, :], in_=d["sinks"][0:1, 4 * g:4 * g + 4].to_broadcast([64, 4])), s_sk)
        if self.debug.get('ckpt', 99) < 1:
            self.out_toks = []
            return
        t_es = P.op('act', lambda e: e.activation(out=self.ES[:], in_=sk_s[:], func=AF.Exp), waits=[t4])
        self.t_es = t_es
        t_o1 = P.op('dve', lambda e: e.memset(self.ones1[:], 1.0))
        self.t_ones1 = t_o1
        if self.debug.get('ckpt', 99) < 2:
            self.out_toks = []
            return
        def tables_gen():
            fps = self.ps[7]
            t_f = P.op('pe', lambda e: e.matmul(fps[0:8, 0:256], rb_s[0:32, 0:8], oh_s[0:32, :], start=True, stop=True),
                       waits=[t1, t2, self.ps_rd[7]])
            for _ in range(6):
                yield
            t_m = P.op('dve', lambda e: e.memset(fm[:], NEG))
            t_c1 = P.op('dve', lambda e: e.tensor_copy(out=fm[:, 0, 127:255], in_=fps[0:8, 0:128]), waits=[t_f, t_m])
            t_c2 = P.op('dve', lambda e: e.tensor_copy(out=fm[:, 1, 0:127], in_=fps[0:8, 1:128]), waits=[t_f, t_m])
            self.ps_rd[7] = t_c2
            t_sc = P.dma('sp', lambda e: e.dma_start(out=self.scr, in_=fm[:]), self.slot("d_scr"), waits=[t_c1, t_c2])
            from concourse.ap import AP as _AP
            scr_t = self.scr.tensor
            s_tp = self.slot("d_tp")
            tl = None
            for ty in range(2):
                for h in range(8):
                    src = _AP(scr_t, (h * 2 + ty) * 256, [[1, 128], [1, 128]])
                    tl = P.dma('sp', lambda e, ty=ty, h=h, src=src: e.dma_start(out=self.stage[:, ty, h, :], in_=src),
                               s_tp, waits=[t_sc])
            P.wait_only('sp', [tl])
            for _ in range(30):
                yield
            t_bt = P.op('dve', lambda e: e.tensor_copy(out=self.BT[:], in_=self.stage[:]), waits=[(s_tp.sem, s_tp.count)])
            stage2 = self.sb("stage2_s", [128, 8, 4], F32)
            stage3 = self.sb("stage3_s", [128, 4, 64], F32)
            s_tp2 = self.slot("d_tp2")
            for h in range(8):
                src = _AP(scr_t, (h * 2 + 1) * 256, [[1, 128], [1, 4]])
                P.dma('sp', lambda e, h=h, src=src: e.dma_start(out=stage2[:, h, :], in_=src), s_tp2, waits=[t_sc])
                src = _AP(scr_t, (h * 2 + 0) * 256 + 64, [[1, 64], [1, 64]])
                P.dma('sp', lambda e, h=h, src=src: e.dma_start(
                    out=stage3[64 * (h // 4):64 * (h // 4) + 64, h % 4, :], in_=src), s_tp2, waits=[t_sc])
            tk2 = (s_tp2.sem, s_tp2.count)
            tb = []
            for g in range(2):
                tb.append(P.op('dve', lambda e, g=g: e.tensor_copy(
                    out=self.BTsc[:, g, :, :, :], in_=stage2[:, 4 * g:4 * g + 4, :].unsqueeze(1).to_broadcast([128, 16, 4, 4])),
                    waits=[tk2]))
            tb.append(P.op('dve', lambda e: e.tensor_tensor(
                out=self.BTsn[:], in0=stage3[:], in1=blk_s[:].unsqueeze(1).to_broadcast([128, 4, 64]), op=ALU.add),
                waits=[tk2, t3]))
            self.t_tables_all = tb
            self.t_tables = [t_bt] + tb
            yield
        self._tables_gen = tables_gen
        if self.debug.get('ckpt', 99) < 6:
            self.out_toks = []
            return
        self.t_ident = P.dma('pool', lambda e: e.dma_start(out=self.ident[:], in_=d["cst_ident"]), self.slot("d_id"))
        self.ssm_alloc()
        s_cp = self.slot("d_cp")
        P.dma('sp', lambda e: e.dma_start(out=d["ks"][:, 0:124, :], in_=d["cK_nat"][:, 4:128, :]), s_cp)
        self.out_toks = [(s_cp.sem, s_cp.count)]
        if self.debug.get('cp', 2) > 1:
            s_cp2 = self.slot("d_cp2")
            self.out_toks.append(P.dma('sp', lambda e: e.dma_start(out=d["vs"][:, 0:124, :], in_=d["cV_nat"][:, 4:128, :]), s_cp2))

    def mixer(self):
        nc, P, d = self.nc, self.P, self.dram
        xT, hT = self.xT, self.hT
        dbg = self.debug
        if dbg.get("setup_only"):
            return
        self.bg_pump(10 ** 9)
        for ti_ in range(3):
            self.norm_tile(1, ti_)
        mix_normed = [3]
        win = d["win"]
        pcnt = 0
        tok_q = {}
        tok_k = {}
        tok_u = {}
        t_last = None
        def u_group(kt, ti, wv, t_w, b):
            t0, n = TT[ti]
            pp = self.ps[b]
            t_p = None
            for kc in range(KC):
                t_p = P.op('pe', lambda e, kc=kc, t0=t0, n=n, pp=pp, wv=wv: e.matmul(
                    pp[:, :n], wv[:, kc, :], hT[:, kc, t0:t0 + n], start=(kc == 0), stop=(kc == KC - 1)),
                    waits=([t_w, self.ps_rd[b], self.aT_rd] if kc == 0 else []) + [self.tok_h[(kc, ti)]],
                    signal=(kc == KC - 1))
            if ti % 2 == 0:
                t_e = P.op('dve', lambda e, kt=kt, t0=t0, n=n, pp=pp: e.tensor_copy(
                    out=self.uT[:, kt, t0:t0 + n], in_=pp[:, :n]), waits=[t_p])
            else:
                t_e = P.op('act', lambda e, kt=kt, t0=t0, n=n, pp=pp: e.activation(
                    out=self.uT[:, kt, t0:t0 + n], in_=pp[:, :n], func=AF.Copy), waits=[t_p])
            tok_u[(kt, ti)] = t_e
            self.ps_rd[b] = t_e
            return t_p

        for ch in [0, 1, 2, 3, 4]:
            wv, t_w, wi = self.ws_load(win[:, :, ch * 128:(ch + 1) * 128], lambda t: t[:, 0:1024].rearrange("p (k n) -> p k n", k=KC))
            for ti, (t0, n) in enumerate(TT):
                if mix_normed[0] < len(TT):
                    self.norm_tile(1, mix_normed[0])
                    mix_normed[0] += 1
                b = pcnt % 4
                pcnt += 1
                pp = self.ps[b]
                for kc in range(KC):
                    t_p = P.op('pe', lambda e, kc=kc, t0=t0, n=n, pp=pp, wv=wv: e.matmul(
                        pp[:, :n], wv[:, kc, :], hT[:, kc, t0:t0 + n], start=(kc == 0), stop=(kc == KC - 1)),
                        waits=([t_w, self.ps_rd[b], self.aT_rd] if kc == 0 else []) + [self.tok_h[(kc, ti)]],
                        signal=(kc == KC - 1))
                t_last = t_p
                if ch < 4:
                    t_e = P.op('act', lambda e, ch=ch, t0=t0, n=n, pp=pp: e.activation(
                        out=self.qT2[:, ch, t0:t0 + n], in_=pp[:, :n], func=AF.Copy, scale=0.125), waits=[t_p])
                    tok_q[(ch, ti)] = t_e
                elif ch == 4:
                    t_e = P.op('dve', lambda e, t0=t0, n=n, pp=pp: e.tensor_copy(
                        out=self.kT[:, t0:t0 + n], in_=pp[:, :n]), waits=[t_p])
                    tok_k[ti] = t_e
                else:
                    kt = ch - 6
                    eng = 'dve' if (ti % 2 == 0) else 'act'
                    if eng == 'dve':
                        t_e = P.op('dve', lambda e, kt=kt, t0=t0, n=n, pp=pp: e.tensor_copy(
                            out=self.uT[:, kt, t0:t0 + n], in_=pp[:, :n]), waits=[t_p])
                    else:
                        t_e = P.op('act', lambda e, kt=kt, t0=t0, n=n, pp=pp: e.activation(
                            out=self.uT[:, kt, t0:t0 + n], in_=pp[:, :n], func=AF.Copy), waits=[t_p])
                    tok_u[(kt, ti)] = t_e
                self.ps_rd[b] = t_e
            self.ws_release(wi, t_last)
        if dbg.get('mix_stop', 99) <= 1:
            self.hT_free = P.last['pe']
            return
        wv, t_w, wi = self.ws_load(win[:, :, 512:768], lambda t: t[:, 0:2048].rearrange("p (k n) -> p k n", k=KC))
        tok_v = {}
        kv_tok = {}
        for blk in range(17):
            t0 = blk * 128
            n = 128 if blk < 16 else 64
            full = blk >= 15
            b = 4 + (blk % 2)
            pp = self.ps[b]
            c0 = 0 if full else 128
            for kc in range(KC):
                t_p = P.op('pe', lambda e, kc=kc, t0=t0, n=n, pp=pp, wv=wv, c0=c0: e.matmul(
                    pp[0:n, c0:256], hT[:, kc, t0:t0 + n], wv[:, kc, c0:256], start=(kc == 0), stop=(kc == KC - 1)),
                    waits=([t_w, self.ps_rd[b]] if kc == 0 else []) + [self.tok_h[(kc, min(blk // 4, 4))]],
                    signal=(kc == KC - 1))
            t_last = t_p
            t_e = P.op('dve', lambda e, blk=blk, n=n, pp=pp: e.tensor_copy(
                out=self.vtok[0:n, blk, :], in_=pp[0:n, 128:256]), waits=[t_p])
            tok_v[blk] = t_e
            if full:
                ko = self.kvo[blk - 15]
                t_e = P.op('act', lambda e, n=n, pp=pp, ko=ko: e.activation(
                    out=ko[0:n, :], in_=pp[0:n, 0:256], func=AF.Copy), waits=[t_p, t_e, self.sq_rd[blk - 15]])
                kv_tok[blk] = t_e
            self.ps_rd[b] = t_e
        self.ws_release(wi, t_last)
        self.hT_free = t_last
        if dbg.get('mix_stop', 99) <= 2:
            self.hT_free = P.last['pe']
            return
        s_kv = self.slot("d_kvo")
        P.dma('sp', lambda e: e.dma_start(out=d["kp"], in_=self.kvo[0][:, 0:128]), s_kv, waits=[kv_tok[15]])
        P.dma('sp', lambda e: e.dma_start(out=d["vp"], in_=self.kvo[0][:, 128:256]), s_kv, waits=[kv_tok[15]])
        for b_ in range(16):
            P.dma('sp', lambda e, b_=b_: e.dma_start(
                out=d["ks"][b_, 124:128, :], in_=self.kvo[1][4 * b_:4 * b_ + 4, 0:128]), s_kv, waits=[kv_tok[16]])
            P.dma('sp', lambda e, b_=b_: e.dma_start(
                out=d["vs"][b_, 124:128, :], in_=self.kvo[1][4 * b_:4 * b_ + 4, 128:256]), s_kv, waits=[kv_tok[16]])
        self.out_toks.append((s_kv.sem, s_kv.count))
        self.sq_rd = [(s_kv.sem, s_kv.count), (s_kv.sem, s_kv.count)]

        if dbg.get('mix_stop', 99) <= 3:
            self.hT_free = P.last['pe']
            return
        qT2, kT, vtok = self.qT2, self.kT, self.vtok
        attn_tok = {}
        pcnt = [0]
        pT_rd = [self.sg_rd[0], self.sg_rd[0], self.sg_rd[1], self.sg_rd[1]]
        tq_all = lambda ti: [tok_q[(r, ti)] for r in range(4)]
        self.tok_u = tok_u
        self.ssm_y_tok = {}

        def attn_block(n_):
            ti = n_ // 4
            q0 = n_ * 128
            kbs = ([(n_ - 1, 1)] if n_ > 0 else []) + [(n_, 0)]
            pts = {}
            for g in range(2):
                gs = slice(64 * g, 64 * g + 64)
                for (kb, ty) in kbs:
                    sl = pcnt[0] % 4
                    bk = pcnt[0] % 2
                    pcnt[0] += 1
                    sb_ = self.ps[bk]
                    P.op('pe', lambda e, sb_=sb_, ty=ty, g=g: e.matmul(
                        sb_[:, :], self.ident[:], self.BT[:, ty, 4 * g:4 * g + 4, :], start=True, stop=False),
                        waits=[self.ps_rd[bk], self.t_ident] + self.t_tables, signal=False)
                    t_s = P.op('pe', lambda e, sb_=sb_, gs=gs, kb=kb, q0=q0: e.matmul(
                        sb_[:, :], kT[gs, kb * 128:(kb + 1) * 128], qT2[gs, :, q0:q0 + 128], start=False, stop=True),
                        waits=tq_all(ti) + [tok_k[ti], tok_k[kb // 4]])
                    t_e = P.op('act', lambda e, sb_=sb_, sl=sl: e.activation(
                        out=self.pT[sl], in_=sb_[:, :], func=AF.Exp), waits=[t_s, pT_rd[sl]])
                    self.ps_rd[bk] = t_e
                    pts[(g, kb)] = (sl, t_e)
            ob, db = 2, 3
            ops_, dps_ = self.ps[ob], self.ps[db]
            for g in range(2):
                gs = slice(64 * g, 64 * g + 64)
                for i, (kb, ty) in enumerate(kbs):
                    sl, t_e = pts[(g, kb)]
                    P.op('pe', lambda e, ops_=ops_, gs=gs, kb=kb, sl=sl, i=i, nkb=len(kbs): e.matmul(
                        ops_[gs, :], vtok[:, kb, gs], self.pT[sl], start=(i == 0), stop=(i == nkb - 1)),
                        waits=[t_e, tok_v[kb], self.ps_rd[ob]], signal=False)
                for i, (kb, ty) in enumerate(kbs):
                    sl, t_e = pts[(g, kb)]
                    t_d = P.op('pe', lambda e, dps_=dps_, gs=gs, sl=sl, i=i, nkb=len(kbs): e.matmul(
                        dps_[gs, :], self.ones1[:, :], self.pT[sl], start=(i == 0), stop=(i == nkb - 1)),
                        waits=[self.ps_rd[db], self.t_ones1], signal=(i == len(kbs) - 1))
                for (kb, ty) in kbs:
                    pT_rd[pts[(g, kb)][0]] = t_d
            rb = n_ % 2
            rt = self.rt[rb]
            for r_ in range(4):
                t_1 = P.op('act', lambda e, dps_=dps_, rt=rt, r_=r_: e.activation(
                    out=rt[:, 128 * r_:128 * r_ + 128], in_=dps_[:, 128 * r_:128 * r_ + 128], func=AF.Ln,
                    bias=self.ES[:, r_:r_ + 1], scale=1.0), waits=[t_d, self.t_es, self.rt_rd[rb]])
            t_2 = P.op('act', lambda e, rt=rt: e.activation(out=rt[:], in_=rt[:], func=AF.Exp, scale=-1.0), waits=[t_1])
            self.ps_rd[db] = t_1

            def back():
                t_3 = P.op('dve', lambda e, ops_=ops_, rt=rt, q0=q0: e.tensor_tensor(
                    out=qT2[:, :, q0:q0 + 128], in0=ops_[:].rearrange("p (r q) -> p r q", r=4),
                    in1=rt[:].rearrange("p (r q) -> p r q", r=4), op=ALU.mult), waits=[t_2, t_d])
                self.rt_rd[rb] = t_3
                self.ps_rd[ob] = t_3
                attn_tok[n_] = t_3
            return back

        do_ssm = not dbg.get("skip_ssm")
        bi = 0
        ucnt = 0
        t_lastu = None
        for ch in (6, 7, 8, 9):
            wv, t_w, wi = self.ws_load(win[:, :, ch * 128:(ch + 1) * 128], lambda t: t[:, 0:1024].rearrange("p (k n) -> p k n", k=KC))
            for ti in range(len(TT)):
                t_lastu = u_group(ch - 6, ti, wv, t_w, 4 + ucnt % 4)
                ucnt += 1
                if bi < 16:
                    attn_block(bi)()
                    bi += 1
            self.ws_release(wi, t_lastu)
        while bi < 16:
            attn_block(bi)()
            bi += 1
        self.hT_free = t_lastu
        if do_ssm:
            self.ssm_begin()
        if do_ssm:
            self.ssm_pads_bz(0)
            self.ssm_pads_cp(0)
            self.ssm_tables(0)
            self.ssm_tables(1)
            self.ssm_z(0)
        for i_ in range(16):
            if do_ssm and i_ + 2 < 16:
                self.ssm_tables(i_ + 2)
            if do_ssm:
                kt_, pl_ = i_ // 4, i_ % 4
                if pl_ == 2 and kt_ > 0:
                    self.ssm_y(kt_ - 1, (0,), True)
                self.ssm_main(i_)
                if pl_ == 3:
                    self.ssm_sample(kt_)
                if i_ < 15:
                    self.ssm_z(i_ + 1)
                if pl_ == 2 and kt_ < 3:
                    self.ssm_pads_bz(kt_ + 1)
                if pl_ == 3:
                    self.ssm_y(kt_, (3,), False)
                    if kt_ == 3:
                        self.ssm_y(kt_, (2,), False)
                        self.ssm_y(kt_, (1, 0), True)
                if pl_ == 0 and kt_ > 0:
                    self.ssm_y(kt_ - 1, (2,), False)
                if pl_ == 1 and kt_ > 0:
                    self.ssm_y(kt_ - 1, (1,), False)
                if pl_ == 2 and kt_ > 0:
                    self.ssm_pads_cp(kt_)
        if do_ssm:
            t_g = P.op('dve', lambda e: e.memset(self.sgc2[:], 0.0), waits=[self.S['dmp'], self.S['dmc']])
            self.rt_rd = [t_g, t_g]
        if do_ssm:
            self.ssm_glu()
        wck = [self.S['pad_bz_rd'], self.S['pad_cp_rd']] if do_ssm else []
        self.t_ckt = P.dma('pool', lambda e: e.dma_start(out=self.cKT[:], in_=d["cKT"]), self.slot("d_ckt"), waits=wck)
        self.t_cv = P.dma('pool', lambda e: e.dma_start(out=self.cV[:], in_=d["cV"]), self.slot("d_cv"), waits=wck)
        if dbg.get('mix_stop', 99) <= 4:
            self.hT_free = P.last['pe']
            return
        S0 = SEQ
        pTc = [self.pT[0], self.pT[1]]
        pTn = [self.pT[2], self.pT[3]]
        te = {}
        for g in range(2):
            gs = slice(64 * g, 64 * g + 64)
            sc, sn = self.ps[2 * g], self.ps[2 * g + 1]
            P.op('pe', lambda e, sc=sc, g=g: e.matmul(
                sc[:, 0:256], self.ident[:], self.BTsc[:, g, :, :].rearrange("p b r t -> p (b r t)"), start=True, stop=False),
                waits=[self.ps_rd[2 * g], self.t_ident] + self.t_tables, signal=False)
            for b in range(16):
                t_s = P.op('pe', lambda e, sc=sc, gs=gs, b=b: e.matmul(
                    sc[:, 16 * b:16 * b + 16], self.cKT[gs, b, :], qT2[gs, :, S0 + 4 * b:S0 + 4 * b + 4],
                    start=False, stop=(b == 15)),
                    waits=tq_all(4) + [self.t_ckt], signal=(b == 15))
            t_e = P.op('act', lambda e, sc=sc, g=g: e.activation(
                out=pTc[g][:, 0:256], in_=sc[:, 0:256], func=AF.Exp), waits=[t_s, pT_rd[g]])
            self.ps_rd[2 * g] = t_e
            te[(g, 'c')] = t_e
            if dbg.get('sa', 9) < 2:
                continue
            P.op('pe', lambda e, sn=sn, g=g, gs=gs: e.matmul(
                sn[0:64, 0:256], self.ident[gs, 64 - 64 * g:128 - 64 * g], self.BTsn[gs, :, :],
                start=True, stop=(dbg.get('sa2', 9) < 2)),
                waits=[self.ps_rd[2 * g + 1]], signal=False)
            if dbg.get('sa2', 9) < 2:
                continue
            t_s = P.op('pe', lambda e, sn=sn, gs=gs: e.matmul(
                sn[0:64, 0:256], kT[gs, S0:S0 + 64], qT2[gs, :, S0:S0 + 64], start=False, stop=True),
                waits=[tok_k[4]])
            if dbg.get('sa2', 9) < 3:
                continue
            t_e = P.op('act', lambda e, sn=sn, g=g: e.activation(
                out=pTn[g][0:64, 0:256].rearrange("p (b r t) -> p r b t", b=16, r=4),
                in_=sn[0:64, 0:256].rearrange("p (r b t) -> p r b t", r=4, b=16), func=AF.Exp),
                waits=[t_s, pT_rd[2 + g]])
            self.ps_rd[2 * g + 1] = t_e
            te[(g, 'n')] = t_e
        ob, db = 0, 1
        ops_, dps_ = self.ps[ob], self.ps[db]
        if dbg.get('sa', 9) < 3:
            self.hT_free = P.last['pe']
            return
        for g in range(2):
            gs = slice(64 * g, 64 * g + 64)
            P.op('pe', lambda e, gs=gs, g=g: e.matmul(
                ops_[gs, 0:256], vtok[0:64, 16, gs], pTn[g][0:64, 0:256], start=True, stop=False),
                waits=[te[(g, 'n')], te[(g, 'c')], tok_v[16], self.ps_rd[ob], self.t_cv], signal=False)
            for b in range(16):
                P.op('pe', lambda e, gs=gs, g=g, b=b: e.matmul(
                    ops_[gs, 16 * b:16 * b + 16], self.cV[:, b, gs], pTc[g][:, 16 * b:16 * b + 16],
                    start=False, stop=(b == 15)), signal=False)
            P.op('pe', lambda e, gs=gs, g=g: e.matmul(
                dps_[gs, 0:256], self.ones1[0:64, :], pTn[g][0:64, 0:256], start=True, stop=False),
                waits=[self.ps_rd[db]], signal=False)
            t_d = P.op('pe', lambda e, gs=gs, g=g: e.matmul(
                dps_[gs, 0:256], self.ones1[:, :], pTc[g][:, 0:256], start=False, stop=True))
        if dbg.get('sa', 9) < 4:
            self.hT_free = P.last['pe']
            return
        rt = self.rt[0]
        v4 = lambda ap: ap.rearrange("p (b r t) -> p b r t", b=16, r=4)
        t_1 = P.op('dve', lambda e: e.tensor_tensor(
            out=v4(rt[:, 0:256]), in0=v4(dps_[:, 0:256]),
            in1=self.ES[:].unsqueeze(1).unsqueeze(3).to_broadcast([128, 16, 4, 4]), op=ALU.add),
            waits=[t_d, self.t_es, self.rt_rd[0]])
        t_2 = P.op('act', lambda e: e.activation(out=rt[:, 0:256], in_=rt[:, 0:256], func=AF.Ln), waits=[t_1])
        t_2 = P.op('act', lambda e: e.activation(out=rt[:, 0:256], in_=rt[:, 0:256], func=AF.Exp, scale=-1.0), waits=[t_2])
        t_3 = P.op('dve', lambda e: e.tensor_tensor(
            out=qT2[:, :, S0:S0 + 64].rearrange("p r (b t) -> p b r t", t=4), in0=v4(ops_[:, 0:256]),
            in1=v4(rt[:, 0:256]), op=ALU.mult), waits=[t_2, t_d])
        self.rt_rd[0] = t_3
        self.ps_rd[ob] = t_3
        self.ps_rd[db] = t_1
        attn_tok[16] = t_3
        self.attn_tok = attn_tok
        self.tok_u = tok_u

        if dbg.get('mix_stop', 99) <= 5:
            self.hT_free = P.last['pe']
            return
        ssm_tok = self.ssm_tok if do_ssm else None

        if dbg.get("dump_attn"):
            s_dbg = self.slot("d_dbg")
            tk = P.dma('sp', lambda e: e.dma_start(out=d["dbg_attn"], in_=qT2), s_dbg,
                       waits=[attn_tok[i] for i in range(17)])
            self.out_toks.append(tk)
            self.dbg_tok = tk

        nk = 4 if ssm_tok is None else 8
        ycnt = 0
        for c in range(KC):
            wv, t_w, wi = self.ws_load(d["wout"][c], lambda t: t[:, 0:1024].rearrange("p (k n) -> p k n", k=8))
            for ti, (t0, n) in enumerate(TT):
                yb = 4 + (ycnt % 2)
                ycnt += 1
                yps = self.ps[yb]
                blks = range(ti * 4, ti * 4 + 4) if ti < 4 else [16]
                for i in range(nk):
                    rhs = qT2[:, i, t0:t0 + n] if i < 4 else self.ssmT[:, i - 4, t0:t0 + n]
                    w_ = ([t_w, self.ps_rd[yb]] + [attn_tok[b_] for b_ in blks]) if i == 0 else []
                    if i == 4:
                        w_ = w_ + [ssm_tok[(k_, ti)] for k_ in range(4)]
                    t_y = P.op('pe', lambda e, i=i, n=n, yps=yps, wv=wv, rhs=rhs: e.matmul(
                        yps[:, :n], wv[:, i, :], rhs, start=(i == 0), stop=(i == nk - 1)),
                        waits=w_, signal=(i == nk - 1))
                t_x = P.op('dve', lambda e, c=c, t0=t0, n=n, yps=yps: e.tensor_tensor(
                    out=xT[:, c, t0:t0 + n], in0=yps[:, :n], in1=xT[:, c, t0:t0 + n], op=ALU.add),
                    waits=[t_y, self.tok_x[(c, ti)]])
                self.tok_x[(c, ti)] = t_x
                self.ps_rd[yb] = t_x
                if c == KC - 1:
                    self.norm_tile(2, ti)
            self.ws_release(wi, t_y)
        self.normed_upto = {2: len(TT)}
        self.aT_rd = t_y
        self.sg_rd = [t_y, t_y]
        self.mix_end_tok = t_y


    def ssm_io(self):
        self.din("ssm_pm", [128, 3, 16])
        self.din("ssm_cm", [128, 3, 4, 64])
        self.din("c_pm", [128, 2, 16, 16])
        self.din("b_pm", [128, 2, 16, 16])
        self.din("b_cm", [128, 2, 4, 64])
        self.din("dskip", [128, 4])
        self.din("bglu", [128, 4])
        self.din("wglu", [128, 4, 512])
        self.din("cst_m2", [128, 4, 8])
        self.din("cst_m3", [128, 4, 2])
        self.din("cst_eye", [128, 128])
        self.din("x0", [128, 16, 2, 16])
        self.dout("st_p", [128, 16, 2])
        self.dout("st_s", [128, 16, 2, 16])

    def ssm_alloc(self):
        sb = self.sb
        self.pm = {}
        for nm in ["L1r", "L1i", "L2r", "L2i", "L3r", "L3i", "L4r", "L4i", "cr", "ci", "rho4", "t0", "t1", "t2", "t3",
                   "t4", "t5", "turns"]:
            self.pm[nm] = sb("pm_" + nm, [128, 16], F32)
        self.pm_in = sb("pm_in", [128, 3, 16], F32)
        self.pm_i = sb("pm_i", [128, 16], I32)
        self.phi2 = sb("pm_phi2", [128, 16], I32)
        self.q30 = sb("q30", [128, 1], I32)
        self.CpC = sb("CpC", [128, 16, 5, 2, 16], BF16)
        self.jota = sb("jota", [128, 512], I32)
        self.m2 = sb("m2", [128, 4, 8], F32)
        self.m3 = sb("m3", [128, 4, 2], F32)
        self.eye = sb("eye", [128, 128], F32)
        self.dsk = sb("dsk", [128, 4], F32)
        self.bgl = sb("bgl", [128, 4], F32)
        self.x0 = sb("x0_s", [128, 16, 2, 16], F32)
        self.stp = sb("stp_s", [128, 16, 2], F32)
        self.sgc = sb("sgc", [128, 1], F32)
        self.sgc2 = sb("sgc2", [128, 1], F32)
        self.XbS = sb("XbS", [128, 2, 4, 2, 16], BF16)
        self.BzC = self.cKT[:].rearrange("p b n -> p (b n)").rearrange("p (k s r q) -> p k s r q", k=4, s=4, r=2)
        self.Kin = self.cV[:].rearrange("p b n -> p (b n)").rearrange("p (k t n) -> p k t n", k=4, t=4)

    def ssm_setup_gen(self, which):
        P, d = self.P, self.dram
        pm = self.pm
        prev = [None]
        self.kin_rd = getattr(self, 'kin_rd', [None] * 4)
        TWO_PI = 2.0 * np.pi
        hf = self.hT[:].rearrange("p k t -> p (k t)").bitcast(F32)

        def V(fn, extra=()):
            prev[0] = P.op('dve', fn, waits=[prev[0]] + list(extra))

        def A(fn, extra=()):
            prev[0] = P.op('act', fn, waits=[prev[0]] + list(extra))

        def lam(aR, aI, ldt, T, is_pm):
            yield A(lambda e: e.activation(out=ldt, in_=ldt, func=AF.Exp))
            yield V(lambda e: e.tensor_tensor(out=T['xr'], in0=aR, in1=ldt, op=ALU.mult))
            yield V(lambda e: e.tensor_tensor(out=T['xi'], in0=aI, in1=ldt, op=ALU.mult))
            yield A(lambda e: e.activation(out=T['mag'], in_=T['xr'], func=AF.Exp))
            yield V(lambda e: e.tensor_scalar(out=T['xi'], in0=T['xi'], scalar1=1.0 / TWO_PI, scalar2=None, op0=ALU.mult))
            if is_pm:
                yield V(lambda e: e.tensor_copy(out=pm['turns'][:], in_=T['xi']))
                yield A(lambda e: e.activation(out=pm['rho4'][:], in_=T['xr'], func=AF.Exp, scale=4.0))
            yield V(lambda e: e.tensor_copy(out=T['ni'], in_=T['xi']))
            yield V(lambda e: e.tensor_copy(out=T['nf'], in_=T['ni']))
            yield V(lambda e: e.tensor_tensor(out=T['xi'], in0=T['xi'], in1=T['nf'], op=ALU.subtract))
            yield A(lambda e: e.activation(out=T['s1'], in_=T['xi'], func=AF.Sin, scale=TWO_PI))
            yield A(lambda e: e.activation(out=T['sh'], in_=T['xi'], func=AF.Sin, scale=float(np.pi)))
            yield V(lambda e: e.tensor_tensor(out=T['sh'], in0=T['sh'], in1=T['sh'], op=ALU.mult))
            yield V(lambda e: e.tensor_scalar(out=T['sh'], in0=T['sh'], scalar1=-2.0, scalar2=1.0, op0=ALU.mult, op1=ALU.add))
            yield V(lambda e: e.tensor_tensor(out=T['L1r'], in0=T['mag'], in1=T['sh'], op=ALU.mult))
            yield V(lambda e: e.tensor_tensor(out=T['L1i'], in0=T['mag'], in1=T['s1'], op=ALU.mult))
            yield V(lambda e: e.tensor_scalar(out=T['mag'], in0=T['L1r'], scalar1=-1.0, scalar2=None, op0=ALU.add))
            yield V(lambda e: e.tensor_tensor(out=T['xr'], in0=aR, in1=aR, op=ALU.mult))
            yield V(lambda e: e.tensor_tensor(out=T['xi'], in0=aI, in1=aI, op=ALU.mult))
            yield V(lambda e: e.tensor_tensor(out=T['xr'], in0=T['xr'], in1=T['xi'], op=ALU.add))
            yield V(lambda e: e.reciprocal(out=T['xr'], in_=T['xr']))
            yield V(lambda e: e.tensor_tensor(out=T['xi'], in0=T['mag'], in1=aR, op=ALU.mult))
            yield V(lambda e: e.tensor_tensor(out=T['s1'], in0=T['L1i'], in1=aI, op=ALU.mult))
            yield V(lambda e: e.tensor_tensor(out=T['xi'], in0=T['xi'], in1=T['s1'], op=ALU.add))
            yield V(lambda e: e.tensor_tensor(out=T['cr'], in0=T['xi'], in1=T['xr'], op=ALU.mult))
            yield V(lambda e: e.tensor_tensor(out=T['xi'], in0=T['L1i'], in1=aR, op=ALU.mult))
            yield V(lambda e: e.tensor_tensor(out=T['s1'], in0=T['mag'], in1=aI, op=ALU.mult))
            yield V(lambda e: e.tensor_tensor(out=T['xi'], in0=T['xi'], in1=T['s1'], op=ALU.subtract))
            yield V(lambda e: e.tensor_tensor(out=T['ci'], in0=T['xi'], in1=T['xr'], op=ALU.mult))

        def cmul(o_r, o_i, a_r, a_i, b_r, b_i, t1, t2):
            yield V(lambda e: e.tensor_tensor(out=t1, in0=a_r, in1=b_r, op=ALU.mult))
            yield V(lambda e: e.tensor_tensor(out=t2, in0=a_i, in1=b_i, op=ALU.mult))
            yield V(lambda e: e.tensor_tensor(out=o_r, in0=t1, in1=t2, op=ALU.subtract))
            yield V(lambda e: e.tensor_tensor(out=t1, in0=a_r, in1=b_i, op=ALU.mult))
            yield V(lambda e: e.tensor_tensor(out=t2, in0=a_i, in1=b_r, op=ALU.mult))
            yield V(lambda e: e.tensor_tensor(out=o_i, in0=t1, in1=t2, op=ALU.add))

        if which == 1:
            prev[0] = None
            st = self.stage[:].rearrange("p a h q -> p (a h q)")
            s = self.slot("d_ssm_in")
            for (dst, src_) in [(self.pm_in[:], d["ssm_pm"]), (self.m2[:], d["cst_m2"]), (self.m3[:], d["cst_m3"]),
                                (self.eye[:], d["cst_eye"]), (self.dsk[:], d["dskip"]), (self.bgl[:], d["bglu"]),
                                (self.x0[:], d["x0"])]:
                t_in = P.dma('sp', lambda e, dst=dst, src_=src_: e.dma_start(out=dst, in_=src_), s)
            self.t_x0 = t_in
            cpm = st[:, 0:512].rearrange("p (r a c) -> p r a c", r=2, a=16)
            t_in2 = P.dma('sp', lambda e: e.dma_start(out=cpm, in_=d["c_pm"]), self.slot("d_ssm_in2"), waits=self.t_tables)
            for _ in range(8):
                yield
            yield V(lambda e: e.memset(self.sgc[:], TWO_PI / 2.0 ** 32), extra=[t_in, t_in2] + self.t_tables)
            yield V(lambda e: e.memset(self.sgc2[:], TWO_PI / 2.0 ** 33))
            aR, aI, ldt = self.pm_in[:, 0, :], self.pm_in[:, 1, :], self.pm_in[:, 2, :]
            T = {'xr': pm['t0'][:], 'xi': pm['t1'][:], 'mag': pm['t2'][:], 'ni': self.pm_i[:], 'nf': pm['t3'][:],
                 's1': pm['t4'][:], 'sh': pm['t5'][:], 'L1r': pm['L1r'][:], 'L1i': pm['L1i'][:], 'cr': pm['cr'][:], 'ci': pm['ci'][:]}
            yield from lam(aR, aI, ldt, T, True)
            t1_, t2_ = pm['t0'][:], pm['t1'][:]
            yield from cmul(pm['L2r'][:], pm['L2i'][:], pm['L1r'][:], pm['L1i'][:], pm['L1r'][:], pm['L1i'][:], t1_, t2_)
            yield from cmul(pm['L3r'][:], pm['L3i'][:], pm['L2r'][:], pm['L2i'][:], pm['L1r'][:], pm['L1i'][:], t1_, t2_)
            yield from cmul(pm['L4r'][:], pm['L4i'][:], pm['L2r'][:], pm['L2i'][:], pm['L2r'][:], pm['L2i'][:], t1_, t2_)
            tu = pm['turns'][:]
            yield V(lambda e: e.tensor_scalar(out=tu, in0=tu, scalar1=4.0, scalar2=None, op0=ALU.mult))
            yield V(lambda e: e.tensor_copy(out=self.pm_i[:], in_=tu))
            yield V(lambda e: e.tensor_copy(out=pm['t3'][:], in_=self.pm_i[:]))
            yield V(lambda e: e.tensor_tensor(out=tu, in0=tu, in1=pm['t3'][:], op=ALU.subtract))
            yield V(lambda e: e.tensor_scalar(out=tu, in0=tu, scalar1=4294967040.0, scalar2=None, op0=ALU.mult))
            yield V(lambda e: e.tensor_copy(out=self.phi2[:], in_=tu))
            cR, cI = cpm[:, 0, :, :], cpm[:, 1, :, :]
            w1 = st[:, 512:768].rearrange("p (a c) -> p a c", a=16)
            w2 = st[:, 768:1024].rearrange("p (a c) -> p a c", a=16)
            bc = lambda ap: ap.unsqueeze(2).to_broadcast([128, 16, 16])
            yield V(lambda e: e.tensor_copy(out=self.CpC[:, :, 0, 0, :], in_=cR))
            yield V(lambda e: e.tensor_scalar(out=self.CpC[:, :, 0, 1, :], in0=cI, scalar1=-1.0, scalar2=None, op0=ALU.mult))
            for k in range(1, 5):
                Lr, Li = pm['L%dr' % k][:], pm['L%di' % k][:]
                yield V(lambda e, Lr=Lr: e.tensor_tensor(out=w1, in0=cR, in1=bc(Lr), op=ALU.mult))
                yield V(lambda e, Li=Li: e.tensor_tensor(out=w2, in0=cI, in1=bc(Li), op=ALU.mult))
                yield V(lambda e, k=k: e.tensor_tensor(out=self.CpC[:, :, k, 0, :], in0=w1, in1=w2, op=ALU.subtract))
                yield V(lambda e, Li=Li: e.tensor_tensor(out=w1, in0=cR, in1=bc(Li), op=ALU.mult))
                yield V(lambda e, Lr=Lr: e.tensor_tensor(out=w2, in0=cI, in1=bc(Lr), op=ALU.mult))
                yield V(lambda e, k=k: e.scalar_tensor_tensor(out=self.CpC[:, :, k, 1, :], in0=w1, scalar=-1.0, in1=w2,
                                                              op0=ALU.mult, op1=ALU.subtract))
            self.t_pm_done = prev[0]
            for hk in range(2):
                cm_in = st[:, 0:384].rearrange("p (q k n) -> p q k n", q=3, k=2)
                bcm = st[:, 384:640].rearrange("p (r k n) -> p r k n", r=2, k=2)
                ct = [st[:, 640 + 128 * i:768 + 128 * i].rearrange("p (k n) -> p k n", k=2) for i in range(10)]
                cti = st[:, 1920:2048].bitcast(I32).rearrange("p (k n) -> p k n", k=2)
                sl_ = self.slot("d_ssm_cm%d" % hk)
                P.dma('sp', lambda e, hk=hk, cm_in=cm_in: e.dma_start(out=cm_in, in_=d["ssm_cm"][:, :, 2 * hk:2 * hk + 2, :]), sl_,
                      waits=[prev[0]])
                t_l = P.dma('sp', lambda e, hk=hk, bcm=bcm: e.dma_start(out=bcm, in_=d["b_cm"][:, :, 2 * hk:2 * hk + 2, :]), sl_,
                            waits=[prev[0]])
                prev[0] = t_l
                for _ in range(8):
                    yield
                aRc, aIc, ldc = cm_in[:, 0, :, :], cm_in[:, 1, :, :], cm_in[:, 2, :, :]
                Tc = {'xr': ct[0], 'xi': ct[1], 'mag': ct[2], 'ni': cti, 'nf': ct[3], 's1': ct[4], 'sh': ct[5],
                      'L1r': ct[6], 'L1i': ct[7], 'cr': ct[8], 'ci': ct[9]}
                yield from lam(aRc, aIc, ldc, Tc, False)
                bRc, bIc = bcm[:, 0, :, :], bcm[:, 1, :, :]
                c_r, c_i, u1, u2, u3, u4 = ct[0], ct[1], ct[2], ct[3], ct[4], ct[5]
                yield from cmul(c_r, c_i, ct[8], ct[9], bRc, bIc, u1, u2)
                for s_ in (3, 2, 1, 0):
                    yield V(lambda e, s_=s_, hk=hk, c_r=c_r: e.tensor_copy(out=self.BzC[:, 2 * hk:2 * hk + 2, s_, 0, :], in_=c_r))
                    yield V(lambda e, s_=s_, hk=hk, c_i=c_i: e.tensor_copy(out=self.BzC[:, 2 * hk:2 * hk + 2, s_, 1, :], in_=c_i))
                    if s_ > 0:
                        yield from cmul(u3, u4, ct[6], ct[7], c_r, c_i, u1, u2)
                        yield V(lambda e, c_r=c_r, u3=u3: e.tensor_copy(out=c_r, in_=u3))
                        yield V(lambda e, c_i=c_i, u4=u4: e.tensor_copy(out=c_i, in_=u4))
            self.t_cm_done = prev[0]
            yield
            return
        prev[0] = self.hT_free
        bpm = hf[:, 512:1024].rearrange("p (r a c) -> p r a c", r=2, a=16)
        t_in2 = P.dma('sp', lambda e: e.dma_start(out=bpm, in_=d["b_pm"]), self.slot("d_ssm_in3"), waits=[self.hT_free])
        for _ in range(9):
            yield
        yield V(lambda e: e.memset(self.q30[:], 1 << 30), extra=[t_in2, self.t_pm_done])
        w = [hf[:, 1024 + 256 * i:1280 + 256 * i].rearrange("p (a c) -> p a c", a=16) for i in range(4)]
        w1, w2, w3, w4 = w
        bc = lambda ap: ap.unsqueeze(2).to_broadcast([128, 16, 16])
        bR, bI = bpm[:, 0, :, :], bpm[:, 1, :, :]
        cur_r = hf[:, 2048:2304].rearrange("p (a c) -> p a c", a=16)
        cur_i = hf[:, 2304:2560].rearrange("p (a c) -> p a c", a=16)
        yield from cmul(cur_r, cur_i, bc(pm['cr'][:]), bc(pm['ci'][:]), bR, bI, w1, w2)
        BLc = hf[:, 2560:3584].bitcast(BF16).rearrange("p (a t r c) -> p a t r c", a=16, t=4, r=2)
        for tau in range(4):
            yield V(lambda e, tau=tau: e.tensor_copy(out=BLc[:, :, tau, 0, :], in_=cur_r))
            yield V(lambda e, tau=tau: e.tensor_copy(out=BLc[:, :, tau, 1, :], in_=cur_i))
            if tau < 3:
                yield from cmul(w3, w4, bc(pm['L1r'][:]), bc(pm['L1i'][:]), cur_r, cur_i, w1, w2)
                yield V(lambda e: e.tensor_copy(out=cur_r, in_=w3))
                yield V(lambda e: e.tensor_copy(out=cur_i, in_=w4))
        BLpads = [hf[:, 3584 + 512 * i:4096 + 512 * i].bitcast(BF16).rearrange("p (l r g c) -> p l r g c", l=4, r=2, g=8)
                  for i in range(2)]
        C0pads = [hf[:, 4608 + 512 * i:5120 + 512 * i].bitcast(BF16).rearrange("p (l r g c) -> p l r g c", l=4, r=2, g=8)
                  for i in range(2)]
        m2b = self.m2[:].unsqueeze(3).to_broadcast([128, 4, 8, 16])
        kb_ = 7
        pad_rd = [None, None]
        c0_rd = [None, None]
        cnt = 0
        pending = None
        for kt in range(4):
            C0pad = C0pads[kt % 2]
            for r in range(2):
                yield V(lambda e, kt=kt, r=r, C0pad=C0pad: e.tensor_tensor(
                    out=C0pad[:, :, r, :, :], in0=self.CpC[:, 4 * kt:4 * kt + 4, 0, r, :].unsqueeze(2).to_broadcast([128, 4, 8, 16]),
                    in1=m2b, op=ALU.mult), extra=[c0_rd[kt % 2]])
            for tau in range(4):
                BLpad = BLpads[cnt % 2]
                for r in range(2):
                    yield V(lambda e, kt=kt, r=r, tau=tau, BLpad=BLpad: e.tensor_tensor(
                        out=BLpad[:, :, r, :, :], in0=BLc[:, 4 * kt:4 * kt + 4, tau, r, :].unsqueeze(2).to_broadcast([128, 4, 8, 16]),
                        in1=m2b, op=ALU.mult), extra=[pad_rd[cnt % 2]])
                kb_ = 6 + cnt % 2
                kps = self.ps[kb_]
                col = 0
                t_mm = None
                for i, (pl, r) in enumerate([(pl, r) for pl in range(4) for r in range(2)]):
                    t_mm = P.op('pe', lambda e, pl=pl, r=r, i=i, kps=kps, col=col, BLpad=BLpad, C0pad=C0pad: e.matmul(
                        kps[:, col:col + 128], BLpad[:, pl, r, :, :].rearrange("p g c -> p (g c)"),
                        C0pad[:, pl, r, :, :].rearrange("p g c -> p (g c)"), start=(i == 0), stop=(i == 7)),
                        waits=[prev[0], self.ps_rd[kb_]] if i == 0 else [], signal=(i == 7))
                pad_rd[cnt % 2] = t_mm
                c0_rd[kt % 2] = t_mm
                if pending is not None:
                    yield self._kin_evac(pending, prev)
                pending = (kt, tau, col, t_mm, kb_)
                cnt += 1
                yield
        yield self._kin_evac(pending, prev)
        self.t_ssm_setup = prev[0]
        yield

    def _kin_evac(self, pending, prev):
        P = self.P
        kt, tau, col, t_mm, slot = pending
        kps = self.ps[slot]
        if tau == 0:
            t = P.op('dve', lambda e: e.scalar_tensor_tensor(
                out=self.Kin[:, kt, 0, :], in0=self.eye[:], scalar=self.dsk[:, kt:kt + 1], in1=kps[:, col:col + 128],
                op0=ALU.mult, op1=ALU.add), waits=[t_mm, prev[0]])
        else:
            t = P.op('dve', lambda e: e.tensor_copy(out=self.Kin[:, kt, tau, :], in_=kps[:, col:col + 128]), waits=[t_mm, prev[0]])
        prev[0] = t
        self.ps_rd[slot] = t
        return None

    def bg_pump(self, n):
        g = getattr(self, "bg", None)
        if g is None:
            return
        for _ in range(n):
            try:
                next(g)
            except StopIteration:
                self.bg = None
                return

    def ssm_begin(self):
        P = self.P
        N = self.WS_N
        i1, i2 = self.ws_next % N, (self.ws_next + 1) % N
        self.ws_next += 2
        self.cp_slots = (i1, i2)
        i3 = self.ws_next % N
        self.ws_next += 1
        self.tmp_slot = i3
        self.pA = self.ws[i3][:, 0:1024].bitcast(F32)
        self.pB = self.ws[i3][:, 1024:2048].bitcast(F32)
        self.t_tmp_free = self.ws_free[i3]
        self.CpPad = [self.ws[i][:, 0:2048].rearrange("p (l k r g c) -> p l k r g c", l=2, k=4, r=2, g=8) for i in (i1, i2)]
        stf = self.stage[:].rearrange("p a h q -> p (a h q)").bitcast(BF16)
        self.BzPad = stf.rearrange("p (l s r g q) -> p l s r g q", l=4, s=4, r=2, g=2)
        hb = self.hT[:].rearrange("p k t -> p (k t)")
        f = lambda a: hb[:, a:a + 1024].bitcast(F32)
        self.tabC = [f(0), f(2048), self.rt[0][:]]
        self.tabS = [f(1024), f(3072), self.rt[1][:]]
        self.ph = hb[:, 4096:5120].bitcast(I32)
        self.ph2 = hb[:, 5120:6144].bitcast(I32)
        self.ta, self.tb = f(6144), f(7168)
        self.Mb = [(f(8192), f(9216)), (f(10240), f(11264))]
        xmain = hb[:, 12288:12288 + 4112].rearrange("p (l r n) -> p l r n", l=4, r=2)
        spare = self.aT[:, 9:11, :].rearrange("p a t -> p (a t)")[:, 2176:4224]
        extra = [spare[:, 0:514], spare[:, 514:1028], spare[:, 1028:1542], hb[:, 5120:5634]]
        self.XbR = [[xmain[:, s, 0, :], xmain[:, s, 1, :]] for s in range(4)] + [[extra[0], extra[1]], [extra[2], extra[3]]]
        self.glu_tmp = hb[:, 0:4096].bitcast(F32).rearrange("p (o n) -> p o n", o=4)
        w0 = [self.hT_free, self.t_ssm_setup, self.t_cm_done]
        t_j = P.op('pool', lambda e: e.iota(self.jota[:], pattern=[[1, 512]], base=0, channel_multiplier=0))
        tz = []
        for k, i in enumerate((i1, i2)):
            tz.append(P.op('pool', lambda e, i=i: e.memset(self.ws[i][:, 0:2048], 0.0), waits=[self.ws_free[i]] + w0))
        for s_ in range(6):
            for r_ in range(2):
                tz.append(P.op('pool', lambda e, s_=s_, r_=r_: e.memset(self.XbR[s_][r_][:, 0:2], 0.0), waits=w0 + [self.aT_rd]))
        self.S = dict(t_j=t_j, tz=tz, ph_rd=None, dve=None, pool=tz[-1], dm_done=[[None], [None]], dm_tok={}, tab={}, dmc=None, dmp=None,
                      pad_bz_rd=None, pad_cp_rd=None, tz_tok={}, slot_rd=[None] * 6, xbs_rd=[None, None], xb_rd=None, zs_rd=None, gel=None, pend=[], xf_rd=None)
        self.ssm_tok = {}

    def ssm_tables(self, pr):
        P, S = self.P, self.S
        tb = pr % 3
        C, Sn = self.tabC[tb], self.tabS[tb]
        w0 = [self.hT_free, self.t_ssm_setup, self.t_cm_done]
        free_t = list(S['dm_tok'].get(pr - 3, [])) + ([self.rt_rd[0], self.rt_rd[1]] if tb == 2 else [])
        t_p1 = P.op('pool', lambda e, pr=pr: e.tensor_tensor(
            out=self.ph[:], in0=self.jota[:], in1=self.phi2[:, pr:pr + 1].to_broadcast([128, 512]), op=ALU.mult),
            waits=w0 + [S['t_j'], S['ph_rd'], S['pool']])
        S['pool'] = t_p1
        t_s = P.op('act', lambda e, Sn=Sn: e.activation(out=Sn, in_=self.ph[:], func=AF.Sin, scale=self.sgc[:, 0:1]),
                   waits=[t_p1] + free_t)
        t_c = P.op('act', lambda e, C=C: e.activation(out=C, in_=self.ph[:], func=AF.Sin, scale=self.sgc2[:, 0:1]),
                   waits=[t_p1] + free_t)
        t_c = P.op('act', lambda e, C=C: e.activation(out=C, in_=C, func=AF.Square), waits=[t_c])
        t_c = P.op('act', lambda e, C=C: e.activation(out=C, in_=C, func=AF.Copy, scale=-2.0, bias=1.0), waits=[t_c])
        S['ph_rd'] = t_c
        S['tab'][pr] = (t_s, t_c)

    def ssm_pads_bz(self, kt):
        P, S = self.P, self.S
        w0 = [self.hT_free, self.t_ssm_setup, self.t_cm_done]
        tb_ = None
        for pl_ in range(4):
            for gg in range(2):
                tb_ = P.op('act', lambda e, pl_=pl_, gg=gg, kt=kt: e.activation(
                    out=self.BzPad[:, pl_, :, :, gg, :], in_=self.BzC[:, kt, :, :, :], func=AF.Copy,
                    scale=self.m3[:, pl_, gg:gg + 1]), waits=w0 + [S['pad_bz_rd']] + self.t_tables)
        S['t_bz'] = tb_

    def ssm_pads_cp(self, kt):
        P, S = self.P, self.S
        w0 = [self.hT_free, self.t_ssm_setup, self.t_cm_done]
        tc_ = []
        for hh in range(2):
            for gg in range(2):
                for r in range(2):
                    for pq in range(2):
                        tc_.append(P.op('pool', lambda e, hh=hh, gg=gg, r=r, pq=pq, kt=kt: e.tensor_copy(
                            out=self.CpPad[hh][64 * gg:64 * gg + 64, pq, :, r, 2 * (2 * hh + pq) + gg, :],
                            in_=self.CpC[64 * gg:64 * gg + 64, 4 * kt + 2 * hh + pq, 1:5, r, :]),
                            waits=w0 + S['tz'] + [S['pad_cp_rd'], S['pool']]))
        S['t_cp'] = tc_
        S['pool'] = tc_[-1]

    def ssm_z(self, pr):
        P, S = self.P, self.S
        kt, pl = pr // 4, pr % 4
        uT = self.uT
        ZB = (4, 5)
        zps = [self.ps[ZB[0]], self.ps[ZB[1]]]
        tz_ = None
        for r in range(2):
            for s_ in range(4):
                tz_ = P.op('pe', lambda e, r=r, s_=s_, pl=pl, kt=kt: e.matmul(
                    zps[r][:, :], self.BzPad[:, pl, s_, r, :, :].rearrange("p g q -> p (g q)"), uT[:, kt, s_:SEQ:4],
                    start=(s_ == 0), stop=(s_ == 3)),
                    waits=([S['t_bz'], self.ps_rd[ZB[r]]] + [self.tok_u[(kt, ti)] for ti in range(5)]) if s_ == 0 else [],
                    signal=(s_ == 3))
        zs = self.ps[7]
        tzs = None
        for r in range(2):
            c0 = 32 * pl + 16 * r
            for s_ in range(4):
                tzs = P.op('pe', lambda e, r=r, s_=s_, pl=pl, kt=kt, c0=c0: e.matmul(
                    zs[:, c0:c0 + 16], self.BzPad[:, pl, s_, r, :, :].rearrange("p g q -> p (g q)"),
                    uT[:, kt, SEQ + s_:NT:4], start=(s_ == 0), stop=(s_ == 3)),
                    waits=[self.ps_rd[7], S['zs_rd']] if (s_ == 0 and r == 0) else [], signal=(s_ == 3))
        if pl == 3:
            S['pad_bz_rd'] = tzs
        S['tz_tok'][pr] = (tz_, tzs)

    def ssm_main(self, pr, mid=None):
        P = self.P
        S = self.S
        kt, pl = pr // 4, pr % 4
        w0 = [self.hT_free, self.t_ssm_setup, self.t_cm_done]
        ZB = (4, 5)
        mul, add, sub = ALU.mult, ALU.add, ALU.subtract
        zps = [self.ps[ZB[0]], self.ps[ZB[1]]]
        tz_, tzs = S['tz_tok'][pr]
        if pl == 0:
            S['t_x0c'] = P.op('act', lambda e, kt=kt: e.activation(
                out=self.XbS[:, kt % 2, :, :, :], in_=self.x0[:, 4 * kt:4 * kt + 4, :, :], func=AF.Copy),
                waits=w0 + [S['xbs_rd'][kt % 2], self.t_x0])
        ta = self.ta
        tb = pr % 2
        C, Sn = self.tabC[pr % 3], self.tabS[pr % 3]
        t_s, t_c = S['tab'][pr]
        Mre, Mim = self.Mb[tb]
        ta, tb2 = self.ta, self.tb
        rho = self.pm['rho4'][:, pr:pr + 1].to_broadcast([128, 512])
        zr, zi = zps[0], zps[1]
        free_m = S['dm_done'][tb]
        o1 = P.op('dve', lambda e: e.tensor_tensor(out=Mre, in0=zr[:, :], in1=C, op=mul), waits=w0 + [tz_, t_c] + free_m)
        o2 = P.op('dve', lambda e: e.tensor_tensor(out=tb2, in0=zi[:, :], in1=Sn, op=mul), waits=[t_s, S['dve']])
        o3 = P.op('dve', lambda e: e.tensor_tensor(out=Mim, in0=zi[:, :], in1=C, op=mul))
        o4 = P.op('dve', lambda e: e.tensor_tensor(out=ta, in0=zr[:, :], in1=Sn, op=mul), waits=[S['dve']])
        self.ps_rd[ZB[0]] = o4
        self.ps_rd[ZB[1]] = o4
        o5 = P.op('dve', lambda e: e.tensor_tensor(out=Mre, in0=Mre, in1=tb2, op=add), waits=[o1, o2])
        o6 = P.op('dve', lambda e: e.tensor_tensor(out=Mim, in0=Mim, in1=ta, op=sub), waits=[o3, o4])
        Wre, Wim, pa, pb = self.ps[0][:, :], self.ps[1][:, :], self.ps[2][:, :], self.ps[3][:, :]
        o7 = P.op('dve', lambda e: e.tensor_tensor_scan(out=Wre, data0=rho, data1=Mre, initial=0.0, op0=mul, op1=add),
                  waits=[o5, self.ps_rd[0], S['dve']])
        o8 = P.op('dve', lambda e: e.tensor_tensor_scan(out=Wim, data0=rho, data1=Mim, initial=0.0, op0=mul, op1=add),
                  waits=[o6, self.ps_rd[1]])
        S['dve'] = o8
        if pr + 1 < 16 and (pr + 1) not in S['tab']:
            self.ssm_tables(pr + 1)
        slot = pr % 6
        xre, xim = self.XbR[slot]
        d1 = P.op('dve', lambda e: e.tensor_tensor(out=pa, in0=Wre, in1=C, op=mul), waits=[o7, self.ps_rd[2]])
        d2 = P.op('dve', lambda e: e.tensor_tensor(out=tb2, in0=Wim, in1=Sn, op=mul), waits=[o8])
        q3 = P.op('dve', lambda e: e.tensor_tensor(out=xre[:, 2:513], in0=pa[:, 0:511], in1=tb2[:, 0:511], op=sub),
                  waits=[d1, d2, S['slot_rd'][slot]] + S['tz'])
        q4 = P.op('dve', lambda e, pr=pr: e.tensor_tensor(out=self.stp[:, pr, 0:1], in0=pa[:, 511:512], in1=tb2[:, 511:512], op=sub),
                  waits=[d1, d2])
        d5 = P.op('dve', lambda e: e.tensor_tensor(out=pb, in0=Wim, in1=C, op=mul), waits=[o8, self.ps_rd[3]])
        d6 = P.op('dve', lambda e: e.tensor_tensor(out=ta, in0=Wre, in1=Sn, op=mul), waits=[o7, q4])
        q7 = P.op('dve', lambda e: e.tensor_tensor(out=xim[:, 2:513], in0=pb[:, 0:511], in1=ta[:, 0:511], op=add),
                  waits=[d5, d6, S['slot_rd'][slot]] + S['tz'])
        q8 = P.op('dve', lambda e, pr=pr: e.tensor_tensor(out=self.stp[:, pr, 1:2], in0=pb[:, 511:512], in1=ta[:, 511:512], op=add),
                  waits=[d5, d6])
        for b_ in range(4):
            self.ps_rd[b_] = q8
        S['dmc'] = q8
        S['dm_done'][tb] = [q8]
        S['dm_tok'][pr] = [q8]
        S['xf_rd'] = [q4, q8]
        S['dve'] = q8
        S['pend'].append((q3, q7, S['t_x0c']))

    def ssm_sample(self, kt):
        P, S = self.P, self.S
        mul, add, sub = ALU.mult, ALU.add, ALU.subtract
        ta = self.ta
        zs = self.ps[7]
        tzs = S['tz_tok'][4 * kt + 3][1]
        zv = zs[:, 0:128].rearrange("p (l r b) -> p l r b", l=4, r=2)
        x0r, x0i = self.x0[:, 4 * kt:4 * kt + 4, 0, :], self.x0[:, 4 * kt:4 * kt + 4, 1, :]
        bcl = lambda ap: ap[:, 4 * kt:4 * kt + 4].unsqueeze(2).to_broadcast([128, 4, 16])
        L4r, L4i = bcl(self.pm['L4r'][:]), bcl(self.pm['L4i'][:])
        q = [ta[:, 64 * i:64 * i + 64].rearrange("p (l b) -> p l b", l=4) for i in range(4)]
        d = [S['dve']]

        def V(fn, extra=()):
            d[0] = P.op('dve', fn, waits=[d[0]] + list(extra))
            return d[0]
        q = [self.pA[:, 64 * i:64 * i + 64].rearrange("p (l b) -> p l b", l=4) for i in range(4)]
        pp = [S['pool']]

        def G(fn, extra=()):
            pp[0] = P.op('pool', fn, waits=[pp[0]] + list(extra))
            return pp[0]
        G(lambda e: e.tensor_tensor(out=q[0], in0=x0r, in1=L4r, op=mul), [S['t_x0c'], S['zs_rd'], self.t_tmp_free])
        G(lambda e: e.tensor_tensor(out=q[1], in0=x0i, in1=L4i, op=mul))
        G(lambda e: e.tensor_tensor(out=q[2], in0=x0i, in1=L4r, op=mul))
        G(lambda e: e.tensor_tensor(out=q[3], in0=x0r, in1=L4i, op=mul))
        G(lambda e: e.tensor_tensor(out=q[0], in0=q[0], in1=q[1], op=sub))
        t_pq = G(lambda e: e.tensor_tensor(out=q[2], in0=q[2], in1=q[3], op=add))
        S['pool'] = t_pq
        V(lambda e: e.tensor_tensor(out=x0r, in0=zv[:, :, 0, :], in1=q[0], op=add), [tzs, t_pq])
        t_zs = V(lambda e: e.tensor_tensor(out=x0i, in0=zv[:, :, 1, :], in1=q[2], op=add))
        S['zs_rd'] = t_zs
        S['dve'] = t_zs

    def ssm_y(self, kt, ls, last):
        P, S = self.P, self.S
        uT = self.uT
        if ls[0] == 3:
            S['y_xb'] = [t for tpl in S['pend'] for t in tpl]
            S['pend'] = []
        xb_toks = S['y_xb']
        yb = 6
        ys = self.ps[7]
        t_ys = None
        for l in ls:
            yps = self.ps[yb]
            i = 0
            for pl in range(4):
                for r in range(2):
                    P.op('pe', lambda e, l=l, pl=pl, r=r, i=i, yps=yps, kt=kt: e.matmul(
                        yps[:, :], self.CpPad[pl // 2][:, pl % 2, l, r, :, :].rearrange("p g c -> p (g c)"),
                        self.XbR[(4 * kt + pl) % 6][r][:, 1:513], start=(i == 0), stop=False),
                        waits=(xb_toks + S['t_cp'] + [self.ps_rd[yb]]) if i == 0 else [], signal=False)
                    i += 1
            for s_ in range(l + 1):
                t_y = P.op('pe', lambda e, l=l, s_=s_, kt=kt, yps=yps: e.matmul(
                    yps[:, :], self.Kin[:, kt, l - s_, :], uT[:, kt, s_:SEQ:4], start=False, stop=(s_ == l)),
                    signal=(s_ == l))
            i = 0
            for pl in range(4):
                for r in range(2):
                    P.op('pe', lambda e, l=l, pl=pl, r=r, i=i, kt=kt: e.matmul(
                        ys[:, 128 + 16 * l:144 + 16 * l], self.CpPad[pl // 2][:, pl % 2, l, r, :, :].rearrange("p g c -> p (g c)"),
                        self.XbS[:, kt % 2, pl, r, :], start=(i == 0), stop=False),
                        waits=[self.ps_rd[7], S['zs_rd']] if i == 0 else [], signal=False)
                    i += 1
            for s_ in range(l + 1):
                t_ys = P.op('pe', lambda e, l=l, s_=s_, kt=kt: e.matmul(
                    ys[:, 128 + 16 * l:144 + 16 * l], self.Kin[:, kt, l - s_, :], uT[:, kt, SEQ + s_:NT:4],
                    start=False, stop=(s_ == l)), signal=(s_ == l))
            self._gelu(yps[:, :], uT[:, kt, l:SEQ:4], self.ta[:, 0:512], [t_y])
            self.ps_rd[yb] = S['gel']
        if not last:
            return
        S['pad_cp_rd'] = t_ys
        S['xb_rd'] = t_ys
        for pl in range(4):
            S['slot_rd'][(4 * kt + pl) % 6] = t_ys
        S['xbs_rd'][kt % 2] = t_ys
        v3 = lambda ap: ap.rearrange("p (t b) -> p t b", t=4)
        self._gelu(v3(ys[:, 128:192]), uT[:, kt, SEQ:NT].rearrange("p (b t) -> p t b", t=4), v3(self.ta[:, 0:64]), [t_ys])
        self.ps_rd[7] = S['gel']
        self.ssm_y_tok[kt] = S['gel']

    def _gelu(self, src, dst, a, waits):
        P, S = self.P, self.S
        t5 = P.op('act', lambda e: e.activation(out=dst, in_=src, func=AF.Gelu_apprx_tanh), waits=list(waits))
        S['gel'] = t5

    def ssm_glu(self):
        P, S, d = self.P, self.S, self.dram
        uT = self.uT
        for i in self.cp_slots:
            self.ws_release(i, S['pad_cp_rd'])
        self.ws_release(self.tmp_slot, S['dmp'])
        wv, t_w, wi = self.ws_load(d["wglu"], lambda t: t[:, 0:2048].rearrange("p (k n) -> p k n", k=4))
        gt = self.glu_tmp
        prod = None
        t_g = None
        for ti, (t0, n) in enumerate(TT):
            ta_ = []
            for oc in range(4):
                b = 4 + oc
                gps = self.ps[b]
                for kt in range(4):
                    t_g = P.op('pe', lambda e, kt=kt, oc=oc, t0=t0, n=n, gps=gps: e.matmul(
                        gps[:, :n], wv[:, kt, oc * 128:(oc + 1) * 128], uT[:, kt, t0:t0 + n], start=(kt == 0), stop=(kt == 3)),
                        waits=([t_w, self.ps_rd[b]] + [self.ssm_y_tok[k] for k in range(4)]) if kt == 0 else [],
                        signal=(kt == 3))
                t_a = P.op('act', lambda e, oc=oc, n=n, gps=gps: e.activation(
                    out=gt[:, oc, :n], in_=gps[:, :n], func=AF.Sigmoid, bias=self.bgl[:, oc:oc + 1], scale=1.0),
                    waits=[t_g, prod, S['dve']])
                self.ps_rd[b] = t_a
                ta_.append(t_a)
            for oc in range(4):
                prod = P.op('dve', lambda e, oc=oc, t0=t0, n=n: e.tensor_tensor(
                    out=uT[:, oc, t0:t0 + n], in0=uT[:, oc, t0:t0 + n], in1=gt[:, oc, :n], op=ALU.mult),
                    waits=ta_ + [t_g])
                self.ssm_tok[(oc, ti)] = prod
        self.ws_release(wi, t_g)
        self.ssmT = uT
        s_o = self.slot("d_st")
        self.out_toks.append(P.dma('sp', lambda e: e.dma_start(out=d["st_p"], in_=self.stp[:]), s_o, waits=list(S['xf_rd'])))
        s_o2 = self.slot("d_st2")
        self.out_toks.append(P.dma('sp', lambda e: e.dma_start(out=d["st_s"], in_=self.x0[:]), s_o2, waits=[S['zs_rd'], S['xb_rd']]))


def _layout(inputs):
    f32 = np.float32
    xp = np.asarray(inputs["x_prompt"], f32)
    xs = np.asarray(inputs["x_sample"], f32)
    shared = {}
    norms = np.stack([inputs["ffn1_norm"][0], inputs["mix_norm"][0], inputs["ffn2_norm"][0], inputs["final_norm"]], 0)
    shared["norms"] = np.ascontiguousarray(np.asarray(norms, f32).reshape(4, KC, 128).transpose(2, 0, 1))
    for f, pre in ((1, "ffn1"), (2, "ffn2")):
        wg = np.asarray(inputs[pre + "_w_gate"][0], f32).reshape(KC, 128, NJ, 128)
        wu = np.asarray(inputs[pre + "_w_up"][0], f32).reshape(KC, 128, NJ, 128)
        wgu = np.stack([wg, wu], 0)
        shared["wgu%d" % f] = np.ascontiguousarray(wgu.transpose(3, 2, 0, 1, 4))
        wd = np.asarray(inputs[pre + "_w_down"][0], f32).reshape(2, NJH, 128, KC, 128)
        shared["wd%d" % f] = np.ascontiguousarray(wd.transpose(0, 3, 2, 1, 4))
    w_in = np.asarray(inputs["w_in"][0], f32)
    qperm = w_in[:, :512].reshape(D, 2, 4, 64).transpose(0, 2, 1, 3).reshape(D, 512)
    winp = np.concatenate([qperm, w_in[:, 512:]], 1)
    shared["win"] = np.ascontiguousarray(winp.reshape(KC, 128, 1280).transpose(1, 0, 2))
    w_out = np.asarray(inputs["w_out"][0], f32)
    wa = w_out[:512].reshape(2, 4, 64, D).transpose(1, 0, 2, 3).reshape(4, 128, D)
    wperm = np.concatenate([wa, w_out[512:].reshape(4, 128, D)], 0)
    shared["wout"] = np.ascontiguousarray(wperm.reshape(8, 128, KC, 128).transpose(2, 1, 0, 3))
    shared["cst_ident"] = np.ascontiguousarray(np.eye(128, dtype=f32)[::-1])
    dd = np.arange(256)
    dfl = np.maximum(dd, 1).astype(f32)
    large = 16 + (np.log(dfl / f32(16)) / f32(np.log(128 / 16)) * f32(16)).astype(np.int32)
    bucket = np.where(dd < 16, dd, np.minimum(large, 31))
    oh = np.zeros((32, 256), f32)
    oh[bucket, dd] = 1.0
    shared["cst_oh"] = oh
    bi = np.arange(64) // 4
    shared["cst_blk"] = np.ascontiguousarray(np.where(bi[:, None] == bi[None, :], 0.0, -30000.0).astype(f32)[::-1])
    shared["rel_bias"] = np.asarray(inputs["rel_bias"], f32)
    shared["sinks"] = np.asarray(inputs["sinks"], f32).reshape(1, 8)
    a_re = np.asarray(inputs["a_re"][0], f32); a_im = np.asarray(inputs["a_im"][0], f32)
    ldt = np.broadcast_to(np.asarray(inputs["log_dt"][0], f32)[:, None], (32, 64))
    pm = lambda a: a.reshape(16, 2, 64).transpose(1, 2, 0).reshape(128, 16)
    shared["ssm_pm"] = np.ascontiguousarray(np.stack([pm(a_re), pm(a_im), pm(ldt)], 1))
    cm = lambda a: np.broadcast_to(a.reshape(4, 8, 1, 64).transpose(1, 2, 0, 3), (8, 16, 4, 64)).reshape(128, 4, 64)
    shared["ssm_cm"] = np.ascontiguousarray(np.stack([cm(a_re), cm(a_im), cm(ldt)], 1))
    cpm = lambda a: a.reshape(16, 2, 16, 64).transpose(1, 3, 0, 2).reshape(128, 16, 16)
    shared["c_pm"] = np.ascontiguousarray(np.stack([cpm(np.asarray(inputs["c_re"][0], f32)), cpm(np.asarray(inputs["c_im"][0], f32))], 1))
    bpm = lambda a: a.reshape(16, 2, 64, 16).transpose(1, 2, 0, 3).reshape(128, 16, 16)
    bcm = lambda a: a.reshape(4, 8, 64, 16).transpose(1, 3, 0, 2).reshape(128, 4, 64)
    b_re = np.asarray(inputs["b_re"][0], f32); b_im = np.asarray(inputs["b_im"][0], f32)
    shared["b_pm"] = np.ascontiguousarray(np.stack([bpm(b_re), bpm(b_im)], 1))
    shared["b_cm"] = np.ascontiguousarray(np.stack([bcm(b_re), bcm(b_im)], 1))
    shared["dskip"] = np.ascontiguousarray(np.asarray(inputs["d_skip"][0], f32).reshape(4, 128).T)
    shared["bglu"] = np.ascontiguousarray(np.asarray(inputs["b_glu"][0], f32).reshape(4, 128).T)
    shared["wglu"] = np.ascontiguousarray(np.asarray(inputs["w_glu"][0], f32).reshape(4, 128, 512).transpose(1, 0, 2))
    pidx = np.arange(128)
    m2 = np.zeros((128, 4, 8), f32); m3 = np.zeros((128, 4, 2), f32)
    for pl in range(4):
        for gg in range(2):
            m2[pidx // 64 == gg, pl, 2 * pl + gg] = 1.0
            m3[pidx // 16 == 2 * pl + gg, pl, gg] = 1.0
    shared["cst_m2"] = m2
    shared["cst_m3"] = m3
    shared["cst_eye"] = np.eye(128, dtype=f32)
    sre = np.asarray(inputs["state_ssm_re"][0], f32); sim = np.asarray(inputs["state_ssm_im"][0], f32)
    ck = np.asarray(inputs["cache_k"][0], f32)
    cv = np.asarray(inputs["cache_v"][0], f32)
    maps = []
    for c in range(NCORES):
        X = np.concatenate([xp[c], xs[16 * c:16 * c + 16].reshape(NS, D)], 0)
        m = dict(shared)
        m["xT"] = np.ascontiguousarray(X.T.reshape(KC, 128, NT).transpose(1, 0, 2))
        ckc, cvc = ck[16 * c:16 * c + 16], cv[16 * c:16 * c + 16]
        m["cKT"] = np.ascontiguousarray(ckc.transpose(2, 3, 0, 1).reshape(128, 16, 128))
        m["cV"] = np.ascontiguousarray(cvc.transpose(1, 0, 2, 3).reshape(128, 16, 128))
        m["cK_nat"] = np.ascontiguousarray(ckc.reshape(16, 128, 128))
        m["cV_nat"] = np.ascontiguousarray(cvc.reshape(16, 128, 128))
        x0 = np.stack([sre[16 * c:16 * c + 16], sim[16 * c:16 * c + 16]], 0)
        m["x0"] = np.ascontiguousarray(x0.reshape(2, 16, 16, 2, 64).transpose(3, 4, 2, 0, 1).reshape(128, 16, 2, 16))
        maps.append(m)
    return maps


_NC_CACHE = {}


def _get_nc(debug=None):
    key = repr(sorted((debug or {}).items()))
    if key not in _NC_CACHE:
        _NC_CACHE[key] = Builder(debug).build()
    return _NC_CACHE[key]


def kernel(**inputs):
    maps = _layout(inputs)
    nc = _get_nc()
    res = run_bass_kernel_spmd(nc, maps, core_ids=list(range(NCORES)))
    outs = res.results
    yp = np.zeros((8, SEQ, D), np.float32)
    ys = np.zeros((128, 4, D), np.float32)
    kp = np.zeros((1, 8, 128, 2, 64), np.float32); vp = np.zeros_like(kp)
    rp = np.zeros((1, 8, 32, 64), np.float32); ip = np.zeros_like(rp)
    ks = np.zeros((1, 128, 128, 2, 64), np.float32); vs = np.zeros_like(ks)
    rs = np.zeros((1, 128, 32, 64), np.float32); is_ = np.zeros_like(rs)
    for c in range(NCORES):
        o = outs[c]
        Y = np.asarray(o["yT"]).transpose(1, 0, 2).reshape(D, NT).T
        yp[c] = Y[:SEQ]
        ys[16 * c:16 * c + 16] = Y[SEQ:].reshape(16, 4, D)
        kp[0, c] = np.asarray(o["kp"]).reshape(128, 2, 64)
        vp[0, c] = np.asarray(o["vp"]).reshape(128, 2, 64)
        ks[0, 16 * c:16 * c + 16] = np.asarray(o["ks"]).reshape(16, 128, 2, 64)
        vs[0, 16 * c:16 * c + 16] = np.asarray(o["vs"]).reshape(16, 128, 2, 64)
        sp = np.asarray(o["st_p"]).reshape(2, 64, 16, 2).transpose(2, 0, 1, 3).reshape(32, 64, 2)
        rp[0, c], ip[0, c] = sp[..., 0], sp[..., 1]
        ss = np.asarray(o["st_s"]).reshape(2, 64, 16, 2, 16).transpose(4, 2, 0, 1, 3).reshape(16, 32, 64, 2)
        rs[0, 16 * c:16 * c + 16], is_[0, 16 * c:16 * c + 16] = ss[..., 0], ss[..., 1]
    return yp, ys, kp, vp, rp, ip, ks, vs, rs, is_
```
